# Optimizing a Trainium2 kernel written in Bass

```python
import math, functools
import jax, jax.numpy as jnp
from jax import lax
import numpy as np

D_MODEL = 1024
BATCH = 8
SEQ = 8192
DEPTH = 2

GRID_W = 64
CTX_LEN = 256
NORM_EPS = 1e-6
D_FF = 4 * D_MODEL
N_MOD = 6

SSD_HEADDIM = 64
SSD_HEADS = 16
SSD_INNER = SSD_HEADS * SSD_HEADDIM
SSD_GROUPS = 4
SSD_STATE = 128
SSD_CHUNK = 128
SSD_XBC = SSD_INNER + 2 * SSD_GROUPS * SSD_STATE
CONV_K = 3
GLA_HEADS = 8
GLA_DK = 64
GLA_DV = 128
GLA_KEY = GLA_HEADS * GLA_DK
GLA_VAL = GLA_HEADS * GLA_DV
GLA_GATE_RANK = 16
GLA_GATE_NORM = 16.0
HGRN_HEADS = 8
HGRN_DK = 128
HGRN_DV = 128
HGRN_WIDTH = HGRN_HEADS * HGRN_DV
S5_GROUP = 16
S5_GROUPS = 24
S5_WIDTH = S5_GROUPS * S5_GROUP
S5_STATE = 64
LIN_CHUNK = 64

EVEN_SPLITS = (SSD_INNER, SSD_XBC, 2 * SSD_HEADS, GLA_KEY, GLA_KEY, GLA_VAL, 2 * GLA_GATE_RANK, GLA_VAL)
ODD_SPLITS = (HGRN_WIDTH, HGRN_WIDTH, 2 * HGRN_WIDTH, HGRN_WIDTH, S5_WIDTH)
EVEN_IN = sum(EVEN_SPLITS)
ODD_IN = sum(ODD_SPLITS)
EVEN_MIX = SSD_INNER + GLA_VAL
ODD_MIX = HGRN_WIDTH + S5_WIDTH

kernel_name = 'hybrid_ssd_gla_hgrn2_s5_dit'


def _split(a, sizes):
    idx = np.cumsum(sizes)[:-1].tolist()
    return jnp.split(a, idx, axis=-1)


def rmsnorm(x, w):
    xf = x.astype(jnp.float32)
    y = xf * lax.rsqrt(jnp.mean(xf * xf, axis=-1, keepdims=True) + NORM_EPS)
    return (y * w.astype(jnp.float32)).astype(x.dtype)


def modulate(h, shift, scale):
    return h * (1.0 + scale) + shift


def sq_relu_mlp(h, w1, w2):
    return jnp.square(jax.nn.relu(h @ w1)) @ w2


def conv_grid(u, w, b, rows):
    bn, t, ch = u.shape
    img = u.reshape(bn, rows, GRID_W, ch)
    out = lax.conv_general_dilated(img, w[:, :, None, :], window_strides=(1, 1), padding='SAME',
                                   dimension_numbers=('NHWC', 'HWIO', 'NHWC'), feature_group_count=ch)
    return out.reshape(bn, t, ch) + b


def conv_seq(u, w, b):
    ch = u.shape[-1]
    out = lax.conv_general_dilated(u, w[:, None, :], window_strides=(1,), padding='SAME',
                                   dimension_numbers=('NWC', 'WIO', 'NWC'), feature_group_count=ch)
    return out + b


def ssd_chunk_scan(xdt, la, bm, cm, s0):
    f32 = jnp.float32
    bn, t, nh, p = xdt.shape
    g, n = bm.shape[-2:]
    r = nh // g
    c = SSD_CHUNK
    nc = t // c
    xc = xdt.astype(f32).reshape(bn, nc, c, g, r, p).transpose(1, 0, 3, 4, 2, 5)
    ac = la.astype(f32).reshape(bn, nc, c, g, r).transpose(1, 0, 3, 4, 2)
    bc = bm.astype(f32).reshape(bn, nc, c, g, n).transpose(1, 0, 3, 2, 4)
    cc = cm.astype(f32).reshape(bn, nc, c, g, n).transpose(1, 0, 3, 2, 4)
    lower = jnp.tril(jnp.ones((c, c), dtype=bool))

    def step(h, inp):
        xi, ai, bi, ci = inp
        acum = jnp.cumsum(ai, axis=-1)
        seg = jnp.where(lower, acum[..., :, None] - acum[..., None, :], -jnp.inf)
        scores = jnp.einsum('bgin,bgjn->bgij', ci, bi)[:, :, None] * jnp.exp(seg)
        y = jnp.einsum('bgrij,bgrjp->bgrip', scores, xi)
        y = y + jnp.einsum('bgin,bgrpn->bgrip', ci, h) * jnp.exp(acum)[..., None]
        xw = xi * jnp.exp(acum[..., -1:] - acum)[..., None]
        h = h * jnp.exp(acum[..., -1])[..., None, None] + jnp.einsum('bgjn,bgrjp->bgrpn', bi, xw)
        return h, y

    h, y = lax.scan(step, s0.astype(f32).reshape(bn, g, r, p, n), (xc, ac, bc, cc))
    y = y.transpose(1, 0, 4, 2, 3, 5).reshape(bn, t, nh, p)
    return y, h.reshape(bn, nh, p, n)


def gla_chunk_scan(q, k, v, lg, s0):
    f32 = jnp.float32
    bn, t, nh, dk = q.shape
    dv = v.shape[-1]
    nc = t // LIN_CHUNK

    def chunks(a):
        return a.astype(f32).reshape(bn, nc, LIN_CHUNK, nh, a.shape[-1]).transpose(1, 0, 3, 2, 4)

    lower = jnp.tril(jnp.ones((LIN_CHUNK, LIN_CHUNK), dtype=bool))

    def step(s, inp):
        qi, ki, vi, gi = inp
        gcum = jnp.cumsum(gi, axis=-2)
        glast = gcum[..., -1:, :]
        q_dec = qi * jnp.exp(gcum)
        k_inv = ki * jnp.exp(-gcum)
        att = jnp.where(lower, jnp.einsum('bhid,bhjd->bhij', q_dec, k_inv), 0.0)
        o = jnp.einsum('bhij,bhjv->bhiv', att, vi) + jnp.einsum('bhid,bhdv->bhiv', q_dec, s)
        k_end = ki * jnp.exp(glast - gcum)
        s = s * jnp.exp(glast)[..., 0, :, None] + jnp.einsum('bhjd,bhjv->bhdv', k_end, vi)
        return s, o

    s, o = lax.scan(step, s0.astype(f32), (chunks(q), chunks(k), chunks(v), chunks(lg)))
    o = o.transpose(1, 0, 3, 2, 4).reshape(bn, t, nh, dv)
    return o, s


def _cplx_affine_combine(e1, e2):
    a1r, a1i, b1r, b1i = e1
    a2r, a2i, b2r, b2i = e2
    ar = a2r * a1r - a2i * a1i
    ai = a2r * a1i + a2i * a1r
    br = a2r * b1r - a2i * b1i + b2r
    bi = a2r * b1i + a2i * b1r + b2i
    return ar, ai, br, bi


def s5_dir_scan(u, state, lam_re, lam_im, bb_re, bb_im, c_re, c_im):
    t = u.shape[1]
    bu_re = jnp.einsum('gpc,btgc->btgp', bb_re, u)
    bu_im = jnp.einsum('gpc,btgc->btgp', bb_im, u)
    ar = jnp.broadcast_to(lam_re, (1, t) + lam_re.shape)
    ai = jnp.broadcast_to(lam_im, (1, t) + lam_im.shape)
    pr, pi, hr, hi = lax.associative_scan(_cplx_affine_combine, (ar, ai, bu_re, bu_im), axis=1)
    s_re, s_im = state
    hr = hr + pr * s_re[:, None] - pi * s_im[:, None]
    hi = hi + pr * s_im[:, None] + pi * s_re[:, None]
    y = jnp.einsum('gcp,btgp->btgc', c_re, hr) - jnp.einsum('gcp,btgp->btgc', c_im, hi)
    return y, (hr[:, -1], hi[:, -1])


def _bidirectional(scan_f, scan_b, ctx_f, lat_f, ctx_b, lat_b, state0):
    oc_f, sc_f = scan_f(*ctx_f, state0)
    ol_f, _ = scan_f(*lat_f, sc_f)
    oc_b, sc_b = scan_b(*[jnp.flip(a, 1) for a in ctx_b], state0)
    ol_b, _ = scan_b(*[jnp.flip(a, 1) for a in lat_b], sc_b)
    return oc_f + jnp.flip(oc_b, 1), ol_f + jnp.flip(ol_b, 1)


def mixer_ssd_gla(hl, hc, w_in, conv_w, conv_b, dt_bias, a_log, d_skip, ssd_norm_w,
                  gate_w, gate_b, gla_norm_w, w_out, need_ctx):
    f32 = jnp.float32
    bn = hl.shape[0]
    rows = hl.shape[1] // GRID_W
    neg_a = -jnp.exp(a_log.astype(f32))

    def project(h, conv):
        t = h.shape[1]
        z, xbc, dt, q, k, v, lr, r = _split(h @ w_in, EVEN_SPLITS)
        xbc = jax.nn.silu(conv(xbc))
        x, bm, cm = _split(xbc, (SSD_INNER, SSD_GROUPS * SSD_STATE, SSD_GROUPS * SSD_STATE))
        x = x.astype(f32).reshape(bn, t, SSD_HEADS, SSD_HEADDIM)
        bm = bm.reshape(bn, t, SSD_GROUPS, SSD_STATE)
        cm = cm.reshape(bn, t, SSD_GROUPS, SSD_STATE)
        dt = jax.nn.softplus(dt.astype(f32).reshape(bn, t, 2, SSD_HEADS) + dt_bias.astype(f32))
        la = dt * neg_a
        xdt = x[:, :, None] * dt[..., None]
        ssd_f = (xdt[:, :, 0], la[:, :, 0], bm, cm)
        ssd_b = (xdt[:, :, 1], la[:, :, 1], bm, cm)
        q = q.reshape(bn, t, GLA_HEADS, GLA_DK) * (GLA_DK ** -0.5)
        k = k.reshape(bn, t, GLA_HEADS, GLA_DK)
        v = v.reshape(bn, t, GLA_HEADS, GLA_DV)
        gk = jnp.einsum('btdr,drk->btdk', lr.reshape(bn, t, 2, GLA_GATE_RANK), gate_w) + gate_b
        gk = (jax.nn.log_sigmoid(gk.astype(f32)) / GLA_GATE_NORM).reshape(bn, t, 2, GLA_HEADS, GLA_DK)
        gla_f = (q, k, v, gk[:, :, 0])
        gla_b = (q, k, v, gk[:, :, 1])
        return z, x, r, ssd_f, ssd_b, gla_f, gla_b

    zl, xl, rl, sfl, sbl, gfl, gbl = project(hl, lambda u: conv_grid(u, conv_w, conv_b, rows))
    zc, xc, rc, sfc, sbc, gfc, gbc = project(hc, lambda u: conv_seq(u, conv_w[CONV_K // 2], conv_b))

    ssd_s0 = jnp.zeros((bn, SSD_HEADS, SSD_HEADDIM, SSD_STATE), f32)
    yc, yl = _bidirectional(ssd_chunk_scan, ssd_chunk_scan, sfc, sfl, sbc, sbl, ssd_s0)
    gla_s0 = jnp.zeros((bn, GLA_HEADS, GLA_DK, GLA_DV), f32)
    oc, ol = _bidirectional(gla_chunk_scan, gla_chunk_scan, gfc, gfl, gbc, gbl, gla_s0)

    def finish(y, x, z, o, r):
        t = z.shape[1]
        y = (y + d_skip.astype(f32)[:, None] * x).reshape(bn, t, SSD_INNER) * jax.nn.silu(z.astype(f32))
        y = rmsnorm(y.reshape(bn, t, SSD_GROUPS, -1), ssd_norm_w.reshape(SSD_GROUPS, -1)).reshape(bn, t, SSD_INNER)
        o = rmsnorm(o, gla_norm_w).reshape(bn, t, GLA_VAL) * jax.nn.silu(r.astype(f32))
        return jnp.concatenate([y, o], axis=-1).astype(hl.dtype) @ w_out

    out_l = finish(yl, xl, zl, ol, rl)
    out_c = finish(yc, xc, zc, oc, rc) if need_ctx else None
    return out_l, out_c


def mixer_hgrn_s5(hl, hc, w_in, lb, hgrn_norm_w, a_re, a_im, log_dt, b_re, b_im, c_re, c_im,
                  d_skip, glu_w, glu_b, w_out, need_ctx):
    f32 = jnp.float32
    bn = hl.shape[0]
    log_lb = jnp.log(lb).reshape(2, HGRN_HEADS, HGRN_DK)
    log_1mlb = jnp.log1p(-lb).reshape(2, HGRN_HEADS, HGRN_DK)
    are, aim = a_re.astype(f32), a_im.astype(f32)
    delta = jnp.exp(log_dt.astype(f32))[..., None]
    mag = jnp.exp(are * delta)
    lbar_re, lbar_im = mag * jnp.cos(aim * delta), mag * jnp.sin(aim * delta)
    den = are * are + aim * aim
    zr = ((lbar_re - 1.0) * are + lbar_im * aim) / den
    zi = (lbar_im * are - (lbar_re - 1.0) * aim) / den
    bre, bim = b_re.astype(f32), b_im.astype(f32)
    bb_re = zr[..., None] * bre - zi[..., None] * bim
    bb_im = zr[..., None] * bim + zi[..., None] * bre
    cre, cim = c_re.astype(f32), c_im.astype(f32)

    def project(h):
        t = h.shape[1]
        q, i, f, g, u = _split(h @ w_in, ODD_SPLITS)
        q = jax.nn.silu(q.astype(f32)).reshape(bn, t, HGRN_HEADS, HGRN_DK)
        v = i.astype(f32).reshape(bn, t, HGRN_HEADS, HGRN_DV)
        f = f.astype(f32).reshape(bn, t, 2, HGRN_HEADS, HGRN_DK)
        log_f = jnp.logaddexp(log_lb, log_1mlb + jax.nn.log_sigmoid(f))
        k = -jnp.expm1(log_f)
        hg_f = (q, k[:, :, 0], v, log_f[:, :, 0])
        hg_b = (q, k[:, :, 1], v, log_f[:, :, 1])
        u = u.astype(f32).reshape(bn, t, S5_GROUPS, S5_GROUP)
        return g, u, hg_f, hg_b

    gl, ul, hfl, hbl = project(hl)
    gc, uc, hfc, hbc = project(hc)

    hg_s0 = jnp.zeros((bn, HGRN_HEADS, HGRN_DK, HGRN_DV), f32)
    oc, ol = _bidirectional(gla_chunk_scan, gla_chunk_scan, hfc, hfl, hbc, hbl, hg_s0)

    s5_f = functools.partial(s5_dir_scan, lam_re=lbar_re[0], lam_im=lbar_im[0], bb_re=bb_re[0],
                             bb_im=bb_im[0], c_re=cre, c_im=cim)
    s5_b = functools.partial(s5_dir_scan, lam_re=lbar_re[1], lam_im=lbar_im[1], bb_re=bb_re[1],
                             bb_im=bb_im[1], c_re=cre, c_im=cim)
    s5_s0 = (jnp.zeros((bn, S5_GROUPS, S5_STATE), f32), jnp.zeros((bn, S5_GROUPS, S5_STATE), f32))
    yc, yl = _bidirectional(s5_f, s5_b, (uc,), (ul,), (uc,), (ul,), s5_s0)
    dsk = d_skip.astype(f32).reshape(S5_GROUPS, S5_GROUP)

    def finish(o, g, y, u):
        t = g.shape[1]
        o = rmsnorm(o, hgrn_norm_w).reshape(bn, t, HGRN_WIDTH) * jax.nn.silu(g.astype(f32))
        y = jax.nn.gelu((y + dsk * u).reshape(bn, t, S5_WIDTH))
        y = y * jax.nn.sigmoid(y @ glu_w.astype(f32) + glu_b.astype(f32))
        return jnp.concatenate([o, y], axis=-1).astype(hl.dtype) @ w_out

    out_l = finish(ol, gl, yl, ul)
    out_c = finish(oc, gc, yc, uc) if need_ctx else None
    return out_l, out_c


def setup_inputs(seed: int = 0) -> dict:
    key = jax.random.key(seed)
    ks = iter(jax.random.split(key, 40))
    f32 = jnp.float32

    def nrm(shape, s):
        return jax.random.normal(next(ks), shape, f32) * s

    def log_uniform(shape, lo, hi):
        return jax.random.uniform(next(ks), shape, f32, minval=math.log(lo), maxval=math.log(hi))

    n_even, n_odd = (DEPTH + 1) // 2, DEPTH // 2
    dt0 = jnp.exp(log_uniform((n_even, 2, SSD_HEADS), 1e-3, 1e-1))
    n_idx = jnp.arange(S5_STATE, dtype=f32)
    return {
        'x': nrm((BATCH, SEQ, D_MODEL), 1.0),
        'c': nrm((BATCH, D_MODEL), 1.0),
        'ctx': nrm((BATCH, CTX_LEN, D_MODEL), 1.0),
        'c_ctx': nrm((D_MODEL,), 1.0),
        'ada_w': nrm((DEPTH, D_MODEL, N_MOD * D_MODEL), 0.5 * D_MODEL ** -0.5),
        'ada_b': nrm((DEPTH, N_MOD * D_MODEL), 0.02),
        'norm1_w': 1.0 + nrm((DEPTH, D_MODEL), 0.02),
        'norm2_w': 1.0 + nrm((DEPTH, D_MODEL), 0.02),
        'ssd_gla_w_in': nrm((n_even, D_MODEL, EVEN_IN), D_MODEL ** -0.5),
        'ssd_conv_w': nrm((n_even, CONV_K, CONV_K, SSD_XBC), 1.0 / CONV_K),
        'ssd_conv_b': nrm((n_even, SSD_XBC), 0.02),
        'ssd_dt_bias': dt0 + jnp.log(-jnp.expm1(-dt0)),
        'ssd_a_log': jnp.log(jax.random.uniform(next(ks), (n_even, 2, SSD_HEADS), f32, minval=1.0, maxval=16.0)),
        'ssd_d': 1.0 + nrm((n_even, SSD_HEADS), 0.1),
        'ssd_norm_w': 1.0 + nrm((n_even, SSD_INNER), 0.02),
        'gla_gate_w': nrm((n_even, 2, GLA_GATE_RANK, GLA_KEY), GLA_GATE_RANK ** -0.5),
        'gla_gate_b': nrm((n_even, 2, GLA_KEY), 0.1),
        'gla_norm_w': 1.0 + nrm((n_even, GLA_DV), 0.02),
        'ssd_gla_w_out': nrm((n_even, EVEN_MIX, D_MODEL), EVEN_MIX ** -0.5),
        'hgrn_s5_w_in': nrm((n_odd, D_MODEL, ODD_IN), D_MODEL ** -0.5),
        'hgrn_lb_logits': nrm((DEPTH, 2, HGRN_WIDTH), 0.1),
        'hgrn_norm_w': 1.0 + nrm((n_odd, HGRN_DV), 0.02),
        's5_a_re': -0.5 + nrm((n_odd, 2, S5_GROUPS, S5_STATE), 0.01),
        's5_a_im': math.pi * n_idx + nrm((n_odd, 2, S5_GROUPS, S5_STATE), 0.01),
        's5_log_dt': log_uniform((n_odd, 2, S5_GROUPS), 1e-3, 1e-1),
        's5_b_re': nrm((n_odd, S5_GROUPS, S5_STATE, S5_GROUP), (2 * S5_GROUP) ** -0.5),
        's5_b_im': nrm((n_odd, S5_GROUPS, S5_STATE, S5_GROUP), (2 * S5_GROUP) ** -0.5),
        's5_c_re': nrm((n_odd, S5_GROUPS, S5_GROUP, S5_STATE), (2 * S5_STATE) ** -0.5),
        's5_c_im': nrm((n_odd, S5_GROUPS, S5_GROUP, S5_STATE), (2 * S5_STATE) ** -0.5),
        's5_d': nrm((n_odd, S5_WIDTH), 1.0),
        's5_glu_w': nrm((n_odd, S5_WIDTH, S5_WIDTH), S5_WIDTH ** -0.5),
        's5_glu_b': nrm((n_odd, S5_WIDTH), 0.02),
        'hgrn_s5_w_out': nrm((n_odd, ODD_MIX, D_MODEL), ODD_MIX ** -0.5),
        'mlp_w1': nrm((DEPTH, D_MODEL, D_FF), D_MODEL ** -0.5),
        'mlp_w2': nrm((DEPTH, D_FF, D_MODEL), 0.5 * D_FF ** -0.5),
        'final_norm_w': 1.0 + nrm((D_MODEL,), 0.02),
    }


def reference(x, c, ctx, c_ctx, ada_w, ada_b, norm1_w, norm2_w,
              ssd_gla_w_in, ssd_conv_w, ssd_conv_b, ssd_dt_bias, ssd_a_log, ssd_d, ssd_norm_w,
              gla_gate_w, gla_gate_b, gla_norm_w, ssd_gla_w_out,
              hgrn_s5_w_in, hgrn_lb_logits, hgrn_norm_w, s5_a_re, s5_a_im, s5_log_dt,
              s5_b_re, s5_b_im, s5_c_re, s5_c_im, s5_d, s5_glu_w, s5_glu_b, hgrn_s5_w_out,
              mlp_w1, mlp_w2, final_norm_w):
    p_lb = jax.nn.softmax(hgrn_lb_logits.astype(jnp.float32), axis=0)
    lb_all = jnp.cumsum(p_lb, axis=0) - p_lb[0]
    xc = ctx
    for layer in range(DEPTH):
        need_ctx = layer < DEPTH - 1
        j = layer // 2
        mod_l = (jax.nn.silu(c) @ ada_w[layer] + ada_b[layer])[:, None, :]
        mod_c = (jax.nn.silu(c_ctx) @ ada_w[layer] + ada_b[layer])[None, None, :]
        sh1l, sc1l, g1l, sh2l, sc2l, g2l = jnp.split(mod_l, N_MOD, axis=-1)
        sh1c, sc1c, g1c, sh2c, sc2c, g2c = jnp.split(mod_c, N_MOD, axis=-1)
        hl = modulate(rmsnorm(x, norm1_w[layer]), sh1l, sc1l)
        hc = modulate(rmsnorm(xc, norm1_w[layer]), sh1c, sc1c)
        if layer % 2 == 0:
            ol, oc = mixer_ssd_gla(hl, hc, ssd_gla_w_in[j], ssd_conv_w[j], ssd_conv_b[j], ssd_dt_bias[j],
                                   ssd_a_log[j], ssd_d[j], ssd_norm_w[j], gla_gate_w[j], gla_gate_b[j],
                                   gla_norm_w[j], ssd_gla_w_out[j], need_ctx)
        else:
            ol, oc = mixer_hgrn_s5(hl, hc, hgrn_s5_w_in[j], lb_all[layer], hgrn_norm_w[j], s5_a_re[j],
                                   s5_a_im[j], s5_log_dt[j], s5_b_re[j], s5_b_im[j], s5_c_re[j], s5_c_im[j],
                                   s5_d[j], s5_glu_w[j], s5_glu_b[j], hgrn_s5_w_out[j], need_ctx)
        x = x + g1l * ol
        x = x + g2l * sq_relu_mlp(modulate(rmsnorm(x, norm2_w[layer]), sh2l, sc2l), mlp_w1[layer], mlp_w2[layer])
        if need_ctx:
            xc = xc + g1c * oc
            xc = xc + g2c * sq_relu_mlp(modulate(rmsnorm(xc, norm2_w[layer]), sh2c, sc2c),
                                        mlp_w1[layer], mlp_w2[layer])
    return rmsnorm(x, final_norm_w)
```

```python
import os
import numpy as np
from contextlib import ExitStack
KDBG = int(os.environ.get('KDBG', '9'))
import concourse.bass as bass
import concourse.mybir as mybir
from concourse.bass_utils import run_bass_kernel_spmd

F32 = mybir.dt.float32
BF16 = mybir.dt.bfloat16
AF = mybir.ActivationFunctionType
ALU = mybir.AluOpType
AX = mybir.AxisListType

D = 1024
CTX = 256
EPS = 1e-6
NEG = -30000.0


class Buf:
    def __init__(self, t, name):
        self.t = t
        self.name = name
        self.w = None
        self.r = {}

    def __getitem__(self, k):
        return self.t[k]


class Sch:
    def __init__(self, nc, es):
        self.nc = nc
        self.E = {'pe': nc.tensor, 'act': nc.scalar, 'dve': nc.vector, 'pool': nc.gpsimd, 'sp': nc.sync}
        self.semobj = {}
        self.cnt = {}
        self.seen = {}
        for e in self.E:
            self.semobj[e] = es.enter_context(nc.semaphore("s_" + e))
            self.cnt[e] = 0
            self.seen[e] = {}
        self.rings = {}
        for q in ('sp', 'pool', 'act'):
            n = 12
            keys = []
            for i in range(n):
                k = ('d', q, i)
                self.semobj[k] = es.enter_context(nc.semaphore("d_%s_%d" % (q, i)))
                keys.append(k)
            self.rings[q] = {'keys': keys, 'vals': [0] * n, 'i': 0}
        self.nins = 0
        for e in self.E:
            self.E[e].sem_clear(self.semobj[e])
        for q, ring in self.rings.items():
            for k in ring['keys']:
                self.E[q].sem_clear(self.semobj[k])
        nc.all_engine_barrier()

    def _wait(self, e, tok):
        key, val = tok
        if self.seen[e].get(key, 0) >= val:
            return
        self.E[e].wait_ge(self.semobj[key], val)
        self.seen[e][key] = val

    def _deps(self, e, r, w, is_dma):
        for b in r:
            if b.w is not None:
                self._wait(e, b.w)
        for b in w:
            if b.w is not None and (is_dma or b.w[0] != e):
                self._wait(e, b.w)
            for key, val in b.r.items():
                if is_dma or key != e:
                    self._wait(e, (key, val))

    def _upd(self, tok, r, w):
        for b in r:
            b.r[tok[0]] = tok[1]
        for b in w:
            b.w = tok
            b.r = {}

    def op(self, e, fn, r=(), w=()):
        self._deps(e, r, w, False)
        ins = fn(self.E[e])
        self.cnt[e] += 1
        ins.then_inc(self.semobj[e], 1)
        self._upd((e, self.cnt[e]), r, w)
        self.nins += 1

    def dma(self, q, out, in_, r=(), w=()):
        ring = self.rings[q]
        i = ring['i']
        ring['i'] = (i + 1) % len(ring['keys'])
        key = ring['keys'][i]
        if ring['vals'][i] > 0:
            self._wait(q, (key, ring['vals'][i]))
        self._deps(q, r, w, True)
        ins = self.E[q].dma_start(out=out, in_=in_)
        ring['vals'][i] += 16
        ins.then_inc(self.semobj[key], 16)
        self._upd((key, ring['vals'][i]), r, w)
        self.nins += 1

    def barrier(self):
        toks = [(e, self.cnt[e]) for e in self.E if self.cnt[e] > 0]
        for q, ring in self.rings.items():
            for k, v in zip(ring['keys'], ring['vals']):
                if v > 0:
                    toks.append((k, v))
        for e in self.E:
            for tok in toks:
                if tok[0] != e or True:
                    if tok[0] == e:
                        continue
                    self._wait(e, tok)
        for e in self.E:
            if self.cnt[e] > 0:
                self._wait(e, (e, self.cnt[e]))


def build_nc(SEQ, depth=2, dbg=(), upto=None):
    T = CTX + SEQ
    NT = T // 128
    NCT = CTX // 128
    nc = bass.Bass("TRN2", target_bir_lowering=False)

    def din(name, shape, dt=F32):
        return nc.dram_tensor(name, list(shape), dt, kind="ExternalInput").ap()

    def dscr(name, shape, dt=F32):
        kind = "ExternalOutput" if name in dbg else "Internal"
        return nc.dram_tensor(name, list(shape), dt, kind=kind).ap()

    x_in = din("x", [SEQ, D])
    ctx_in = din("ctx", [CTX, D])
    cs_in = din("cs", [128, 8, 2])
    ada_w = din("ada_w", [2, D, 6 * D])
    ada_b = din("ada_b", [2, 6 * D])
    norm1_w = din("norm1_w", [2, D])
    norm2_w = din("norm2_w", [2, D])
    final_w = din("final_norm_w", [1, D])
    consts = din("consts", [128, 128 * 8 + 516])
    w_in0 = din("w_in0", [D, 6208])
    convw = din("convw", [128, 16 * 9])
    convb = din("convb", [128, 16])
    dtb = din("dtb", [1, 32])
    alog = din("alog", [1, 32])
    ssdd = din("ssdd", [1, 16])
    ssdnw = din("ssdnw", [1, 1024])
    wg = din("wg", [33, 1024])
    glanw = din("glanw", [1, 128])
    w_out0 = din("w_out0", [2048, D])
    mlp_w1 = din("mlp_w1", [2, D, 4096])
    mlp_w2 = din("mlp_w2", [2, 4096, D])
    if depth > 1:
        w_in1 = din("w_in1", [D, 5504])
        lbl = din("lbl", [2, 2048])
        lblT = din("lblT", [128, 2, 16])
        hgnw = din("hgnw", [1, 128])
        s5p = din("s5p", [128, 2 * 3 * 12])
        s5b = din("s5b", [128, 2 * 12 * 16])
        s5c = din("s5c", [128, 2 * 12 * 16])
        s5d = din("s5d", [128, 3])
        gluw = din("gluw", [384, 384])
        glub = din("glub", [128, 3])
        w_out1 = din("w_out1", [1408, D])
    out = nc.dram_tensor("out", [SEQ, D], F32, kind="ExternalOutput").ap()

    modv = dscr("modv", [2, 2, 6 * D])
    xres = dscr("xres", [T, D])
    xmid = dscr("xmid", [T, D])
    zs = dscr("zs", [T, D])
    rs = dscr("rs", [T, D])
    vtok = dscr("vtok", [T, D])
    TP = T + 256
    xbcT = dscr("xbcT", [2048, TP], BF16)
    dts = dscr("dts", [T, 32])
    qT = dscr("qT", [1024, T])
    kTs = [dscr("kT0", [1024, T]), dscr("kT1", [1024, T])]
    ktoks = [dscr("ktok0", [T, 1024]), dscr("ktok1", [T, 1024])]
    lgs = [dscr("lg0", [T, 1024]), dscr("lg1", [T, 1024])]
    ydir = [dscr("ydir0", [T, D]), dscr("ydir1", [T, D])]
    odir = [dscr("odir0", [T, D]), dscr("odir1", [T, D])]
    utok = dscr("utok", [T, 384])
    yT5 = [dscr("yT5_0", [384, T]), dscr("yT5_1", [384, T])]

    es = ExitStack()
    with es:
        S = Sch(nc, es)
        banks = [Buf(es.enter_context(nc.psum_tensor("bank%d" % i, [128, 512], F32)), "bank%d" % i) for i in range(8)]

        class Phase:
            def __init__(self):
                self.es = ExitStack()
                self.n = 0

            def tile(self, shape, dt=F32, name=None):
                self.n += 1
                S.uid = getattr(S, "uid", 0) + 1
                nm = "%s_%d" % (name or "t", S.uid)
                return Buf(self.es.enter_context(nc.sbuf_tensor(nm, list(shape), dt)), nm)

            def close(self):
                S.barrier()
                self.es.close()

        def load_consts(ph):
            cst = ph.tile([128, 128 * 8 + 516], F32, "cst")
            S.dma('sp', cst[:, :], consts[:, :], w=[cst])
            return cst

        C_I, C_TF, C_TB, C_SF, C_SB, C_MBF, C_MBB, C_ONE = [i * 128 for i in range(8)]
        C_ML = 1024
        C_MR = 1024 + 258

        def xsrc(layer, m):
            if layer == 0:
                if m < NCT:
                    return ctx_in[m * 128:(m + 1) * 128, :]
                return x_in[(m - NCT) * 128:(m - NCT + 1) * 128, :]
            return xres[m * 128:(m + 1) * 128, :]

        def bc_load(ph, src_row_ap, n, name):
            t = ph.tile([128, n], F32, name)
            S.dma('sp', t[:, :], src_row_ap.partition_broadcast(128), w=[t])
            return t

        def mod_consts(ph, layer, normw, k_shift, k_scale, src):
            A = bc_load(ph, modv[layer, src:src + 1, k_scale * D:(k_scale + 1) * D], D, "A")
            SH = bc_load(ph, modv[layer, src:src + 1, k_shift * D:(k_shift + 1) * D], D, "SH")
            NW = bc_load(ph, normw[layer:layer + 1, :], D, "NW")
            S.op('dve', lambda e: e.scalar_tensor_tensor(out=A[:, :], in0=A[:, :], scalar=1.0, in1=NW[:, :], op0=ALU.add, op1=ALU.mult), r=[A, NW], w=[A])
            return A, SH

        def load_weight_bf16(ph, Wb, wsrc, K, N, nblk=512):
            J = K // 128
            stg = [ph.tile([128, 2 * 512], F32, "wstg") for _ in range(3)]
            cnt = 0
            for n0 in range(0, N, nblk):
                n1 = min(N, n0 + nblk)
                nn = n1 - n0
                for j0 in range(0, J, 2):
                    j1 = min(J, j0 + 2)
                    st = stg[cnt % 3]
                    S.dma('sp' if cnt % 2 == 0 else 'act', st[:, 0:(j1 - j0) * nn].rearrange("p (j n) -> p j n", n=nn),
                          wsrc[j0 * 128:j1 * 128, n0:n1].rearrange("(j p) n -> p j n", p=128), w=[st])
                    eng = 'dve'
                    for j in range(j0, j1):
                        S.op(eng, lambda e, j=j, st=st: e.tensor_copy(out=Wb[:, j * N + n0:j * N + n1], in_=st[:, (j - j0) * nn:(j - j0 + 1) * nn]), r=[st], w=[Wb])
                    cnt += 1

        def rms_rstd(ph, xt, junk, ss, rstd, epsb, n, dscale):
            S.op('dve', lambda e: e.memset(ss[:, 0:1], 0.0), w=[ss])
            S.op('act', lambda e: e.activation(out=junk[:, 0:n], in_=xt[:, 0:n], func=AF.Square, accum_out=ss[:, 0:1]), r=[ss, xt], w=[junk, ss])
            S.op('act', lambda e: e.activation(out=rstd[:, 0:1], in_=ss[:, 0:1], func=AF.Sqrt, scale=dscale, bias=epsb[:, 0:1]), r=[ss, epsb], w=[rstd])
            S.op('dve', lambda e: e.reciprocal(out=rstd[:, 0:1], in_=rstd[:, 0:1]), r=[rstd], w=[rstd])

        ph = Phase()
        cs = ph.tile([128, 16], F32, "cs")
        S.dma('sp', cs[:, :], cs_in[:, :, :].rearrange("p j s -> p (j s)"), w=[cs])
        S.op('act', lambda e: e.activation(out=cs[:, :], in_=cs[:, :], func=AF.Silu), r=[cs], w=[cs])
        for layer in range(depth):
            adab = ph.tile([2, 6 * D], F32, "adab")
            S.dma('sp', adab[:, :], ada_b[layer:layer + 1, :].partition_broadcast(2), w=[adab])
            mv = ph.tile([2, 6 * D], F32, "mv")
            wst = [ph.tile([128, 8 * 512], F32, "adaw") for _ in range(2)]
            for nb in range(12):
                st = wst[nb % 2]
                S.dma('sp' if nb % 2 == 0 else 'act', st[:, :].rearrange("p (j n) -> p j n", n=512),
                      ada_w[layer, :, nb * 512:(nb + 1) * 512].rearrange("(j p) n -> p j n", p=128), w=[st])
                bk = banks[nb % 8]
                for j in range(8):
                    S.op('pe', lambda e, j=j, st=st, bk=bk: e.matmul(bk[0:2, 0:512], lhsT=cs[:, 2 * j:2 * j + 2], rhs=st[:, j * 512:(j + 1) * 512], start=(j == 0), stop=(j == 7)), r=[cs, st], w=[bk])
                S.op('dve', lambda e, bk=bk, nb=nb: e.tensor_tensor(out=mv[0:2, nb * 512:(nb + 1) * 512], in0=bk[0:2, 0:512], in1=adab[0:2, nb * 512:(nb + 1) * 512], op=ALU.add), r=[bk, adab], w=[mv])
            S.dma('pool', modv[layer, :, :], mv[0:2, :], r=[mv])
        ph.close()

        def proj_phase(layer, w_src, NIN, tok_groups, feat_groups, extra=None, tiles=None):
            ph = Phase()
            cst = load_consts(ph)
            NW = bc_load(ph, norm1_w[layer:layer + 1, :], D, "NW")
            Am = ph.tile([128, D], F32, "Am")
            SHm = ph.tile([128, D], F32, "SHm")
            cur = [None]

            def set_mod(src):
                if cur[0] == src:
                    return
                cur[0] = src
                S.dma('sp', Am[:, :], modv[layer, src:src + 1, D:2 * D].partition_broadcast(128), w=[Am])
                S.dma('sp', SHm[:, :], modv[layer, src:src + 1, 0:D].partition_broadcast(128), w=[SHm])
                S.op('dve', lambda e: e.scalar_tensor_tensor(out=Am[:, :], in0=Am[:, :], scalar=1.0, in1=NW[:, :], op0=ALU.add, op1=ALU.mult), r=[Am, NW], w=[Am])
            Wb = ph.tile([128, 8 * NIN], BF16, "Wb")
            load_weight_bf16(ph, Wb, w_src, D, NIN)
            epsb = ph.tile([128, 1], F32, "epsb")
            S.op('dve', lambda e: e.memset(epsb[:, :], EPS), w=[epsb])
            idb = ph.tile([128, 128], BF16, "idb")
            S.op('dve', lambda e: e.tensor_copy(out=idb[:, :], in_=cst[:, C_I:C_I + 128]), r=[cst], w=[idb])
            xts = [ph.tile([128, D], F32, "xt") for _ in range(2)]
            junk = ph.tile([128, D], BF16, "junk")
            tmp = ph.tile([128, D], F32, "tmp")
            hb = ph.tile([128, D], BF16, "hb")
            hTs = [ph.tile([128, D], BF16, "hT") for _ in range(2)]
            ss = ph.tile([128, 1], F32, "ss")
            rstd = ph.tile([128, 1], F32, "rstd")
            stg32 = [ph.tile([128, 512], F32, "stg32") for _ in range(4)]
            stg16 = [ph.tile([128, 512], BF16, "stg16") for _ in range(3)]
            ctxp = dict(ph=ph, cst=cst, Wb=Wb, NIN=NIN, stg32=stg32, stg16=stg16, k32=[0], k16=[0])
            if extra is not None:
                extra(ctxp)
            bi = [0]

            def nextbank():
                b = banks[bi[0] % 8]
                bi[0] += 1
                return b
            for m in (tiles if tiles is not None else range(NT)):
                if KDBG < 2:
                    break
                xt = xts[m % 2]
                hT = hTs[m % 2]
                S.dma('sp', xt[:, :], xsrc(layer, m), w=[xt])
                rms_rstd(ph, xt, junk, ss, rstd, epsb, D, 1.0 / D)
                set_mod(1 if m < NCT else 0)
                A, SH = Am, SHm
                S.op('dve', lambda e, xt=xt, A=A: e.scalar_tensor_tensor(out=tmp[:, :], in0=xt[:, :], scalar=rstd[:, 0:1], in1=A[:, :], op0=ALU.mult, op1=ALU.mult), r=[xt, rstd, A], w=[tmp])
                S.op('dve', lambda e, SH=SH: e.tensor_tensor(out=hb[:, :], in0=tmp[:, :], in1=SH[:, :], op=ALU.add), r=[tmp, SH], w=[hb])
                bk = nextbank()
                bkb = bk[:, :].bitcast(BF16)
                for j in range(8):
                    S.op('pe', lambda e, j=j, bkb=bkb: e.transpose(bkb[:, j * 128:(j + 1) * 128], hb[:, j * 128:(j + 1) * 128], idb[:, :]), r=[hb, idb], w=[bk])
                S.op('act', lambda e, bkb=bkb, hT=hT: e.activation(out=hT[:, :], in_=bkb[:, 0:1024], func=AF.Copy), r=[bk], w=[hT])
                for (c0, ncols, fn) in (tok_groups if KDBG >= 3 else []):
                    for b0 in range(c0, c0 + ncols, 512):
                        n = min(512, c0 + ncols - b0)
                        bk = nextbank()
                        for j in range(8):
                            S.op('pe', lambda e, j=j, bk=bk, b0=b0, n=n, hT=hT: e.matmul(bk[:, 0:n], lhsT=hT[:, j * 128:(j + 1) * 128], rhs=Wb[:, j * NIN + b0:j * NIN + b0 + n], start=(j == 0), stop=(j == 7)), r=[hT, Wb], w=[bk])
                        fn(m, bk, b0 - c0, n)
                for (c0, nch_total, Mw, fn) in (feat_groups[:KDBG - 3] if KDBG >= 4 else []):
                    for q0 in range(0, nch_total, 4):
                        nch = min(4, nch_total - q0)
                        bk = nextbank()
                        for c in range(nch):
                            col = c0 + (q0 + c) * Mw
                            for j in range(8):
                                S.op('pe', lambda e, j=j, bk=bk, c=c, col=col, hT=hT: e.matmul(bk[0:Mw, c * 128:(c + 1) * 128], lhsT=Wb[:, j * NIN + col:j * NIN + col + Mw], rhs=hT[:, j * 128:(j + 1) * 128], start=(j == 0), stop=(j == 7)), r=[hT, Wb], w=[bk])
                        fn(m, bk, q0, nch)
            ph.close()

        def make_tok_store(ctxp, dst, dcol0, width, func=None, scale=1.0):
            stg = ctxp['stg32']
            k = ctxp['k32']

            def fn(m, bk, off, n):
                st = stg[k[0] % len(stg)]
                k[0] += 1
                if k[0] % 2 == 0:
                    S.op('act', lambda e: e.activation(out=st[:, 0:n], in_=bk[:, 0:n], func=AF.Copy), r=[bk], w=[st])
                else:
                    S.op('dve', lambda e: e.tensor_copy(out=st[:, 0:n], in_=bk[:, 0:n]), r=[bk], w=[st])
                S.dma('pool', dst[m * 128:(m + 1) * 128, dcol0 + off:dcol0 + off + n], st[:, 0:n], r=[st])
            return fn

        def make_feat_store(ctxp, dst, drow0, dt=F32, scale=None, func=AF.Copy, coloff=0):
            stg = ctxp['stg32'] if dt == F32 else ctxp['stg16']
            k = ctxp['k32'] if dt == F32 else ctxp['k16']

            def fn(m, bk, q0, nch):
                st = stg[k[0] % len(stg)]
                k[0] += 1
                if scale is None:
                    S.op('act', lambda e: e.activation(out=st[:, 0:nch * 128], in_=bk[:, 0:nch * 128], func=func), r=[bk], w=[st])
                else:
                    S.op('act', lambda e: e.activation(out=st[:, 0:nch * 128], in_=bk[:, 0:nch * 128], func=func, scale=scale), r=[bk], w=[st])
                r0 = drow0 + q0 * 128
                S.dma('pool', dst[r0:r0 + nch * 128, coloff + m * 128:coloff + (m + 1) * 128].rearrange("(c p) t -> p c t", p=128),
                      st[:, 0:nch * 128].rearrange("p (c t) -> p c t", c=nch), r=[st])
            return fn

        def phaseA0():
            tok_groups = []
            feat_groups = []

            def extra(c):
                ph = c['ph']
                tok_groups.append((0, 1024, make_tok_store(c, zs, 0, 1024)))
                tok_groups.append((3072, 32, make_tok_store(c, dts, 0, 32)))
                tok_groups.append((3616, 512, make_tok_store(c, ktoks[0], 0, 512)))
                tok_groups.append((4128, 1024, make_tok_store(c, vtok, 0, 1024)))
                tok_groups.append((5184, 1024, make_tok_store(c, rs, 0, 1024)))
                feat_groups.append((1024, 16, 128, make_feat_store(c, xbcT, 0, dt=BF16, coloff=128)))
                feat_groups.append((3104, 4, 128, make_feat_store(c, qT, 0, scale=0.125)))
                feat_groups.append((3616, 4, 128, make_feat_store(c, kTs[0], 0)))
                lrT = ph.tile([33, 128], F32, "lrT")
                S.op('dve', lambda e: e.memset(lrT[32:33, :], 1.0), w=[lrT])
                wgt = ph.tile([33, 1024], F32, "wgt")
                S.dma('sp', wgt[:, :], wg[:, :], w=[wgt])
                gst = [ph.tile([128, 512], F32, "gst") for _ in range(2)]

                def lr_fn(m, bk, q0, nch):
                    S.op('act', lambda e: e.activation(out=lrT[0:32, :], in_=bk[0:32, 0:128], func=AF.Copy), r=[bk], w=[lrT])
                    for d in range(2):
                        bg = banks[(4 + d) % 8]
                        S.op('pe', lambda e, bg=bg, d=d: e.matmul(bg[:, 0:512], lhsT=lrT[0:33, :], rhs=wgt[0:33, d * 512:(d + 1) * 512], start=True, stop=True), r=[lrT, wgt], w=[bg])
                        g = gst[d]
                        S.op('act', lambda e, bg=bg, g=g: e.activation(out=g[:, :], in_=bg[:, 0:512], func=AF.Exp, scale=-1.0), r=[bg], w=[g])
                        S.op('act', lambda e, g=g: e.activation(out=g[:, :], in_=g[:, :], func=AF.Ln, bias=1.0), r=[g], w=[g])
                        S.op('dve', lambda e, g=g: e.tensor_scalar(out=g[:, :], in0=g[:, :], scalar1=-1.0 / 16.0, scalar2=None, op0=ALU.mult), r=[g], w=[g])
                        S.dma('pool', lgs[d][m * 128:(m + 1) * 128, 0:512], g[:, :], r=[g])
                feat_groups.append((5152, 1, 32, lr_fn))
            proj_phase(0, w_in0, 6208, tok_groups, feat_groups, extra)

        def phaseB0():
            ph = Phase()
            cst = load_consts(ph)
            cw = ph.tile([128, 144], F32, "cw")
            S.dma('sp', cw[:, :], convw[:, :], w=[cw])
            cb = ph.tile([128, 16], F32, "cb")
            S.dma('sp', cb[:, :], convb[:, :], w=[cb])
            Dg = ph.tile([128, 144 * 128], BF16, "Dg")
            for i in range(144):
                S.op('dve', lambda e, i=i: e.tensor_scalar(out=Dg[:, i * 128:(i + 1) * 128], in0=cst[:, C_I:C_I + 128], scalar1=cw[:, i:i + 1], scalar2=None, op0=ALU.mult), r=[cst, cw], w=[Dg])
            idb = ph.tile([128, 128], BF16, "idb")
            S.op('dve', lambda e: e.tensor_copy(out=idb[:, :], in_=cst[:, C_I:C_I + 128]), r=[cst], w=[idb])
            mlr = ph.tile([128, 516], BF16, "mlr")
            S.op('dve', lambda e: e.tensor_copy(out=mlr[:, :], in_=cst[:, C_ML:C_ML + 516]), r=[cst], w=[mlr])
            mb4 = []
            for d in range(2):
                t = ph.tile([128, 512], F32, "mb4")
                c0 = C_MBF if d == 0 else C_MBB
                S.op('dve', lambda e, t=t, c0=c0: e.tensor_copy(out=t[:, :].rearrange("p (a i) -> p a i", a=4), in_=cst[:, c0:c0 + 128].unsqueeze(1).to_broadcast([128, 4, 128])), r=[cst], w=[t])
                mb4.append(t)
            dtbb = bc_load(ph, dtb[0:1, :], 32, "dtbb")
            nega = bc_load(ph, alog[0:1, :], 32, "nega")
            S.op('act', lambda e: e.activation(out=nega[:, :], in_=nega[:, :], func=AF.Exp), r=[nega], w=[nega])
            S.op('dve', lambda e: e.tensor_scalar(out=nega[:, :], in0=nega[:, :], scalar1=-1.0, scalar2=None, op0=ALU.mult), r=[nega], w=[nega])
            dskb = bc_load(ph, ssdd[0:1, :], 16, "dskb")
            hT = [ph.tile([128, 1024], F32, "hT") for _ in range(2)]
            hTb = [ph.tile([128, 1024], BF16, "hTb") for _ in range(2)]
            for d in range(2):
                S.op('dve', lambda e, d=d: e.memset(hT[d][:, :], 0.0), w=[hT[d]])
                S.op('dve', lambda e, d=d: e.memset(hTb[d][:, :], 0.0), w=[hTb[d]])

            def mk(shape, dt, name):
                return [ph.tile(shape, dt, name) for _ in range(2)]
            xin = mk([128, 16 * 258], BF16, "xin")
            xl1 = ph.tile([128, 16 * 258], BF16, "xl")
            xr1 = ph.tile([128, 16 * 258], BF16, "xr")
            xl = [xl1, xl1]
            xr = [xr1, xr1]
            xTt = mk([128, 1024], F32, "xTt")
            BT = mk([128, 512], BF16, "BT")
            CT = mk([128, 512], BF16, "CT")
            xtok = mk([128, 1024], F32, "xtok")
            Btok = mk([128, 512], BF16, "Btok")
            dtr = mk([128, 32], F32, "dtr")
            la = mk([128, 32], F32, "la")
            sm = mk([128, 128], F32, "sm")
            Dx1 = ph.tile([128, 2048], F32, "Dx")
            Dx = [Dx1, Dx1]
            seg = mk([128, 2048], F32, "seg")
            scT = mk([128, 2048], BF16, "scT")
            xdt = mk([128, 1024], BF16, "xdt")
            xw = mk([128, 1024], BF16, "xw")
            ysb = mk([128, 1024], F32, "ysb")
            ytmp1 = ph.tile([128, 1024], F32, "ytmp")
            ytmp = [ytmp1, ytmp1]

            def step(m, d):
                t0 = m * 128
                isctx = m < NCT
                first = (m == 0) or (m == NCT)
                last = (m == NCT - 1) or (m == NT - 1)
                X = xin[d]
                X3 = X[:, :].rearrange("p (c w) -> p c w", c=16)
                w0, w1 = 0, 258
                if first:
                    w0 = 65
                if last:
                    w1 = 193
                if first or last:
                    S.op('dve', lambda e: e.memset(X[:, :], 0.0), w=[X])
                for half in range(2):
                    S.dma('sp', X3[:, half * 8:(half + 1) * 8, w0:w1],
                          xbcT[half * 1024:(half + 1) * 1024, t0 + 63 + w0:t0 + 63 + w1].rearrange("(c p) t -> p c t", p=128), w=[X])
                if isctx:
                    srcs = {-1: X, 0: X, 1: X}
                    taps = [(0, -1), (0, 0), (0, 1)]
                else:
                    S.op('dve', lambda e: e.tensor_tensor(out=xl[d][:, :].rearrange("p (c w) -> p c w", c=16), in0=X3, in1=mlr[:, 0:258].unsqueeze(1).to_broadcast([128, 16, 258]), op=ALU.mult), r=[X, mlr], w=[xl[d]])
                    S.op('dve', lambda e: e.tensor_tensor(out=xr[d][:, :].rearrange("p (c w) -> p c w", c=16), in0=X3, in1=mlr[:, 258:516].unsqueeze(1).to_broadcast([128, 16, 258]), op=ALU.mult), r=[X, mlr], w=[xr[d]])
                    srcs = {-1: xl[d], 0: X, 1: xr[d]}
                    taps = [(dr, dc) for dr in (-1, 0, 1) for dc in (-1, 0, 1)]
                for c in range(16):
                    bk = banks[c // 4]
                    for ti, (dr, dc) in enumerate(taps):
                        sb = srcs[dc]
                        o0 = c * 258 + 65 + dr * 64 + dc
                        tap = (dr + 1) * 3 + (dc + 1)
                        S.op('pe', lambda e, bk=bk, c=c, sb=sb, o0=o0, tap=tap, ti=ti: e.matmul(bk[:, (c % 4) * 128:(c % 4 + 1) * 128], lhsT=Dg[:, (c * 9 + tap) * 128:(c * 9 + tap + 1) * 128], rhs=sb[:, o0:o0 + 128], start=(ti == 0), stop=(ti == len(taps) - 1)), r=[Dg, sb], w=[bk])
                for c in range(16):
                    bk = banks[c // 4]
                    if c < 8:
                        dst, oc = xTt[d], c
                    elif c < 12:
                        dst, oc = BT[d], c - 8
                    else:
                        dst, oc = CT[d], c - 12
                    S.op('act', lambda e, bk=bk, c=c, dst=dst, oc=oc: e.activation(out=dst[:, oc * 128:(oc + 1) * 128], in_=bk[:, (c % 4) * 128:(c % 4 + 1) * 128], func=AF.Silu, bias=cb[:, c:c + 1]), r=[bk, cb], w=[dst])
                for c in range(8):
                    bk = banks[4 + c // 4]
                    S.op('pe', lambda e, bk=bk, c=c: e.transpose(bk[:, (c % 4) * 128:(c % 4 + 1) * 128], xTt[d][:, c * 128:(c + 1) * 128], cst[:, C_I:C_I + 128]), r=[xTt[d], cst], w=[bk])
                for h2 in range(2):
                    S.op('act' if h2 == 0 else 'dve', (lambda e, h2=h2: e.activation(out=xtok[d][:, h2 * 512:(h2 + 1) * 512], in_=banks[4 + h2][:, :], func=AF.Copy)) if h2 == 0 else (lambda e, h2=h2: e.tensor_copy(out=xtok[d][:, h2 * 512:(h2 + 1) * 512], in_=banks[4 + h2][:, :])), r=[banks[4 + h2]], w=[xtok[d]])
                b6b = banks[6][:, :].bitcast(BF16)
                for g in range(4):
                    S.op('pe', lambda e, g=g: e.transpose(b6b[:, g * 128:(g + 1) * 128], BT[d][:, g * 128:(g + 1) * 128], idb[:, :]), r=[BT[d], idb], w=[banks[6]])
                S.op('dve', lambda e: e.tensor_copy(out=Btok[d][:, :], in_=b6b[:, 0:512]), r=[banks[6]], w=[Btok[d]])
                S.dma('sp', dtr[d][:, :], dts[t0:t0 + 128, :], w=[dtr[d]])
                S.op('dve', lambda e: e.tensor_tensor(out=dtr[d][:, :], in0=dtr[d][:, :], in1=dtbb[:, :], op=ALU.add), r=[dtr[d], dtbb], w=[dtr[d]])
                S.op('act', lambda e: e.activation(out=dtr[d][:, :], in_=dtr[d][:, :], func=AF.Exp), r=[dtr[d]], w=[dtr[d]])
                S.op('act', lambda e: e.activation(out=dtr[d][:, :], in_=dtr[d][:, :], func=AF.Ln, bias=1.0), r=[dtr[d]], w=[dtr[d]])
                S.op('dve', lambda e: e.tensor_tensor(out=la[d][:, :], in0=dtr[d][:, :], in1=nega[:, :], op=ALU.mult), r=[dtr[d], nega], w=[la[d]])
                lad = la[d][:, d * 16:(d + 1) * 16]
                dtd = dtr[d][:, d * 16:(d + 1) * 16]
                tri = C_TF if d == 0 else C_TB
                b7 = banks[7]
                S.op('pe', lambda e: e.matmul(b7[:, 0:16], lhsT=cst[:, tri:tri + 128], rhs=lad, start=True, stop=True), r=[cst, la[d]], w=[b7])
                S.op('pe', lambda e: e.matmul(b7[:, 16:32], lhsT=cst[:, C_ONE:C_ONE + 128], rhs=lad, start=True, stop=True), r=[cst, la[d]], w=[b7])
                s = sm[d]
                S.op('dve', lambda e: e.tensor_copy(out=s[:, 0:16], in_=b7[:, 0:16]), r=[b7], w=[s])
                S.op('act', lambda e: e.activation(out=s[:, 16:32], in_=b7[:, 0:16], func=AF.Exp), r=[b7], w=[s])
                S.op('act', lambda e: e.activation(out=s[:, 32:48], in_=b7[:, 16:32], func=AF.Exp), r=[b7], w=[s])
                S.op('dve', lambda e: e.tensor_tensor(out=s[:, 48:64], in0=b7[:, 16:32], in1=s[:, 0:16], op=ALU.subtract), r=[b7, s], w=[s])
                S.op('act', lambda e: e.activation(out=s[:, 48:64], in_=s[:, 48:64], func=AF.Exp), r=[s], w=[s])
                S.op('dve', lambda e: e.tensor_tensor(out=s[:, 64:80], in0=s[:, 48:64], in1=dtd, op=ALU.mult), r=[s, dtr[d]], w=[s])
                S.op('dve', lambda e: e.tensor_tensor(out=Dx[d][:, :].rearrange("p (h i) -> p h i", h=16), in0=cst[:, C_I:C_I + 128].unsqueeze(1).to_broadcast([128, 16, 128]), in1=s[:, 0:16].unsqueeze(2).to_broadcast([128, 16, 128]), op=ALU.mult), r=[cst, s], w=[Dx[d]])
                for q in range(4):
                    bk = banks[q]
                    S.op('pe', lambda e, bk=bk, q=q: e.matmul(bk[:, 0:512], lhsT=cst[:, C_ONE:C_ONE + 128], rhs=Dx[d][:, q * 512:(q + 1) * 512], start=True, stop=False), r=[cst, Dx[d]], w=[bk])
                    S.op('pe', lambda e, bk=bk, q=q: e.matmul(bk[:, 0:512], lhsT=cst[:, C_I:C_I + 128], rhs=mb4[d][:, :], start=False, stop=True), r=[cst, mb4[d]], w=[bk])
                    S.op('dve', lambda e, bk=bk, q=q: e.tensor_tensor(out=seg[d][:, q * 512:(q + 1) * 512].rearrange("p (h i) -> p h i", h=4), in0=bk[:, 0:512].rearrange("p (h i) -> p h i", h=4), in1=s[:, 4 * q:4 * q + 4].unsqueeze(2).to_broadcast([128, 4, 128]), op=ALU.subtract), r=[bk, s], w=[seg[d]])
                S.op('act', lambda e: e.activation(out=seg[d][:, :], in_=seg[d][:, :], func=AF.Exp), r=[seg[d]], w=[seg[d]])
                for g in range(4):
                    S.op('pe', lambda e, g=g: e.matmul(banks[6][:, g * 128:(g + 1) * 128], lhsT=BT[d][:, g * 128:(g + 1) * 128], rhs=CT[d][:, g * 128:(g + 1) * 128], start=True, stop=True), r=[BT[d], CT[d]], w=[banks[6]])
                for g in range(4):
                    S.op('dve', lambda e, g=g: e.tensor_tensor(out=scT[d][:, g * 512:(g + 1) * 512].rearrange("p (h i) -> p h i", h=4), in0=seg[d][:, g * 512:(g + 1) * 512].rearrange("p (h i) -> p h i", h=4), in1=banks[6][:, g * 128:(g + 1) * 128].unsqueeze(1).to_broadcast([128, 4, 128]), op=ALU.mult), r=[seg[d], banks[6]], w=[scT[d]])
                S.op('dve', lambda e: e.tensor_tensor(out=xdt[d][:, :].rearrange("p (h q) -> p h q", h=16), in0=xtok[d][:, :].rearrange("p (h q) -> p h q", h=16), in1=dtd.unsqueeze(2).to_broadcast([128, 16, 64]), op=ALU.mult), r=[xtok[d], dtr[d]], w=[xdt[d]])
                S.op('dve', lambda e: e.tensor_tensor(out=xw[d][:, :].rearrange("p (h q) -> p h q", h=16), in0=xtok[d][:, :].rearrange("p (h q) -> p h q", h=16), in1=s[:, 64:80].unsqueeze(2).to_broadcast([128, 16, 64]), op=ALU.mult), r=[xtok[d], s], w=[xw[d]])
                for hd in range(16):
                    bk = banks[4 + hd // 8]
                    S.op('pe', lambda e, bk=bk, hd=hd: e.matmul(bk[:, (hd % 8) * 64:(hd % 8 + 1) * 64], lhsT=scT[d][:, hd * 128:(hd + 1) * 128], rhs=xdt[d][:, hd * 64:(hd + 1) * 64], start=True, stop=True), r=[scT[d], xdt[d]], w=[bk])
                for g in range(4):
                    bk = banks[6 + g // 2]
                    S.op('pe', lambda e, bk=bk, g=g: e.matmul(bk[:, (g % 2) * 256:(g % 2 + 1) * 256], lhsT=CT[d][:, g * 128:(g + 1) * 128], rhs=hTb[d][:, g * 256:(g + 1) * 256], start=True, stop=True), r=[CT[d], hTb[d]], w=[bk])
                for h2 in range(2):
                    S.op('dve', lambda e, h2=h2: e.tensor_tensor(out=ytmp[d][:, h2 * 512:(h2 + 1) * 512].rearrange("p (h q) -> p h q", h=8), in0=banks[6 + h2][:, :].rearrange("p (h q) -> p h q", h=8), in1=s[:, 16 + 8 * h2:24 + 8 * h2].unsqueeze(2).to_broadcast([128, 8, 64]), op=ALU.mult), r=[banks[6 + h2], s], w=[ytmp[d]])
                    S.op('dve', lambda e, h2=h2: e.tensor_tensor(out=ysb[d][:, h2 * 512:(h2 + 1) * 512], in0=banks[4 + h2][:, :], in1=ytmp[d][:, h2 * 512:(h2 + 1) * 512], op=ALU.add), r=[banks[4 + h2], ytmp[d]], w=[ysb[d]])
                if d == 0:
                    S.op('dve', lambda e: e.tensor_tensor(out=ytmp[d][:, :].rearrange("p (h q) -> p h q", h=16), in0=xtok[d][:, :].rearrange("p (h q) -> p h q", h=16), in1=dskb[:, 0:16].unsqueeze(2).to_broadcast([128, 16, 64]), op=ALU.mult), r=[xtok[d], dskb], w=[ytmp[d]])
                    S.op('dve', lambda e: e.tensor_tensor(out=ysb[d][:, :], in0=ysb[d][:, :], in1=ytmp[d][:, :], op=ALU.add), r=[ysb[d], ytmp[d]], w=[ysb[d]])
                S.dma('pool', ydir[d][t0:t0 + 128, :], ysb[d][:, :], r=[ysb[d]])
                for g in range(4):
                    bk = banks[g // 2]
                    S.op('pe', lambda e, bk=bk, g=g: e.matmul(bk[:, (g % 2) * 256:(g % 2 + 1) * 256], lhsT=Btok[d][:, g * 128:(g + 1) * 128], rhs=xw[d][:, g * 256:(g + 1) * 256], start=True, stop=True), r=[Btok[d], xw[d]], w=[bk])
                S.op('dve', lambda e: e.tensor_tensor(out=hT[d][:, :].rearrange("p (h q) -> p h q", h=16), in0=hT[d][:, :].rearrange("p (h q) -> p h q", h=16), in1=s[:, 32:48].unsqueeze(2).to_broadcast([128, 16, 64]), op=ALU.mult), r=[hT[d], s], w=[hT[d]])
                for h2 in range(2):
                    S.op('dve', lambda e, h2=h2: e.tensor_tensor(out=hT[d][:, h2 * 512:(h2 + 1) * 512], in0=hT[d][:, h2 * 512:(h2 + 1) * 512], in1=banks[h2][:, :], op=ALU.add), r=[hT[d], banks[h2]], w=[hT[d]])
                S.op('act', lambda e: e.activation(out=hTb[d][:, :], in_=hT[d][:, :], func=AF.Copy), r=[hT[d]], w=[hTb[d]])

            order_f = list(range(NT))
            order_b = list(range(NCT - 1, -1, -1)) + list(range(NT - 1, NCT - 1, -1))
            for i in range(NT):
                step(order_f[i], 0)
                step(order_b[i], 1)
            ph.close()

        def gla_phase(H, dk, dv, qT_s, kT_s, ktok_s, v_s, lg_s, o_s):
            ph = Phase()
            cst = load_consts(ph)
            HK = H * dk
            HV = H * dv
            nkc = HK // 128
            hp = 128 // dk
            NCH = T // 64
            NCC = CTX // 64
            Sst = [ph.tile([128, nkc * dv], F32, "Sst") for _ in range(2)]
            Sb = [ph.tile([128, nkc * dv], BF16, "Sb") for _ in range(2)]
            for d in range(2):
                S.op('dve', lambda e, d=d: e.memset(Sst[d][:, :], 0.0), w=[Sst[d]])
                S.op('dve', lambda e, d=d: e.memset(Sb[d][:, :], 0.0), w=[Sb[d]])

            def mk(shape, dt, name):
                return [ph.tile(shape, dt, name) for _ in range(2)]
            lgt = mk([64, HK], F32, "lgt")
            kt = mk([64, HK], F32, "kt")
            vt = mk([64, HV], F32, "vt")
            vb = mk([64, HV], BF16, "vb")
            qTt = mk([128, nkc * 64], F32, "qTt")
            kTt = mk([128, nkc * 64], F32, "kTt")
            eg = mk([128, nkc * 64], F32, "eg")
            eng = mk([128, nkc * 64], F32, "eng")
            qd = [[ph.tile([128, nkc * 64], BF16, "qd") for _ in range(hp)] for _ in range(2)]
            for d in range(2):
                for sl in range(hp):
                    S.op('dve', lambda e, d=d, sl=sl: e.memset(qd[d][sl][:, :], 0.0), w=[qd[d][sl]])
            ki = mk([128, nkc * 64], BF16, "ki")
            eex = mk([64, HK], F32, "eex")
            kend = mk([64, HK], BF16, "kend")
            att = mk([64, H * 64], BF16, "att")
            osb = mk([64, HV], F32, "osb")

            def step(ci, d):
                t0 = ci * 64
                S.dma('sp', lgt[d][:, :], lg_s[d][t0:t0 + 64, 0:HK], w=[lgt[d]])
                S.dma('sp', kt[d][:, :], ktok_s[d][t0:t0 + 64, 0:HK], w=[kt[d]])
                S.dma('sp', vt[d][:, :], v_s[t0:t0 + 64, 0:HV], w=[vt[d]])
                S.dma('act', qTt[d][:, :].rearrange("p (c t) -> p c t", c=nkc), qT_s[0:HK, t0:t0 + 64].rearrange("(c p) t -> p c t", p=128), w=[qTt[d]])
                S.dma('act', kTt[d][:, :].rearrange("p (c t) -> p c t", c=nkc), kT_s[d][0:HK, t0:t0 + 64].rearrange("(c p) t -> p c t", p=128), w=[kTt[d]])
                tri = C_TF if d == 0 else C_TB
                stri = C_SF if d == 0 else C_SB
                bA = banks[0 + 4 * d]
                for kc in range(nkc):
                    S.op('pe', lambda e, kc=kc: e.matmul(bA[:, kc * 64:(kc + 1) * 64], lhsT=lgt[d][0:64, kc * 128:(kc + 1) * 128], rhs=cst[0:64, tri:tri + 64], start=True, stop=True), r=[lgt[d], cst], w=[bA])
                nb = HK // 512
                bB = [banks[1 + 4 * d], banks[2 + 4 * d]]
                for b in range(nb):
                    S.op('pe', lambda e, b=b: e.matmul(bB[b][0:64, 0:512], lhsT=cst[0:64, stri:stri + 64], rhs=lgt[d][0:64, b * 512:(b + 1) * 512], start=True, stop=True), r=[lgt[d], cst], w=[bB[b]])
                S.op('act', lambda e: e.activation(out=eg[d][:, :], in_=bA[:, 0:nkc * 64], func=AF.Exp), r=[bA], w=[eg[d]])
                S.op('act', lambda e: e.activation(out=eng[d][:, :], in_=bA[:, 0:nkc * 64], func=AF.Exp, scale=-1.0), r=[bA], w=[eng[d]])
                for sl in range(hp):
                    p0, p1 = sl * dk, (sl + 1) * dk
                    S.op('dve', lambda e, sl=sl, p0=p0, p1=p1: e.tensor_tensor(out=qd[d][sl][p0:p1, :], in0=qTt[d][p0:p1, :], in1=eg[d][p0:p1, :], op=ALU.mult), r=[qTt[d], eg[d]], w=[qd[d][sl]])
                S.op('dve', lambda e: e.tensor_tensor(out=ki[d][:, :], in0=kTt[d][:, :], in1=eng[d][:, :], op=ALU.mult), r=[kTt[d], eng[d]], w=[ki[d]])
                for b in range(nb):
                    S.op('act', lambda e, b=b: e.activation(out=eex[d][:, b * 512:(b + 1) * 512], in_=bB[b][0:64, 0:512], func=AF.Exp), r=[bB[b]], w=[eex[d]])
                S.op('dve', lambda e: e.tensor_tensor(out=kend[d][:, :], in0=kt[d][:, :], in1=eex[d][:, :], op=ALU.mult), r=[kt[d], eex[d]], w=[kend[d]])
                S.op('act', lambda e: e.activation(out=vb[d][:, :], in_=vt[d][:, :], func=AF.Copy), r=[vt[d]], w=[vb[d]])
                bC = banks[3 + 4 * d]
                for h in range(H):
                    kc, sl = divmod(h, hp)
                    S.op('pe', lambda e, h=h, kc=kc, sl=sl: e.matmul(bC[0:64, h * 64:(h + 1) * 64], lhsT=ki[d][:, kc * 64:(kc + 1) * 64], rhs=qd[d][sl][:, kc * 64:(kc + 1) * 64], start=True, stop=True), r=[ki[d], qd[d][sl]], w=[bC])
                S.op('dve', lambda e: e.tensor_tensor(out=att[d][:, :].rearrange("p (h i) -> p h i", h=H), in0=bC[0:64, 0:H * 64].rearrange("p (h i) -> p h i", h=H), in1=cst[0:64, tri:tri + 64].unsqueeze(1).to_broadcast([64, H, 64]), op=ALU.mult), r=[bC, cst], w=[att[d]])
                bO = [banks[1 + 4 * d], banks[2 + 4 * d]]
                for h in range(H):
                    bo = bO[(h * dv) // 512]
                    oc = (h * dv) % 512
                    S.op('pe', lambda e, h=h, bo=bo, oc=oc: e.matmul(bo[0:64, oc:oc + dv], lhsT=att[d][0:64, h * 64:(h + 1) * 64], rhs=vb[d][0:64, h * dv:(h + 1) * dv], start=True, stop=True), r=[att[d], vb[d]], w=[bo])
                for b in range(HV // 512):
                    S.op('act', lambda e, b=b: e.activation(out=osb[d][:, b * 512:(b + 1) * 512], in_=bO[b][0:64, 0:512], func=AF.Copy), r=[bO[b]], w=[osb[d]])
                for h in range(H):
                    kc, sl = divmod(h, hp)
                    bo = bO[(h * dv) // 512]
                    oc = (h * dv) % 512
                    S.op('pe', lambda e, h=h, bo=bo, oc=oc, kc=kc, sl=sl: e.matmul(bo[0:64, oc:oc + dv], lhsT=qd[d][sl][:, kc * 64:(kc + 1) * 64], rhs=Sb[d][:, kc * dv:(kc + 1) * dv], start=True, stop=True), r=[qd[d][sl], Sb[d]], w=[bo])
                for b in range(HV // 512):
                    S.op('dve', lambda e, b=b: e.tensor_tensor(out=osb[d][:, b * 512:(b + 1) * 512], in0=osb[d][:, b * 512:(b + 1) * 512], in1=bO[b][0:64, 0:512], op=ALU.add), r=[bO[b], osb[d]], w=[osb[d]])
                S.dma('pool', o_s[d][t0:t0 + 64, 0:HV], osb[d][:, :], r=[osb[d]])
                bS = [banks[0 + 4 * d], banks[3 + 4 * d]]
                wS = hp * dv
                for kc in range(nkc):
                    bs = bS[(kc * wS) // 512]
                    oc = (kc * wS) % 512
                    S.op('pe', lambda e, kc=kc, bs=bs, oc=oc: e.matmul(bs[:, oc:oc + wS], lhsT=kend[d][0:64, kc * 128:(kc + 1) * 128], rhs=vb[d][0:64, kc * wS:(kc + 1) * wS], start=True, stop=True), r=[kend[d], vb[d]], w=[bs])
                lastc = 63 if d == 0 else 0
                for kc in range(nkc):
                    bs = bS[(kc * wS) // 512]
                    oc = (kc * wS) % 512
                    for sl in range(hp):
                        p0, p1 = sl * dk, (sl + 1) * dk
                        S.op('dve', lambda e, kc=kc, bs=bs, oc=oc, sl=sl, p0=p0, p1=p1: e.scalar_tensor_tensor(out=Sst[d][p0:p1, kc * dv:(kc + 1) * dv], in0=Sst[d][p0:p1, kc * dv:(kc + 1) * dv], scalar=eg[d][p0:p1, kc * 64 + lastc:kc * 64 + lastc + 1], in1=bs[p0:p1, oc + sl * dv:oc + (sl + 1) * dv], op0=ALU.mult, op1=ALU.add), r=[Sst[d], eg[d], bs], w=[Sst[d]])
                S.op('act', lambda e: e.activation(out=Sb[d][:, :], in_=Sst[d][:, :], func=AF.Copy), r=[Sst[d]], w=[Sb[d]])

            order_f = list(range(NCH))
            order_b = list(range(NCC - 1, -1, -1)) + list(range(NCH - 1, NCC - 1, -1))
            for i in range(NCH):
                step(order_f[i], 0)
                step(order_b[i], 1)
            ph.close()

        def head_rms(ph, src, dstbuf, dst_ap, H, hd, nwbuf, nw_bc, gate, sq, ss, rstd, epsb, tmp):
            n = H * hd
            v3 = lambda ap: ap.rearrange("p (h q) -> p h q", h=H)
            S.op('dve', lambda e: e.tensor_tensor(out=sq[:, 0:n], in0=src[:, 0:n], in1=src[:, 0:n], op=ALU.mult), r=[src], w=[sq])
            S.op('dve', lambda e: e.tensor_reduce(out=ss[:, 0:H], in_=v3(sq[:, 0:n]), axis=AX.X, op=ALU.add), r=[sq], w=[ss])
            S.op('act', lambda e: e.activation(out=rstd[:, 0:H], in_=ss[:, 0:H], func=AF.Sqrt, scale=1.0 / hd, bias=epsb[:, 0:1]), r=[ss, epsb], w=[rstd])
            S.op('dve', lambda e: e.reciprocal(out=rstd[:, 0:H], in_=rstd[:, 0:H]), r=[rstd], w=[rstd])
            S.op('dve', lambda e: e.tensor_tensor(out=v3(tmp[:, 0:n]), in0=v3(src[:, 0:n]), in1=rstd[:, 0:H].unsqueeze(2).to_broadcast([128, H, hd]), op=ALU.mult), r=[src, rstd], w=[tmp])
            S.op('dve', lambda e: e.tensor_tensor(out=v3(tmp[:, 0:n]), in0=v3(tmp[:, 0:n]), in1=nw_bc, op=ALU.mult), r=[tmp, nwbuf], w=[tmp])
            S.op('dve', lambda e: e.tensor_tensor(out=dst_ap, in0=tmp[:, 0:n], in1=gate[:, 0:n], op=ALU.mult), r=[tmp, gate], w=[dstbuf])

        def phaseD0a():
            ph = Phase()
            cst = load_consts(ph)
            Wo = ph.tile([128, 16 * D], BF16, "Wo")
            load_weight_bf16(ph, Wo, w_out0, 2048, D)
            snw = bc_load(ph, ssdnw[0:1, :], 1024, "snw")
            gnw = bc_load(ph, glanw[0:1, :], 128, "gnw")
            g1 = [bc_load(ph, modv[0, s:s + 1, 2 * D:3 * D], D, "g1") for s in range(2)]
            epsb = ph.tile([128, 1], F32, "epsb")
            S.op('dve', lambda e: e.memset(epsb[:, :], EPS), w=[epsb])
            idb = ph.tile([128, 128], BF16, "idb")
            S.op('dve', lambda e: e.tensor_copy(out=idb[:, :], in_=cst[:, C_I:C_I + 128]), r=[cst], w=[idb])
            a = ph.tile([128, D], F32, "a")
            b = ph.tile([128, D], F32, "b")
            g = ph.tile([128, D], F32, "g")
            sq = ph.tile([128, D], F32, "sq")
            tmp = ph.tile([128, D], F32, "tmp")
            xt = ph.tile([128, D], F32, "xt")
            mix = ph.tile([128, 2048], BF16, "mix")
            mixT = ph.tile([128, 2048], BF16, "mixT")
            ss = ph.tile([128, 16], F32, "ss")
            rstd = ph.tile([128, 16], F32, "rstd")
            xo = ph.tile([128, D], F32, "xo")
            for m in range(NT):
                r0 = m * 128
                for part in range(2):
                    srcs = ydir if part == 0 else odir
                    gsrc = zs if part == 0 else rs
                    S.dma('sp', a[:, :], srcs[0][r0:r0 + 128, :], w=[a])
                    S.dma('act', b[:, :], srcs[1][r0:r0 + 128, :], w=[b])
                    S.dma('sp', g[:, :], gsrc[r0:r0 + 128, :], w=[g])
                    S.op('dve', lambda e: e.tensor_tensor(out=a[:, :], in0=a[:, :], in1=b[:, :], op=ALU.add), r=[a, b], w=[a])
                    S.op('act', lambda e: e.activation(out=g[:, :], in_=g[:, :], func=AF.Silu), r=[g], w=[g])
                    if part == 0:
                        S.op('dve', lambda e: e.tensor_tensor(out=a[:, :], in0=a[:, :], in1=g[:, :], op=ALU.mult), r=[a, g], w=[a])
                        S.op('dve', lambda e: e.tensor_tensor(out=sq[:, :], in0=a[:, :], in1=a[:, :], op=ALU.mult), r=[a], w=[sq])
                        S.op('dve', lambda e: e.tensor_reduce(out=ss[:, 0:4], in_=sq[:, :].rearrange("p (h q) -> p h q", h=4), axis=AX.X, op=ALU.add), r=[sq], w=[ss])
                        S.op('act', lambda e: e.activation(out=rstd[:, 0:4], in_=ss[:, 0:4], func=AF.Sqrt, scale=1.0 / 256, bias=epsb[:, 0:1]), r=[ss, epsb], w=[rstd])
                        S.op('dve', lambda e: e.reciprocal(out=rstd[:, 0:4], in_=rstd[:, 0:4]), r=[rstd], w=[rstd])
                        S.op('dve', lambda e: e.tensor_tensor(out=tmp[:, :].rearrange("p (h q) -> p h q", h=4), in0=a[:, :].rearrange("p (h q) -> p h q", h=4), in1=rstd[:, 0:4].unsqueeze(2).to_broadcast([128, 4, 256]), op=ALU.mult), r=[a, rstd], w=[tmp])
                        S.op('dve', lambda e: e.tensor_tensor(out=mix[:, 0:1024], in0=tmp[:, :], in1=snw[:, :], op=ALU.mult), r=[tmp, snw], w=[mix])
                    else:
                        head_rms(ph, a, mix, mix[:, 1024:2048], 8, 128, gnw, gnw[:, 0:128].unsqueeze(1).to_broadcast([128, 8, 128]), g, sq, ss, rstd, epsb, tmp)
                for q in range(4):
                    bk = banks[q]
                    bkb = bk[:, 0:256].bitcast(BF16)
                    for c in range(4):
                        cc = q * 4 + c
                        S.op('pe', lambda e, bkb=bkb, c=c, cc=cc: e.transpose(bkb[:, c * 128:(c + 1) * 128], mix[:, cc * 128:(cc + 1) * 128], idb[:, :]), r=[mix, idb], w=[bk])
                    S.op('act' if q % 2 == 0 else 'dve', (lambda e, bkb=bkb, q=q: e.activation(out=mixT[:, q * 512:(q + 1) * 512], in_=bkb[:, 0:512], func=AF.Copy)) if q % 2 == 0 else (lambda e, bkb=bkb, q=q: e.tensor_copy(out=mixT[:, q * 512:(q + 1) * 512], in_=bkb[:, 0:512])), r=[bk], w=[mixT])
                S.dma('sp', xt[:, :], xsrc(0, m), w=[xt])
                for h2 in range(2):
                    bk = banks[4 + h2]
                    for c in range(16):
                        S.op('pe', lambda e, bk=bk, c=c, h2=h2: e.matmul(bk[:, 0:512], lhsT=mixT[:, c * 128:(c + 1) * 128], rhs=Wo[:, c * D + h2 * 512:c * D + (h2 + 1) * 512], start=(c == 0), stop=(c == 15)), r=[mixT, Wo], w=[bk])
                    gg = g1[1] if m < NCT else g1[0]
                    S.op('dve', lambda e, bk=bk, h2=h2, gg=gg: e.tensor_tensor(out=xo[:, h2 * 512:(h2 + 1) * 512], in0=bk[:, 0:512], in1=gg[:, h2 * 512:(h2 + 1) * 512], op=ALU.mult), r=[bk, gg], w=[xo])
                S.op('dve', lambda e: e.tensor_tensor(out=xo[:, :], in0=xo[:, :], in1=xt[:, :], op=ALU.add), r=[xo, xt], w=[xo])
                S.dma('pool', xmid[r0:r0 + 128, :], xo[:, :], r=[xo])
            ph.close()

        def phaseMLP(layer, tiles, final):
            ph = Phase()
            cst = ph.tile([128, 128], F32, "cstI")
            S.dma('sp', cst[:, :], consts[:, 0:128], w=[cst])
            W1 = ph.tile([128, 8 * 4096], BF16, "W1")
            load_weight_bf16(ph, W1, mlp_w1[layer], D, 4096)
            W2 = ph.tile([128, 32 * D], BF16, "W2")
            load_weight_bf16(ph, W2, mlp_w2[layer], 4096, D)
            NW = bc_load(ph, norm2_w[layer:layer + 1, :], D, "NW")
            Am = ph.tile([128, D], F32, "Am")
            SHm = ph.tile([128, D], F32, "SHm")
            Gm = ph.tile([128, D], F32, "Gm")
            cur = [None]

            def set_mod(src):
                if cur[0] == src:
                    return
                cur[0] = src
                S.dma('sp', Am[:, :], modv[layer, src:src + 1, 4 * D:5 * D].partition_broadcast(128), w=[Am])
                S.dma('sp', SHm[:, :], modv[layer, src:src + 1, 3 * D:4 * D].partition_broadcast(128), w=[SHm])
                S.dma('sp', Gm[:, :], modv[layer, src:src + 1, 5 * D:6 * D].partition_broadcast(128), w=[Gm])
                S.op('dve', lambda e: e.scalar_tensor_tensor(out=Am[:, :], in0=Am[:, :], scalar=1.0, in1=NW[:, :], op0=ALU.add, op1=ALU.mult), r=[Am, NW], w=[Am])
            if final:
                fnw = bc_load(ph, final_w[0:1, :], D, "fnw")
            epsb = ph.tile([128, 1], F32, "epsb")
            S.op('dve', lambda e: e.memset(epsb[:, :], EPS), w=[epsb])
            idb = ph.tile([128, 128], BF16, "idb")
            S.op('dve', lambda e: e.tensor_copy(out=idb[:, :], in_=cst[:, C_I:C_I + 128]), r=[cst], w=[idb])
            xt = ph.tile([128, D], F32, "xt")
            junk = ph.tile([128, D], BF16, "junk")
            tmp = ph.tile([128, D], F32, "tmp")
            hb = ph.tile([128, D], BF16, "hb")
            hT = ph.tile([128, D], BF16, "hT")
            h1 = ph.tile([128, 512], F32, "h1")
            h1T = ph.tile([128, 4096], BF16, "h1T")
            xo = ph.tile([128, D], F32, "xo")
            ss = ph.tile([128, 1], F32, "ss")
            rstd = ph.tile([128, 1], F32, "rstd")
            for m in tiles:
                r0 = m * 128
                S.dma('sp', xt[:, :], xmid[r0:r0 + 128, :], w=[xt])
                rms_rstd(ph, xt, junk, ss, rstd, epsb, D, 1.0 / D)
                set_mod(1 if m < NCT else 0)
                A, SH, G2 = Am, SHm, Gm
                S.op('dve', lambda e, A=A: e.scalar_tensor_tensor(out=tmp[:, :], in0=xt[:, :], scalar=rstd[:, 0:1], in1=A[:, :], op0=ALU.mult, op1=ALU.mult), r=[xt, rstd, A], w=[tmp])
                S.op('dve', lambda e, SH=SH: e.tensor_tensor(out=hb[:, :], in0=tmp[:, :], in1=SH[:, :], op=ALU.add), r=[tmp, SH], w=[hb])
                bk = banks[0]
                bkb = bk[:, :].bitcast(BF16)
                for j in range(8):
                    S.op('pe', lambda e, j=j: e.transpose(bkb[:, j * 128:(j + 1) * 128], hb[:, j * 128:(j + 1) * 128], idb[:, :]), r=[hb, idb], w=[bk])
                S.op('act', lambda e: e.activation(out=hT[:, :], in_=bkb[:, 0:1024], func=AF.Copy), r=[bk], w=[hT])
                for q in range(8):
                    bk = banks[1 + q % 5]
                    for c in range(4):
                        fc = q * 4 + c
                        for j in range(8):
                            S.op('pe', lambda e, bk=bk, c=c, fc=fc, j=j: e.matmul(bk[:, c * 128:(c + 1) * 128], lhsT=W1[:, j * 4096 + fc * 128:j * 4096 + (fc + 1) * 128], rhs=hT[:, j * 128:(j + 1) * 128], start=(j == 0), stop=(j == 7)), r=[W1, hT], w=[bk])
                    S.op('act', lambda e, bk=bk: e.activation(out=h1[:, :], in_=bk[:, :], func=AF.Relu), r=[bk], w=[h1])
                    S.op('dve', lambda e, q=q: e.tensor_tensor(out=h1T[:, q * 512:(q + 1) * 512], in0=h1[:, :], in1=h1[:, :], op=ALU.mult), r=[h1], w=[h1T])
                for h2 in range(2):
                    bk = banks[6 + h2]
                    for fc in range(32):
                        S.op('pe', lambda e, bk=bk, fc=fc, h2=h2: e.matmul(bk[:, 0:512], lhsT=h1T[:, fc * 128:(fc + 1) * 128], rhs=W2[:, fc * D + h2 * 512:fc * D + (h2 + 1) * 512], start=(fc == 0), stop=(fc == 31)), r=[h1T, W2], w=[bk])
                    S.op('dve', lambda e, bk=bk, h2=h2, G2=G2: e.tensor_tensor(out=xo[:, h2 * 512:(h2 + 1) * 512], in0=bk[:, 0:512], in1=G2[:, h2 * 512:(h2 + 1) * 512], op=ALU.mult), r=[bk, G2], w=[xo])
                S.op('dve', lambda e: e.tensor_tensor(out=xo[:, :], in0=xo[:, :], in1=xt[:, :], op=ALU.add), r=[xo, xt], w=[xo])
                if not final:
                    S.dma('pool', xres[r0:r0 + 128, :], xo[:, :], r=[xo])
                else:
                    rms_rstd(ph, xo, junk, ss, rstd, epsb, D, 1.0 / D)
                    S.op('dve', lambda e: e.scalar_tensor_tensor(out=tmp[:, :], in0=xo[:, :], scalar=rstd[:, 0:1], in1=fnw[:, :], op0=ALU.mult, op1=ALU.mult), r=[xo, rstd, fnw], w=[tmp])
                    S.dma('pool', out[r0 - CTX:r0 - CTX + 128, :], tmp[:, :], r=[tmp])
            ph.close()


        def phaseA1():
            tok_groups = []
            feat_groups = []

            def extra(c):
                ph = c['ph']
                l0 = bc_load(ph, lbl[0:1, :], 2048, "l0")
                oml = bc_load(ph, lbl[1:2, :], 2048, "oml")
                S.op('dve', lambda e: e.tensor_tensor(out=oml[:, :], in0=oml[:, :], in1=l0[:, :], op=ALU.subtract), r=[oml, l0], w=[oml])
                S.op('act', lambda e: e.activation(out=oml[:, :], in_=oml[:, :], func=AF.Sigmoid, scale=-1.0), r=[oml], w=[oml])
                omlT = ph.tile([128, 32], F32, "omlT")
                S.dma('sp', omlT[:, :], lblT[:, :, :].rearrange("p l c -> p (l c)"), w=[omlT])
                S.op('dve', lambda e: e.tensor_tensor(out=omlT[:, 16:32], in0=omlT[:, 16:32], in1=omlT[:, 0:16], op=ALU.subtract), r=[omlT], w=[omlT])
                S.op('act', lambda e: e.activation(out=omlT[:, 16:32], in_=omlT[:, 16:32], func=AF.Sigmoid, scale=-1.0), r=[omlT], w=[omlT])
                oneb = ph.tile([128, 1], F32, "oneb")
                S.op('dve', lambda e: e.memset(oneb[:, :], 1.0), w=[oneb])
                stg32, k32 = c['stg32'], c['k32']

                def nxt():
                    st = stg32[k32[0] % len(stg32)]
                    k32[0] += 1
                    return st

                def f_tok(m, bk, off, n):
                    d = off // 1024
                    col = off % 1024
                    st = nxt()
                    S.op('act', lambda e: e.activation(out=st[:, 0:n], in_=bk[:, 0:n], func=AF.Sigmoid, scale=-1.0), r=[bk], w=[st])
                    S.op('dve', lambda e: e.tensor_tensor(out=st[:, 0:n], in0=st[:, 0:n], in1=oml[:, off:off + n], op=ALU.mult), r=[st, oml], w=[st])
                    S.dma('pool', ktoks[d][m * 128:(m + 1) * 128, col:col + n], st[:, 0:n], r=[st])
                    st2 = nxt()
                    S.op('act', lambda e: e.activation(out=st2[:, 0:n], in_=st[:, 0:n], func=AF.Ln, scale=-1.0, bias=oneb[:, 0:1]), r=[st, oneb], w=[st2])
                    S.dma('pool', lgs[d][m * 128:(m + 1) * 128, col:col + n], st2[:, 0:n], r=[st2])

                def f_feat(m, bk, q0, nch):
                    st = nxt()
                    S.op('act', lambda e: e.activation(out=st[:, 0:nch * 128], in_=bk[:, 0:nch * 128], func=AF.Sigmoid, scale=-1.0), r=[bk], w=[st])
                    for cc in range(nch):
                        ch = q0 + cc
                        S.op('dve', lambda e, cc=cc, ch=ch: e.tensor_scalar(out=st[:, cc * 128:(cc + 1) * 128], in0=st[:, cc * 128:(cc + 1) * 128], scalar1=omlT[:, 16 + ch:17 + ch], scalar2=None, op0=ALU.mult), r=[st, omlT], w=[st])
                    d = q0 // 8
                    r0 = (q0 % 8) * 128
                    S.dma('pool', kTs[d][r0:r0 + nch * 128, m * 128:(m + 1) * 128].rearrange("(c p) t -> p c t", p=128),
                          st[:, 0:nch * 128].rearrange("p (c t) -> p c t", c=nch), r=[st])
                tok_groups.append((1024, 1024, make_tok_store(c, vtok, 0, 1024)))
                tok_groups.append((2048, 2048, f_tok))
                tok_groups.append((4096, 1024, make_tok_store(c, rs, 0, 1024)))
                tok_groups.append((5120, 384, make_tok_store(c, utok, 0, 384)))
                feat_groups.append((0, 8, 128, make_feat_store(c, qT, 0, func=AF.Silu)))
                feat_groups.append((2048, 16, 128, f_feat))
            proj_phase(1, w_in1, 5504, tok_groups, feat_groups, extra)

        def phaseC1():
            ph = Phase()
            cst = load_consts(ph)
            PI = float(np.pi)
            prm = ph.tile([128, 72], F32, "prm")
            S.dma('sp', prm[:, :], s5p[:, :], w=[prm])
            bri = ph.tile([128, 384], F32, "bri")
            S.dma('sp', bri[:, :], s5b[:, :], w=[bri])
            cri = ph.tile([128, 384], F32, "cri")
            S.dma('sp', cri[:, :], s5c[:, :], w=[cri])
            dsk = ph.tile([128, 3], F32, "dsk")
            S.dma('sp', dsk[:, :], s5d[:, :], w=[dsk])
            negpi = ph.tile([128, 1], F32, "negpi")
            S.op('dve', lambda e: e.memset(negpi[:, :], -PI), w=[negpi])
            w12 = ph.tile([128, 12 * 12], F32, "w12")

            def V(i):
                return w12[:, i * 12:(i + 1) * 12]
            CX = [ph.tile([128, 12 * 128], F32, "CX") for _ in range(2)]
            bbX = ph.tile([128, 12 * 128], F32, "bbX")
            BbT = [[ph.tile([128, 12 * 128], F32, "BbT") for _ in range(2)] for _ in range(2)]
            Ecs = [[ph.tile([128, 12 * 128], F32, "E") for _ in range(2)] for _ in range(2)]
            rmag = [ph.tile([128, 12 * 128], F32, "rmag") for _ in range(2)]
            bb = ph.tile([128, 2 * 192], F32, "bb")
            tA = ph.tile([128, 12 * 128], F32, "tA")
            tB = ph.tile([128, 12 * 128], F32, "tB")

            def place(dst, src_ap3, negate=False):
                S.op('dve', lambda e: e.memset(dst[:, :], 0.0), w=[dst])
                for sc in range(12):
                    for g2 in range(2):
                        gl = (2 * sc + g2) % 8
                        p0, p1 = g2 * 64, (g2 + 1) * 64
                        if negate:
                            S.op('dve', lambda e, sc=sc, gl=gl, p0=p0, p1=p1: e.tensor_scalar(out=dst[p0:p1, sc * 128 + gl * 16:sc * 128 + gl * 16 + 16], in0=src_ap3(sc, p0, p1), scalar1=-1.0, scalar2=None, op0=ALU.mult), r=[bb, cri], w=[dst])
                        else:
                            S.op('dve', lambda e, sc=sc, gl=gl, p0=p0, p1=p1: e.tensor_copy(out=dst[p0:p1, sc * 128 + gl * 16:sc * 128 + gl * 16 + 16], in_=src_ap3(sc, p0, p1)), r=[bb, cri], w=[dst])
            place(CX[0], lambda sc, p0, p1: cri[p0:p1, sc * 16:(sc + 1) * 16])
            place(CX[1], lambda sc, p0, p1: cri[p0:p1, 192 + sc * 16:192 + (sc + 1) * 16], negate=True)

            def tt(out, a, b, op, r, w):
                S.op('dve', lambda e: e.tensor_tensor(out=out, in0=a, in1=b, op=op), r=r, w=w)

            def sin_of(out, theta, shift):
                S.op('dve', lambda e: e.tensor_scalar(out=V(10), in0=theta, scalar1=shift + PI, scalar2=None, op0=ALU.add), r=[w12], w=[w12])
                for kk in range(1, 5):
                    S.op('dve', lambda e, kk=kk: e.tensor_scalar(out=V(11), in0=V(10), scalar1=2.0 * PI * kk, scalar2=-2.0 * PI, op0=ALU.is_ge, op1=ALU.mult), r=[w12], w=[w12])
                    if kk == 1:
                        S.op('dve', lambda e: e.tensor_tensor(out=V(9), in0=V(10), in1=V(11), op=ALU.add), r=[w12], w=[w12])
                    else:
                        S.op('dve', lambda e: e.tensor_tensor(out=V(9), in0=V(9), in1=V(11), op=ALU.add), r=[w12], w=[w12])
                S.op('act', lambda e: e.activation(out=out, in_=V(9), func=AF.Sin, bias=negpi[:, 0:1]), r=[w12, negpi], w=[w12])

            for d in range(2):
                are = prm[:, d * 36:d * 36 + 12]
                aim = prm[:, d * 36 + 12:d * 36 + 24]
                ldt = prm[:, d * 36 + 24:d * 36 + 36]
                S.op('act', lambda e, ldt=ldt: e.activation(out=V(0), in_=ldt, func=AF.Exp), r=[prm], w=[w12])
                tt(V(1), are, V(0), ALU.mult, [prm, w12], [w12])
                S.op('act', lambda e: e.activation(out=V(1), in_=V(1), func=AF.Exp), r=[w12], w=[w12])
                tt(V(2), aim, V(0), ALU.mult, [prm, w12], [w12])
                sin_of(V(3), V(2), 0.0)
                sin_of(V(4), V(2), PI / 2)
                tt(V(5), V(1), V(4), ALU.mult, [w12], [w12])
                tt(V(6), V(1), V(3), ALU.mult, [w12], [w12])
                tt(V(7), are, are, ALU.mult, [prm], [w12])
                tt(V(8), aim, aim, ALU.mult, [prm], [w12])
                tt(V(7), V(7), V(8), ALU.add, [w12], [w12])
                S.op('dve', lambda e: e.reciprocal(out=V(7), in_=V(7)), r=[w12], w=[w12])
                S.op('dve', lambda e: e.tensor_scalar(out=V(5), in0=V(5), scalar1=-1.0, scalar2=None, op0=ALU.add), r=[w12], w=[w12])
                tt(V(8), V(5), are, ALU.mult, [w12, prm], [w12])
                tt(V(9), V(6), aim, ALU.mult, [w12, prm], [w12])
                tt(V(8), V(8), V(9), ALU.add, [w12], [w12])
                tt(V(8), V(8), V(7), ALU.mult, [w12], [w12])
                tt(V(9), V(6), are, ALU.mult, [w12, prm], [w12])
                tt(V(10), V(5), aim, ALU.mult, [w12, prm], [w12])
                tt(V(9), V(9), V(10), ALU.subtract, [w12], [w12])
                tt(V(9), V(9), V(7), ALU.mult, [w12], [w12])
                b3 = lambda c0: bri[:, c0:c0 + 192].rearrange("p (s c) -> p s c", s=12)
                o3 = lambda c0: bb[:, c0:c0 + 192].rearrange("p (s c) -> p s c", s=12)
                t3 = tA[:, 0:192].rearrange("p (s c) -> p s c", s=12)
                zr3 = V(8).unsqueeze(2).to_broadcast([128, 12, 16])
                zi3 = V(9).unsqueeze(2).to_broadcast([128, 12, 16])
                tt(o3(0), b3(0), zr3, ALU.mult, [bri, w12], [bb])
                tt(t3, b3(192), zi3, ALU.mult, [bri, w12], [tA])
                tt(o3(0), o3(0), t3, ALU.subtract, [bb, tA], [bb])
                tt(o3(192), b3(192), zr3, ALU.mult, [bri, w12], [bb])
                tt(t3, b3(0), zi3, ALU.mult, [bri, w12], [tA])
                tt(o3(192), o3(192), t3, ALU.add, [bb, tA], [bb])
                for ri in range(2):
                    place(bbX, lambda sc, p0, p1, ri=ri: bb[p0:p1, ri * 192 + sc * 16:ri * 192 + (sc + 1) * 16])
                    for q in range(3):
                        bk = banks[q]
                        for c4 in range(4):
                            sc = q * 4 + c4
                            S.op('pe', lambda e, bk=bk, c4=c4, sc=sc: e.transpose(bk[:, c4 * 128:(c4 + 1) * 128], bbX[:, sc * 128:(sc + 1) * 128], cst[:, C_I:C_I + 128]), r=[bbX, cst], w=[bk])
                        S.op('act', lambda e, bk=bk, q=q, ri=ri, d=d: e.activation(out=BbT[d][ri][:, q * 512:(q + 1) * 512], in_=bk[:, :], func=AF.Copy), r=[bk], w=[BbT[d][ri]])
                Ec, Es = Ecs[d]
                Ec3 = Ec[:, :].rearrange("p (s t) -> p s t", s=12)
                Es3 = Es[:, :].rearrange("p (s t) -> p s t", s=12)
                i0 = 0 if d == 0 else 127
                S.op('dve', lambda e: e.tensor_copy(out=Ec3[:, :, i0:i0 + 1], in_=V(4).unsqueeze(2)), r=[w12], w=[Ec])
                S.op('dve', lambda e: e.tensor_copy(out=Es3[:, :, i0:i0 + 1], in_=V(3).unsqueeze(2)), r=[w12], w=[Es])
                tA3 = tA[:, :].rearrange("p (s t) -> p s t", s=12)
                tB3 = tB[:, :].rearrange("p (s t) -> p s t", s=12)
                for k in range(7):
                    n = 1 << k
                    if d == 0:
                        src = slice(0, n); dst = slice(n, 2 * n); piv = n - 1
                    else:
                        src = slice(128 - n, 128); dst = slice(128 - 2 * n, 128 - n); piv = 128 - n
                    pc = Ec3[:, :, piv:piv + 1].to_broadcast([128, 12, n])
                    ps_ = Es3[:, :, piv:piv + 1].to_broadcast([128, 12, n])
                    tt(tA3[:, :, 0:n], Ec3[:, :, src], pc, ALU.mult, [Ec], [tA])
                    tt(tB3[:, :, 0:n], Es3[:, :, src], ps_, ALU.mult, [Es], [tB])
                    tt(tA3[:, :, 0:n], tA3[:, :, 0:n], tB3[:, :, 0:n], ALU.subtract, [tA, tB], [tA])
                    tt(tB3[:, :, 0:n], Ec3[:, :, src], ps_, ALU.mult, [Ec, Es], [tB])
                    tt(tA3[:, :, 64:64 + n], Es3[:, :, src], pc, ALU.mult, [Ec, Es], [tA])
                    tt(Es3[:, :, dst], tB3[:, :, 0:n], tA3[:, :, 64:64 + n], ALU.add, [tA, tB], [Es])
                    S.op('dve', lambda e, dst=dst, n=n: e.tensor_copy(out=Ec3[:, :, dst], in_=tA3[:, :, 0:n]), r=[tA], w=[Ec])
                S.op('dve', lambda e, d=d: e.tensor_copy(out=rmag[d][:, :].rearrange("p (s t) -> p s t", s=12), in_=V(1).unsqueeze(2).to_broadcast([128, 12, 128])), r=[w12], w=[rmag[d]])

            def mk(shape, dt, name):
                return [ph.tile(shape, dt, name) for _ in range(2)]
            ut = mk([128, 384], F32, "ut")
            uT = mk([128, 384], F32, "uT")
            wre = mk([128, 1536], F32, "wre")
            wim = mk([128, 1536], F32, "wim")
            zre = mk([128, 1536], F32, "zre")
            zim = mk([128, 1536], F32, "zim")
            t1 = ph.tile([128, 1536], F32, "t1")
            t2 = ph.tile([128, 1536], F32, "t2")
            xst = mk([128, 24], F32, "xst")
            ysb = mk([128, 384], F32, "ysb")
            for d in range(2):
                S.op('dve', lambda e, d=d: e.memset(xst[d][:, :], 0.0), w=[xst[d]])

            def rev(buf, c0, n):
                a = buf[:, c0:c0 + n]
                return bass.AP(a.tensor, a.offset + (n - 1), [[a.ap[0][0], 128], [-1, n]])

            def step(m, d):
                t0 = m * 128
                Ec, Es = Ecs[d]
                S.dma('sp', ut[d][:, :], utok[t0:t0 + 128, :], w=[ut[d]])
                b6 = banks[6]
                for cc in range(3):
                    S.op('pe', lambda e, cc=cc: e.transpose(b6[:, cc * 128:(cc + 1) * 128], ut[d][:, cc * 128:(cc + 1) * 128], cst[:, C_I:C_I + 128]), r=[ut[d], cst], w=[b6])
                S.op('act', lambda e: e.activation(out=uT[d][:, :], in_=b6[:, 0:384], func=AF.Copy), r=[b6], w=[uT[d]])
                for ri in range(2):
                    for sc in range(12):
                        bk = banks[ri * 3 + sc // 4]
                        cc = sc // 4
                        S.op('pe', lambda e, bk=bk, sc=sc, cc=cc, ri=ri: e.matmul(bk[:, (sc % 4) * 128:(sc % 4 + 1) * 128], lhsT=BbT[d][ri][:, sc * 128:(sc + 1) * 128], rhs=uT[d][:, cc * 128:(cc + 1) * 128], start=True, stop=True), r=[BbT[d][ri], uT[d]], w=[bk])
                for q in range(3):
                    sl = slice(q * 512, (q + 1) * 512)
                    bre, bim = banks[q], banks[3 + q]
                    S.op('dve', lambda e, sl=sl, bre=bre: e.tensor_tensor(out=t1[:, sl], in0=bre[:, :], in1=Ec[:, sl], op=ALU.mult), r=[bre, Ec], w=[t1])
                    S.op('dve', lambda e, sl=sl, bim=bim: e.tensor_tensor(out=t2[:, sl], in0=bim[:, :], in1=Es[:, sl], op=ALU.mult), r=[bim, Es], w=[t2])
                    S.op('dve', lambda e, sl=sl: e.tensor_tensor(out=wre[d][:, sl], in0=t1[:, sl], in1=t2[:, sl], op=ALU.add), r=[t1, t2], w=[wre[d]])
                    S.op('dve', lambda e, sl=sl, bim=bim: e.tensor_tensor(out=t1[:, sl], in0=bim[:, :], in1=Ec[:, sl], op=ALU.mult), r=[bim, Ec], w=[t1])
                    S.op('dve', lambda e, sl=sl, bre=bre: e.tensor_tensor(out=t2[:, sl], in0=bre[:, :], in1=Es[:, sl], op=ALU.mult), r=[bre, Es], w=[t2])
                    S.op('dve', lambda e, sl=sl: e.tensor_tensor(out=wim[d][:, sl], in0=t1[:, sl], in1=t2[:, sl], op=ALU.subtract), r=[t1, t2], w=[wim[d]])
                for sc in range(12):
                    for (wb, zb, ci) in ((wre[d], zre[d], sc), (wim[d], zim[d], 12 + sc)):
                        if d == 0:
                            S.op('dve', lambda e, wb=wb, zb=zb, ci=ci, sc=sc: e.tensor_tensor_scan(out=zb[:, sc * 128:(sc + 1) * 128], data0=rmag[d][:, sc * 128:(sc + 1) * 128], data1=wb[:, sc * 128:(sc + 1) * 128], initial=xst[d][:, ci:ci + 1], op0=ALU.mult, op1=ALU.add), r=[wb, rmag[d], xst[d]], w=[zb])
                        else:
                            S.op('dve', lambda e, wb=wb, zb=zb, ci=ci, sc=sc: e.tensor_tensor_scan(out=rev(zb, sc * 128, 128), data0=rmag[d][:, sc * 128:(sc + 1) * 128], data1=rev(wb, sc * 128, 128), initial=xst[d][:, ci:ci + 1], op0=ALU.mult, op1=ALU.add), r=[wb, rmag[d], xst[d]], w=[zb])
                for q in range(3):
                    sl = slice(q * 512, (q + 1) * 512)
                    S.op('dve', lambda e, sl=sl: e.tensor_tensor(out=t1[:, sl], in0=zre[d][:, sl], in1=Ec[:, sl], op=ALU.mult), r=[zre[d], Ec], w=[t1])
                    S.op('dve', lambda e, sl=sl: e.tensor_tensor(out=t2[:, sl], in0=zim[d][:, sl], in1=Es[:, sl], op=ALU.mult), r=[zim[d], Es], w=[t2])
                    S.op('dve', lambda e, sl=sl: e.tensor_tensor(out=wre[d][:, sl], in0=t1[:, sl], in1=t2[:, sl], op=ALU.subtract), r=[t1, t2], w=[wre[d]])
                    S.op('dve', lambda e, sl=sl: e.tensor_tensor(out=t1[:, sl], in0=zre[d][:, sl], in1=Es[:, sl], op=ALU.mult), r=[zre[d], Es], w=[t1])
                    S.op('dve', lambda e, sl=sl: e.tensor_tensor(out=t2[:, sl], in0=zim[d][:, sl], in1=Ec[:, sl], op=ALU.mult), r=[zim[d], Ec], w=[t2])
                    S.op('dve', lambda e, sl=sl: e.tensor_tensor(out=wim[d][:, sl], in0=t1[:, sl], in1=t2[:, sl], op=ALU.add), r=[t1, t2], w=[wim[d]])
                last = 127 if d == 0 else 0
                S.op('dve', lambda e: e.tensor_copy(out=xst[d][:, 0:12].unsqueeze(2), in_=wre[d][:, :].rearrange("p (s t) -> p s t", s=12)[:, :, last:last + 1]), r=[wre[d]], w=[xst[d]])
                S.op('dve', lambda e: e.tensor_copy(out=xst[d][:, 12:24].unsqueeze(2), in_=wim[d][:, :].rearrange("p (s t) -> p s t", s=12)[:, :, last:last + 1]), r=[wim[d]], w=[xst[d]])
                b7 = banks[7]
                for cc in range(3):
                    for k4 in range(4):
                        sc = cc * 4 + k4
                        S.op('pe', lambda e, cc=cc, sc=sc, k4=k4: e.matmul(b7[:, cc * 128:(cc + 1) * 128], lhsT=CX[0][:, sc * 128:(sc + 1) * 128], rhs=wre[d][:, sc * 128:(sc + 1) * 128], start=(k4 == 0), stop=False), r=[CX[0], wre[d]], w=[b7])
                        S.op('pe', lambda e, cc=cc, sc=sc, k4=k4: e.matmul(b7[:, cc * 128:(cc + 1) * 128], lhsT=CX[1][:, sc * 128:(sc + 1) * 128], rhs=wim[d][:, sc * 128:(sc + 1) * 128], start=False, stop=(k4 == 3)), r=[CX[1], wim[d]], w=[b7])
                if d == 0:
                    for cc in range(3):
                        S.op('dve', lambda e, cc=cc: e.scalar_tensor_tensor(out=ysb[d][:, cc * 128:(cc + 1) * 128], in0=uT[d][:, cc * 128:(cc + 1) * 128], scalar=dsk[:, cc:cc + 1], in1=b7[:, cc * 128:(cc + 1) * 128], op0=ALU.mult, op1=ALU.add), r=[uT[d], dsk, b7], w=[ysb[d]])
                else:
                    S.op('act', lambda e: e.activation(out=ysb[d][:, :], in_=b7[:, 0:384], func=AF.Copy), r=[b7], w=[ysb[d]])
                S.dma('pool', yT5[d][:, t0:t0 + 128].rearrange("(c p) t -> p c t", p=128), ysb[d][:, :].rearrange("p (c t) -> p c t", c=3), r=[ysb[d]])

            order_f = list(range(NT))
            order_b = list(range(NCT - 1, -1, -1)) + list(range(NT - 1, NCT - 1, -1))
            for i in range(NT):
                step(order_f[i], 0)
                step(order_b[i], 1)
            ph.close()

        def phaseD1a():
            ph = Phase()
            cst = load_consts(ph)
            Wo = ph.tile([128, 11 * D], BF16, "Wo1")
            load_weight_bf16(ph, Wo, w_out1, 1408, D)
            hnw = bc_load(ph, hgnw[0:1, :], 128, "hnw")
            g1 = bc_load(ph, modv[1, 0:1, 2 * D:3 * D], D, "g1")
            gw = ph.tile([128, 3 * 384], F32, "gw")
            S.dma('sp', gw[:, :].rearrange("p (c n) -> p c n", c=3), gluw[:, :].rearrange("(c p) n -> p c n", p=128), w=[gw])
            gb = ph.tile([128, 3], F32, "gb")
            S.dma('sp', gb[:, :], glub[:, :], w=[gb])
            epsb = ph.tile([128, 1], F32, "epsb")
            S.op('dve', lambda e: e.memset(epsb[:, :], EPS), w=[epsb])
            idb = ph.tile([128, 128], BF16, "idb")
            S.op('dve', lambda e: e.tensor_copy(out=idb[:, :], in_=cst[:, C_I:C_I + 128]), r=[cst], w=[idb])
            a = ph.tile([128, D], F32, "a")
            b = ph.tile([128, D], F32, "b")
            g = ph.tile([128, D], F32, "g")
            sq = ph.tile([128, D], F32, "sq")
            tmp = ph.tile([128, D], F32, "tmp")
            xt = ph.tile([128, D], F32, "xt")
            mix = ph.tile([128, 1024], BF16, "mix")
            mixT = ph.tile([128, 11 * 128], BF16, "mixT")
            ss = ph.tile([128, 16], F32, "ss")
            rstd = ph.tile([128, 16], F32, "rstd")
            xo = ph.tile([128, D], F32, "xo")
            ya = ph.tile([128, 384], F32, "ya")
            yb = ph.tile([128, 384], F32, "yb")
            yc = ph.tile([128, 384], F32, "yc")
            for m in lat_tiles:
                r0 = m * 128
                S.dma('sp', a[:, :], odir[0][r0:r0 + 128, :], w=[a])
                S.dma('act', b[:, :], odir[1][r0:r0 + 128, :], w=[b])
                S.dma('sp', g[:, :], rs[r0:r0 + 128, :], w=[g])
                S.op('dve', lambda e: e.tensor_tensor(out=a[:, :], in0=a[:, :], in1=b[:, :], op=ALU.add), r=[a, b], w=[a])
                S.op('act', lambda e: e.activation(out=g[:, :], in_=g[:, :], func=AF.Silu), r=[g], w=[g])
                head_rms(ph, a, mix, mix[:, 0:1024], 8, 128, hnw, hnw[:, 0:128].unsqueeze(1).to_broadcast([128, 8, 128]), g, sq, ss, rstd, epsb, tmp)
                for q in range(2):
                    bk = banks[q]
                    bkb = bk[:, 0:256].bitcast(BF16)
                    for c in range(4):
                        cc = q * 4 + c
                        S.op('pe', lambda e, bkb=bkb, c=c, cc=cc: e.transpose(bkb[:, c * 128:(c + 1) * 128], mix[:, cc * 128:(cc + 1) * 128], idb[:, :]), r=[mix, idb], w=[bk])
                    S.op('act', lambda e, bkb=bkb, q=q: e.activation(out=mixT[:, q * 512:(q + 1) * 512], in_=bkb[:, 0:512], func=AF.Copy), r=[bk], w=[mixT])
                S.dma('sp', ya[:, :].rearrange("p (c t) -> p c t", c=3), yT5[0][:, r0:r0 + 128].rearrange("(c p) t -> p c t", p=128), w=[ya])
                S.dma('act', yb[:, :].rearrange("p (c t) -> p c t", c=3), yT5[1][:, r0:r0 + 128].rearrange("(c p) t -> p c t", p=128), w=[yb])
                S.op('dve', lambda e: e.tensor_tensor(out=ya[:, :], in0=ya[:, :], in1=yb[:, :], op=ALU.add), r=[ya, yb], w=[ya])
                S.op('dve', lambda e: e.tensor_tensor(out=yb[:, :], in0=ya[:, :], in1=ya[:, :], op=ALU.mult), r=[ya], w=[yb])
                S.op('dve', lambda e: e.tensor_scalar(out=yb[:, :], in0=yb[:, :], scalar1=0.044715, scalar2=1.0, op0=ALU.mult, op1=ALU.add), r=[yb], w=[yb])
                S.op('dve', lambda e: e.tensor_tensor(out=yb[:, :], in0=yb[:, :], in1=ya[:, :], op=ALU.mult), r=[yb, ya], w=[yb])
                S.op('act', lambda e: e.activation(out=yb[:, :], in_=yb[:, :], func=AF.Tanh, scale=0.7978845608028654), r=[yb], w=[yb])
                S.op('dve', lambda e: e.scalar_tensor_tensor(out=yb[:, :], in0=yb[:, :], scalar=1.0, in1=ya[:, :], op0=ALU.add, op1=ALU.mult), r=[yb, ya], w=[yb])
                S.op('dve', lambda e: e.tensor_scalar(out=yb[:, :], in0=yb[:, :], scalar1=0.5, scalar2=None, op0=ALU.mult), r=[yb], w=[yb])
                b2 = banks[2]
                for co in range(3):
                    for ci in range(3):
                        S.op('pe', lambda e, co=co, ci=ci: e.matmul(b2[:, co * 128:(co + 1) * 128], lhsT=gw[:, ci * 384 + co * 128:ci * 384 + (co + 1) * 128], rhs=yb[:, ci * 128:(ci + 1) * 128], start=(ci == 0), stop=(ci == 2)), r=[gw, yb], w=[b2])
                for co in range(3):
                    S.op('act', lambda e, co=co: e.activation(out=yc[:, co * 128:(co + 1) * 128], in_=b2[:, co * 128:(co + 1) * 128], func=AF.Sigmoid, bias=gb[:, co:co + 1]), r=[b2, gb], w=[yc])
                S.op('dve', lambda e: e.tensor_tensor(out=mixT[:, 1024:1408], in0=yb[:, :], in1=yc[:, :], op=ALU.mult), r=[yb, yc], w=[mixT])
                S.dma('sp', xt[:, :], xsrc(1, m), w=[xt])
                for h2 in range(2):
                    bk = banks[4 + h2]
                    for c in range(11):
                        S.op('pe', lambda e, bk=bk, c=c, h2=h2: e.matmul(bk[:, 0:512], lhsT=mixT[:, c * 128:(c + 1) * 128], rhs=Wo[:, c * D + h2 * 512:c * D + (h2 + 1) * 512], start=(c == 0), stop=(c == 10)), r=[mixT, Wo], w=[bk])
                    S.op('dve', lambda e, bk=bk, h2=h2: e.tensor_tensor(out=xo[:, h2 * 512:(h2 + 1) * 512], in0=bk[:, 0:512], in1=g1[:, h2 * 512:(h2 + 1) * 512], op=ALU.mult), r=[bk, g1], w=[xo])
                S.op('dve', lambda e: e.tensor_tensor(out=xo[:, :], in0=xo[:, :], in1=xt[:, :], op=ALU.add), r=[xo, xt], w=[xo])
                S.dma('pool', xmid[r0:r0 + 128, :], xo[:, :], r=[xo])
            ph.close()

        lat_tiles = list(range(NCT, NT))
        all_tiles = list(range(NT))
        plist = ['A0', 'B0', 'C0', 'D0a', 'MLP0', 'A1', 'B1', 'C1', 'D1a', 'MLP1']
        nph = len(plist) if upto is None else plist.index(upto) + 1 if upto in plist else 0
        if nph >= 1:
            phaseA0()
        if nph >= 2:
            phaseB0()
        if nph >= 3 and os.environ.get('KGLA', '1') == '1':
            gla_phase(8, 64, 128, qT, [kTs[0], kTs[0]], [ktoks[0], ktoks[0]], vtok, lgs, odir)
        if nph >= 4:
            phaseD0a()
        if nph < 5:
            pass
        elif depth == 1:
            phaseMLP(0, lat_tiles, True)
        else:
            phaseMLP(0, all_tiles, False)
            if nph >= 6:
                phaseA1()
            if nph >= 7:
                gla_phase(8, 128, 128, qT, kTs, ktoks, vtok, lgs, odir)
            if nph >= 8:
                phaseC1()
            if nph >= 9:
                phaseD1a()
            if nph >= 10:
                phaseMLP(1, lat_tiles, True)
        S.barrier()
    return nc


def make_consts():
    i = np.arange(128)
    t = i[:, None]
    j = i[None, :]
    eye = (t == j)
    TF = (t <= j)
    TB = (t >= j)
    SF = (t > j)
    SB = (t < j)
    MBF = np.where(j >= t, 0.0, NEG)
    MBB = np.where(j <= t, 0.0, NEG)
    ONE = np.ones((128, 128))
    w = np.arange(258)
    ML = np.broadcast_to((w % 64 != 0).astype(np.float32)[None, :], (128, 258))
    MR = np.broadcast_to((w % 64 != 1).astype(np.float32)[None, :], (128, 258))
    return np.concatenate([eye, TF, TB, SF, SB, MBF, MBB, ONE, ML, MR], axis=1).astype(np.float32)


def host_inputs(inp, b, depth=2):
    f = lambda a: np.ascontiguousarray(np.asarray(a, dtype=np.float32))
    m = {}
    m["x"] = f(inp["x"][b])
    m["ctx"] = f(inp["ctx"][b])
    cs = np.stack([np.asarray(inp["c"][b]).reshape(8, 128).T, np.asarray(inp["c_ctx"]).reshape(8, 128).T], axis=-1)
    m["cs"] = f(cs)
    m["ada_w"] = f(inp["ada_w"])
    m["ada_b"] = f(inp["ada_b"])
    m["norm1_w"] = f(inp["norm1_w"])
    m["norm2_w"] = f(inp["norm2_w"])
    m["final_norm_w"] = f(np.asarray(inp["final_norm_w"]).reshape(1, D))
    m["consts"] = make_consts()
    m["w_in0"] = f(inp["ssd_gla_w_in"][0])
    cw = np.asarray(inp["ssd_conv_w"][0]).reshape(9, 16, 128)
    m["convw"] = f(cw.transpose(2, 1, 0).reshape(128, 144))
    m["convb"] = f(np.asarray(inp["ssd_conv_b"][0]).reshape(16, 128).T)
    m["dtb"] = f(np.asarray(inp["ssd_dt_bias"][0]).reshape(1, 32))
    m["alog"] = f(np.asarray(inp["ssd_a_log"][0]).reshape(1, 32))
    m["ssdd"] = f(np.asarray(inp["ssd_d"][0]).reshape(1, 16))
    m["ssdnw"] = f(np.asarray(inp["ssd_norm_w"][0]).reshape(1, 1024))
    wgm = np.zeros((33, 1024), np.float32)
    gw = np.asarray(inp["gla_gate_w"][0])
    wgm[0:16, 0:512] = gw[0]
    wgm[16:32, 512:1024] = gw[1]
    wgm[32, :] = np.asarray(inp["gla_gate_b"][0]).reshape(1024)
    m["wg"] = wgm
    m["glanw"] = f(np.asarray(inp["gla_norm_w"][0]).reshape(1, 128))
    m["w_out0"] = f(inp["ssd_gla_w_out"][0])
    m["mlp_w1"] = f(inp["mlp_w1"])
    m["mlp_w2"] = f(inp["mlp_w2"])
    if depth > 1:
        m["w_in1"] = f(inp["hgrn_s5_w_in"][0])
        lb = np.asarray(inp["hgrn_lb_logits"], dtype=np.float32)
        m["lbl"] = f(lb.reshape(2, 2048))
        m["lblT"] = f(np.stack([lb[l].reshape(16, 128).T for l in range(2)], axis=1))
        m["hgnw"] = f(np.asarray(inp["hgrn_norm_w"][0]).reshape(1, 128))
        prm = []
        for d in range(2):
            prm.append(np.asarray(inp["s5_a_re"][0][d]).reshape(12, 128).T)
            prm.append(np.asarray(inp["s5_a_im"][0][d]).reshape(12, 128).T)
            prm.append(np.repeat(np.asarray(inp["s5_log_dt"][0][d]).reshape(12, 2, 1), 64, axis=2).reshape(12, 128).T)
        m["s5p"] = f(np.concatenate(prm, axis=1))
        sb = lambda a: np.asarray(a).reshape(12, 128, 16).transpose(1, 0, 2).reshape(128, 192)
        m["s5b"] = f(np.concatenate([sb(inp["s5_b_re"][0]), sb(inp["s5_b_im"][0])], axis=1))
        scf = lambda a: np.asarray(a).reshape(12, 2, 16, 64).transpose(1, 3, 0, 2).reshape(128, 192)
        m["s5c"] = f(np.concatenate([scf(inp["s5_c_re"][0]), scf(inp["s5_c_im"][0])], axis=1))
        m["s5d"] = f(np.asarray(inp["s5_d"][0]).reshape(3, 128).T)
        m["gluw"] = f(inp["s5_glu_w"][0])
        m["glub"] = f(np.asarray(inp["s5_glu_b"][0]).reshape(3, 128).T)
        m["w_out1"] = f(inp["hgrn_s5_w_out"][0])
    return m


_NC_CACHE = {}


def kernel(**inputs):
    B, SEQ = inputs["x"].shape[0], inputs["x"].shape[1]
    key = (SEQ, 2)
    if key not in _NC_CACHE:
        _NC_CACHE[key] = build_nc(SEQ, 2)
    nc = _NC_CACHE[key]
    in_maps = [host_inputs(inputs, b) for b in range(B)]
    res = run_bass_kernel_spmd(nc, in_maps, core_ids=list(range(B)))
    return np.stack([r["out"] for r in res.results], axis=0).astype(np.float32)
```

```python
import os
import numpy as np
from contextlib import ExitStack
KDBG = int(os.environ.get('KDBG', '9'))
import concourse.bass as bass
import concourse.mybir as mybir
from concourse.bass_utils import run_bass_kernel_spmd

F32 = mybir.dt.float32
BF16 = mybir.dt.bfloat16
AF = mybir.ActivationFunctionType
ALU = mybir.AluOpType
AX = mybir.AxisListType

D = 1024
CTX = 256
EPS = 1e-6
NEG = -30000.0


class Buf:
    def __init__(self, t, name):
        self.t = t
        self.name = name
        self.w = None
        self.r = {}

    def __getitem__(self, k):
        return self.t[k]


class Sch:
    def __init__(self, nc, es):
        self.nc = nc
        self.E = {'pe': nc.tensor, 'act': nc.scalar, 'dve': nc.vector, 'pool': nc.gpsimd, 'sp': nc.sync}
        self.semobj = {}
        self.cnt = {}
        self.seen = {}
        for e in self.E:
            self.semobj[e] = es.enter_context(nc.semaphore("s_" + e))
            self.cnt[e] = 0
            self.seen[e] = {}
        self.rings = {}
        for q in ('sp', 'pool', 'act'):
            n = 12
            keys = []
            for i in range(n):
                k = ('d', q, i)
                self.semobj[k] = es.enter_context(nc.semaphore("d_%s_%d" % (q, i)))
                keys.append(k)
            self.rings[q] = {'keys': keys, 'vals': [0] * n, 'i': 0}
        self.nins = 0
        for e in self.E:
            self.E[e].sem_clear(self.semobj[e])
        for q, ring in self.rings.items():
            for k in ring['keys']:
                self.E[q].sem_clear(self.semobj[k])
        nc.all_engine_barrier()

    def _wait(self, e, tok):
        key, val = tok
        if self.seen[e].get(key, 0) >= val:
            return
        self.E[e].wait_ge(self.semobj[key], val)
        self.seen[e][key] = val

    def _deps(self, e, r, w, is_dma):
        for b in r:
            if b.w is not None:
                self._wait(e, b.w)
        for b in w:
            if b.w is not None and (is_dma or b.w[0] != e):
                self._wait(e, b.w)
            for key, val in b.r.items():
                if is_dma or key != e:
                    self._wait(e, (key, val))

    def _upd(self, tok, r, w):
        for b in r:
            b.r[tok[0]] = tok[1]
        for b in w:
            b.w = tok
            b.r = {}

    def op(self, e, fn, r=(), w=()):
        self._deps(e, r, w, False)
        ins = fn(self.E[e])
        self.cnt[e] += 1
        ins.then_inc(self.semobj[e], 1)
        self._upd((e, self.cnt[e]), r, w)
        self.nins += 1

    def dma(self, q, out, in_, r=(), w=()):
        ring = self.rings[q]
        i = ring['i']
        ring['i'] = (i + 1) % len(ring['keys'])
        key = ring['keys'][i]
        if ring['vals'][i] > 0:
            self._wait(q, (key, ring['vals'][i]))
        self._deps(q, r, w, True)
        ins = self.E[q].dma_start(out=out, in_=in_)
        ring['vals'][i] += 16
        ins.then_inc(self.semobj[key], 16)
        self._upd((key, ring['vals'][i]), r, w)
        self.nins += 1

    def barrier(self):
        toks = [(e, self.cnt[e]) for e in self.E if self.cnt[e] > 0]
        for q, ring in self.rings.items():
            for k, v in zip(ring['keys'], ring['vals']):
                if v > 0:
                    toks.append((k, v))
        for e in self.E:
            for tok in toks:
                if tok[0] != e or True:
                    if tok[0] == e:
                        continue
                    self._wait(e, tok)
        for e in self.E:
            if self.cnt[e] > 0:
                self._wait(e, (e, self.cnt[e]))


def build_nc(SEQ, depth=2, dbg=(), upto=None):
    T = CTX + SEQ
    NT = T // 128
    NCT = CTX // 128
    nc = bass.Bass("TRN2", target_bir_lowering=False)

    def din(name, shape, dt=F32):
        return nc.dram_tensor(name, list(shape), dt, kind="ExternalInput").ap()

    def dscr(name, shape, dt=F32):
        kind = "ExternalOutput" if name in dbg else "Internal"
        return nc.dram_tensor(name, list(shape), dt, kind=kind).ap()

    x_in = din("x", [SEQ, D])
    ctx_in = din("ctx", [CTX, D])
    cs_in = din("cs", [128, 8, 2])
    ada_w = din("ada_w", [2, D, 6 * D])
    ada_b = din("ada_b", [2, 6 * D])
    norm1_w = din("norm1_w", [2, D])
    norm2_w = din("norm2_w", [2, D])
    final_w = din("final_norm_w", [1, D])
    consts = din("consts", [128, 128 * 8 + 516])
    w_in0 = din("w_in0", [D, 6208])
    convw = din("convw", [128, 16 * 9])
    convb = din("convb", [128, 16])
    dtb = din("dtb", [1, 32])
    alog = din("alog", [1, 32])
    ssdd = din("ssdd", [1, 16])
    ssdnw = din("ssdnw", [1, 1024])
    wg = din("wg", [33, 1024])
    glanw = din("glanw", [1, 128])
    w_out0 = din("w_out0", [2048, D])
    mlp_w1 = din("mlp_w1", [2, D, 4096])
    mlp_w2 = din("mlp_w2", [2, 4096, D])
    if depth > 1:
        w_in1 = din("w_in1", [D, 5504])
        lbl = din("lbl", [2, 2048])
        lblT = din("lblT", [128, 2, 16])
        hgnw = din("hgnw", [1, 128])
        s5p = din("s5p", [128, 2 * 3 * 12])
        s5b = din("s5b", [128, 2 * 12 * 16])
        s5c = din("s5c", [128, 2 * 12 * 16])
        s5d = din("s5d", [128, 3])
        gluw = din("gluw", [384, 384])
        glub = din("glub", [128, 3])
        w_out1 = din("w_out1", [1408, D])
    out = nc.dram_tensor("out", [SEQ, D], F32, kind="ExternalOutput").ap()

    modv = dscr("modv", [2, 2, 6 * D])
    xres = dscr("xres", [T, D])
    xmid = dscr("xmid", [T, D])
    zs = dscr("zs", [T, D])
    rs = dscr("rs", [T, D])
    vtok = dscr("vtok", [T, D])
    TP = T + 256
    xbcT = dscr("xbcT", [2048, TP], BF16)
    dts = dscr("dts", [T, 32])
    qT = dscr("qT", [1024, T])
    kTs = [dscr("kT0", [1024, T]), dscr("kT1", [1024, T])]
    ktoks = [dscr("ktok0", [T, 1024]), dscr("ktok1", [T, 1024])]
    lgs = [dscr("lg0", [T, 1024]), dscr("lg1", [T, 1024])]
    ydir = [dscr("ydir0", [T, D]), dscr("ydir1", [T, D])]
    odir = [dscr("odir0", [T, D]), dscr("odir1", [T, D])]
    utok = dscr("utok", [T, 384])
    yT5 = [dscr("yT5_0", [384, T]), dscr("yT5_1", [384, T])]

    es = ExitStack()
    with es:
        S = Sch(nc, es)
        banks = [Buf(es.enter_context(nc.psum_tensor("bank%d" % i, [128, 512], F32)), "bank%d" % i) for i in range(8)]

        class Phase:
            def __init__(self):
                self.es = ExitStack()
                self.n = 0

            def tile(self, shape, dt=F32, name=None):
                self.n += 1
                S.uid = getattr(S, "uid", 0) + 1
                nm = "%s_%d" % (name or "t", S.uid)
                return Buf(self.es.enter_context(nc.sbuf_tensor(nm, list(shape), dt)), nm)

            def close(self):
                S.barrier()
                self.es.close()

        def run_seq(g0, g1):
            for _ in g0:
                pass
            for _ in g1:
                pass

        def run_pair(g0, g1):
            gens = [g0, g1]
            alive = [True, True]
            while alive[0] or alive[1]:
                for i in range(2):
                    if alive[i]:
                        try:
                            next(gens[i])
                        except StopIteration:
                            alive[i] = False

        def load_consts(ph):
            cst = ph.tile([128, 128 * 8 + 516], F32, "cst")
            S.dma('sp', cst[:, :], consts[:, :], w=[cst])
            return cst

        C_I, C_TF, C_TB, C_SF, C_SB, C_MBF, C_MBB, C_ONE = [i * 128 for i in range(8)]
        C_ML = 1024
        C_MR = 1024 + 258

        def xsrc(layer, m):
            if layer == 0:
                if m < NCT:
                    return ctx_in[m * 128:(m + 1) * 128, :]
                return x_in[(m - NCT) * 128:(m - NCT + 1) * 128, :]
            return xres[m * 128:(m + 1) * 128, :]

        def bc_load(ph, src_row_ap, n, name):
            t = ph.tile([128, n], F32, name)
            S.dma('sp', t[:, :], src_row_ap.partition_broadcast(128), w=[t])
            return t

        def mod_consts(ph, layer, normw, k_shift, k_scale, src):
            A = bc_load(ph, modv[layer, src:src + 1, k_scale * D:(k_scale + 1) * D], D, "A")
            SH = bc_load(ph, modv[layer, src:src + 1, k_shift * D:(k_shift + 1) * D], D, "SH")
            NW = bc_load(ph, normw[layer:layer + 1, :], D, "NW")
            S.op('dve', lambda e: e.scalar_tensor_tensor(out=A[:, :], in0=A[:, :], scalar=1.0, in1=NW[:, :], op0=ALU.add, op1=ALU.mult), r=[A, NW], w=[A])
            return A, SH

        def load_weight_bf16(ph, Wb, wsrc, K, N, nblk=512):
            J = K // 128
            stg = [ph.tile([128, 2 * 512], F32, "wstg") for _ in range(3)]
            cnt = 0
            for n0 in range(0, N, nblk):
                n1 = min(N, n0 + nblk)
                nn = n1 - n0
                for j0 in range(0, J, 2):
                    j1 = min(J, j0 + 2)
                    st = stg[cnt % 3]
                    S.dma('sp' if cnt % 2 == 0 else 'act', st[:, 0:(j1 - j0) * nn].rearrange("p (j n) -> p j n", n=nn),
                          wsrc[j0 * 128:j1 * 128, n0:n1].rearrange("(j p) n -> p j n", p=128), w=[st])
                    eng = 'dve'
                    for j in range(j0, j1):
                        S.op(eng, lambda e, j=j, st=st: e.tensor_copy(out=Wb[:, j * N + n0:j * N + n1], in_=st[:, (j - j0) * nn:(j - j0 + 1) * nn]), r=[st], w=[Wb])
                    cnt += 1

        def rms_rstd(ph, xt, junk, ss, rstd, epsb, n, dscale):
            S.op('dve', lambda e: e.memset(ss[:, 0:1], 0.0), w=[ss])
            S.op('act', lambda e: e.activation(out=junk[:, 0:n], in_=xt[:, 0:n], func=AF.Square, accum_out=ss[:, 0:1]), r=[ss, xt], w=[junk, ss])
            S.op('act', lambda e: e.activation(out=rstd[:, 0:1], in_=ss[:, 0:1], func=AF.Sqrt, scale=dscale, bias=epsb[:, 0:1]), r=[ss, epsb], w=[rstd])
            S.op('dve', lambda e: e.reciprocal(out=rstd[:, 0:1], in_=rstd[:, 0:1]), r=[rstd], w=[rstd])

        ph = Phase()
        cs = ph.tile([128, 16], F32, "cs")
        S.dma('sp', cs[:, :], cs_in[:, :, :].rearrange("p j s -> p (j s)"), w=[cs])
        S.op('act', lambda e: e.activation(out=cs[:, :], in_=cs[:, :], func=AF.Silu), r=[cs], w=[cs])
        for layer in range(depth):
            adab = ph.tile([2, 6 * D], F32, "adab")
            S.dma('sp', adab[:, :], ada_b[layer:layer + 1, :].partition_broadcast(2), w=[adab])
            mv = ph.tile([2, 6 * D], F32, "mv")
            wst = [ph.tile([128, 8 * 512], F32, "adaw") for _ in range(2)]
            for nb in range(12):
                st = wst[nb % 2]
                S.dma('sp' if nb % 2 == 0 else 'act', st[:, :].rearrange("p (j n) -> p j n", n=512),
                      ada_w[layer, :, nb * 512:(nb + 1) * 512].rearrange("(j p) n -> p j n", p=128), w=[st])
                bk = banks[nb % 8]
                for j in range(8):
                    S.op('pe', lambda e, j=j, st=st, bk=bk: e.matmul(bk[0:2, 0:512], lhsT=cs[:, 2 * j:2 * j + 2], rhs=st[:, j * 512:(j + 1) * 512], start=(j == 0), stop=(j == 7)), r=[cs, st], w=[bk])
                S.op('dve', lambda e, bk=bk, nb=nb: e.tensor_tensor(out=mv[0:2, nb * 512:(nb + 1) * 512], in0=bk[0:2, 0:512], in1=adab[0:2, nb * 512:(nb + 1) * 512], op=ALU.add), r=[bk, adab], w=[mv])
            S.dma('pool', modv[layer, :, :], mv[0:2, :], r=[mv])
        ph.close()

        def proj_phase(layer, w_src, NIN, tok_groups, feat_groups, extra=None, tiles=None):
            ph = Phase()
            cst = load_consts(ph)
            NW = bc_load(ph, norm1_w[layer:layer + 1, :], D, "NW")
            Am = ph.tile([128, D], F32, "Am")
            SHm = ph.tile([128, D], F32, "SHm")
            cur = [None]

            def set_mod(src):
                if cur[0] == src:
                    return
                cur[0] = src
                S.dma('sp', Am[:, :], modv[layer, src:src + 1, D:2 * D].partition_broadcast(128), w=[Am])
                S.dma('sp', SHm[:, :], modv[layer, src:src + 1, 0:D].partition_broadcast(128), w=[SHm])
                S.op('dve', lambda e: e.scalar_tensor_tensor(out=Am[:, :], in0=Am[:, :], scalar=1.0, in1=NW[:, :], op0=ALU.add, op1=ALU.mult), r=[Am, NW], w=[Am])
            Wb = ph.tile([128, 8 * NIN], BF16, "Wb")
            load_weight_bf16(ph, Wb, w_src, D, NIN)
            epsb = ph.tile([128, 1], F32, "epsb")
            S.op('dve', lambda e: e.memset(epsb[:, :], EPS), w=[epsb])
            idb = ph.tile([128, 128], BF16, "idb")
            S.op('dve', lambda e: e.tensor_copy(out=idb[:, :], in_=cst[:, C_I:C_I + 128]), r=[cst], w=[idb])
            xts = [ph.tile([128, D], F32, "xt") for _ in range(2)]
            junk = ph.tile([128, D], BF16, "junk")
            tmp = ph.tile([128, D], F32, "tmp")
            hb = ph.tile([128, D], BF16, "hb")
            hTs = [ph.tile([128, D], BF16, "hT") for _ in range(2)]
            ss = ph.tile([128, 1], F32, "ss")
            rstd = ph.tile([128, 1], F32, "rstd")
            stg32 = [ph.tile([128, 512], F32, "stg32") for _ in range(4)]
            stg16 = [ph.tile([128, 512], BF16, "stg16") for _ in range(3)]
            ctxp = dict(ph=ph, cst=cst, Wb=Wb, NIN=NIN, stg32=stg32, stg16=stg16, k32=[0], k16=[0])
            if extra is not None:
                extra(ctxp)
            bi = [0]

            def nextbank():
                b = banks[bi[0] % 8]
                bi[0] += 1
                return b
            for m in (tiles if tiles is not None else range(NT)):
                if KDBG < 2:
                    break
                xt = xts[m % 2]
                hT = hTs[m % 2]
                S.dma('sp', xt[:, :], xsrc(layer, m), w=[xt])
                rms_rstd(ph, xt, junk, ss, rstd, epsb, D, 1.0 / D)
                set_mod(1 if m < NCT else 0)
                A, SH = Am, SHm
                S.op('dve', lambda e, xt=xt, A=A: e.scalar_tensor_tensor(out=tmp[:, :], in0=xt[:, :], scalar=rstd[:, 0:1], in1=A[:, :], op0=ALU.mult, op1=ALU.mult), r=[xt, rstd, A], w=[tmp])
                S.op('dve', lambda e, SH=SH: e.tensor_tensor(out=hb[:, :], in0=tmp[:, :], in1=SH[:, :], op=ALU.add), r=[tmp, SH], w=[hb])
                bk = nextbank()
                bkb = bk[:, :].bitcast(BF16)
                for j in range(8):
                    S.op('pe', lambda e, j=j, bkb=bkb: e.transpose(bkb[:, j * 128:(j + 1) * 128], hb[:, j * 128:(j + 1) * 128], idb[:, :]), r=[hb, idb], w=[bk])
                S.op('act', lambda e, bkb=bkb, hT=hT: e.activation(out=hT[:, :], in_=bkb[:, 0:1024], func=AF.Copy), r=[bk], w=[hT])
                for (c0, ncols, fn) in (tok_groups if KDBG >= 3 else []):
                    for b0 in range(c0, c0 + ncols, 512):
                        n = min(512, c0 + ncols - b0)
                        bk = nextbank()
                        for j in range(8):
                            S.op('pe', lambda e, j=j, bk=bk, b0=b0, n=n, hT=hT: e.matmul(bk[:, 0:n], lhsT=hT[:, j * 128:(j + 1) * 128], rhs=Wb[:, j * NIN + b0:j * NIN + b0 + n], start=(j == 0), stop=(j == 7)), r=[hT, Wb], w=[bk])
                        fn(m, bk, b0 - c0, n)
                for (c0, nch_total, Mw, fn) in (feat_groups[:KDBG - 3] if KDBG >= 4 else []):
                    for q0 in range(0, nch_total, 4):
                        nch = min(4, nch_total - q0)
                        bk = nextbank()
                        for c in range(nch):
                            col = c0 + (q0 + c) * Mw
                            for j in range(8):
                                S.op('pe', lambda e, j=j, bk=bk, c=c, col=col, hT=hT: e.matmul(bk[0:Mw, c * 128:(c + 1) * 128], lhsT=Wb[:, j * NIN + col:j * NIN + col + Mw], rhs=hT[:, j * 128:(j + 1) * 128], start=(j == 0), stop=(j == 7)), r=[hT, Wb], w=[bk])
                        fn(m, bk, q0, nch)
            ph.close()

        def make_tok_store(ctxp, dst, dcol0, width, func=None, scale=1.0):
            stg = ctxp['stg32']
            k = ctxp['k32']

            def fn(m, bk, off, n):
                st = stg[k[0] % len(stg)]
                k[0] += 1
                if k[0] % 2 == 0:
                    S.op('act', lambda e: e.activation(out=st[:, 0:n], in_=bk[:, 0:n], func=AF.Copy), r=[bk], w=[st])
                else:
                    S.op('dve', lambda e: e.tensor_copy(out=st[:, 0:n], in_=bk[:, 0:n]), r=[bk], w=[st])
                S.dma('pool', dst[m * 128:(m + 1) * 128, dcol0 + off:dcol0 + off + n], st[:, 0:n], r=[st])
            return fn

        def make_feat_store(ctxp, dst, drow0, dt=F32, scale=None, func=AF.Copy, coloff=0):
            stg = ctxp['stg32'] if dt == F32 else ctxp['stg16']
            k = ctxp['k32'] if dt == F32 else ctxp['k16']

            def fn(m, bk, q0, nch):
                st = stg[k[0] % len(stg)]
                k[0] += 1
                if scale is None:
                    S.op('act', lambda e: e.activation(out=st[:, 0:nch * 128], in_=bk[:, 0:nch * 128], func=func), r=[bk], w=[st])
                else:
                    S.op('act', lambda e: e.activation(out=st[:, 0:nch * 128], in_=bk[:, 0:nch * 128], func=func, scale=scale), r=[bk], w=[st])
                r0 = drow0 + q0 * 128
                S.dma('pool', dst[r0:r0 + nch * 128, coloff + m * 128:coloff + (m + 1) * 128].rearrange("(c p) t -> p c t", p=128),
                      st[:, 0:nch * 128].rearrange("p (c t) -> p c t", c=nch), r=[st])
            return fn

        def phaseA0():
            tok_groups = []
            feat_groups = []

            def extra(c):
                ph = c['ph']
                tok_groups.append((0, 1024, make_tok_store(c, zs, 0, 1024)))
                tok_groups.append((3072, 32, make_tok_store(c, dts, 0, 32)))
                tok_groups.append((3616, 512, make_tok_store(c, ktoks[0], 0, 512)))
                tok_groups.append((4128, 1024, make_tok_store(c, vtok, 0, 1024)))
                tok_groups.append((5184, 1024, make_tok_store(c, rs, 0, 1024)))
                feat_groups.append((1024, 16, 128, make_feat_store(c, xbcT, 0, dt=BF16, coloff=128)))
                feat_groups.append((3104, 4, 128, make_feat_store(c, qT, 0, scale=0.125)))
                feat_groups.append((3616, 4, 128, make_feat_store(c, kTs[0], 0)))
                lrT = ph.tile([33, 128], F32, "lrT")
                S.op('dve', lambda e: e.memset(lrT[32:33, :], 1.0), w=[lrT])
                wgt = ph.tile([33, 1024], F32, "wgt")
                S.dma('sp', wgt[:, :], wg[:, :], w=[wgt])
                gst = [ph.tile([128, 512], F32, "gst") for _ in range(2)]

                def lr_fn(m, bk, q0, nch):
                    S.op('act', lambda e: e.activation(out=lrT[0:32, :], in_=bk[0:32, 0:128], func=AF.Copy), r=[bk], w=[lrT])
                    for d in range(2):
                        bg = banks[(4 + d) % 8]
                        S.op('pe', lambda e, bg=bg, d=d: e.matmul(bg[:, 0:512], lhsT=lrT[0:33, :], rhs=wgt[0:33, d * 512:(d + 1) * 512], start=True, stop=True), r=[lrT, wgt], w=[bg])
                        g = gst[d]
                        S.op('act', lambda e, bg=bg, g=g: e.activation(out=g[:, :], in_=bg[:, 0:512], func=AF.Exp, scale=-1.0), r=[bg], w=[g])
                        S.op('act', lambda e, g=g: e.activation(out=g[:, :], in_=g[:, :], func=AF.Ln, bias=1.0), r=[g], w=[g])
                        S.op('dve', lambda e, g=g: e.tensor_scalar(out=g[:, :], in0=g[:, :], scalar1=-1.0 / 16.0, scalar2=None, op0=ALU.mult), r=[g], w=[g])
                        S.dma('pool', lgs[d][m * 128:(m + 1) * 128, 0:512], g[:, :], r=[g])
                feat_groups.append((5152, 1, 32, lr_fn))
            proj_phase(0, w_in0, 6208, tok_groups, feat_groups, extra)

        def phaseB0():
            ph = Phase()
            cst = load_consts(ph)
            cw = ph.tile([128, 144], F32, "cw")
            S.dma('sp', cw[:, :], convw[:, :], w=[cw])
            cb = ph.tile([128, 16], F32, "cb")
            S.dma('sp', cb[:, :], convb[:, :], w=[cb])
            Dg = ph.tile([128, 144 * 128], BF16, "Dg")
            for i in range(144):
                S.op('dve', lambda e, i=i: e.tensor_scalar(out=Dg[:, i * 128:(i + 1) * 128], in0=cst[:, C_I:C_I + 128], scalar1=cw[:, i:i + 1], scalar2=None, op0=ALU.mult), r=[cst, cw], w=[Dg])
            idb = ph.tile([128, 128], BF16, "idb")
            S.op('dve', lambda e: e.tensor_copy(out=idb[:, :], in_=cst[:, C_I:C_I + 128]), r=[cst], w=[idb])
            mlr = ph.tile([128, 516], BF16, "mlr")
            S.op('dve', lambda e: e.tensor_copy(out=mlr[:, :], in_=cst[:, C_ML:C_ML + 516]), r=[cst], w=[mlr])
            mb4 = []
            for d in range(2):
                t = ph.tile([128, 512], F32, "mb4")
                c0 = C_MBF if d == 0 else C_MBB
                S.op('dve', lambda e, t=t, c0=c0: e.tensor_copy(out=t[:, :].rearrange("p (a i) -> p a i", a=4), in_=cst[:, c0:c0 + 128].unsqueeze(1).to_broadcast([128, 4, 128])), r=[cst], w=[t])
                mb4.append(t)
            dtbb = bc_load(ph, dtb[0:1, :], 32, "dtbb")
            nega = bc_load(ph, alog[0:1, :], 32, "nega")
            S.op('act', lambda e: e.activation(out=nega[:, :], in_=nega[:, :], func=AF.Exp), r=[nega], w=[nega])
            S.op('dve', lambda e: e.tensor_scalar(out=nega[:, :], in0=nega[:, :], scalar1=-1.0, scalar2=None, op0=ALU.mult), r=[nega], w=[nega])
            dskb = bc_load(ph, ssdd[0:1, :], 16, "dskb")
            hT = [ph.tile([128, 1024], F32, "hT") for _ in range(2)]
            hTb = [ph.tile([128, 1024], BF16, "hTb") for _ in range(2)]
            for d in range(2):
                S.op('dve', lambda e, d=d: e.memset(hT[d][:, :], 0.0), w=[hT[d]])
                S.op('dve', lambda e, d=d: e.memset(hTb[d][:, :], 0.0), w=[hTb[d]])

            def mk(shape, dt, name):
                return [ph.tile(shape, dt, name) for _ in range(2)]
            xin = mk([128, 16 * 258], BF16, "xin")
            xl1 = ph.tile([128, 16 * 258], BF16, "xl")
            xr1 = ph.tile([128, 16 * 258], BF16, "xr")
            xl = [xl1, xl1]
            xr = [xr1, xr1]
            xTt = mk([128, 1024], F32, "xTt")
            BT = mk([128, 512], BF16, "BT")
            CT = mk([128, 512], BF16, "CT")
            xtok = mk([128, 1024], F32, "xtok")
            Btok = mk([128, 512], BF16, "Btok")
            dtr = mk([128, 32], F32, "dtr")
            la = mk([128, 32], F32, "la")
            sm = mk([128, 128], F32, "sm")
            Dx1 = ph.tile([128, 2048], F32, "Dx")
            Dx = [Dx1, Dx1]
            seg = mk([128, 2048], F32, "seg")
            scT = mk([128, 2048], BF16, "scT")
            xdt = mk([128, 1024], BF16, "xdt")
            xw = mk([128, 1024], BF16, "xw")
            ysb = mk([128, 1024], F32, "ysb")
            ytmp1 = ph.tile([128, 1024], F32, "ytmp")
            ytmp = [ytmp1, ytmp1]

            def step(m, d):
                t0 = m * 128
                isctx = m < NCT
                first = (m == 0) or (m == NCT)
                last = (m == NCT - 1) or (m == NT - 1)
                X = xin[d]
                X3 = X[:, :].rearrange("p (c w) -> p c w", c=16)
                w0, w1 = 0, 258
                if first:
                    w0 = 65
                if last:
                    w1 = 193
                if first or last:
                    S.op('dve', lambda e: e.memset(X[:, :], 0.0), w=[X])
                for half in range(2):
                    S.dma('sp', X3[:, half * 8:(half + 1) * 8, w0:w1],
                          xbcT[half * 1024:(half + 1) * 1024, t0 + 63 + w0:t0 + 63 + w1].rearrange("(c p) t -> p c t", p=128), w=[X])
                if isctx:
                    srcs = {-1: X, 0: X, 1: X}
                    taps = [(0, -1), (0, 0), (0, 1)]
                else:
                    S.op('dve', lambda e: e.tensor_tensor(out=xl[d][:, :].rearrange("p (c w) -> p c w", c=16), in0=X3, in1=mlr[:, 0:258].unsqueeze(1).to_broadcast([128, 16, 258]), op=ALU.mult), r=[X, mlr], w=[xl[d]])
                    S.op('dve', lambda e: e.tensor_tensor(out=xr[d][:, :].rearrange("p (c w) -> p c w", c=16), in0=X3, in1=mlr[:, 258:516].unsqueeze(1).to_broadcast([128, 16, 258]), op=ALU.mult), r=[X, mlr], w=[xr[d]])
                    srcs = {-1: xl[d], 0: X, 1: xr[d]}
                    taps = [(dr, dc) for dr in (-1, 0, 1) for dc in (-1, 0, 1)]
                yield
                for c in range(16):
                    bk = banks[c // 4]
                    for ti, (dr, dc) in enumerate(taps):
                        sb = srcs[dc]
                        o0 = c * 258 + 65 + dr * 64 + dc
                        tap = (dr + 1) * 3 + (dc + 1)
                        S.op('pe', lambda e, bk=bk, c=c, sb=sb, o0=o0, tap=tap, ti=ti: e.matmul(bk[:, (c % 4) * 128:(c % 4 + 1) * 128], lhsT=Dg[:, (c * 9 + tap) * 128:(c * 9 + tap + 1) * 128], rhs=sb[:, o0:o0 + 128], start=(ti == 0), stop=(ti == len(taps) - 1)), r=[Dg, sb], w=[bk])
                yield
                for c in range(16):
                    bk = banks[c // 4]
                    if c < 8:
                        dst, oc = xTt[d], c
                    elif c < 12:
                        dst, oc = BT[d], c - 8
                    else:
                        dst, oc = CT[d], c - 12
                    S.op('act', lambda e, bk=bk, c=c, dst=dst, oc=oc: e.activation(out=dst[:, oc * 128:(oc + 1) * 128], in_=bk[:, (c % 4) * 128:(c % 4 + 1) * 128], func=AF.Silu, bias=cb[:, c:c + 1]), r=[bk, cb], w=[dst])
                yield
                for c in range(8):
                    bk = banks[4 + c // 4]
                    S.op('pe', lambda e, bk=bk, c=c: e.transpose(bk[:, (c % 4) * 128:(c % 4 + 1) * 128], xTt[d][:, c * 128:(c + 1) * 128], cst[:, C_I:C_I + 128]), r=[xTt[d], cst], w=[bk])
                yield
                for h2 in range(2):
                    S.op('act' if h2 == 0 else 'dve', (lambda e, h2=h2: e.activation(out=xtok[d][:, h2 * 512:(h2 + 1) * 512], in_=banks[4 + h2][:, :], func=AF.Copy)) if h2 == 0 else (lambda e, h2=h2: e.tensor_copy(out=xtok[d][:, h2 * 512:(h2 + 1) * 512], in_=banks[4 + h2][:, :])), r=[banks[4 + h2]], w=[xtok[d]])
                b6b = banks[6][:, :].bitcast(BF16)
                for g in range(4):
                    S.op('pe', lambda e, g=g: e.transpose(b6b[:, g * 128:(g + 1) * 128], BT[d][:, g * 128:(g + 1) * 128], idb[:, :]), r=[BT[d], idb], w=[banks[6]])
                S.op('dve', lambda e: e.tensor_copy(out=Btok[d][:, :], in_=b6b[:, 0:512]), r=[banks[6]], w=[Btok[d]])
                yield
                S.dma('sp', dtr[d][:, :], dts[t0:t0 + 128, :], w=[dtr[d]])
                S.op('dve', lambda e: e.tensor_tensor(out=dtr[d][:, :], in0=dtr[d][:, :], in1=dtbb[:, :], op=ALU.add), r=[dtr[d], dtbb], w=[dtr[d]])
                S.op('act', lambda e: e.activation(out=dtr[d][:, :], in_=dtr[d][:, :], func=AF.Exp), r=[dtr[d]], w=[dtr[d]])
                S.op('act', lambda e: e.activation(out=dtr[d][:, :], in_=dtr[d][:, :], func=AF.Ln, bias=1.0), r=[dtr[d]], w=[dtr[d]])
                S.op('dve', lambda e: e.tensor_tensor(out=la[d][:, :], in0=dtr[d][:, :], in1=nega[:, :], op=ALU.mult), r=[dtr[d], nega], w=[la[d]])
                yield
                lad = la[d][:, d * 16:(d + 1) * 16]
                dtd = dtr[d][:, d * 16:(d + 1) * 16]
                tri = C_TF if d == 0 else C_TB
                b7 = banks[7]
                S.op('pe', lambda e: e.matmul(b7[:, 0:16], lhsT=cst[:, tri:tri + 128], rhs=lad, start=True, stop=True), r=[cst, la[d]], w=[b7])
                S.op('pe', lambda e: e.matmul(b7[:, 16:32], lhsT=cst[:, C_ONE:C_ONE + 128], rhs=lad, start=True, stop=True), r=[cst, la[d]], w=[b7])
                yield
                s = sm[d]
                S.op('dve', lambda e: e.tensor_copy(out=s[:, 0:16], in_=b7[:, 0:16]), r=[b7], w=[s])
                S.op('act', lambda e: e.activation(out=s[:, 16:32], in_=b7[:, 0:16], func=AF.Exp), r=[b7], w=[s])
                S.op('act', lambda e: e.activation(out=s[:, 32:48], in_=b7[:, 16:32], func=AF.Exp), r=[b7], w=[s])
                S.op('dve', lambda e: e.tensor_tensor(out=s[:, 48:64], in0=b7[:, 16:32], in1=s[:, 0:16], op=ALU.subtract), r=[b7, s], w=[s])
                S.op('act', lambda e: e.activation(out=s[:, 48:64], in_=s[:, 48:64], func=AF.Exp), r=[s], w=[s])
                S.op('dve', lambda e: e.tensor_tensor(out=s[:, 64:80], in0=s[:, 48:64], in1=dtd, op=ALU.mult), r=[s, dtr[d]], w=[s])
                yield
                S.op('dve', lambda e: e.tensor_tensor(out=Dx[d][:, :].rearrange("p (h i) -> p h i", h=16), in0=cst[:, C_I:C_I + 128].unsqueeze(1).to_broadcast([128, 16, 128]), in1=s[:, 0:16].unsqueeze(2).to_broadcast([128, 16, 128]), op=ALU.mult), r=[cst, s], w=[Dx[d]])
                for q in range(4):
                    bk = banks[q]
                    S.op('pe', lambda e, bk=bk, q=q: e.matmul(bk[:, 0:512], lhsT=cst[:, C_ONE:C_ONE + 128], rhs=Dx[d][:, q * 512:(q + 1) * 512], start=True, stop=False), r=[cst, Dx[d]], w=[bk])
                    S.op('pe', lambda e, bk=bk, q=q: e.matmul(bk[:, 0:512], lhsT=cst[:, C_I:C_I + 128], rhs=mb4[d][:, :], start=False, stop=True), r=[cst, mb4[d]], w=[bk])
                    S.op('dve', lambda e, bk=bk, q=q: e.tensor_tensor(out=seg[d][:, q * 512:(q + 1) * 512].rearrange("p (h i) -> p h i", h=4), in0=bk[:, 0:512].rearrange("p (h i) -> p h i", h=4), in1=s[:, 4 * q:4 * q + 4].unsqueeze(2).to_broadcast([128, 4, 128]), op=ALU.subtract), r=[bk, s], w=[seg[d]])
                yield
                S.op('act', lambda e: e.activation(out=seg[d][:, :], in_=seg[d][:, :], func=AF.Exp), r=[seg[d]], w=[seg[d]])
                yield
                for g in range(4):
                    S.op('pe', lambda e, g=g: e.matmul(banks[6][:, g * 128:(g + 1) * 128], lhsT=BT[d][:, g * 128:(g + 1) * 128], rhs=CT[d][:, g * 128:(g + 1) * 128], start=True, stop=True), r=[BT[d], CT[d]], w=[banks[6]])
                yield
                for g in range(4):
                    S.op('dve', lambda e, g=g: e.tensor_tensor(out=scT[d][:, g * 512:(g + 1) * 512].rearrange("p (h i) -> p h i", h=4), in0=seg[d][:, g * 512:(g + 1) * 512].rearrange("p (h i) -> p h i", h=4), in1=banks[6][:, g * 128:(g + 1) * 128].unsqueeze(1).to_broadcast([128, 4, 128]), op=ALU.mult), r=[seg[d], banks[6]], w=[scT[d]])
                S.op('dve', lambda e: e.tensor_tensor(out=xdt[d][:, :].rearrange("p (h q) -> p h q", h=16), in0=xtok[d][:, :].rearrange("p (h q) -> p h q", h=16), in1=dtd.unsqueeze(2).to_broadcast([128, 16, 64]), op=ALU.mult), r=[xtok[d], dtr[d]], w=[xdt[d]])
                S.op('dve', lambda e: e.tensor_tensor(out=xw[d][:, :].rearrange("p (h q) -> p h q", h=16), in0=xtok[d][:, :].rearrange("p (h q) -> p h q", h=16), in1=s[:, 64:80].unsqueeze(2).to_broadcast([128, 16, 64]), op=ALU.mult), r=[xtok[d], s], w=[xw[d]])
                yield
                for hd in range(16):
                    bk = banks[4 + hd // 8]
                    S.op('pe', lambda e, bk=bk, hd=hd: e.matmul(bk[:, (hd % 8) * 64:(hd % 8 + 1) * 64], lhsT=scT[d][:, hd * 128:(hd + 1) * 128], rhs=xdt[d][:, hd * 64:(hd + 1) * 64], start=True, stop=True), r=[scT[d], xdt[d]], w=[bk])
                for g in range(4):
                    bk = banks[6 + g // 2]
                    S.op('pe', lambda e, bk=bk, g=g: e.matmul(bk[:, (g % 2) * 256:(g % 2 + 1) * 256], lhsT=CT[d][:, g * 128:(g + 1) * 128], rhs=hTb[d][:, g * 256:(g + 1) * 256], start=True, stop=True), r=[CT[d], hTb[d]], w=[bk])
                yield
                for h2 in range(2):
                    S.op('dve', lambda e, h2=h2: e.tensor_tensor(out=ytmp[d][:, h2 * 512:(h2 + 1) * 512].rearrange("p (h q) -> p h q", h=8), in0=banks[6 + h2][:, :].rearrange("p (h q) -> p h q", h=8), in1=s[:, 16 + 8 * h2:24 + 8 * h2].unsqueeze(2).to_broadcast([128, 8, 64]), op=ALU.mult), r=[banks[6 + h2], s], w=[ytmp[d]])
                    S.op('dve', lambda e, h2=h2: e.tensor_tensor(out=ysb[d][:, h2 * 512:(h2 + 1) * 512], in0=banks[4 + h2][:, :], in1=ytmp[d][:, h2 * 512:(h2 + 1) * 512], op=ALU.add), r=[banks[4 + h2], ytmp[d]], w=[ysb[d]])
                if d == 0:
                    S.op('dve', lambda e: e.tensor_tensor(out=ytmp[d][:, :].rearrange("p (h q) -> p h q", h=16), in0=xtok[d][:, :].rearrange("p (h q) -> p h q", h=16), in1=dskb[:, 0:16].unsqueeze(2).to_broadcast([128, 16, 64]), op=ALU.mult), r=[xtok[d], dskb], w=[ytmp[d]])
                    S.op('dve', lambda e: e.tensor_tensor(out=ysb[d][:, :], in0=ysb[d][:, :], in1=ytmp[d][:, :], op=ALU.add), r=[ysb[d], ytmp[d]], w=[ysb[d]])
                S.dma('pool', ydir[d][t0:t0 + 128, :], ysb[d][:, :], r=[ysb[d]])
                yield
                for g in range(4):
                    bk = banks[g // 2]
                    S.op('pe', lambda e, bk=bk, g=g: e.matmul(bk[:, (g % 2) * 256:(g % 2 + 1) * 256], lhsT=Btok[d][:, g * 128:(g + 1) * 128], rhs=xw[d][:, g * 256:(g + 1) * 256], start=True, stop=True), r=[Btok[d], xw[d]], w=[bk])
                yield
                S.op('dve', lambda e: e.tensor_tensor(out=hT[d][:, :].rearrange("p (h q) -> p h q", h=16), in0=hT[d][:, :].rearrange("p (h q) -> p h q", h=16), in1=s[:, 32:48].unsqueeze(2).to_broadcast([128, 16, 64]), op=ALU.mult), r=[hT[d], s], w=[hT[d]])
                for h2 in range(2):
                    S.op('dve', lambda e, h2=h2: e.tensor_tensor(out=hT[d][:, h2 * 512:(h2 + 1) * 512], in0=hT[d][:, h2 * 512:(h2 + 1) * 512], in1=banks[h2][:, :], op=ALU.add), r=[hT[d], banks[h2]], w=[hT[d]])
                S.op('act', lambda e: e.activation(out=hTb[d][:, :], in_=hT[d][:, :], func=AF.Copy), r=[hT[d]], w=[hTb[d]])

            order_f = list(range(NT))
            order_b = list(range(NCT - 1, -1, -1)) + list(range(NT - 1, NCT - 1, -1))
            for i in range(NT):
                run_seq(step(order_f[i], 0), step(order_b[i], 1))
            ph.close()

        def gla_phase(H, dk, dv, qT_s, kT_s, ktok_s, v_s, lg_s, o_s):
            ph = Phase()
            cst = load_consts(ph)
            HK = H * dk
            HV = H * dv
            nkc = HK // 128
            hp = 128 // dk
            NCH = T // 64
            NCC = CTX // 64
            Sst = [ph.tile([128, nkc * dv], F32, "Sst") for _ in range(2)]
            Sb = [ph.tile([128, nkc * dv], BF16, "Sb") for _ in range(2)]
            for d in range(2):
                S.op('dve', lambda e, d=d: e.memset(Sst[d][:, :], 0.0), w=[Sst[d]])
                S.op('dve', lambda e, d=d: e.memset(Sb[d][:, :], 0.0), w=[Sb[d]])

            def mk(shape, dt, name):
                return [ph.tile(shape, dt, name) for _ in range(2)]
            lgt = mk([64, HK], F32, "lgt")
            kt = mk([64, HK], F32, "kt")
            vt = mk([64, HV], F32, "vt")
            vb = mk([64, HV], BF16, "vb")
            qTt = mk([128, nkc * 64], F32, "qTt")
            kTt = mk([128, nkc * 64], F32, "kTt")
            eg = mk([128, nkc * 64], F32, "eg")
            eng = mk([128, nkc * 64], F32, "eng")
            qd = [[ph.tile([128, nkc * 64], BF16, "qd") for _ in range(hp)] for _ in range(2)]
            for d in range(2):
                for sl in range(hp):
                    S.op('dve', lambda e, d=d, sl=sl: e.memset(qd[d][sl][:, :], 0.0), w=[qd[d][sl]])
            ki = mk([128, nkc * 64], BF16, "ki")
            eex = mk([64, HK], F32, "eex")
            kend = mk([64, HK], BF16, "kend")
            att = mk([64, H * 64], BF16, "att")
            osb = mk([64, HV], F32, "osb")

            def step(ci, d):
                t0 = ci * 64
                S.dma('sp', lgt[d][:, :], lg_s[d][t0:t0 + 64, 0:HK], w=[lgt[d]])
                S.dma('sp', kt[d][:, :], ktok_s[d][t0:t0 + 64, 0:HK], w=[kt[d]])
                S.dma('sp', vt[d][:, :], v_s[t0:t0 + 64, 0:HV], w=[vt[d]])
                S.dma('act', qTt[d][:, :].rearrange("p (c t) -> p c t", c=nkc), qT_s[0:HK, t0:t0 + 64].rearrange("(c p) t -> p c t", p=128), w=[qTt[d]])
                S.dma('act', kTt[d][:, :].rearrange("p (c t) -> p c t", c=nkc), kT_s[d][0:HK, t0:t0 + 64].rearrange("(c p) t -> p c t", p=128), w=[kTt[d]])
                yield
                tri = C_TF if d == 0 else C_TB
                stri = C_SF if d == 0 else C_SB
                bA = banks[0 + 4 * d]
                for kc in range(nkc):
                    S.op('pe', lambda e, kc=kc: e.matmul(bA[:, kc * 64:(kc + 1) * 64], lhsT=lgt[d][0:64, kc * 128:(kc + 1) * 128], rhs=cst[0:64, tri:tri + 64], start=True, stop=True), r=[lgt[d], cst], w=[bA])
                nb = HK // 512
                bB = [banks[1 + 4 * d], banks[2 + 4 * d]]
                for b in range(nb):
                    S.op('pe', lambda e, b=b: e.matmul(bB[b][0:64, 0:512], lhsT=cst[0:64, stri:stri + 64], rhs=lgt[d][0:64, b * 512:(b + 1) * 512], start=True, stop=True), r=[lgt[d], cst], w=[bB[b]])
                yield
                S.op('act', lambda e: e.activation(out=eg[d][:, :], in_=bA[:, 0:nkc * 64], func=AF.Exp), r=[bA], w=[eg[d]])
                S.op('act', lambda e: e.activation(out=eng[d][:, :], in_=bA[:, 0:nkc * 64], func=AF.Exp, scale=-1.0), r=[bA], w=[eng[d]])
                yield
                for sl in range(hp):
                    p0, p1 = sl * dk, (sl + 1) * dk
                    S.op('dve', lambda e, sl=sl, p0=p0, p1=p1: e.tensor_tensor(out=qd[d][sl][p0:p1, :], in0=qTt[d][p0:p1, :], in1=eg[d][p0:p1, :], op=ALU.mult), r=[qTt[d], eg[d]], w=[qd[d][sl]])
                S.op('dve', lambda e: e.tensor_tensor(out=ki[d][:, :], in0=kTt[d][:, :], in1=eng[d][:, :], op=ALU.mult), r=[kTt[d], eng[d]], w=[ki[d]])
                for b in range(nb):
                    S.op('act', lambda e, b=b: e.activation(out=eex[d][:, b * 512:(b + 1) * 512], in_=bB[b][0:64, 0:512], func=AF.Exp), r=[bB[b]], w=[eex[d]])
                S.op('dve', lambda e: e.tensor_tensor(out=kend[d][:, :], in0=kt[d][:, :], in1=eex[d][:, :], op=ALU.mult), r=[kt[d], eex[d]], w=[kend[d]])
                S.op('act', lambda e: e.activation(out=vb[d][:, :], in_=vt[d][:, :], func=AF.Copy), r=[vt[d]], w=[vb[d]])
                yield
                bC = banks[3 + 4 * d]
                for h in range(H):
                    kc, sl = divmod(h, hp)
                    S.op('pe', lambda e, h=h, kc=kc, sl=sl: e.matmul(bC[0:64, h * 64:(h + 1) * 64], lhsT=ki[d][:, kc * 64:(kc + 1) * 64], rhs=qd[d][sl][:, kc * 64:(kc + 1) * 64], start=True, stop=True), r=[ki[d], qd[d][sl]], w=[bC])
                yield
                S.op('dve', lambda e: e.tensor_tensor(out=att[d][:, :].rearrange("p (h i) -> p h i", h=H), in0=bC[0:64, 0:H * 64].rearrange("p (h i) -> p h i", h=H), in1=cst[0:64, tri:tri + 64].unsqueeze(1).to_broadcast([64, H, 64]), op=ALU.mult), r=[bC, cst], w=[att[d]])
                yield
                bO = [banks[1 + 4 * d], banks[2 + 4 * d]]
                for h in range(H):
                    bo = bO[(h * dv) // 512]
                    oc = (h * dv) % 512
                    S.op('pe', lambda e, h=h, bo=bo, oc=oc: e.matmul(bo[0:64, oc:oc + dv], lhsT=att[d][0:64, h * 64:(h + 1) * 64], rhs=vb[d][0:64, h * dv:(h + 1) * dv], start=True, stop=True), r=[att[d], vb[d]], w=[bo])
                yield
                for b in range(HV // 512):
                    S.op('act', lambda e, b=b: e.activation(out=osb[d][:, b * 512:(b + 1) * 512], in_=bO[b][0:64, 0:512], func=AF.Copy), r=[bO[b]], w=[osb[d]])
                for h in range(H):
                    kc, sl = divmod(h, hp)
                    bo = bO[(h * dv) // 512]
                    oc = (h * dv) % 512
                    S.op('pe', lambda e, h=h, bo=bo, oc=oc, kc=kc, sl=sl: e.matmul(bo[0:64, oc:oc + dv], lhsT=qd[d][sl][:, kc * 64:(kc + 1) * 64], rhs=Sb[d][:, kc * dv:(kc + 1) * dv], start=True, stop=True), r=[qd[d][sl], Sb[d]], w=[bo])
                yield
                for b in range(HV // 512):
                    S.op('dve', lambda e, b=b: e.tensor_tensor(out=osb[d][:, b * 512:(b + 1) * 512], in0=osb[d][:, b * 512:(b + 1) * 512], in1=bO[b][0:64, 0:512], op=ALU.add), r=[bO[b], osb[d]], w=[osb[d]])
                S.dma('pool', o_s[d][t0:t0 + 64, 0:HV], osb[d][:, :], r=[osb[d]])
                yield
                bS = [banks[0 + 4 * d], banks[3 + 4 * d]]
                wS = hp * dv
                for kc in range(nkc):
                    bs = bS[(kc * wS) // 512]
                    oc = (kc * wS) % 512
                    S.op('pe', lambda e, kc=kc, bs=bs, oc=oc: e.matmul(bs[:, oc:oc + wS], lhsT=kend[d][0:64, kc * 128:(kc + 1) * 128], rhs=vb[d][0:64, kc * wS:(kc + 1) * wS], start=True, stop=True), r=[kend[d], vb[d]], w=[bs])
                yield
                lastc = 63 if d == 0 else 0
                for kc in range(nkc):
                    bs = bS[(kc * wS) // 512]
                    oc = (kc * wS) % 512
                    for sl in range(hp):
                        p0, p1 = sl * dk, (sl + 1) * dk
                        S.op('dve', lambda e, kc=kc, bs=bs, oc=oc, sl=sl, p0=p0, p1=p1: e.scalar_tensor_tensor(out=Sst[d][p0:p1, kc * dv:(kc + 1) * dv], in0=Sst[d][p0:p1, kc * dv:(kc + 1) * dv], scalar=eg[d][p0:p1, kc * 64 + lastc:kc * 64 + lastc + 1], in1=bs[p0:p1, oc + sl * dv:oc + (sl + 1) * dv], op0=ALU.mult, op1=ALU.add), r=[Sst[d], eg[d], bs], w=[Sst[d]])
                S.op('act', lambda e: e.activation(out=Sb[d][:, :], in_=Sst[d][:, :], func=AF.Copy), r=[Sst[d]], w=[Sb[d]])

            order_f = list(range(NCH))
            order_b = list(range(NCC - 1, -1, -1)) + list(range(NCH - 1, NCC - 1, -1))
            for i in range(NCH):
                run_pair(step(order_f[i], 0), step(order_b[i], 1))
            ph.close()

        def head_rms(ph, src, dstbuf, dst_ap, H, hd, nwbuf, nw_bc, gate, sq, ss, rstd, epsb, tmp):
            n = H * hd
            v3 = lambda ap: ap.rearrange("p (h q) -> p h q", h=H)
            S.op('dve', lambda e: e.tensor_tensor(out=sq[:, 0:n], in0=src[:, 0:n], in1=src[:, 0:n], op=ALU.mult), r=[src], w=[sq])
            S.op('dve', lambda e: e.tensor_reduce(out=ss[:, 0:H], in_=v3(sq[:, 0:n]), axis=AX.X, op=ALU.add), r=[sq], w=[ss])
            S.op('act', lambda e: e.activation(out=rstd[:, 0:H], in_=ss[:, 0:H], func=AF.Sqrt, scale=1.0 / hd, bias=epsb[:, 0:1]), r=[ss, epsb], w=[rstd])
            S.op('dve', lambda e: e.reciprocal(out=rstd[:, 0:H], in_=rstd[:, 0:H]), r=[rstd], w=[rstd])
            S.op('dve', lambda e: e.tensor_tensor(out=v3(tmp[:, 0:n]), in0=v3(src[:, 0:n]), in1=rstd[:, 0:H].unsqueeze(2).to_broadcast([128, H, hd]), op=ALU.mult), r=[src, rstd], w=[tmp])
            S.op('dve', lambda e: e.tensor_tensor(out=v3(tmp[:, 0:n]), in0=v3(tmp[:, 0:n]), in1=nw_bc, op=ALU.mult), r=[tmp, nwbuf], w=[tmp])
            S.op('dve', lambda e: e.tensor_tensor(out=dst_ap, in0=tmp[:, 0:n], in1=gate[:, 0:n], op=ALU.mult), r=[tmp, gate], w=[dstbuf])

        def phaseD0a():
            ph = Phase()
            cst = load_consts(ph)
            Wo = ph.tile([128, 16 * D], BF16, "Wo")
            load_weight_bf16(ph, Wo, w_out0, 2048, D)
            snw = bc_load(ph, ssdnw[0:1, :], 1024, "snw")
            gnw = bc_load(ph, glanw[0:1, :], 128, "gnw")
            g1 = [bc_load(ph, modv[0, s:s + 1, 2 * D:3 * D], D, "g1") for s in range(2)]
            epsb = ph.tile([128, 1], F32, "epsb")
            S.op('dve', lambda e: e.memset(epsb[:, :], EPS), w=[epsb])
            idb = ph.tile([128, 128], BF16, "idb")
            S.op('dve', lambda e: e.tensor_copy(out=idb[:, :], in_=cst[:, C_I:C_I + 128]), r=[cst], w=[idb])
            a = ph.tile([128, D], F32, "a")
            b = ph.tile([128, D], F32, "b")
            g = ph.tile([128, D], F32, "g")
            sq = ph.tile([128, D], F32, "sq")
            tmp = ph.tile([128, D], F32, "tmp")
            xt = ph.tile([128, D], F32, "xt")
            mix = ph.tile([128, 2048], BF16, "mix")
            mixT = ph.tile([128, 2048], BF16, "mixT")
            ss = ph.tile([128, 16], F32, "ss")
            rstd = ph.tile([128, 16], F32, "rstd")
            xo = ph.tile([128, D], F32, "xo")
            for m in range(NT):
                r0 = m * 128
                for part in range(2):
                    srcs = ydir if part == 0 else odir
                    gsrc = zs if part == 0 else rs
                    S.dma('sp', a[:, :], srcs[0][r0:r0 + 128, :], w=[a])
                    S.dma('act', b[:, :], srcs[1][r0:r0 + 128, :], w=[b])
                    S.dma('sp', g[:, :], gsrc[r0:r0 + 128, :], w=[g])
                    S.op('dve', lambda e: e.tensor_tensor(out=a[:, :], in0=a[:, :], in1=b[:, :], op=ALU.add), r=[a, b], w=[a])
                    S.op('act', lambda e: e.activation(out=g[:, :], in_=g[:, :], func=AF.Silu), r=[g], w=[g])
                    if part == 0:
                        S.op('dve', lambda e: e.tensor_tensor(out=a[:, :], in0=a[:, :], in1=g[:, :], op=ALU.mult), r=[a, g], w=[a])
                        S.op('dve', lambda e: e.tensor_tensor(out=sq[:, :], in0=a[:, :], in1=a[:, :], op=ALU.mult), r=[a], w=[sq])
                        S.op('dve', lambda e: e.tensor_reduce(out=ss[:, 0:4], in_=sq[:, :].rearrange("p (h q) -> p h q", h=4), axis=AX.X, op=ALU.add), r=[sq], w=[ss])
                        S.op('act', lambda e: e.activation(out=rstd[:, 0:4], in_=ss[:, 0:4], func=AF.Sqrt, scale=1.0 / 256, bias=epsb[:, 0:1]), r=[ss, epsb], w=[rstd])
                        S.op('dve', lambda e: e.reciprocal(out=rstd[:, 0:4], in_=rstd[:, 0:4]), r=[rstd], w=[rstd])
                        S.op('dve', lambda e: e.tensor_tensor(out=tmp[:, :].rearrange("p (h q) -> p h q", h=4), in0=a[:, :].rearrange("p (h q) -> p h q", h=4), in1=rstd[:, 0:4].unsqueeze(2).to_broadcast([128, 4, 256]), op=ALU.mult), r=[a, rstd], w=[tmp])
                        S.op('dve', lambda e: e.tensor_tensor(out=mix[:, 0:1024], in0=tmp[:, :], in1=snw[:, :], op=ALU.mult), r=[tmp, snw], w=[mix])
                    else:
                        head_rms(ph, a, mix, mix[:, 1024:2048], 8, 128, gnw, gnw[:, 0:128].unsqueeze(1).to_broadcast([128, 8, 128]), g, sq, ss, rstd, epsb, tmp)
                for q in range(4):
                    bk = banks[q]
                    bkb = bk[:, 0:256].bitcast(BF16)
                    for c in range(4):
                        cc = q * 4 + c
                        S.op('pe', lambda e, bkb=bkb, c=c, cc=cc: e.transpose(bkb[:, c * 128:(c + 1) * 128], mix[:, cc * 128:(cc + 1) * 128], idb[:, :]), r=[mix, idb], w=[bk])
                    S.op('act' if q % 2 == 0 else 'dve', (lambda e, bkb=bkb, q=q: e.activation(out=mixT[:, q * 512:(q + 1) * 512], in_=bkb[:, 0:512], func=AF.Copy)) if q % 2 == 0 else (lambda e, bkb=bkb, q=q: e.tensor_copy(out=mixT[:, q * 512:(q + 1) * 512], in_=bkb[:, 0:512])), r=[bk], w=[mixT])
                S.dma('sp', xt[:, :], xsrc(0, m), w=[xt])
                for h2 in range(2):
                    bk = banks[4 + h2]
                    for c in range(16):
                        S.op('pe', lambda e, bk=bk, c=c, h2=h2: e.matmul(bk[:, 0:512], lhsT=mixT[:, c * 128:(c + 1) * 128], rhs=Wo[:, c * D + h2 * 512:c * D + (h2 + 1) * 512], start=(c == 0), stop=(c == 15)), r=[mixT, Wo], w=[bk])
                    gg = g1[1] if m < NCT else g1[0]
                    S.op('dve', lambda e, bk=bk, h2=h2, gg=gg: e.tensor_tensor(out=xo[:, h2 * 512:(h2 + 1) * 512], in0=bk[:, 0:512], in1=gg[:, h2 * 512:(h2 + 1) * 512], op=ALU.mult), r=[bk, gg], w=[xo])
                S.op('dve', lambda e: e.tensor_tensor(out=xo[:, :], in0=xo[:, :], in1=xt[:, :], op=ALU.add), r=[xo, xt], w=[xo])
                S.dma('pool', xmid[r0:r0 + 128, :], xo[:, :], r=[xo])
            ph.close()

        def phaseMLP(layer, tiles, final):
            ph = Phase()
            cst = ph.tile([128, 128], F32, "cstI")
            S.dma('sp', cst[:, :], consts[:, 0:128], w=[cst])
            W1 = ph.tile([128, 8 * 4096], BF16, "W1")
            load_weight_bf16(ph, W1, mlp_w1[layer], D, 4096)
            W2 = ph.tile([128, 32 * D], BF16, "W2")
            load_weight_bf16(ph, W2, mlp_w2[layer], 4096, D)
            NW = bc_load(ph, norm2_w[layer:layer + 1, :], D, "NW")
            Am = ph.tile([128, D], F32, "Am")
            SHm = ph.tile([128, D], F32, "SHm")
            Gm = ph.tile([128, D], F32, "Gm")
            cur = [None]

            def set_mod(src):
                if cur[0] == src:
                    return
                cur[0] = src
                S.dma('sp', Am[:, :], modv[layer, src:src + 1, 4 * D:5 * D].partition_broadcast(128), w=[Am])
                S.dma('sp', SHm[:, :], modv[layer, src:src + 1, 3 * D:4 * D].partition_broadcast(128), w=[SHm])
                S.dma('sp', Gm[:, :], modv[layer, src:src + 1, 5 * D:6 * D].partition_broadcast(128), w=[Gm])
                S.op('dve', lambda e: e.scalar_tensor_tensor(out=Am[:, :], in0=Am[:, :], scalar=1.0, in1=NW[:, :], op0=ALU.add, op1=ALU.mult), r=[Am, NW], w=[Am])
            if final:
                fnw = bc_load(ph, final_w[0:1, :], D, "fnw")
            epsb = ph.tile([128, 1], F32, "epsb")
            S.op('dve', lambda e: e.memset(epsb[:, :], EPS), w=[epsb])
            idb = ph.tile([128, 128], BF16, "idb")
            S.op('dve', lambda e: e.tensor_copy(out=idb[:, :], in_=cst[:, C_I:C_I + 128]), r=[cst], w=[idb])
            xt = ph.tile([128, D], F32, "xt")
            junk = ph.tile([128, D], BF16, "junk")
            tmp = ph.tile([128, D], F32, "tmp")
            hb = ph.tile([128, D], BF16, "hb")
            hT = ph.tile([128, D], BF16, "hT")
            h1 = ph.tile([128, 512], F32, "h1")
            h1T = ph.tile([128, 4096], BF16, "h1T")
            xo = ph.tile([128, D], F32, "xo")
            ss = ph.tile([128, 1], F32, "ss")
            rstd = ph.tile([128, 1], F32, "rstd")
            for m in tiles:
                r0 = m * 128
                S.dma('sp', xt[:, :], xmid[r0:r0 + 128, :], w=[xt])
                rms_rstd(ph, xt, junk, ss, rstd, epsb, D, 1.0 / D)
                set_mod(1 if m < NCT else 0)
                A, SH, G2 = Am, SHm, Gm
                S.op('dve', lambda e, A=A: e.scalar_tensor_tensor(out=tmp[:, :], in0=xt[:, :], scalar=rstd[:, 0:1], in1=A[:, :], op0=ALU.mult, op1=ALU.mult), r=[xt, rstd, A], w=[tmp])
                S.op('dve', lambda e, SH=SH: e.tensor_tensor(out=hb[:, :], in0=tmp[:, :], in1=SH[:, :], op=ALU.add), r=[tmp, SH], w=[hb])
                bk = banks[0]
                bkb = bk[:, :].bitcast(BF16)
                for j in range(8):
                    S.op('pe', lambda e, j=j: e.transpose(bkb[:, j * 128:(j + 1) * 128], hb[:, j * 128:(j + 1) * 128], idb[:, :]), r=[hb, idb], w=[bk])
                S.op('act', lambda e: e.activation(out=hT[:, :], in_=bkb[:, 0:1024], func=AF.Copy), r=[bk], w=[hT])
                for q in range(8):
                    bk = banks[1 + q % 5]
                    for c in range(4):
                        fc = q * 4 + c
                        for j in range(8):
                            S.op('pe', lambda e, bk=bk, c=c, fc=fc, j=j: e.matmul(bk[:, c * 128:(c + 1) * 128], lhsT=W1[:, j * 4096 + fc * 128:j * 4096 + (fc + 1) * 128], rhs=hT[:, j * 128:(j + 1) * 128], start=(j == 0), stop=(j == 7)), r=[W1, hT], w=[bk])
                    S.op('act', lambda e, bk=bk: e.activation(out=h1[:, :], in_=bk[:, :], func=AF.Relu), r=[bk], w=[h1])
                    S.op('dve', lambda e, q=q: e.tensor_tensor(out=h1T[:, q * 512:(q + 1) * 512], in0=h1[:, :], in1=h1[:, :], op=ALU.mult), r=[h1], w=[h1T])
                for h2 in range(2):
                    bk = banks[6 + h2]
                    for fc in range(32):
                        S.op('pe', lambda e, bk=bk, fc=fc, h2=h2: e.matmul(bk[:, 0:512], lhsT=h1T[:, fc * 128:(fc + 1) * 128], rhs=W2[:, fc * D + h2 * 512:fc * D + (h2 + 1) * 512], start=(fc == 0), stop=(fc == 31)), r=[h1T, W2], w=[bk])
                    S.op('dve', lambda e, bk=bk, h2=h2, G2=G2: e.tensor_tensor(out=xo[:, h2 * 512:(h2 + 1) * 512], in0=bk[:, 0:512], in1=G2[:, h2 * 512:(h2 + 1) * 512], op=ALU.mult), r=[bk, G2], w=[xo])
                S.op('dve', lambda e: e.tensor_tensor(out=xo[:, :], in0=xo[:, :], in1=xt[:, :], op=ALU.add), r=[xo, xt], w=[xo])
                if not final:
                    S.dma('pool', xres[r0:r0 + 128, :], xo[:, :], r=[xo])
                else:
                    rms_rstd(ph, xo, junk, ss, rstd, epsb, D, 1.0 / D)
                    S.op('dve', lambda e: e.scalar_tensor_tensor(out=tmp[:, :], in0=xo[:, :], scalar=rstd[:, 0:1], in1=fnw[:, :], op0=ALU.mult, op1=ALU.mult), r=[xo, rstd, fnw], w=[tmp])
                    S.dma('pool', out[r0 - CTX:r0 - CTX + 128, :], tmp[:, :], r=[tmp])
            ph.close()


        def phaseA1():
            tok_groups = []
            feat_groups = []

            def extra(c):
                ph = c['ph']
                l0 = bc_load(ph, lbl[0:1, :], 2048, "l0")
                oml = bc_load(ph, lbl[1:2, :], 2048, "oml")
                S.op('dve', lambda e: e.tensor_tensor(out=oml[:, :], in0=oml[:, :], in1=l0[:, :], op=ALU.subtract), r=[oml, l0], w=[oml])
                S.op('act', lambda e: e.activation(out=oml[:, :], in_=oml[:, :], func=AF.Sigmoid, scale=-1.0), r=[oml], w=[oml])
                omlT = ph.tile([128, 32], F32, "omlT")
                S.dma('sp', omlT[:, :], lblT[:, :, :].rearrange("p l c -> p (l c)"), w=[omlT])
                S.op('dve', lambda e: e.tensor_tensor(out=omlT[:, 16:32], in0=omlT[:, 16:32], in1=omlT[:, 0:16], op=ALU.subtract), r=[omlT], w=[omlT])
                S.op('act', lambda e: e.activation(out=omlT[:, 16:32], in_=omlT[:, 16:32], func=AF.Sigmoid, scale=-1.0), r=[omlT], w=[omlT])
                oneb = ph.tile([128, 1], F32, "oneb")
                S.op('dve', lambda e: e.memset(oneb[:, :], 1.0), w=[oneb])
                stg32, k32 = c['stg32'], c['k32']

                def nxt():
                    st = stg32[k32[0] % len(stg32)]
                    k32[0] += 1
                    return st

                def f_tok(m, bk, off, n):
                    d = off // 1024
                    col = off % 1024
                    st = nxt()
                    S.op('act', lambda e: e.activation(out=st[:, 0:n], in_=bk[:, 0:n], func=AF.Sigmoid, scale=-1.0), r=[bk], w=[st])
                    S.op('dve', lambda e: e.tensor_tensor(out=st[:, 0:n], in0=st[:, 0:n], in1=oml[:, off:off + n], op=ALU.mult), r=[st, oml], w=[st])
                    S.dma('pool', ktoks[d][m * 128:(m + 1) * 128, col:col + n], st[:, 0:n], r=[st])
                    st2 = nxt()
                    S.op('act', lambda e: e.activation(out=st2[:, 0:n], in_=st[:, 0:n], func=AF.Ln, scale=-1.0, bias=oneb[:, 0:1]), r=[st, oneb], w=[st2])
                    S.dma('pool', lgs[d][m * 128:(m + 1) * 128, col:col + n], st2[:, 0:n], r=[st2])

                def f_feat(m, bk, q0, nch):
                    st = nxt()
                    S.op('act', lambda e: e.activation(out=st[:, 0:nch * 128], in_=bk[:, 0:nch * 128], func=AF.Sigmoid, scale=-1.0), r=[bk], w=[st])
                    for cc in range(nch):
                        ch = q0 + cc
                        S.op('dve', lambda e, cc=cc, ch=ch: e.tensor_scalar(out=st[:, cc * 128:(cc + 1) * 128], in0=st[:, cc * 128:(cc + 1) * 128], scalar1=omlT[:, 16 + ch:17 + ch], scalar2=None, op0=ALU.mult), r=[st, omlT], w=[st])
                    d = q0 // 8
                    r0 = (q0 % 8) * 128
                    S.dma('pool', kTs[d][r0:r0 + nch * 128, m * 128:(m + 1) * 128].rearrange("(c p) t -> p c t", p=128),
                          st[:, 0:nch * 128].rearrange("p (c t) -> p c t", c=nch), r=[st])
                tok_groups.append((1024, 1024, make_tok_store(c, vtok, 0, 1024)))
                tok_groups.append((2048, 2048, f_tok))
                tok_groups.append((4096, 1024, make_tok_store(c, rs, 0, 1024)))
                tok_groups.append((5120, 384, make_tok_store(c, utok, 0, 384)))
                feat_groups.append((0, 8, 128, make_feat_store(c, qT, 0, func=AF.Silu)))
                feat_groups.append((2048, 16, 128, f_feat))
            proj_phase(1, w_in1, 5504, tok_groups, feat_groups, extra)

        def phaseC1():
            ph = Phase()
            cst = load_consts(ph)
            PI = float(np.pi)
            prm = ph.tile([128, 72], F32, "prm")
            S.dma('sp', prm[:, :], s5p[:, :], w=[prm])
            bri = ph.tile([128, 384], F32, "bri")
            S.dma('sp', bri[:, :], s5b[:, :], w=[bri])
            cri = ph.tile([128, 384], F32, "cri")
            S.dma('sp', cri[:, :], s5c[:, :], w=[cri])
            dsk = ph.tile([128, 3], F32, "dsk")
            S.dma('sp', dsk[:, :], s5d[:, :], w=[dsk])
            negpi = ph.tile([128, 1], F32, "negpi")
            S.op('dve', lambda e: e.memset(negpi[:, :], -PI), w=[negpi])
            w12 = ph.tile([128, 12 * 12], F32, "w12")

            def V(i):
                return w12[:, i * 12:(i + 1) * 12]
            CX = [ph.tile([128, 12 * 128], F32, "CX") for _ in range(2)]
            bbX = ph.tile([128, 12 * 128], F32, "bbX")
            BbT = [[ph.tile([128, 12 * 128], F32, "BbT") for _ in range(2)] for _ in range(2)]
            Ecs = [[ph.tile([128, 12 * 128], F32, "E") for _ in range(2)] for _ in range(2)]
            rmag = [ph.tile([128, 12 * 128], F32, "rmag") for _ in range(2)]
            bb = ph.tile([128, 2 * 192], F32, "bb")
            tA = ph.tile([128, 12 * 128], F32, "tA")
            tB = ph.tile([128, 12 * 128], F32, "tB")

            def place(dst, src_ap3, negate=False):
                S.op('dve', lambda e: e.memset(dst[:, :], 0.0), w=[dst])
                for sc in range(12):
                    for g2 in range(2):
                        gl = (2 * sc + g2) % 8
                        p0, p1 = g2 * 64, (g2 + 1) * 64
                        if negate:
                            S.op('dve', lambda e, sc=sc, gl=gl, p0=p0, p1=p1: e.tensor_scalar(out=dst[p0:p1, sc * 128 + gl * 16:sc * 128 + gl * 16 + 16], in0=src_ap3(sc, p0, p1), scalar1=-1.0, scalar2=None, op0=ALU.mult), r=[bb, cri], w=[dst])
                        else:
                            S.op('dve', lambda e, sc=sc, gl=gl, p0=p0, p1=p1: e.tensor_copy(out=dst[p0:p1, sc * 128 + gl * 16:sc * 128 + gl * 16 + 16], in_=src_ap3(sc, p0, p1)), r=[bb, cri], w=[dst])
            place(CX[0], lambda sc, p0, p1: cri[p0:p1, sc * 16:(sc + 1) * 16])
            place(CX[1], lambda sc, p0, p1: cri[p0:p1, 192 + sc * 16:192 + (sc + 1) * 16], negate=True)

            def tt(out, a, b, op, r, w):
                S.op('dve', lambda e: e.tensor_tensor(out=out, in0=a, in1=b, op=op), r=r, w=w)

            def sin_of(out, theta, shift):
                S.op('dve', lambda e: e.tensor_scalar(out=V(10), in0=theta, scalar1=shift + PI, scalar2=None, op0=ALU.add), r=[w12], w=[w12])
                for kk in range(1, 5):
                    S.op('dve', lambda e, kk=kk: e.tensor_scalar(out=V(11), in0=V(10), scalar1=2.0 * PI * kk, scalar2=-2.0 * PI, op0=ALU.is_ge, op1=ALU.mult), r=[w12], w=[w12])
                    if kk == 1:
                        S.op('dve', lambda e: e.tensor_tensor(out=V(9), in0=V(10), in1=V(11), op=ALU.add), r=[w12], w=[w12])
                    else:
                        S.op('dve', lambda e: e.tensor_tensor(out=V(9), in0=V(9), in1=V(11), op=ALU.add), r=[w12], w=[w12])
                S.op('act', lambda e: e.activation(out=out, in_=V(9), func=AF.Sin, bias=negpi[:, 0:1]), r=[w12, negpi], w=[w12])

            for d in range(2):
                are = prm[:, d * 36:d * 36 + 12]
                aim = prm[:, d * 36 + 12:d * 36 + 24]
                ldt = prm[:, d * 36 + 24:d * 36 + 36]
                S.op('act', lambda e, ldt=ldt: e.activation(out=V(0), in_=ldt, func=AF.Exp), r=[prm], w=[w12])
                tt(V(1), are, V(0), ALU.mult, [prm, w12], [w12])
                S.op('act', lambda e: e.activation(out=V(1), in_=V(1), func=AF.Exp), r=[w12], w=[w12])
                tt(V(2), aim, V(0), ALU.mult, [prm, w12], [w12])
                sin_of(V(3), V(2), 0.0)
                sin_of(V(4), V(2), PI / 2)
                tt(V(5), V(1), V(4), ALU.mult, [w12], [w12])
                tt(V(6), V(1), V(3), ALU.mult, [w12], [w12])
                tt(V(7), are, are, ALU.mult, [prm], [w12])
                tt(V(8), aim, aim, ALU.mult, [prm], [w12])
                tt(V(7), V(7), V(8), ALU.add, [w12], [w12])
                S.op('dve', lambda e: e.reciprocal(out=V(7), in_=V(7)), r=[w12], w=[w12])
                S.op('dve', lambda e: e.tensor_scalar(out=V(5), in0=V(5), scalar1=-1.0, scalar2=None, op0=ALU.add), r=[w12], w=[w12])
                tt(V(8), V(5), are, ALU.mult, [w12, prm], [w12])
                tt(V(9), V(6), aim, ALU.mult, [w12, prm], [w12])
                tt(V(8), V(8), V(9), ALU.add, [w12], [w12])
                tt(V(8), V(8), V(7), ALU.mult, [w12], [w12])
                tt(V(9), V(6), are, ALU.mult, [w12, prm], [w12])
                tt(V(10), V(5), aim, ALU.mult, [w12, prm], [w12])
                tt(V(9), V(9), V(10), ALU.subtract, [w12], [w12])
                tt(V(9), V(9), V(7), ALU.mult, [w12], [w12])
                b3 = lambda c0: bri[:, c0:c0 + 192].rearrange("p (s c) -> p s c", s=12)
                o3 = lambda c0: bb[:, c0:c0 + 192].rearrange("p (s c) -> p s c", s=12)
                t3 = tA[:, 0:192].rearrange("p (s c) -> p s c", s=12)
                zr3 = V(8).unsqueeze(2).to_broadcast([128, 12, 16])
                zi3 = V(9).unsqueeze(2).to_broadcast([128, 12, 16])
                tt(o3(0), b3(0), zr3, ALU.mult, [bri, w12], [bb])
                tt(t3, b3(192), zi3, ALU.mult, [bri, w12], [tA])
                tt(o3(0), o3(0), t3, ALU.subtract, [bb, tA], [bb])
                tt(o3(192), b3(192), zr3, ALU.mult, [bri, w12], [bb])
                tt(t3, b3(0), zi3, ALU.mult, [bri, w12], [tA])
                tt(o3(192), o3(192), t3, ALU.add, [bb, tA], [bb])
                for ri in range(2):
                    place(bbX, lambda sc, p0, p1, ri=ri: bb[p0:p1, ri * 192 + sc * 16:ri * 192 + (sc + 1) * 16])
                    for q in range(3):
                        bk = banks[q]
                        for c4 in range(4):
                            sc = q * 4 + c4
                            S.op('pe', lambda e, bk=bk, c4=c4, sc=sc: e.transpose(bk[:, c4 * 128:(c4 + 1) * 128], bbX[:, sc * 128:(sc + 1) * 128], cst[:, C_I:C_I + 128]), r=[bbX, cst], w=[bk])
                        S.op('act', lambda e, bk=bk, q=q, ri=ri, d=d: e.activation(out=BbT[d][ri][:, q * 512:(q + 1) * 512], in_=bk[:, :], func=AF.Copy), r=[bk], w=[BbT[d][ri]])
                Ec, Es = Ecs[d]
                Ec3 = Ec[:, :].rearrange("p (s t) -> p s t", s=12)
                Es3 = Es[:, :].rearrange("p (s t) -> p s t", s=12)
                i0 = 0 if d == 0 else 127
                S.op('dve', lambda e: e.tensor_copy(out=Ec3[:, :, i0:i0 + 1], in_=V(4).unsqueeze(2)), r=[w12], w=[Ec])
                S.op('dve', lambda e: e.tensor_copy(out=Es3[:, :, i0:i0 + 1], in_=V(3).unsqueeze(2)), r=[w12], w=[Es])
                tA3 = tA[:, :].rearrange("p (s t) -> p s t", s=12)
                tB3 = tB[:, :].rearrange("p (s t) -> p s t", s=12)
                for k in range(7):
                    n = 1 << k
                    if d == 0:
                        src = slice(0, n); dst = slice(n, 2 * n); piv = n - 1
                    else:
                        src = slice(128 - n, 128); dst = slice(128 - 2 * n, 128 - n); piv = 128 - n
                    pc = Ec3[:, :, piv:piv + 1].to_broadcast([128, 12, n])
                    ps_ = Es3[:, :, piv:piv + 1].to_broadcast([128, 12, n])
                    tt(tA3[:, :, 0:n], Ec3[:, :, src], pc, ALU.mult, [Ec], [tA])
                    tt(tB3[:, :, 0:n], Es3[:, :, src], ps_, ALU.mult, [Es], [tB])
                    tt(tA3[:, :, 0:n], tA3[:, :, 0:n], tB3[:, :, 0:n], ALU.subtract, [tA, tB], [tA])
                    tt(tB3[:, :, 0:n], Ec3[:, :, src], ps_, ALU.mult, [Ec, Es], [tB])
                    tt(tA3[:, :, 64:64 + n], Es3[:, :, src], pc, ALU.mult, [Ec, Es], [tA])
                    tt(Es3[:, :, dst], tB3[:, :, 0:n], tA3[:, :, 64:64 + n], ALU.add, [tA, tB], [Es])
                    S.op('dve', lambda e, dst=dst, n=n: e.tensor_copy(out=Ec3[:, :, dst], in_=tA3[:, :, 0:n]), r=[tA], w=[Ec])
                S.op('dve', lambda e, d=d: e.tensor_copy(out=rmag[d][:, :].rearrange("p (s t) -> p s t", s=12), in_=V(1).unsqueeze(2).to_broadcast([128, 12, 128])), r=[w12], w=[rmag[d]])

            def mk(shape, dt, name):
                return [ph.tile(shape, dt, name) for _ in range(2)]
            ut = mk([128, 384], F32, "ut")
            uT = mk([128, 384], F32, "uT")
            wre = mk([128, 1536], F32, "wre")
            wim = mk([128, 1536], F32, "wim")
            zre = mk([128, 1536], F32, "zre")
            zim = mk([128, 1536], F32, "zim")
            t1 = ph.tile([128, 1536], F32, "t1")
            t2 = ph.tile([128, 1536], F32, "t2")
            t3 = ph.tile([128, 1536], F32, "t3")
            t4 = ph.tile([128, 1536], F32, "t4")
            PE2 = 'pool' if os.environ.get('KPOOL', '1') == '1' else 'dve'
            xst = mk([128, 24], F32, "xst")
            ysb = mk([128, 384], F32, "ysb")
            for d in range(2):
                S.op('dve', lambda e, d=d: e.memset(xst[d][:, :], 0.0), w=[xst[d]])

            def rev(buf, c0, n):
                a = buf[:, c0:c0 + n]
                return bass.AP(a.tensor, a.offset + (n - 1), [[a.ap[0][0], 128], [-1, n]])

            def step(m, d):
                t0 = m * 128
                Ec, Es = Ecs[d]
                S.dma('sp', ut[d][:, :], utok[t0:t0 + 128, :], w=[ut[d]])
                b6 = banks[6]
                for cc in range(3):
                    S.op('pe', lambda e, cc=cc: e.transpose(b6[:, cc * 128:(cc + 1) * 128], ut[d][:, cc * 128:(cc + 1) * 128], cst[:, C_I:C_I + 128]), r=[ut[d], cst], w=[b6])
                yield
                S.op('act', lambda e: e.activation(out=uT[d][:, :], in_=b6[:, 0:384], func=AF.Copy), r=[b6], w=[uT[d]])
                for ri in range(2):
                    for sc in range(12):
                        bk = banks[ri * 3 + sc // 4]
                        cc = sc // 4
                        S.op('pe', lambda e, bk=bk, sc=sc, cc=cc, ri=ri: e.matmul(bk[:, (sc % 4) * 128:(sc % 4 + 1) * 128], lhsT=BbT[d][ri][:, sc * 128:(sc + 1) * 128], rhs=uT[d][:, cc * 128:(cc + 1) * 128], start=True, stop=True), r=[BbT[d][ri], uT[d]], w=[bk])
                yield
                for q in range(3):
                    sl = slice(q * 512, (q + 1) * 512)
                    bre, bim = banks[q], banks[3 + q]
                    S.op('dve', lambda e, sl=sl, bre=bre: e.tensor_tensor(out=t1[:, sl], in0=bre[:, :], in1=Ec[:, sl], op=ALU.mult), r=[bre, Ec], w=[t1])
                    S.op('dve', lambda e, sl=sl, bim=bim: e.tensor_tensor(out=t2[:, sl], in0=bim[:, :], in1=Es[:, sl], op=ALU.mult), r=[bim, Es], w=[t2])
                    S.op('dve', lambda e, sl=sl: e.tensor_tensor(out=wre[d][:, sl], in0=t1[:, sl], in1=t2[:, sl], op=ALU.add), r=[t1, t2], w=[wre[d]])
                    S.op('dve', lambda e, sl=sl, bim=bim: e.tensor_tensor(out=t1[:, sl], in0=bim[:, :], in1=Ec[:, sl], op=ALU.mult), r=[bim, Ec], w=[t1])
                    S.op('dve', lambda e, sl=sl, bre=bre: e.tensor_tensor(out=t2[:, sl], in0=bre[:, :], in1=Es[:, sl], op=ALU.mult), r=[bre, Es], w=[t2])
                    S.op('dve', lambda e, sl=sl: e.tensor_tensor(out=wim[d][:, sl], in0=t1[:, sl], in1=t2[:, sl], op=ALU.subtract), r=[t1, t2], w=[wim[d]])
                yield
                for sc in range(12):
                    for (wb, zb, ci) in ((wre[d], zre[d], sc), (wim[d], zim[d], 12 + sc)):
                        if d == 0:
                            S.op('dve', lambda e, wb=wb, zb=zb, ci=ci, sc=sc: e.tensor_tensor_scan(out=zb[:, sc * 128:(sc + 1) * 128], data0=rmag[d][:, sc * 128:(sc + 1) * 128], data1=wb[:, sc * 128:(sc + 1) * 128], initial=xst[d][:, ci:ci + 1], op0=ALU.mult, op1=ALU.add), r=[wb, rmag[d], xst[d]], w=[zb])
                        else:
                            S.op('dve', lambda e, wb=wb, zb=zb, ci=ci, sc=sc: e.tensor_tensor_scan(out=rev(zb, sc * 128, 128), data0=rmag[d][:, sc * 128:(sc + 1) * 128], data1=rev(wb, sc * 128, 128), initial=xst[d][:, ci:ci + 1], op0=ALU.mult, op1=ALU.add), r=[wb, rmag[d], xst[d]], w=[zb])
                yield
                for q in range(3):
                    sl = slice(q * 512, (q + 1) * 512)
                    S.op('dve', lambda e, sl=sl: e.tensor_tensor(out=t1[:, sl], in0=zre[d][:, sl], in1=Ec[:, sl], op=ALU.mult), r=[zre[d], Ec], w=[t1])
                    S.op('dve', lambda e, sl=sl: e.tensor_tensor(out=t2[:, sl], in0=zim[d][:, sl], in1=Es[:, sl], op=ALU.mult), r=[zim[d], Es], w=[t2])
                    S.op('dve', lambda e, sl=sl: e.tensor_tensor(out=wre[d][:, sl], in0=t1[:, sl], in1=t2[:, sl], op=ALU.subtract), r=[t1, t2], w=[wre[d]])
                    S.op(PE2, lambda e, sl=sl: e.tensor_tensor(out=t3[:, sl], in0=zre[d][:, sl], in1=Es[:, sl], op=ALU.mult), r=[zre[d], Es], w=[t3])
                    S.op(PE2, lambda e, sl=sl: e.tensor_tensor(out=t4[:, sl], in0=zim[d][:, sl], in1=Ec[:, sl], op=ALU.mult), r=[zim[d], Ec], w=[t4])
                    S.op(PE2, lambda e, sl=sl: e.tensor_tensor(out=wim[d][:, sl], in0=t3[:, sl], in1=t4[:, sl], op=ALU.add), r=[t3, t4], w=[wim[d]])
                yield
                last = 127 if d == 0 else 0
                S.op('dve', lambda e: e.tensor_copy(out=xst[d][:, 0:12].unsqueeze(2), in_=wre[d][:, :].rearrange("p (s t) -> p s t", s=12)[:, :, last:last + 1]), r=[wre[d]], w=[xst[d]])
                S.op('dve', lambda e: e.tensor_copy(out=xst[d][:, 12:24].unsqueeze(2), in_=wim[d][:, :].rearrange("p (s t) -> p s t", s=12)[:, :, last:last + 1]), r=[wim[d]], w=[xst[d]])
                b7 = banks[7]
                for cc in range(3):
                    for k4 in range(4):
                        sc = cc * 4 + k4
                        S.op('pe', lambda e, cc=cc, sc=sc, k4=k4: e.matmul(b7[:, cc * 128:(cc + 1) * 128], lhsT=CX[0][:, sc * 128:(sc + 1) * 128], rhs=wre[d][:, sc * 128:(sc + 1) * 128], start=(k4 == 0), stop=False), r=[CX[0], wre[d]], w=[b7])
                        S.op('pe', lambda e, cc=cc, sc=sc, k4=k4: e.matmul(b7[:, cc * 128:(cc + 1) * 128], lhsT=CX[1][:, sc * 128:(sc + 1) * 128], rhs=wim[d][:, sc * 128:(sc + 1) * 128], start=False, stop=(k4 == 3)), r=[CX[1], wim[d]], w=[b7])
                yield
                if d == 0:
                    for cc in range(3):
                        S.op('dve', lambda e, cc=cc: e.scalar_tensor_tensor(out=ysb[d][:, cc * 128:(cc + 1) * 128], in0=uT[d][:, cc * 128:(cc + 1) * 128], scalar=dsk[:, cc:cc + 1], in1=b7[:, cc * 128:(cc + 1) * 128], op0=ALU.mult, op1=ALU.add), r=[uT[d], dsk, b7], w=[ysb[d]])
                else:
                    S.op('act', lambda e: e.activation(out=ysb[d][:, :], in_=b7[:, 0:384], func=AF.Copy), r=[b7], w=[ysb[d]])
                S.dma('pool', yT5[d][:, t0:t0 + 128].rearrange("(c p) t -> p c t", p=128), ysb[d][:, :].rearrange("p (c t) -> p c t", c=3), r=[ysb[d]])

            order_f = list(range(NT))
            order_b = list(range(NCT - 1, -1, -1)) + list(range(NT - 1, NCT - 1, -1))
            for i in range(NT):
                run_seq(step(order_f[i], 0), step(order_b[i], 1))
            ph.close()

        def phaseD1a():
            ph = Phase()
            cst = load_consts(ph)
            Wo = ph.tile([128, 11 * D], BF16, "Wo1")
            load_weight_bf16(ph, Wo, w_out1, 1408, D)
            hnw = bc_load(ph, hgnw[0:1, :], 128, "hnw")
            g1 = bc_load(ph, modv[1, 0:1, 2 * D:3 * D], D, "g1")
            gw = ph.tile([128, 3 * 384], F32, "gw")
            S.dma('sp', gw[:, :].rearrange("p (c n) -> p c n", c=3), gluw[:, :].rearrange("(c p) n -> p c n", p=128), w=[gw])
            gb = ph.tile([128, 3], F32, "gb")
            S.dma('sp', gb[:, :], glub[:, :], w=[gb])
            epsb = ph.tile([128, 1], F32, "epsb")
            S.op('dve', lambda e: e.memset(epsb[:, :], EPS), w=[epsb])
            idb = ph.tile([128, 128], BF16, "idb")
            S.op('dve', lambda e: e.tensor_copy(out=idb[:, :], in_=cst[:, C_I:C_I + 128]), r=[cst], w=[idb])
            a = ph.tile([128, D], F32, "a")
            b = ph.tile([128, D], F32, "b")
            g = ph.tile([128, D], F32, "g")
            sq = ph.tile([128, D], F32, "sq")
            tmp = ph.tile([128, D], F32, "tmp")
            xt = ph.tile([128, D], F32, "xt")
            mix = ph.tile([128, 1024], BF16, "mix")
            mixT = ph.tile([128, 11 * 128], BF16, "mixT")
            ss = ph.tile([128, 16], F32, "ss")
            rstd = ph.tile([128, 16], F32, "rstd")
            xo = ph.tile([128, D], F32, "xo")
            ya = ph.tile([128, 384], F32, "ya")
            yb = ph.tile([128, 384], F32, "yb")
            yc = ph.tile([128, 384], F32, "yc")
            for m in lat_tiles:
                r0 = m * 128
                S.dma('sp', a[:, :], odir[0][r0:r0 + 128, :], w=[a])
                S.dma('act', b[:, :], odir[1][r0:r0 + 128, :], w=[b])
                S.dma('sp', g[:, :], rs[r0:r0 + 128, :], w=[g])
                S.op('dve', lambda e: e.tensor_tensor(out=a[:, :], in0=a[:, :], in1=b[:, :], op=ALU.add), r=[a, b], w=[a])
                S.op('act', lambda e: e.activation(out=g[:, :], in_=g[:, :], func=AF.Silu), r=[g], w=[g])
                head_rms(ph, a, mix, mix[:, 0:1024], 8, 128, hnw, hnw[:, 0:128].unsqueeze(1).to_broadcast([128, 8, 128]), g, sq, ss, rstd, epsb, tmp)
                for q in range(2):
                    bk = banks[q]
                    bkb = bk[:, 0:256].bitcast(BF16)
                    for c in range(4):
                        cc = q * 4 + c
                        S.op('pe', lambda e, bkb=bkb, c=c, cc=cc: e.transpose(bkb[:, c * 128:(c + 1) * 128], mix[:, cc * 128:(cc + 1) * 128], idb[:, :]), r=[mix, idb], w=[bk])
                    S.op('act', lambda e, bkb=bkb, q=q: e.activation(out=mixT[:, q * 512:(q + 1) * 512], in_=bkb[:, 0:512], func=AF.Copy), r=[bk], w=[mixT])
                S.dma('sp', ya[:, :].rearrange("p (c t) -> p c t", c=3), yT5[0][:, r0:r0 + 128].rearrange("(c p) t -> p c t", p=128), w=[ya])
                S.dma('act', yb[:, :].rearrange("p (c t) -> p c t", c=3), yT5[1][:, r0:r0 + 128].rearrange("(c p) t -> p c t", p=128), w=[yb])
                S.op('dve', lambda e: e.tensor_tensor(out=ya[:, :], in0=ya[:, :], in1=yb[:, :], op=ALU.add), r=[ya, yb], w=[ya])
                S.op('dve', lambda e: e.tensor_tensor(out=yb[:, :], in0=ya[:, :], in1=ya[:, :], op=ALU.mult), r=[ya], w=[yb])
                S.op('dve', lambda e: e.tensor_scalar(out=yb[:, :], in0=yb[:, :], scalar1=0.044715, scalar2=1.0, op0=ALU.mult, op1=ALU.add), r=[yb], w=[yb])
                S.op('dve', lambda e: e.tensor_tensor(out=yb[:, :], in0=yb[:, :], in1=ya[:, :], op=ALU.mult), r=[yb, ya], w=[yb])
                S.op('act', lambda e: e.activation(out=yb[:, :], in_=yb[:, :], func=AF.Tanh, scale=0.7978845608028654), r=[yb], w=[yb])
                S.op('dve', lambda e: e.scalar_tensor_tensor(out=yb[:, :], in0=yb[:, :], scalar=1.0, in1=ya[:, :], op0=ALU.add, op1=ALU.mult), r=[yb, ya], w=[yb])
                S.op('dve', lambda e: e.tensor_scalar(out=yb[:, :], in0=yb[:, :], scalar1=0.5, scalar2=None, op0=ALU.mult), r=[yb], w=[yb])
                b2 = banks[2]
                for co in range(3):
                    for ci in range(3):
                        S.op('pe', lambda e, co=co, ci=ci: e.matmul(b2[:, co * 128:(co + 1) * 128], lhsT=gw[:, ci * 384 + co * 128:ci * 384 + (co + 1) * 128], rhs=yb[:, ci * 128:(ci + 1) * 128], start=(ci == 0), stop=(ci == 2)), r=[gw, yb], w=[b2])
                for co in range(3):
                    S.op('act', lambda e, co=co: e.activation(out=yc[:, co * 128:(co + 1) * 128], in_=b2[:, co * 128:(co + 1) * 128], func=AF.Sigmoid, bias=gb[:, co:co + 1]), r=[b2, gb], w=[yc])
                S.op('dve', lambda e: e.tensor_tensor(out=mixT[:, 1024:1408], in0=yb[:, :], in1=yc[:, :], op=ALU.mult), r=[yb, yc], w=[mixT])
                S.dma('sp', xt[:, :], xsrc(1, m), w=[xt])
                for h2 in range(2):
                    bk = banks[4 + h2]
                    for c in range(11):
                        S.op('pe', lambda e, bk=bk, c=c, h2=h2: e.matmul(bk[:, 0:512], lhsT=mixT[:, c * 128:(c + 1) * 128], rhs=Wo[:, c * D + h2 * 512:c * D + (h2 + 1) * 512], start=(c == 0), stop=(c == 10)), r=[mixT, Wo], w=[bk])
                    S.op('dve', lambda e, bk=bk, h2=h2: e.tensor_tensor(out=xo[:, h2 * 512:(h2 + 1) * 512], in0=bk[:, 0:512], in1=g1[:, h2 * 512:(h2 + 1) * 512], op=ALU.mult), r=[bk, g1], w=[xo])
                S.op('dve', lambda e: e.tensor_tensor(out=xo[:, :], in0=xo[:, :], in1=xt[:, :], op=ALU.add), r=[xo, xt], w=[xo])
                S.dma('pool', xmid[r0:r0 + 128, :], xo[:, :], r=[xo])
            ph.close()

        lat_tiles = list(range(NCT, NT))
        all_tiles = list(range(NT))
        plist = ['A0', 'B0', 'C0', 'D0a', 'MLP0', 'A1', 'B1', 'C1', 'D1a', 'MLP1']
        nph = len(plist) if upto is None else plist.index(upto) + 1 if upto in plist else 0
        if nph >= 1:
            phaseA0()
        if nph >= 2:
            phaseB0()
        if nph >= 3 and os.environ.get('KGLA', '1') == '1':
            gla_phase(8, 64, 128, qT, [kTs[0], kTs[0]], [ktoks[0], ktoks[0]], vtok, lgs, odir)
        if nph >= 4:
            phaseD0a()
        if nph < 5:
            pass
        elif depth == 1:
            phaseMLP(0, lat_tiles, True)
        else:
            phaseMLP(0, all_tiles, False)
            if nph >= 6:
                phaseA1()
            if nph >= 7:
                gla_phase(8, 128, 128, qT, kTs, ktoks, vtok, lgs, odir)
            if nph >= 8:
                phaseC1()
            if nph >= 9:
                phaseD1a()
            if nph >= 10:
                phaseMLP(1, lat_tiles, True)
        S.barrier()
    return nc


def make_consts():
    i = np.arange(128)
    t = i[:, None]
    j = i[None, :]
    eye = (t == j)
    TF = (t <= j)
    TB = (t >= j)
    SF = (t > j)
    SB = (t < j)
    MBF = np.where(j >= t, 0.0, NEG)
    MBB = np.where(j <= t, 0.0, NEG)
    ONE = np.ones((128, 128))
    w = np.arange(258)
    ML = np.broadcast_to((w % 64 != 0).astype(np.float32)[None, :], (128, 258))
    MR = np.broadcast_to((w % 64 != 1).astype(np.float32)[None, :], (128, 258))
    return np.concatenate([eye, TF, TB, SF, SB, MBF, MBB, ONE, ML, MR], axis=1).astype(np.float32)


def host_inputs(inp, b, depth=2):
    f = lambda a: np.ascontiguousarray(np.asarray(a, dtype=np.float32))
    m = {}
    m["x"] = f(inp["x"][b])
    m["ctx"] = f(inp["ctx"][b])
    cs = np.stack([np.asarray(inp["c"][b]).reshape(8, 128).T, np.asarray(inp["c_ctx"]).reshape(8, 128).T], axis=-1)
    m["cs"] = f(cs)
    m["ada_w"] = f(inp["ada_w"])
    m["ada_b"] = f(inp["ada_b"])
    m["norm1_w"] = f(inp["norm1_w"])
    m["norm2_w"] = f(inp["norm2_w"])
    m["final_norm_w"] = f(np.asarray(inp["final_norm_w"]).reshape(1, D))
    m["consts"] = make_consts()
    m["w_in0"] = f(inp["ssd_gla_w_in"][0])
    cw = np.asarray(inp["ssd_conv_w"][0]).reshape(9, 16, 128)
    m["convw"] = f(cw.transpose(2, 1, 0).reshape(128, 144))
    m["convb"] = f(np.asarray(inp["ssd_conv_b"][0]).reshape(16, 128).T)
    m["dtb"] = f(np.asarray(inp["ssd_dt_bias"][0]).reshape(1, 32))
    m["alog"] = f(np.asarray(inp["ssd_a_log"][0]).reshape(1, 32))
    m["ssdd"] = f(np.asarray(inp["ssd_d"][0]).reshape(1, 16))
    m["ssdnw"] = f(np.asarray(inp["ssd_norm_w"][0]).reshape(1, 1024))
    wgm = np.zeros((33, 1024), np.float32)
    gw = np.asarray(inp["gla_gate_w"][0])
    wgm[0:16, 0:512] = gw[0]
    wgm[16:32, 512:1024] = gw[1]
    wgm[32, :] = np.asarray(inp["gla_gate_b"][0]).reshape(1024)
    m["wg"] = wgm
    m["glanw"] = f(np.asarray(inp["gla_norm_w"][0]).reshape(1, 128))
    m["w_out0"] = f(inp["ssd_gla_w_out"][0])
    m["mlp_w1"] = f(inp["mlp_w1"])
    m["mlp_w2"] = f(inp["mlp_w2"])
    if depth > 1:
        m["w_in1"] = f(inp["hgrn_s5_w_in"][0])
        lb = np.asarray(inp["hgrn_lb_logits"], dtype=np.float32)
        m["lbl"] = f(lb.reshape(2, 2048))
        m["lblT"] = f(np.stack([lb[l].reshape(16, 128).T for l in range(2)], axis=1))
        m["hgnw"] = f(np.asarray(inp["hgrn_norm_w"][0]).reshape(1, 128))
        prm = []
        for d in range(2):
            prm.append(np.asarray(inp["s5_a_re"][0][d]).reshape(12, 128).T)
            prm.append(np.asarray(inp["s5_a_im"][0][d]).reshape(12, 128).T)
            prm.append(np.repeat(np.asarray(inp["s5_log_dt"][0][d]).reshape(12, 2, 1), 64, axis=2).reshape(12, 128).T)
        m["s5p"] = f(np.concatenate(prm, axis=1))
        sb = lambda a: np.asarray(a).reshape(12, 128, 16).transpose(1, 0, 2).reshape(128, 192)
        m["s5b"] = f(np.concatenate([sb(inp["s5_b_re"][0]), sb(inp["s5_b_im"][0])], axis=1))
        scf = lambda a: np.asarray(a).reshape(12, 2, 16, 64).transpose(1, 3, 0, 2).reshape(128, 192)
        m["s5c"] = f(np.concatenate([scf(inp["s5_c_re"][0]), scf(inp["s5_c_im"][0])], axis=1))
        m["s5d"] = f(np.asarray(inp["s5_d"][0]).reshape(3, 128).T)
        m["gluw"] = f(inp["s5_glu_w"][0])
        m["glub"] = f(np.asarray(inp["s5_glu_b"][0]).reshape(3, 128).T)
        m["w_out1"] = f(inp["hgrn_s5_w_out"][0])
    return m


_NC_CACHE = {}


def kernel(**inputs):
    B, SEQ = inputs["x"].shape[0], inputs["x"].shape[1]
    key = (SEQ, 2)
    if key not in _NC_CACHE:
        _NC_CACHE[key] = build_nc(SEQ, 2)
    nc = _NC_CACHE[key]
    in_maps = [host_inputs(inputs, b) for b in range(B)]
    res = run_bass_kernel_spmd(nc, in_maps, core_ids=list(range(B)))
    return np.stack([r["out"] for r in res.results], axis=0).astype(np.float32)
```

```python
import os
import numpy as np
from contextlib import ExitStack
KDBG = int(os.environ.get('KDBG', '9'))
import concourse.bass as bass
import concourse.mybir as mybir
from concourse.bass_utils import run_bass_kernel_spmd

F32 = mybir.dt.float32
BF16 = mybir.dt.bfloat16
AF = mybir.ActivationFunctionType
ALU = mybir.AluOpType
AX = mybir.AxisListType

D = 1024
CTX = 256
EPS = 1e-6
NEG = -30000.0


class Buf:
    def __init__(self, t, name):
        self.t = t
        self.name = name
        self.w = None
        self.r = {}

    def __getitem__(self, k):
        return self.t[k]


class Sch:
    def __init__(self, nc, es):
        self.nc = nc
        self.E = {'pe': nc.tensor, 'act': nc.scalar, 'dve': nc.vector, 'pool': nc.gpsimd, 'sp': nc.sync}
        self.semobj = {}
        self.cnt = {}
        self.seen = {}
        for e in self.E:
            self.semobj[e] = es.enter_context(nc.semaphore("s_" + e))
            self.cnt[e] = 0
            self.seen[e] = {}
        self.rings = {}
        for q in ('sp', 'pool', 'act'):
            n = 12
            keys = []
            for i in range(n):
                k = ('d', q, i)
                self.semobj[k] = es.enter_context(nc.semaphore("d_%s_%d" % (q, i)))
                keys.append(k)
            self.rings[q] = {'keys': keys, 'vals': [0] * n, 'i': 0}
        self.nins = 0
        for e in self.E:
            self.E[e].sem_clear(self.semobj[e])
        for q, ring in self.rings.items():
            for k in ring['keys']:
                self.E[q].sem_clear(self.semobj[k])
        nc.all_engine_barrier()

    def _wait(self, e, tok):
        key, val = tok
        if self.seen[e].get(key, 0) >= val:
            return
        self.E[e].wait_ge(self.semobj[key], val)
        self.seen[e][key] = val

    def _deps(self, e, r, w, is_dma):
        for b in r:
            if b.w is not None:
                self._wait(e, b.w)
        for b in w:
            if b.w is not None and (is_dma or b.w[0] != e):
                self._wait(e, b.w)
            for key, val in b.r.items():
                if is_dma or key != e:
                    self._wait(e, (key, val))

    def _upd(self, tok, r, w):
        for b in r:
            b.r[tok[0]] = tok[1]
        for b in w:
            b.w = tok
            b.r = {}

    def op(self, e, fn, r=(), w=()):
        self._deps(e, r, w, False)
        ins = fn(self.E[e])
        self.cnt[e] += 1
        ins.then_inc(self.semobj[e], 1)
        self._upd((e, self.cnt[e]), r, w)
        self.nins += 1

    def dma(self, q, out, in_, r=(), w=()):
        ring = self.rings[q]
        i = ring['i']
        ring['i'] = (i + 1) % len(ring['keys'])
        key = ring['keys'][i]
        if ring['vals'][i] > 0:
            self._wait(q, (key, ring['vals'][i]))
        self._deps(q, r, w, True)
        ins = self.E[q].dma_start(out=out, in_=in_)
        ring['vals'][i] += 16
        ins.then_inc(self.semobj[key], 16)
        self._upd((key, ring['vals'][i]), r, w)
        self.nins += 1

    def barrier(self):
        toks = [(e, self.cnt[e]) for e in self.E if self.cnt[e] > 0]
        for q, ring in self.rings.items():
            for k, v in zip(ring['keys'], ring['vals']):
                if v > 0:
                    toks.append((k, v))
        for e in self.E:
            for tok in toks:
                if tok[0] != e or True:
                    if tok[0] == e:
                        continue
                    self._wait(e, tok)
        for e in self.E:
            if self.cnt[e] > 0:
                self._wait(e, (e, self.cnt[e]))


def build_nc(SEQ, depth=2, dbg=(), upto=None):
    T = CTX + SEQ
    NT = T // 128
    NCT = CTX // 128
    nc = bass.Bass("TRN2", target_bir_lowering=False)

    def din(name, shape, dt=F32):
        return nc.dram_tensor(name, list(shape), dt, kind="ExternalInput").ap()

    def dscr(name, shape, dt=F32):
        kind = "ExternalOutput" if name in dbg else "Internal"
        return nc.dram_tensor(name, list(shape), dt, kind=kind).ap()

    x_in = din("x", [SEQ, D])
    ctx_in = din("ctx", [CTX, D])
    cs_in = din("cs", [128, 8, 2])
    ada_w = din("ada_w", [2, D, 6 * D])
    ada_b = din("ada_b", [2, 6 * D])
    norm1_w = din("norm1_w", [2, D])
    norm2_w = din("norm2_w", [2, D])
    final_w = din("final_norm_w", [1, D])
    consts = din("consts", [128, 128 * 8 + 516])
    w_in0 = din("w_in0", [D, 6208])
    convw = din("convw", [128, 16 * 9])
    convb = din("convb", [128, 16])
    dtb = din("dtb", [1, 32])
    alog = din("alog", [1, 32])
    ssdd = din("ssdd", [1, 16])
    ssdnw = din("ssdnw", [1, 1024])
    wg = din("wg", [33, 1024])
    glanw = din("glanw", [1, 128])
    w_out0 = din("w_out0", [2048, D])
    mlp_w1 = din("mlp_w1", [2, D, 4096])
    mlp_w2 = din("mlp_w2", [2, 4096, D])
    if depth > 1:
        w_in1 = din("w_in1", [D, 5504])
        lbl = din("lbl", [2, 2048])
        lblT = din("lblT", [128, 2, 16])
        hgnw = din("hgnw", [1, 128])
        s5p = din("s5p", [128, 2 * 3 * 12])
        s5b = din("s5b", [128, 2 * 12 * 16])
        s5c = din("s5c", [128, 2 * 12 * 16])
        s5d = din("s5d", [128, 3])
        gluw = din("gluw", [384, 384])
        glub = din("glub", [128, 3])
        w_out1 = din("w_out1", [1408, D])
    out = nc.dram_tensor("out", [SEQ, D], F32, kind="ExternalOutput").ap()

    modv = dscr("modv", [2, 2, 6 * D])
    xres = dscr("xres", [T, D])
    xmid = dscr("xmid", [T, D])
    zs = dscr("zs", [T, D])
    rs = dscr("rs", [T, D])
    vtok = dscr("vtok", [T, D])
    TP = T + 256
    xbcT = dscr("xbcT", [2048, TP], BF16)
    dts = dscr("dts", [T, 32])
    qT = dscr("qT", [1024, T])
    kTs = [dscr("kT0", [1024, T]), dscr("kT1", [1024, T])]
    ktoks = [dscr("ktok0", [T, 1024]), dscr("ktok1", [T, 1024])]
    lgs = [dscr("lg0", [T, 1024]), dscr("lg1", [T, 1024])]
    ydir = [dscr("ydir0", [T, D]), dscr("ydir1", [T, D])]
    odir = [dscr("odir0", [T, D]), dscr("odir1", [T, D])]
    utok = dscr("utok", [T, 384])
    yT5 = [dscr("yT5_0", [384, T]), dscr("yT5_1", [384, T])]

    es = ExitStack()
    with es:
        S = Sch(nc, es)
        banks = [Buf(es.enter_context(nc.psum_tensor("bank%d" % i, [128, 512], F32)), "bank%d" % i) for i in range(8)]

        class Phase:
            def __init__(self):
                self.es = ExitStack()
                self.n = 0

            def tile(self, shape, dt=F32, name=None):
                self.n += 1
                S.uid = getattr(S, "uid", 0) + 1
                nm = "%s_%d" % (name or "t", S.uid)
                return Buf(self.es.enter_context(nc.sbuf_tensor(nm, list(shape), dt)), nm)

            def close(self):
                S.barrier()
                self.es.close()

        def run_seq(g0, g1):
            for _ in g0:
                pass
            for _ in g1:
                pass

        def run_pair(g0, g1):
            gens = [g0, g1]
            alive = [True, True]
            while alive[0] or alive[1]:
                for i in range(2):
                    if alive[i]:
                        try:
                            next(gens[i])
                        except StopIteration:
                            alive[i] = False

        def load_consts(ph):
            cst = ph.tile([128, 128 * 8 + 516], F32, "cst")
            S.dma('sp', cst[:, :], consts[:, :], w=[cst])
            return cst

        C_I, C_TF, C_TB, C_SF, C_SB, C_MBF, C_MBB, C_ONE = [i * 128 for i in range(8)]
        C_ML = 1024
        C_MR = 1024 + 258

        def xsrc(layer, m):
            if layer == 0:
                if m < NCT:
                    return ctx_in[m * 128:(m + 1) * 128, :]
                return x_in[(m - NCT) * 128:(m - NCT + 1) * 128, :]
            return xres[m * 128:(m + 1) * 128, :]

        def bc_load(ph, src_row_ap, n, name):
            t = ph.tile([128, n], F32, name)
            S.dma('sp', t[:, :], src_row_ap.partition_broadcast(128), w=[t])
            return t

        def mod_consts(ph, layer, normw, k_shift, k_scale, src):
            A = bc_load(ph, modv[layer, src:src + 1, k_scale * D:(k_scale + 1) * D], D, "A")
            SH = bc_load(ph, modv[layer, src:src + 1, k_shift * D:(k_shift + 1) * D], D, "SH")
            NW = bc_load(ph, normw[layer:layer + 1, :], D, "NW")
            S.op('dve', lambda e: e.scalar_tensor_tensor(out=A[:, :], in0=A[:, :], scalar=1.0, in1=NW[:, :], op0=ALU.add, op1=ALU.mult), r=[A, NW], w=[A])
            return A, SH

        def load_weight_bf16(ph, Wb, wsrc, K, N, nblk=512):
            J = K // 128
            stg = [ph.tile([128, 2 * 512], F32, "wstg") for _ in range(3)]
            cnt = 0
            for n0 in range(0, N, nblk):
                n1 = min(N, n0 + nblk)
                nn = n1 - n0
                for j0 in range(0, J, 2):
                    j1 = min(J, j0 + 2)
                    st = stg[cnt % 3]
                    S.dma('sp' if cnt % 2 == 0 else 'act', st[:, 0:(j1 - j0) * nn].rearrange("p (j n) -> p j n", n=nn),
                          wsrc[j0 * 128:j1 * 128, n0:n1].rearrange("(j p) n -> p j n", p=128), w=[st])
                    eng = 'dve'
                    for j in range(j0, j1):
                        S.op(eng, lambda e, j=j, st=st: e.tensor_copy(out=Wb[:, j * N + n0:j * N + n1], in_=st[:, (j - j0) * nn:(j - j0 + 1) * nn]), r=[st], w=[Wb])
                    cnt += 1

        def rms_rstd(ph, xt, junk, ss, rstd, epsb, n, dscale):
            S.op('dve', lambda e: e.memset(ss[:, 0:1], 0.0), w=[ss])
            S.op('act', lambda e: e.activation(out=junk[:, 0:n], in_=xt[:, 0:n], func=AF.Square, accum_out=ss[:, 0:1]), r=[ss, xt], w=[junk, ss])
            S.op('act', lambda e: e.activation(out=rstd[:, 0:1], in_=ss[:, 0:1], func=AF.Sqrt, scale=dscale, bias=epsb[:, 0:1]), r=[ss, epsb], w=[rstd])
            S.op('dve', lambda e: e.reciprocal(out=rstd[:, 0:1], in_=rstd[:, 0:1]), r=[rstd], w=[rstd])

        ph = Phase()
        cs = ph.tile([128, 16], F32, "cs")
        S.dma('sp', cs[:, :], cs_in[:, :, :].rearrange("p j s -> p (j s)"), w=[cs])
        S.op('act', lambda e: e.activation(out=cs[:, :], in_=cs[:, :], func=AF.Silu), r=[cs], w=[cs])
        for layer in range(depth):
            adab = ph.tile([2, 6 * D], F32, "adab")
            S.dma('sp', adab[:, :], ada_b[layer:layer + 1, :].partition_broadcast(2), w=[adab])
            mv = ph.tile([2, 6 * D], F32, "mv")
            wst = [ph.tile([128, 8 * 512], F32, "adaw") for _ in range(2)]
            for nb in range(12):
                st = wst[nb % 2]
                S.dma('sp' if nb % 2 == 0 else 'act', st[:, :].rearrange("p (j n) -> p j n", n=512),
                      ada_w[layer, :, nb * 512:(nb + 1) * 512].rearrange("(j p) n -> p j n", p=128), w=[st])
                bk = banks[nb % 8]
                for j in range(8):
                    S.op('pe', lambda e, j=j, st=st, bk=bk: e.matmul(bk[0:2, 0:512], lhsT=cs[:, 2 * j:2 * j + 2], rhs=st[:, j * 512:(j + 1) * 512], start=(j == 0), stop=(j == 7)), r=[cs, st], w=[bk])
                S.op('dve', lambda e, bk=bk, nb=nb: e.tensor_tensor(out=mv[0:2, nb * 512:(nb + 1) * 512], in0=bk[0:2, 0:512], in1=adab[0:2, nb * 512:(nb + 1) * 512], op=ALU.add), r=[bk, adab], w=[mv])
            S.dma('pool', modv[layer, :, :], mv[0:2, :], r=[mv])
        ph.close()

        def proj_phase(layer, w_src, NIN, tok_groups, feat_groups, extra=None, tiles=None):
            ph = Phase()
            cst = load_consts(ph)
            NW = bc_load(ph, norm1_w[layer:layer + 1, :], D, "NW")
            Am = ph.tile([128, D], F32, "Am")
            SHm = ph.tile([128, D], F32, "SHm")
            cur = [None]

            def set_mod(src):
                if cur[0] == src:
                    return
                cur[0] = src
                S.dma('sp', Am[:, :], modv[layer, src:src + 1, D:2 * D].partition_broadcast(128), w=[Am])
                S.dma('sp', SHm[:, :], modv[layer, src:src + 1, 0:D].partition_broadcast(128), w=[SHm])
                S.op('dve', lambda e: e.scalar_tensor_tensor(out=Am[:, :], in0=Am[:, :], scalar=1.0, in1=NW[:, :], op0=ALU.add, op1=ALU.mult), r=[Am, NW], w=[Am])
            Wb = ph.tile([128, 8 * NIN], BF16, "Wb")
            load_weight_bf16(ph, Wb, w_src, D, NIN)
            epsb = ph.tile([128, 1], F32, "epsb")
            S.op('dve', lambda e: e.memset(epsb[:, :], EPS), w=[epsb])
            idb = ph.tile([128, 128], BF16, "idb")
            S.op('dve', lambda e: e.tensor_copy(out=idb[:, :], in_=cst[:, C_I:C_I + 128]), r=[cst], w=[idb])
            xts = [ph.tile([128, D], F32, "xt") for _ in range(2)]
            junk = ph.tile([128, D], BF16, "junk")
            tmp = ph.tile([128, D], F32, "tmp")
            hb = ph.tile([128, D], BF16, "hb")
            hTs = [ph.tile([128, D], BF16, "hT") for _ in range(2)]
            ss = ph.tile([128, 1], F32, "ss")
            rstd = ph.tile([128, 1], F32, "rstd")
            stg32 = [ph.tile([128, 512], F32, "stg32") for _ in range(4)]
            stg16 = [ph.tile([128, 512], BF16, "stg16") for _ in range(3)]
            ctxp = dict(ph=ph, cst=cst, Wb=Wb, NIN=NIN, stg32=stg32, stg16=stg16, k32=[0], k16=[0])
            if extra is not None:
                extra(ctxp)
            bi = [0]

            def nextbank():
                b = banks[bi[0] % 8]
                bi[0] += 1
                return b
            for m in (tiles if tiles is not None else range(NT)):
                if KDBG < 2:
                    break
                xt = xts[m % 2]
                hT = hTs[m % 2]
                S.dma('sp', xt[:, :], xsrc(layer, m), w=[xt])
                rms_rstd(ph, xt, junk, ss, rstd, epsb, D, 1.0 / D)
                set_mod(1 if m < NCT else 0)
                A, SH = Am, SHm
                S.op('dve', lambda e, xt=xt, A=A: e.scalar_tensor_tensor(out=tmp[:, :], in0=xt[:, :], scalar=rstd[:, 0:1], in1=A[:, :], op0=ALU.mult, op1=ALU.mult), r=[xt, rstd, A], w=[tmp])
                S.op('dve', lambda e, SH=SH: e.tensor_tensor(out=hb[:, :], in0=tmp[:, :], in1=SH[:, :], op=ALU.add), r=[tmp, SH], w=[hb])
                bk = nextbank()
                bkb = bk[:, :].bitcast(BF16)
                for j in range(8):
                    S.op('pe', lambda e, j=j, bkb=bkb: e.transpose(bkb[:, j * 128:(j + 1) * 128], hb[:, j * 128:(j + 1) * 128], idb[:, :]), r=[hb, idb], w=[bk])
                S.op('act', lambda e, bkb=bkb, hT=hT: e.activation(out=hT[:, :], in_=bkb[:, 0:1024], func=AF.Copy), r=[bk], w=[hT])
                for (c0, ncols, fn) in (tok_groups if KDBG >= 3 else []):
                    for b0 in range(c0, c0 + ncols, 512):
                        n = min(512, c0 + ncols - b0)
                        bk = nextbank()
                        for j in range(8):
                            S.op('pe', lambda e, j=j, bk=bk, b0=b0, n=n, hT=hT: e.matmul(bk[:, 0:n], lhsT=hT[:, j * 128:(j + 1) * 128], rhs=Wb[:, j * NIN + b0:j * NIN + b0 + n], start=(j == 0), stop=(j == 7)), r=[hT, Wb], w=[bk])
                        fn(m, bk, b0 - c0, n)
                for (c0, nch_total, Mw, fn) in (feat_groups[:KDBG - 3] if KDBG >= 4 else []):
                    for q0 in range(0, nch_total, 4):
                        nch = min(4, nch_total - q0)
                        bk = nextbank()
                        for c in range(nch):
                            col = c0 + (q0 + c) * Mw
                            for j in range(8):
                                S.op('pe', lambda e, j=j, bk=bk, c=c, col=col, hT=hT: e.matmul(bk[0:Mw, c * 128:(c + 1) * 128], lhsT=Wb[:, j * NIN + col:j * NIN + col + Mw], rhs=hT[:, j * 128:(j + 1) * 128], start=(j == 0), stop=(j == 7)), r=[hT, Wb], w=[bk])
                        fn(m, bk, q0, nch)
            ph.close()

        def make_tok_store(ctxp, dst, dcol0, width, func=None, scale=1.0):
            stg = ctxp['stg32']
            k = ctxp['k32']

            def fn(m, bk, off, n):
                st = stg[k[0] % len(stg)]
                k[0] += 1
                if k[0] % 2 == 0:
                    S.op('act', lambda e: e.activation(out=st[:, 0:n], in_=bk[:, 0:n], func=AF.Copy), r=[bk], w=[st])
                else:
                    S.op('dve', lambda e: e.tensor_copy(out=st[:, 0:n], in_=bk[:, 0:n]), r=[bk], w=[st])
                S.dma('pool', dst[m * 128:(m + 1) * 128, dcol0 + off:dcol0 + off + n], st[:, 0:n], r=[st])
            return fn

        def make_feat_store(ctxp, dst, drow0, dt=F32, scale=None, func=AF.Copy, coloff=0):
            stg = ctxp['stg32'] if dt == F32 else ctxp['stg16']
            k = ctxp['k32'] if dt == F32 else ctxp['k16']

            def fn(m, bk, q0, nch):
                st = stg[k[0] % len(stg)]
                k[0] += 1
                if scale is None:
                    S.op('act', lambda e: e.activation(out=st[:, 0:nch * 128], in_=bk[:, 0:nch * 128], func=func), r=[bk], w=[st])
                else:
                    S.op('act', lambda e: e.activation(out=st[:, 0:nch * 128], in_=bk[:, 0:nch * 128], func=func, scale=scale), r=[bk], w=[st])
                r0 = drow0 + q0 * 128
                S.dma('pool', dst[r0:r0 + nch * 128, coloff + m * 128:coloff + (m + 1) * 128].rearrange("(c p) t -> p c t", p=128),
                      st[:, 0:nch * 128].rearrange("p (c t) -> p c t", c=nch), r=[st])
            return fn

        def phaseA0():
            tok_groups = []
            feat_groups = []

            def extra(c):
                ph = c['ph']
                tok_groups.append((0, 1024, make_tok_store(c, zs, 0, 1024)))
                tok_groups.append((3072, 32, make_tok_store(c, dts, 0, 32)))
                tok_groups.append((3616, 512, make_tok_store(c, ktoks[0], 0, 512)))
                tok_groups.append((4128, 1024, make_tok_store(c, vtok, 0, 1024)))
                tok_groups.append((5184, 1024, make_tok_store(c, rs, 0, 1024)))
                feat_groups.append((1024, 16, 128, make_feat_store(c, xbcT, 0, dt=BF16, coloff=128)))
                feat_groups.append((3104, 4, 128, make_feat_store(c, qT, 0, scale=0.125)))
                feat_groups.append((3616, 4, 128, make_feat_store(c, kTs[0], 0)))
                lrT = ph.tile([33, 128], F32, "lrT")
                S.op('dve', lambda e: e.memset(lrT[32:33, :], 1.0), w=[lrT])
                wgt = ph.tile([33, 1024], F32, "wgt")
                S.dma('sp', wgt[:, :], wg[:, :], w=[wgt])
                gst = [ph.tile([128, 512], F32, "gst") for _ in range(2)]

                def lr_fn(m, bk, q0, nch):
                    S.op('act', lambda e: e.activation(out=lrT[0:32, :], in_=bk[0:32, 0:128], func=AF.Copy), r=[bk], w=[lrT])
                    for d in range(2):
                        bg = banks[(4 + d) % 8]
                        S.op('pe', lambda e, bg=bg, d=d: e.matmul(bg[:, 0:512], lhsT=lrT[0:33, :], rhs=wgt[0:33, d * 512:(d + 1) * 512], start=True, stop=True), r=[lrT, wgt], w=[bg])
                        g = gst[d]
                        S.op('act', lambda e, bg=bg, g=g: e.activation(out=g[:, :], in_=bg[:, 0:512], func=AF.Exp, scale=-1.0), r=[bg], w=[g])
                        S.op('act', lambda e, g=g: e.activation(out=g[:, :], in_=g[:, :], func=AF.Ln, bias=1.0), r=[g], w=[g])
                        S.op('dve', lambda e, g=g: e.tensor_scalar(out=g[:, :], in0=g[:, :], scalar1=-1.0 / 16.0, scalar2=None, op0=ALU.mult), r=[g], w=[g])
                        S.dma('pool', lgs[d][m * 128:(m + 1) * 128, 0:512], g[:, :], r=[g])
                feat_groups.append((5152, 1, 32, lr_fn))
            proj_phase(0, w_in0, 6208, tok_groups, feat_groups, extra)

        def phaseB0():
            ph = Phase()
            cst = load_consts(ph)
            cw = ph.tile([128, 144], F32, "cw")
            S.dma('sp', cw[:, :], convw[:, :], w=[cw])
            cb = ph.tile([128, 16], F32, "cb")
            S.dma('sp', cb[:, :], convb[:, :], w=[cb])
            Dg = ph.tile([128, 144 * 128], BF16, "Dg")
            for i in range(144):
                S.op('dve', lambda e, i=i: e.tensor_scalar(out=Dg[:, i * 128:(i + 1) * 128], in0=cst[:, C_I:C_I + 128], scalar1=cw[:, i:i + 1], scalar2=None, op0=ALU.mult), r=[cst, cw], w=[Dg])
            idb = ph.tile([128, 128], BF16, "idb")
            S.op('dve', lambda e: e.tensor_copy(out=idb[:, :], in_=cst[:, C_I:C_I + 128]), r=[cst], w=[idb])
            mlr = ph.tile([128, 516], BF16, "mlr")
            S.op('dve', lambda e: e.tensor_copy(out=mlr[:, :], in_=cst[:, C_ML:C_ML + 516]), r=[cst], w=[mlr])
            mb4 = []
            for d in range(2):
                t = ph.tile([128, 512], F32, "mb4")
                c0 = C_MBF if d == 0 else C_MBB
                S.op('dve', lambda e, t=t, c0=c0: e.tensor_copy(out=t[:, :].rearrange("p (a i) -> p a i", a=4), in_=cst[:, c0:c0 + 128].unsqueeze(1).to_broadcast([128, 4, 128])), r=[cst], w=[t])
                mb4.append(t)
            dtbb = bc_load(ph, dtb[0:1, :], 32, "dtbb")
            nega = bc_load(ph, alog[0:1, :], 32, "nega")
            S.op('act', lambda e: e.activation(out=nega[:, :], in_=nega[:, :], func=AF.Exp), r=[nega], w=[nega])
            S.op('dve', lambda e: e.tensor_scalar(out=nega[:, :], in0=nega[:, :], scalar1=-1.0, scalar2=None, op0=ALU.mult), r=[nega], w=[nega])
            dskb = bc_load(ph, ssdd[0:1, :], 16, "dskb")
            hT = [ph.tile([128, 1024], F32, "hT") for _ in range(2)]
            hTb = [ph.tile([128, 1024], BF16, "hTb") for _ in range(2)]
            for d in range(2):
                S.op('dve', lambda e, d=d: e.memset(hT[d][:, :], 0.0), w=[hT[d]])
                S.op('dve', lambda e, d=d: e.memset(hTb[d][:, :], 0.0), w=[hTb[d]])

            def mk(shape, dt, name):
                return [ph.tile(shape, dt, name) for _ in range(2)]
            xin = mk([128, 8 * 258], BF16, "xin")
            xl = mk([128, 8 * 258], BF16, "xl")
            xr = mk([128, 8 * 258], BF16, "xr")
            xTt = mk([128, 1024], F32, "xTt")
            BT = mk([128, 512], BF16, "BT")
            CT = mk([128, 512], BF16, "CT")
            xtok = mk([128, 1024], F32, "xtok")
            Btok = mk([128, 512], BF16, "Btok")
            dtr = mk([128, 32], F32, "dtr")
            la = mk([128, 32], F32, "la")
            sm = mk([128, 128], F32, "sm")
            Dx = mk([128, 2048], F32, "Dx")
            seg = mk([128, 2048], F32, "seg")
            scT = mk([128, 2048], BF16, "scT")
            xdt = mk([128, 1024], BF16, "xdt")
            xw = mk([128, 1024], BF16, "xw")
            ysb = mk([128, 1024], F32, "ysb")
            ytmp = mk([128, 1024], F32, "ytmp")

            def step(m, d):
                bk4 = banks[4 * d:4 * d + 4]
                t0 = m * 128
                isctx = m < NCT
                first = (m == 0) or (m == NCT)
                last = (m == NCT - 1) or (m == NT - 1)
                X = xin[d]
                X3 = X[:, :].rearrange("p (c w) -> p c w", c=8)
                w0, w1 = 0, 258
                if first:
                    w0 = 65
                if last:
                    w1 = 193
                if isctx:
                    taps = [(0, -1), (0, 0), (0, 1)]
                else:
                    taps = [(dr, dc) for dr in (-1, 0, 1) for dc in (-1, 0, 1)]
                for half in range(2):
                    if first or last:
                        S.op('dve', lambda e: e.memset(X[:, :], 0.0), w=[X])
                    S.dma('sp', X3[:, :, w0:w1],
                          xbcT[half * 1024:(half + 1) * 1024, t0 + 63 + w0:t0 + 63 + w1].rearrange("(c p) t -> p c t", p=128), w=[X])
                    if isctx:
                        srcs = {-1: X, 0: X, 1: X}
                    else:
                        S.op('dve', lambda e: e.tensor_tensor(out=xl[d][:, :].rearrange("p (c w) -> p c w", c=8), in0=X3, in1=mlr[:, 0:258].unsqueeze(1).to_broadcast([128, 8, 258]), op=ALU.mult), r=[X, mlr], w=[xl[d]])
                        S.op('pool', lambda e: e.tensor_tensor(out=xr[d][:, :].rearrange("p (c w) -> p c w", c=8), in0=X3, in1=mlr[:, 258:516].unsqueeze(1).to_broadcast([128, 8, 258]), op=ALU.mult), r=[X, mlr], w=[xr[d]])
                        srcs = {-1: xl[d], 0: X, 1: xr[d]}
                    yield
                    for c8 in range(8):
                        c = half * 8 + c8
                        bk = bk4[c // 4]
                        for ti, (dr, dc) in enumerate(taps):
                            sb = srcs[dc]
                            o0 = c8 * 258 + 65 + dr * 64 + dc
                            tap = (dr + 1) * 3 + (dc + 1)
                            S.op('pe', lambda e, bk=bk, c=c, sb=sb, o0=o0, tap=tap, ti=ti: e.matmul(bk[:, (c % 4) * 128:(c % 4 + 1) * 128], lhsT=Dg[:, (c * 9 + tap) * 128:(c * 9 + tap + 1) * 128], rhs=sb[:, o0:o0 + 128], start=(ti == 0), stop=(ti == len(taps) - 1)), r=[Dg, sb], w=[bk])
                    yield
                    for c8 in range(8):
                        c = half * 8 + c8
                        bk = bk4[c // 4]
                        if c < 8:
                            dst, oc = xTt[d], c
                        elif c < 12:
                            dst, oc = BT[d], c - 8
                        else:
                            dst, oc = CT[d], c - 12
                        S.op('act', lambda e, bk=bk, c=c, dst=dst, oc=oc: e.activation(out=dst[:, oc * 128:(oc + 1) * 128], in_=bk[:, (c % 4) * 128:(c % 4 + 1) * 128], func=AF.Silu, bias=cb[:, c:c + 1]), r=[bk, cb], w=[dst])
                    yield
                for c in range(8):
                    bk = bk4[c // 4]
                    S.op('pe', lambda e, bk=bk, c=c: e.transpose(bk[:, (c % 4) * 128:(c % 4 + 1) * 128], xTt[d][:, c * 128:(c + 1) * 128], cst[:, C_I:C_I + 128]), r=[xTt[d], cst], w=[bk])
                b2b = bk4[2][:, :].bitcast(BF16)
                for g in range(4):
                    S.op('pe', lambda e, g=g: e.transpose(b2b[:, g * 128:(g + 1) * 128], BT[d][:, g * 128:(g + 1) * 128], idb[:, :]), r=[BT[d], idb], w=[bk4[2]])
                yield
                S.op('act', lambda e: e.activation(out=xtok[d][:, 0:512], in_=bk4[0][:, :], func=AF.Copy), r=[bk4[0]], w=[xtok[d]])
                S.op('dve', lambda e: e.tensor_copy(out=xtok[d][:, 512:1024], in_=bk4[1][:, :]), r=[bk4[1]], w=[xtok[d]])
                S.op('act', lambda e: e.activation(out=Btok[d][:, :], in_=b2b[:, 0:512], func=AF.Copy), r=[bk4[2]], w=[Btok[d]])
                S.dma('sp', dtr[d][:, :], dts[t0:t0 + 128, :], w=[dtr[d]])
                S.op('dve', lambda e: e.tensor_tensor(out=dtr[d][:, :], in0=dtr[d][:, :], in1=dtbb[:, :], op=ALU.add), r=[dtr[d], dtbb], w=[dtr[d]])
                S.op('act', lambda e: e.activation(out=dtr[d][:, :], in_=dtr[d][:, :], func=AF.Exp), r=[dtr[d]], w=[dtr[d]])
                S.op('act', lambda e: e.activation(out=dtr[d][:, :], in_=dtr[d][:, :], func=AF.Ln, bias=1.0), r=[dtr[d]], w=[dtr[d]])
                S.op('dve', lambda e: e.tensor_tensor(out=la[d][:, :], in0=dtr[d][:, :], in1=nega[:, :], op=ALU.mult), r=[dtr[d], nega], w=[la[d]])
                yield
                lad = la[d][:, d * 16:(d + 1) * 16]
                dtd = dtr[d][:, d * 16:(d + 1) * 16]
                tri = C_TF if d == 0 else C_TB
                b7 = bk4[3]
                S.op('pe', lambda e: e.matmul(b7[:, 0:16], lhsT=cst[:, tri:tri + 128], rhs=lad, start=True, stop=True), r=[cst, la[d]], w=[b7])
                S.op('pe', lambda e: e.matmul(b7[:, 16:32], lhsT=cst[:, C_ONE:C_ONE + 128], rhs=lad, start=True, stop=True), r=[cst, la[d]], w=[b7])
                yield
                s = sm[d]
                S.op('dve', lambda e: e.tensor_copy(out=s[:, 0:16], in_=b7[:, 0:16]), r=[b7], w=[s])
                S.op('act', lambda e: e.activation(out=s[:, 16:32], in_=b7[:, 0:16], func=AF.Exp), r=[b7], w=[s])
                S.op('act', lambda e: e.activation(out=s[:, 32:48], in_=b7[:, 16:32], func=AF.Exp), r=[b7], w=[s])
                S.op('dve', lambda e: e.tensor_tensor(out=s[:, 48:64], in0=b7[:, 16:32], in1=s[:, 0:16], op=ALU.subtract), r=[b7, s], w=[s])
                S.op('act', lambda e: e.activation(out=s[:, 48:64], in_=s[:, 48:64], func=AF.Exp), r=[s], w=[s])
                S.op('dve', lambda e: e.tensor_tensor(out=s[:, 64:80], in0=s[:, 48:64], in1=dtd, op=ALU.mult), r=[s, dtr[d]], w=[s])
                S.op('dve', lambda e: e.tensor_tensor(out=Dx[d][:, :].rearrange("p (h i) -> p h i", h=16), in0=cst[:, C_I:C_I + 128].unsqueeze(1).to_broadcast([128, 16, 128]), in1=s[:, 0:16].unsqueeze(2).to_broadcast([128, 16, 128]), op=ALU.mult), r=[cst, s], w=[Dx[d]])
                S.op('pool', lambda e: e.tensor_tensor(out=xdt[d][:, :].rearrange("p (h q) -> p h q", h=16), in0=xtok[d][:, :].rearrange("p (h q) -> p h q", h=16), in1=dtd.unsqueeze(2).to_broadcast([128, 16, 64]), op=ALU.mult), r=[xtok[d], dtr[d]], w=[xdt[d]])
                S.op('pool', lambda e: e.tensor_tensor(out=xw[d][:, :].rearrange("p (h q) -> p h q", h=16), in0=xtok[d][:, :].rearrange("p (h q) -> p h q", h=16), in1=s[:, 64:80].unsqueeze(2).to_broadcast([128, 16, 64]), op=ALU.mult), r=[xtok[d], s], w=[xw[d]])
                yield
                for q in range(4):
                    bk = bk4[q]
                    S.op('pe', lambda e, bk=bk, q=q: e.matmul(bk[:, 0:512], lhsT=cst[:, C_ONE:C_ONE + 128], rhs=Dx[d][:, q * 512:(q + 1) * 512], start=True, stop=False), r=[cst, Dx[d]], w=[bk])
                    S.op('pe', lambda e, bk=bk, q=q: e.matmul(bk[:, 0:512], lhsT=cst[:, C_I:C_I + 128], rhs=mb4[d][:, :], start=False, stop=True), r=[cst, mb4[d]], w=[bk])
                yield
                for q in range(4):
                    bk = bk4[q]
                    S.op('dve', lambda e, bk=bk, q=q: e.tensor_tensor(out=seg[d][:, q * 512:(q + 1) * 512].rearrange("p (h i) -> p h i", h=4), in0=bk[:, 0:512].rearrange("p (h i) -> p h i", h=4), in1=s[:, 4 * q:4 * q + 4].unsqueeze(2).to_broadcast([128, 4, 128]), op=ALU.subtract), r=[bk, s], w=[seg[d]])
                yield
                S.op('act', lambda e: e.activation(out=seg[d][:, :], in_=seg[d][:, :], func=AF.Exp), r=[seg[d]], w=[seg[d]])
                for g in range(4):
                    S.op('pe', lambda e, g=g: e.matmul(bk4[0][:, g * 128:(g + 1) * 128], lhsT=BT[d][:, g * 128:(g + 1) * 128], rhs=CT[d][:, g * 128:(g + 1) * 128], start=True, stop=True), r=[BT[d], CT[d]], w=[bk4[0]])
                yield
                for g in range(4):
                    S.op('dve', lambda e, g=g: e.tensor_tensor(out=scT[d][:, g * 512:(g + 1) * 512].rearrange("p (h i) -> p h i", h=4), in0=seg[d][:, g * 512:(g + 1) * 512].rearrange("p (h i) -> p h i", h=4), in1=bk4[0][:, g * 128:(g + 1) * 128].unsqueeze(1).to_broadcast([128, 4, 128]), op=ALU.mult), r=[seg[d], bk4[0]], w=[scT[d]])
                yield
                for hd in range(16):
                    bk = bk4[1 + hd // 8]
                    S.op('pe', lambda e, bk=bk, hd=hd: e.matmul(bk[:, (hd % 8) * 64:(hd % 8 + 1) * 64], lhsT=scT[d][:, hd * 128:(hd + 1) * 128], rhs=xdt[d][:, hd * 64:(hd + 1) * 64], start=True, stop=True), r=[scT[d], xdt[d]], w=[bk])
                bint = [bk4[3], bk4[0]]
                for g in range(4):
                    bk = bint[g // 2]
                    S.op('pe', lambda e, bk=bk, g=g: e.matmul(bk[:, (g % 2) * 256:(g % 2 + 1) * 256], lhsT=CT[d][:, g * 128:(g + 1) * 128], rhs=hTb[d][:, g * 256:(g + 1) * 256], start=True, stop=True), r=[CT[d], hTb[d]], w=[bk])
                yield
                for h2 in range(2):
                    S.op('dve', lambda e, h2=h2: e.tensor_tensor(out=ytmp[d][:, h2 * 512:(h2 + 1) * 512].rearrange("p (h q) -> p h q", h=8), in0=bint[h2][:, :].rearrange("p (h q) -> p h q", h=8), in1=s[:, 16 + 8 * h2:24 + 8 * h2].unsqueeze(2).to_broadcast([128, 8, 64]), op=ALU.mult), r=[bint[h2], s], w=[ytmp[d]])
                for h2 in range(2):
                    S.op('dve', lambda e, h2=h2: e.tensor_tensor(out=ysb[d][:, h2 * 512:(h2 + 1) * 512], in0=bk4[1 + h2][:, :], in1=ytmp[d][:, h2 * 512:(h2 + 1) * 512], op=ALU.add), r=[bk4[1 + h2], ytmp[d]], w=[ysb[d]])
                if d == 0:
                    S.op('pool', lambda e: e.tensor_tensor(out=ytmp[d][:, :].rearrange("p (h q) -> p h q", h=16), in0=xtok[d][:, :].rearrange("p (h q) -> p h q", h=16), in1=dskb[:, 0:16].unsqueeze(2).to_broadcast([128, 16, 64]), op=ALU.mult), r=[xtok[d], dskb], w=[ytmp[d]])
                    S.op('pool', lambda e: e.tensor_tensor(out=ysb[d][:, :], in0=ysb[d][:, :], in1=ytmp[d][:, :], op=ALU.add), r=[ysb[d], ytmp[d]], w=[ysb[d]])
                S.dma('pool', ydir[d][t0:t0 + 128, :], ysb[d][:, :], r=[ysb[d]])
                yield
                for g in range(4):
                    bk = bk4[1 + g // 2]
                    S.op('pe', lambda e, bk=bk, g=g: e.matmul(bk[:, (g % 2) * 256:(g % 2 + 1) * 256], lhsT=Btok[d][:, g * 128:(g + 1) * 128], rhs=xw[d][:, g * 256:(g + 1) * 256], start=True, stop=True), r=[Btok[d], xw[d]], w=[bk])
                S.op('pool', lambda e: e.tensor_tensor(out=hT[d][:, :].rearrange("p (h q) -> p h q", h=16), in0=hT[d][:, :].rearrange("p (h q) -> p h q", h=16), in1=s[:, 32:48].unsqueeze(2).to_broadcast([128, 16, 64]), op=ALU.mult), r=[hT[d], s], w=[hT[d]])
                yield
                for h2 in range(2):
                    S.op('dve', lambda e, h2=h2: e.tensor_tensor(out=hT[d][:, h2 * 512:(h2 + 1) * 512], in0=hT[d][:, h2 * 512:(h2 + 1) * 512], in1=bk4[1 + h2][:, :], op=ALU.add), r=[hT[d], bk4[1 + h2]], w=[hT[d]])
                S.op('act', lambda e: e.activation(out=hTb[d][:, :], in_=hT[d][:, :], func=AF.Copy), r=[hT[d]], w=[hTb[d]])

            order_f = list(range(NT))
            order_b = list(range(NCT - 1, -1, -1)) + list(range(NT - 1, NCT - 1, -1))
            for i in range(NT):
                run_pair(step(order_f[i], 0), step(order_b[i], 1))
            ph.close()

        def gla_phase(H, dk, dv, qT_s, kT_s, ktok_s, v_s, lg_s, o_s):
            ph = Phase()
            cst = load_consts(ph)
            HK = H * dk
            HV = H * dv
            nkc = HK // 128
            hp = 128 // dk
            NCH = T // 64
            NCC = CTX // 64
            Sst = [ph.tile([128, nkc * dv], F32, "Sst") for _ in range(2)]
            Sb = [ph.tile([128, nkc * dv], BF16, "Sb") for _ in range(2)]
            for d in range(2):
                S.op('dve', lambda e, d=d: e.memset(Sst[d][:, :], 0.0), w=[Sst[d]])
                S.op('dve', lambda e, d=d: e.memset(Sb[d][:, :], 0.0), w=[Sb[d]])

            def mk(shape, dt, name):
                return [ph.tile(shape, dt, name) for _ in range(2)]
            lgt = mk([64, HK], F32, "lgt")
            kt = mk([64, HK], F32, "kt")
            vt = mk([64, HV], F32, "vt")
            vb = mk([64, HV], BF16, "vb")
            qTt = mk([128, nkc * 64], F32, "qTt")
            kTt = mk([128, nkc * 64], F32, "kTt")
            eg = mk([128, nkc * 64], F32, "eg")
            eng = mk([128, nkc * 64], F32, "eng")
            qd = [[ph.tile([128, nkc * 64], BF16, "qd") for _ in range(hp)] for _ in range(2)]
            for d in range(2):
                for sl in range(hp):
                    S.op('dve', lambda e, d=d, sl=sl: e.memset(qd[d][sl][:, :], 0.0), w=[qd[d][sl]])
            ki = mk([128, nkc * 64], BF16, "ki")
            eex = mk([64, HK], F32, "eex")
            kend = mk([64, HK], BF16, "kend")
            att = mk([64, H * 64], BF16, "att")
            osb = mk([64, HV], F32, "osb")

            def step(ci, d):
                t0 = ci * 64
                S.dma('sp', lgt[d][:, :], lg_s[d][t0:t0 + 64, 0:HK], w=[lgt[d]])
                S.dma('sp', kt[d][:, :], ktok_s[d][t0:t0 + 64, 0:HK], w=[kt[d]])
                S.dma('sp', vt[d][:, :], v_s[t0:t0 + 64, 0:HV], w=[vt[d]])
                S.dma('act', qTt[d][:, :].rearrange("p (c t) -> p c t", c=nkc), qT_s[0:HK, t0:t0 + 64].rearrange("(c p) t -> p c t", p=128), w=[qTt[d]])
                S.dma('act', kTt[d][:, :].rearrange("p (c t) -> p c t", c=nkc), kT_s[d][0:HK, t0:t0 + 64].rearrange("(c p) t -> p c t", p=128), w=[kTt[d]])
                yield
                tri = C_TF if d == 0 else C_TB
                stri = C_SF if d == 0 else C_SB
                bA = banks[0 + 4 * d]
                for kc in range(nkc):
                    S.op('pe', lambda e, kc=kc: e.matmul(bA[:, kc * 64:(kc + 1) * 64], lhsT=lgt[d][0:64, kc * 128:(kc + 1) * 128], rhs=cst[0:64, tri:tri + 64], start=True, stop=True), r=[lgt[d], cst], w=[bA])
                nb = HK // 512
                bB = [banks[1 + 4 * d], banks[2 + 4 * d]]
                for b in range(nb):
                    S.op('pe', lambda e, b=b: e.matmul(bB[b][0:64, 0:512], lhsT=cst[0:64, stri:stri + 64], rhs=lgt[d][0:64, b * 512:(b + 1) * 512], start=True, stop=True), r=[lgt[d], cst], w=[bB[b]])
                yield
                S.op('act', lambda e: e.activation(out=eg[d][:, :], in_=bA[:, 0:nkc * 64], func=AF.Exp), r=[bA], w=[eg[d]])
                S.op('act', lambda e: e.activation(out=eng[d][:, :], in_=bA[:, 0:nkc * 64], func=AF.Exp, scale=-1.0), r=[bA], w=[eng[d]])
                yield
                for sl in range(hp):
                    p0, p1 = sl * dk, (sl + 1) * dk
                    S.op('dve', lambda e, sl=sl, p0=p0, p1=p1: e.tensor_tensor(out=qd[d][sl][p0:p1, :], in0=qTt[d][p0:p1, :], in1=eg[d][p0:p1, :], op=ALU.mult), r=[qTt[d], eg[d]], w=[qd[d][sl]])
                S.op('dve', lambda e: e.tensor_tensor(out=ki[d][:, :], in0=kTt[d][:, :], in1=eng[d][:, :], op=ALU.mult), r=[kTt[d], eng[d]], w=[ki[d]])
                for b in range(nb):
                    S.op('act', lambda e, b=b: e.activation(out=eex[d][:, b * 512:(b + 1) * 512], in_=bB[b][0:64, 0:512], func=AF.Exp), r=[bB[b]], w=[eex[d]])
                S.op('dve', lambda e: e.tensor_tensor(out=kend[d][:, :], in0=kt[d][:, :], in1=eex[d][:, :], op=ALU.mult), r=[kt[d], eex[d]], w=[kend[d]])
                S.op('act', lambda e: e.activation(out=vb[d][:, :], in_=vt[d][:, :], func=AF.Copy), r=[vt[d]], w=[vb[d]])
                yield
                bC = banks[3 + 4 * d]
                for h in range(H):
                    kc, sl = divmod(h, hp)
                    S.op('pe', lambda e, h=h, kc=kc, sl=sl: e.matmul(bC[0:64, h * 64:(h + 1) * 64], lhsT=ki[d][:, kc * 64:(kc + 1) * 64], rhs=qd[d][sl][:, kc * 64:(kc + 1) * 64], start=True, stop=True), r=[ki[d], qd[d][sl]], w=[bC])
                yield
                S.op('dve', lambda e: e.tensor_tensor(out=att[d][:, :].rearrange("p (h i) -> p h i", h=H), in0=bC[0:64, 0:H * 64].rearrange("p (h i) -> p h i", h=H), in1=cst[0:64, tri:tri + 64].unsqueeze(1).to_broadcast([64, H, 64]), op=ALU.mult), r=[bC, cst], w=[att[d]])
                yield
                bO = [banks[1 + 4 * d], banks[2 + 4 * d]]
                for h in range(H):
                    bo = bO[(h * dv) // 512]
                    oc = (h * dv) % 512
                    S.op('pe', lambda e, h=h, bo=bo, oc=oc: e.matmul(bo[0:64, oc:oc + dv], lhsT=att[d][0:64, h * 64:(h + 1) * 64], rhs=vb[d][0:64, h * dv:(h + 1) * dv], start=True, stop=True), r=[att[d], vb[d]], w=[bo])
                yield
                for b in range(HV // 512):
                    S.op('act', lambda e, b=b: e.activation(out=osb[d][:, b * 512:(b + 1) * 512], in_=bO[b][0:64, 0:512], func=AF.Copy), r=[bO[b]], w=[osb[d]])
                for h in range(H):
                    kc, sl = divmod(h, hp)
                    bo = bO[(h * dv) // 512]
                    oc = (h * dv) % 512
                    S.op('pe', lambda e, h=h, bo=bo, oc=oc, kc=kc, sl=sl: e.matmul(bo[0:64, oc:oc + dv], lhsT=qd[d][sl][:, kc * 64:(kc + 1) * 64], rhs=Sb[d][:, kc * dv:(kc + 1) * dv], start=True, stop=True), r=[qd[d][sl], Sb[d]], w=[bo])
                yield
                for b in range(HV // 512):
                    S.op('dve', lambda e, b=b: e.tensor_tensor(out=osb[d][:, b * 512:(b + 1) * 512], in0=osb[d][:, b * 512:(b + 1) * 512], in1=bO[b][0:64, 0:512], op=ALU.add), r=[bO[b], osb[d]], w=[osb[d]])
                S.dma('pool', o_s[d][t0:t0 + 64, 0:HV], osb[d][:, :], r=[osb[d]])
                yield
                bS = [banks[0 + 4 * d], banks[3 + 4 * d]]
                wS = hp * dv
                for kc in range(nkc):
                    bs = bS[(kc * wS) // 512]
                    oc = (kc * wS) % 512
                    S.op('pe', lambda e, kc=kc, bs=bs, oc=oc: e.matmul(bs[:, oc:oc + wS], lhsT=kend[d][0:64, kc * 128:(kc + 1) * 128], rhs=vb[d][0:64, kc * wS:(kc + 1) * wS], start=True, stop=True), r=[kend[d], vb[d]], w=[bs])
                yield
                lastc = 63 if d == 0 else 0
                for kc in range(nkc):
                    bs = bS[(kc * wS) // 512]
                    oc = (kc * wS) % 512
                    for sl in range(hp):
                        p0, p1 = sl * dk, (sl + 1) * dk
                        S.op('dve', lambda e, kc=kc, bs=bs, oc=oc, sl=sl, p0=p0, p1=p1: e.scalar_tensor_tensor(out=Sst[d][p0:p1, kc * dv:(kc + 1) * dv], in0=Sst[d][p0:p1, kc * dv:(kc + 1) * dv], scalar=eg[d][p0:p1, kc * 64 + lastc:kc * 64 + lastc + 1], in1=bs[p0:p1, oc + sl * dv:oc + (sl + 1) * dv], op0=ALU.mult, op1=ALU.add), r=[Sst[d], eg[d], bs], w=[Sst[d]])
                S.op('act', lambda e: e.activation(out=Sb[d][:, :], in_=Sst[d][:, :], func=AF.Copy), r=[Sst[d]], w=[Sb[d]])

            order_f = list(range(NCH))
            order_b = list(range(NCC - 1, -1, -1)) + list(range(NCH - 1, NCC - 1, -1))
            for i in range(NCH):
                run_pair(step(order_f[i], 0), step(order_b[i], 1))
            ph.close()

        def head_rms(ph, src, dstbuf, dst_ap, H, hd, nwbuf, nw_bc, gate, sq, ss, rstd, epsb, tmp):
            n = H * hd
            v3 = lambda ap: ap.rearrange("p (h q) -> p h q", h=H)
            S.op('dve', lambda e: e.tensor_tensor(out=sq[:, 0:n], in0=src[:, 0:n], in1=src[:, 0:n], op=ALU.mult), r=[src], w=[sq])
            S.op('dve', lambda e: e.tensor_reduce(out=ss[:, 0:H], in_=v3(sq[:, 0:n]), axis=AX.X, op=ALU.add), r=[sq], w=[ss])
            S.op('act', lambda e: e.activation(out=rstd[:, 0:H], in_=ss[:, 0:H], func=AF.Sqrt, scale=1.0 / hd, bias=epsb[:, 0:1]), r=[ss, epsb], w=[rstd])
            S.op('dve', lambda e: e.reciprocal(out=rstd[:, 0:H], in_=rstd[:, 0:H]), r=[rstd], w=[rstd])
            S.op('dve', lambda e: e.tensor_tensor(out=v3(tmp[:, 0:n]), in0=v3(src[:, 0:n]), in1=rstd[:, 0:H].unsqueeze(2).to_broadcast([128, H, hd]), op=ALU.mult), r=[src, rstd], w=[tmp])
            S.op('dve', lambda e: e.tensor_tensor(out=v3(tmp[:, 0:n]), in0=v3(tmp[:, 0:n]), in1=nw_bc, op=ALU.mult), r=[tmp, nwbuf], w=[tmp])
            S.op('dve', lambda e: e.tensor_tensor(out=dst_ap, in0=tmp[:, 0:n], in1=gate[:, 0:n], op=ALU.mult), r=[tmp, gate], w=[dstbuf])

        def phaseD0a():
            ph = Phase()
            cst = load_consts(ph)
            Wo = ph.tile([128, 16 * D], BF16, "Wo")
            load_weight_bf16(ph, Wo, w_out0, 2048, D)
            snw = bc_load(ph, ssdnw[0:1, :], 1024, "snw")
            gnw = bc_load(ph, glanw[0:1, :], 128, "gnw")
            g1 = [bc_load(ph, modv[0, s:s + 1, 2 * D:3 * D], D, "g1") for s in range(2)]
            epsb = ph.tile([128, 1], F32, "epsb")
            S.op('dve', lambda e: e.memset(epsb[:, :], EPS), w=[epsb])
            idb = ph.tile([128, 128], BF16, "idb")
            S.op('dve', lambda e: e.tensor_copy(out=idb[:, :], in_=cst[:, C_I:C_I + 128]), r=[cst], w=[idb])
            a = ph.tile([128, D], F32, "a")
            b = ph.tile([128, D], F32, "b")
            g = ph.tile([128, D], F32, "g")
            sq = ph.tile([128, D], F32, "sq")
            tmp = ph.tile([128, D], F32, "tmp")
            xt = ph.tile([128, D], F32, "xt")
            mix = ph.tile([128, 2048], BF16, "mix")
            mixT = ph.tile([128, 2048], BF16, "mixT")
            ss = ph.tile([128, 16], F32, "ss")
            rstd = ph.tile([128, 16], F32, "rstd")
            xo = ph.tile([128, D], F32, "xo")
            for m in range(NT):
                r0 = m * 128
                for part in range(2):
                    srcs = ydir if part == 0 else odir
                    gsrc = zs if part == 0 else rs
                    S.dma('sp', a[:, :], srcs[0][r0:r0 + 128, :], w=[a])
                    S.dma('act', b[:, :], srcs[1][r0:r0 + 128, :], w=[b])
                    S.dma('sp', g[:, :], gsrc[r0:r0 + 128, :], w=[g])
                    S.op('dve', lambda e: e.tensor_tensor(out=a[:, :], in0=a[:, :], in1=b[:, :], op=ALU.add), r=[a, b], w=[a])
                    S.op('act', lambda e: e.activation(out=g[:, :], in_=g[:, :], func=AF.Silu), r=[g], w=[g])
                    if part == 0:
                        S.op('dve', lambda e: e.tensor_tensor(out=a[:, :], in0=a[:, :], in1=g[:, :], op=ALU.mult), r=[a, g], w=[a])
                        S.op('dve', lambda e: e.tensor_tensor(out=sq[:, :], in0=a[:, :], in1=a[:, :], op=ALU.mult), r=[a], w=[sq])
                        S.op('dve', lambda e: e.tensor_reduce(out=ss[:, 0:4], in_=sq[:, :].rearrange("p (h q) -> p h q", h=4), axis=AX.X, op=ALU.add), r=[sq], w=[ss])
                        S.op('act', lambda e: e.activation(out=rstd[:, 0:4], in_=ss[:, 0:4], func=AF.Sqrt, scale=1.0 / 256, bias=epsb[:, 0:1]), r=[ss, epsb], w=[rstd])
                        S.op('dve', lambda e: e.reciprocal(out=rstd[:, 0:4], in_=rstd[:, 0:4]), r=[rstd], w=[rstd])
                        S.op('dve', lambda e: e.tensor_tensor(out=tmp[:, :].rearrange("p (h q) -> p h q", h=4), in0=a[:, :].rearrange("p (h q) -> p h q", h=4), in1=rstd[:, 0:4].unsqueeze(2).to_broadcast([128, 4, 256]), op=ALU.mult), r=[a, rstd], w=[tmp])
                        S.op('dve', lambda e: e.tensor_tensor(out=mix[:, 0:1024], in0=tmp[:, :], in1=snw[:, :], op=ALU.mult), r=[tmp, snw], w=[mix])
                    else:
                        head_rms(ph, a, mix, mix[:, 1024:2048], 8, 128, gnw, gnw[:, 0:128].unsqueeze(1).to_broadcast([128, 8, 128]), g, sq, ss, rstd, epsb, tmp)
                for q in range(4):
                    bk = banks[q]
                    bkb = bk[:, 0:256].bitcast(BF16)
                    for c in range(4):
                        cc = q * 4 + c
                        S.op('pe', lambda e, bkb=bkb, c=c, cc=cc: e.transpose(bkb[:, c * 128:(c + 1) * 128], mix[:, cc * 128:(cc + 1) * 128], idb[:, :]), r=[mix, idb], w=[bk])
                    S.op('act' if q % 2 == 0 else 'dve', (lambda e, bkb=bkb, q=q: e.activation(out=mixT[:, q * 512:(q + 1) * 512], in_=bkb[:, 0:512], func=AF.Copy)) if q % 2 == 0 else (lambda e, bkb=bkb, q=q: e.tensor_copy(out=mixT[:, q * 512:(q + 1) * 512], in_=bkb[:, 0:512])), r=[bk], w=[mixT])
                S.dma('sp', xt[:, :], xsrc(0, m), w=[xt])
                for h2 in range(2):
                    bk = banks[4 + h2]
                    for c in range(16):
                        S.op('pe', lambda e, bk=bk, c=c, h2=h2: e.matmul(bk[:, 0:512], lhsT=mixT[:, c * 128:(c + 1) * 128], rhs=Wo[:, c * D + h2 * 512:c * D + (h2 + 1) * 512], start=(c == 0), stop=(c == 15)), r=[mixT, Wo], w=[bk])
                    gg = g1[1] if m < NCT else g1[0]
                    S.op('dve', lambda e, bk=bk, h2=h2, gg=gg: e.tensor_tensor(out=xo[:, h2 * 512:(h2 + 1) * 512], in0=bk[:, 0:512], in1=gg[:, h2 * 512:(h2 + 1) * 512], op=ALU.mult), r=[bk, gg], w=[xo])
                S.op('dve', lambda e: e.tensor_tensor(out=xo[:, :], in0=xo[:, :], in1=xt[:, :], op=ALU.add), r=[xo, xt], w=[xo])
                S.dma('pool', xmid[r0:r0 + 128, :], xo[:, :], r=[xo])
            ph.close()

        def phaseMLP(layer, tiles, final):
            ph = Phase()
            cst = ph.tile([128, 128], F32, "cstI")
            S.dma('sp', cst[:, :], consts[:, 0:128], w=[cst])
            W1 = ph.tile([128, 8 * 4096], BF16, "W1")
            load_weight_bf16(ph, W1, mlp_w1[layer], D, 4096)
            W2 = ph.tile([128, 32 * D], BF16, "W2")
            load_weight_bf16(ph, W2, mlp_w2[layer], 4096, D)
            NW = bc_load(ph, norm2_w[layer:layer + 1, :], D, "NW")
            Am = ph.tile([128, D], F32, "Am")
            SHm = ph.tile([128, D], F32, "SHm")
            Gm = ph.tile([128, D], F32, "Gm")
            cur = [None]

            def set_mod(src):
                if cur[0] == src:
                    return
                cur[0] = src
                S.dma('sp', Am[:, :], modv[layer, src:src + 1, 4 * D:5 * D].partition_broadcast(128), w=[Am])
                S.dma('sp', SHm[:, :], modv[layer, src:src + 1, 3 * D:4 * D].partition_broadcast(128), w=[SHm])
                S.dma('sp', Gm[:, :], modv[layer, src:src + 1, 5 * D:6 * D].partition_broadcast(128), w=[Gm])
                S.op('dve', lambda e: e.scalar_tensor_tensor(out=Am[:, :], in0=Am[:, :], scalar=1.0, in1=NW[:, :], op0=ALU.add, op1=ALU.mult), r=[Am, NW], w=[Am])
            if final:
                fnw = bc_load(ph, final_w[0:1, :], D, "fnw")
            epsb = ph.tile([128, 1], F32, "epsb")
            S.op('dve', lambda e: e.memset(epsb[:, :], EPS), w=[epsb])
            idb = ph.tile([128, 128], BF16, "idb")
            S.op('dve', lambda e: e.tensor_copy(out=idb[:, :], in_=cst[:, C_I:C_I + 128]), r=[cst], w=[idb])
            xt = ph.tile([128, D], F32, "xt")
            junk = ph.tile([128, D], BF16, "junk")
            tmp = ph.tile([128, D], F32, "tmp")
            hb = ph.tile([128, D], BF16, "hb")
            hT = ph.tile([128, D], BF16, "hT")
            h1 = ph.tile([128, 512], F32, "h1")
            h1T = ph.tile([128, 4096], BF16, "h1T")
            xo = ph.tile([128, D], F32, "xo")
            ss = ph.tile([128, 1], F32, "ss")
            rstd = ph.tile([128, 1], F32, "rstd")
            for m in tiles:
                r0 = m * 128
                S.dma('sp', xt[:, :], xmid[r0:r0 + 128, :], w=[xt])
                rms_rstd(ph, xt, junk, ss, rstd, epsb, D, 1.0 / D)
                set_mod(1 if m < NCT else 0)
                A, SH, G2 = Am, SHm, Gm
                S.op('dve', lambda e, A=A: e.scalar_tensor_tensor(out=tmp[:, :], in0=xt[:, :], scalar=rstd[:, 0:1], in1=A[:, :], op0=ALU.mult, op1=ALU.mult), r=[xt, rstd, A], w=[tmp])
                S.op('dve', lambda e, SH=SH: e.tensor_tensor(out=hb[:, :], in0=tmp[:, :], in1=SH[:, :], op=ALU.add), r=[tmp, SH], w=[hb])
                bk = banks[0]
                bkb = bk[:, :].bitcast(BF16)
                for j in range(8):
                    S.op('pe', lambda e, j=j: e.transpose(bkb[:, j * 128:(j + 1) * 128], hb[:, j * 128:(j + 1) * 128], idb[:, :]), r=[hb, idb], w=[bk])
                S.op('act', lambda e: e.activation(out=hT[:, :], in_=bkb[:, 0:1024], func=AF.Copy), r=[bk], w=[hT])
                for q in range(8):
                    bk = banks[1 + q % 5]
                    for c in range(4):
                        fc = q * 4 + c
                        for j in range(8):
                            S.op('pe', lambda e, bk=bk, c=c, fc=fc, j=j: e.matmul(bk[:, c * 128:(c + 1) * 128], lhsT=W1[:, j * 4096 + fc * 128:j * 4096 + (fc + 1) * 128], rhs=hT[:, j * 128:(j + 1) * 128], start=(j == 0), stop=(j == 7)), r=[W1, hT], w=[bk])
                    S.op('act', lambda e, bk=bk: e.activation(out=h1[:, :], in_=bk[:, :], func=AF.Relu), r=[bk], w=[h1])
                    S.op('dve', lambda e, q=q: e.tensor_tensor(out=h1T[:, q * 512:(q + 1) * 512], in0=h1[:, :], in1=h1[:, :], op=ALU.mult), r=[h1], w=[h1T])
                for h2 in range(2):
                    bk = banks[6 + h2]
                    for fc in range(32):
                        S.op('pe', lambda e, bk=bk, fc=fc, h2=h2: e.matmul(bk[:, 0:512], lhsT=h1T[:, fc * 128:(fc + 1) * 128], rhs=W2[:, fc * D + h2 * 512:fc * D + (h2 + 1) * 512], start=(fc == 0), stop=(fc == 31)), r=[h1T, W2], w=[bk])
                    S.op('dve', lambda e, bk=bk, h2=h2, G2=G2: e.tensor_tensor(out=xo[:, h2 * 512:(h2 + 1) * 512], in0=bk[:, 0:512], in1=G2[:, h2 * 512:(h2 + 1) * 512], op=ALU.mult), r=[bk, G2], w=[xo])
                S.op('dve', lambda e: e.tensor_tensor(out=xo[:, :], in0=xo[:, :], in1=xt[:, :], op=ALU.add), r=[xo, xt], w=[xo])
                if not final:
                    S.dma('pool', xres[r0:r0 + 128, :], xo[:, :], r=[xo])
                else:
                    rms_rstd(ph, xo, junk, ss, rstd, epsb, D, 1.0 / D)
                    S.op('dve', lambda e: e.scalar_tensor_tensor(out=tmp[:, :], in0=xo[:, :], scalar=rstd[:, 0:1], in1=fnw[:, :], op0=ALU.mult, op1=ALU.mult), r=[xo, rstd, fnw], w=[tmp])
                    S.dma('pool', out[r0 - CTX:r0 - CTX + 128, :], tmp[:, :], r=[tmp])
            ph.close()


        def phaseA1():
            tok_groups = []
            feat_groups = []

            def extra(c):
                ph = c['ph']
                l0 = bc_load(ph, lbl[0:1, :], 2048, "l0")
                oml = bc_load(ph, lbl[1:2, :], 2048, "oml")
                S.op('dve', lambda e: e.tensor_tensor(out=oml[:, :], in0=oml[:, :], in1=l0[:, :], op=ALU.subtract), r=[oml, l0], w=[oml])
                S.op('act', lambda e: e.activation(out=oml[:, :], in_=oml[:, :], func=AF.Sigmoid, scale=-1.0), r=[oml], w=[oml])
                omlT = ph.tile([128, 32], F32, "omlT")
                S.dma('sp', omlT[:, :], lblT[:, :, :].rearrange("p l c -> p (l c)"), w=[omlT])
                S.op('dve', lambda e: e.tensor_tensor(out=omlT[:, 16:32], in0=omlT[:, 16:32], in1=omlT[:, 0:16], op=ALU.subtract), r=[omlT], w=[omlT])
                S.op('act', lambda e: e.activation(out=omlT[:, 16:32], in_=omlT[:, 16:32], func=AF.Sigmoid, scale=-1.0), r=[omlT], w=[omlT])
                oneb = ph.tile([128, 1], F32, "oneb")
                S.op('dve', lambda e: e.memset(oneb[:, :], 1.0), w=[oneb])
                stg32, k32 = c['stg32'], c['k32']

                def nxt():
                    st = stg32[k32[0] % len(stg32)]
                    k32[0] += 1
                    return st

                def f_tok(m, bk, off, n):
                    d = off // 1024
                    col = off % 1024
                    st = nxt()
                    S.op('act', lambda e: e.activation(out=st[:, 0:n], in_=bk[:, 0:n], func=AF.Sigmoid, scale=-1.0), r=[bk], w=[st])
                    S.op('dve', lambda e: e.tensor_tensor(out=st[:, 0:n], in0=st[:, 0:n], in1=oml[:, off:off + n], op=ALU.mult), r=[st, oml], w=[st])
                    S.dma('pool', ktoks[d][m * 128:(m + 1) * 128, col:col + n], st[:, 0:n], r=[st])
                    st2 = nxt()
                    S.op('act', lambda e: e.activation(out=st2[:, 0:n], in_=st[:, 0:n], func=AF.Ln, scale=-1.0, bias=oneb[:, 0:1]), r=[st, oneb], w=[st2])
                    S.dma('pool', lgs[d][m * 128:(m + 1) * 128, col:col + n], st2[:, 0:n], r=[st2])

                def f_feat(m, bk, q0, nch):
                    st = nxt()
                    S.op('act', lambda e: e.activation(out=st[:, 0:nch * 128], in_=bk[:, 0:nch * 128], func=AF.Sigmoid, scale=-1.0), r=[bk], w=[st])
                    for cc in range(nch):
                        ch = q0 + cc
                        S.op('dve', lambda e, cc=cc, ch=ch: e.tensor_scalar(out=st[:, cc * 128:(cc + 1) * 128], in0=st[:, cc * 128:(cc + 1) * 128], scalar1=omlT[:, 16 + ch:17 + ch], scalar2=None, op0=ALU.mult), r=[st, omlT], w=[st])
                    d = q0 // 8
                    r0 = (q0 % 8) * 128
                    S.dma('pool', kTs[d][r0:r0 + nch * 128, m * 128:(m + 1) * 128].rearrange("(c p) t -> p c t", p=128),
                          st[:, 0:nch * 128].rearrange("p (c t) -> p c t", c=nch), r=[st])
                tok_groups.append((1024, 1024, make_tok_store(c, vtok, 0, 1024)))
                tok_groups.append((2048, 2048, f_tok))
                tok_groups.append((4096, 1024, make_tok_store(c, rs, 0, 1024)))
                tok_groups.append((5120, 384, make_tok_store(c, utok, 0, 384)))
                feat_groups.append((0, 8, 128, make_feat_store(c, qT, 0, func=AF.Silu)))
                feat_groups.append((2048, 16, 128, f_feat))
            proj_phase(1, w_in1, 5504, tok_groups, feat_groups, extra)

        def phaseC1():
            ph = Phase()
            cst = load_consts(ph)
            PI = float(np.pi)
            prm = ph.tile([128, 72], F32, "prm")
            S.dma('sp', prm[:, :], s5p[:, :], w=[prm])
            bri = ph.tile([128, 384], F32, "bri")
            S.dma('sp', bri[:, :], s5b[:, :], w=[bri])
            cri = ph.tile([128, 384], F32, "cri")
            S.dma('sp', cri[:, :], s5c[:, :], w=[cri])
            dsk = ph.tile([128, 3], F32, "dsk")
            S.dma('sp', dsk[:, :], s5d[:, :], w=[dsk])
            negpi = ph.tile([128, 1], F32, "negpi")
            S.op('dve', lambda e: e.memset(negpi[:, :], -PI), w=[negpi])
            w12 = ph.tile([128, 12 * 12], F32, "w12")

            def V(i):
                return w12[:, i * 12:(i + 1) * 12]
            CX = [ph.tile([128, 12 * 128], F32, "CX") for _ in range(2)]
            bbX = ph.tile([128, 12 * 128], F32, "bbX")
            BbT = [[ph.tile([128, 12 * 128], F32, "BbT") for _ in range(2)] for _ in range(2)]
            Ecs = [[ph.tile([128, 12 * 128], F32, "E") for _ in range(2)] for _ in range(2)]
            rmag = [ph.tile([128, 12 * 128], F32, "rmag") for _ in range(2)]
            bb = ph.tile([128, 2 * 192], F32, "bb")
            tA = ph.tile([128, 12 * 128], F32, "tA")
            tB = ph.tile([128, 12 * 128], F32, "tB")

            def place(dst, src_ap3, negate=False):
                S.op('dve', lambda e: e.memset(dst[:, :], 0.0), w=[dst])
                for sc in range(12):
                    for g2 in range(2):
                        gl = (2 * sc + g2) % 8
                        p0, p1 = g2 * 64, (g2 + 1) * 64
                        if negate:
                            S.op('dve', lambda e, sc=sc, gl=gl, p0=p0, p1=p1: e.tensor_scalar(out=dst[p0:p1, sc * 128 + gl * 16:sc * 128 + gl * 16 + 16], in0=src_ap3(sc, p0, p1), scalar1=-1.0, scalar2=None, op0=ALU.mult), r=[bb, cri], w=[dst])
                        else:
                            S.op('dve', lambda e, sc=sc, gl=gl, p0=p0, p1=p1: e.tensor_copy(out=dst[p0:p1, sc * 128 + gl * 16:sc * 128 + gl * 16 + 16], in_=src_ap3(sc, p0, p1)), r=[bb, cri], w=[dst])
            place(CX[0], lambda sc, p0, p1: cri[p0:p1, sc * 16:(sc + 1) * 16])
            place(CX[1], lambda sc, p0, p1: cri[p0:p1, 192 + sc * 16:192 + (sc + 1) * 16], negate=True)

            def tt(out, a, b, op, r, w):
                S.op('dve', lambda e: e.tensor_tensor(out=out, in0=a, in1=b, op=op), r=r, w=w)

            def sin_of(out, theta, shift):
                S.op('dve', lambda e: e.tensor_scalar(out=V(10), in0=theta, scalar1=shift + PI, scalar2=None, op0=ALU.add), r=[w12], w=[w12])
                for kk in range(1, 5):
                    S.op('dve', lambda e, kk=kk: e.tensor_scalar(out=V(11), in0=V(10), scalar1=2.0 * PI * kk, scalar2=-2.0 * PI, op0=ALU.is_ge, op1=ALU.mult), r=[w12], w=[w12])
                    if kk == 1:
                        S.op('dve', lambda e: e.tensor_tensor(out=V(9), in0=V(10), in1=V(11), op=ALU.add), r=[w12], w=[w12])
                    else:
                        S.op('dve', lambda e: e.tensor_tensor(out=V(9), in0=V(9), in1=V(11), op=ALU.add), r=[w12], w=[w12])
                S.op('act', lambda e: e.activation(out=out, in_=V(9), func=AF.Sin, bias=negpi[:, 0:1]), r=[w12, negpi], w=[w12])

            for d in range(2):
                are = prm[:, d * 36:d * 36 + 12]
                aim = prm[:, d * 36 + 12:d * 36 + 24]
                ldt = prm[:, d * 36 + 24:d * 36 + 36]
                S.op('act', lambda e, ldt=ldt: e.activation(out=V(0), in_=ldt, func=AF.Exp), r=[prm], w=[w12])
                tt(V(1), are, V(0), ALU.mult, [prm, w12], [w12])
                S.op('act', lambda e: e.activation(out=V(1), in_=V(1), func=AF.Exp), r=[w12], w=[w12])
                tt(V(2), aim, V(0), ALU.mult, [prm, w12], [w12])
                sin_of(V(3), V(2), 0.0)
                sin_of(V(4), V(2), PI / 2)
                tt(V(5), V(1), V(4), ALU.mult, [w12], [w12])
                tt(V(6), V(1), V(3), ALU.mult, [w12], [w12])
                tt(V(7), are, are, ALU.mult, [prm], [w12])
                tt(V(8), aim, aim, ALU.mult, [prm], [w12])
                tt(V(7), V(7), V(8), ALU.add, [w12], [w12])
                S.op('dve', lambda e: e.reciprocal(out=V(7), in_=V(7)), r=[w12], w=[w12])
                S.op('dve', lambda e: e.tensor_scalar(out=V(5), in0=V(5), scalar1=-1.0, scalar2=None, op0=ALU.add), r=[w12], w=[w12])
                tt(V(8), V(5), are, ALU.mult, [w12, prm], [w12])
                tt(V(9), V(6), aim, ALU.mult, [w12, prm], [w12])
                tt(V(8), V(8), V(9), ALU.add, [w12], [w12])
                tt(V(8), V(8), V(7), ALU.mult, [w12], [w12])
                tt(V(9), V(6), are, ALU.mult, [w12, prm], [w12])
                tt(V(10), V(5), aim, ALU.mult, [w12, prm], [w12])
                tt(V(9), V(9), V(10), ALU.subtract, [w12], [w12])
                tt(V(9), V(9), V(7), ALU.mult, [w12], [w12])
                b3 = lambda c0: bri[:, c0:c0 + 192].rearrange("p (s c) -> p s c", s=12)
                o3 = lambda c0: bb[:, c0:c0 + 192].rearrange("p (s c) -> p s c", s=12)
                t3 = tA[:, 0:192].rearrange("p (s c) -> p s c", s=12)
                zr3 = V(8).unsqueeze(2).to_broadcast([128, 12, 16])
                zi3 = V(9).unsqueeze(2).to_broadcast([128, 12, 16])
                tt(o3(0), b3(0), zr3, ALU.mult, [bri, w12], [bb])
                tt(t3, b3(192), zi3, ALU.mult, [bri, w12], [tA])
                tt(o3(0), o3(0), t3, ALU.subtract, [bb, tA], [bb])
                tt(o3(192), b3(192), zr3, ALU.mult, [bri, w12], [bb])
                tt(t3, b3(0), zi3, ALU.mult, [bri, w12], [tA])
                tt(o3(192), o3(192), t3, ALU.add, [bb, tA], [bb])
                for ri in range(2):
                    place(bbX, lambda sc, p0, p1, ri=ri: bb[p0:p1, ri * 192 + sc * 16:ri * 192 + (sc + 1) * 16])
                    for q in range(3):
                        bk = banks[q]
                        for c4 in range(4):
                            sc = q * 4 + c4
                            S.op('pe', lambda e, bk=bk, c4=c4, sc=sc: e.transpose(bk[:, c4 * 128:(c4 + 1) * 128], bbX[:, sc * 128:(sc + 1) * 128], cst[:, C_I:C_I + 128]), r=[bbX, cst], w=[bk])
                        S.op('act', lambda e, bk=bk, q=q, ri=ri, d=d: e.activation(out=BbT[d][ri][:, q * 512:(q + 1) * 512], in_=bk[:, :], func=AF.Copy), r=[bk], w=[BbT[d][ri]])
                Ec, Es = Ecs[d]
                Ec3 = Ec[:, :].rearrange("p (s t) -> p s t", s=12)
                Es3 = Es[:, :].rearrange("p (s t) -> p s t", s=12)
                i0 = 0 if d == 0 else 127
                S.op('dve', lambda e: e.tensor_copy(out=Ec3[:, :, i0:i0 + 1], in_=V(4).unsqueeze(2)), r=[w12], w=[Ec])
                S.op('dve', lambda e: e.tensor_copy(out=Es3[:, :, i0:i0 + 1], in_=V(3).unsqueeze(2)), r=[w12], w=[Es])
                tA3 = tA[:, :].rearrange("p (s t) -> p s t", s=12)
                tB3 = tB[:, :].rearrange("p (s t) -> p s t", s=12)
                for k in range(7):
                    n = 1 << k
                    if d == 0:
                        src = slice(0, n); dst = slice(n, 2 * n); piv = n - 1
                    else:
                        src = slice(128 - n, 128); dst = slice(128 - 2 * n, 128 - n); piv = 128 - n
                    pc = Ec3[:, :, piv:piv + 1].to_broadcast([128, 12, n])
                    ps_ = Es3[:, :, piv:piv + 1].to_broadcast([128, 12, n])
                    tt(tA3[:, :, 0:n], Ec3[:, :, src], pc, ALU.mult, [Ec], [tA])
                    tt(tB3[:, :, 0:n], Es3[:, :, src], ps_, ALU.mult, [Es], [tB])
                    tt(tA3[:, :, 0:n], tA3[:, :, 0:n], tB3[:, :, 0:n], ALU.subtract, [tA, tB], [tA])
                    tt(tB3[:, :, 0:n], Ec3[:, :, src], ps_, ALU.mult, [Ec, Es], [tB])
                    tt(tA3[:, :, 64:64 + n], Es3[:, :, src], pc, ALU.mult, [Ec, Es], [tA])
                    tt(Es3[:, :, dst], tB3[:, :, 0:n], tA3[:, :, 64:64 + n], ALU.add, [tA, tB], [Es])
                    S.op('dve', lambda e, dst=dst, n=n: e.tensor_copy(out=Ec3[:, :, dst], in_=tA3[:, :, 0:n]), r=[tA], w=[Ec])
                S.op('dve', lambda e, d=d: e.tensor_copy(out=rmag[d][:, :].rearrange("p (s t) -> p s t", s=12), in_=V(1).unsqueeze(2).to_broadcast([128, 12, 128])), r=[w12], w=[rmag[d]])

            def mk(shape, dt, name):
                return [ph.tile(shape, dt, name) for _ in range(2)]
            ut = mk([128, 384], F32, "ut")
            uT = mk([128, 384], F32, "uT")
            wre = mk([128, 1536], F32, "wre")
            wim = mk([128, 1536], F32, "wim")
            zre = mk([128, 1536], F32, "zre")
            zim = mk([128, 1536], F32, "zim")
            tmpA = [bbX, ph.tile([128, 1536], F32, "tmpA")]
            tmpB = [tA, ph.tile([128, 1536], F32, "tmpB")]
            tmpP = [tB, ph.tile([128, 1536], F32, "tmpP")]
            xst = mk([128, 24], F32, "xst")
            ysb = mk([128, 384], F32, "ysb")
            for d in range(2):
                S.op('dve', lambda e, d=d: e.memset(xst[d][:, :], 0.0), w=[xst[d]])

            def rev(buf, c0, n):
                a = buf[:, c0:c0 + n]
                return bass.AP(a.tensor, a.offset + (n - 1), [[a.ap[0][0], 128], [-1, n]])

            def step(m, d):
                bk4 = banks[4 * d:4 * d + 4]
                t0 = m * 128
                Ec, Es = Ecs[d]
                tA_, tB_, tP_ = tmpA[d], tmpB[d], tmpP[d]
                S.dma('sp', ut[d][:, :], utok[t0:t0 + 128, :], w=[ut[d]])
                b6 = bk4[3]
                for cc in range(3):
                    S.op('pe', lambda e, cc=cc: e.transpose(b6[:, cc * 128:(cc + 1) * 128], ut[d][:, cc * 128:(cc + 1) * 128], cst[:, C_I:C_I + 128]), r=[ut[d], cst], w=[b6])
                yield
                S.op('act', lambda e: e.activation(out=uT[d][:, :], in_=b6[:, 0:384], func=AF.Copy), r=[b6], w=[uT[d]])
                for ri in range(2):
                    for sc in range(12):
                        bk = bk4[sc // 4]
                        cc = sc // 4
                        S.op('pe', lambda e, bk=bk, sc=sc, cc=cc, ri=ri: e.matmul(bk[:, (sc % 4) * 128:(sc % 4 + 1) * 128], lhsT=BbT[d][ri][:, sc * 128:(sc + 1) * 128], rhs=uT[d][:, cc * 128:(cc + 1) * 128], start=True, stop=True), r=[BbT[d][ri], uT[d]], w=[bk])
                    yield
                    for q in range(3):
                        sl = slice(q * 512, (q + 1) * 512)
                        bq = bk4[q]
                        if ri == 0:
                            S.op('dve', lambda e, sl=sl, bq=bq: e.tensor_tensor(out=wre[d][:, sl], in0=bq[:, :], in1=Ec[:, sl], op=ALU.mult), r=[bq, Ec], w=[wre[d]])
                            S.op('dve', lambda e, sl=sl, bq=bq: e.tensor_tensor(out=wim[d][:, sl], in0=bq[:, :], in1=Es[:, sl], op=ALU.mult), r=[bq, Es], w=[wim[d]])
                        else:
                            S.op('dve', lambda e, sl=sl, bq=bq: e.tensor_tensor(out=tA_[:, sl], in0=bq[:, :], in1=Es[:, sl], op=ALU.mult), r=[bq, Es], w=[tA_])
                            S.op('dve', lambda e, sl=sl, bq=bq: e.tensor_tensor(out=tB_[:, sl], in0=bq[:, :], in1=Ec[:, sl], op=ALU.mult), r=[bq, Ec], w=[tB_])
                    yield
                S.op('pool', lambda e: e.tensor_tensor(out=wre[d][:, :], in0=wre[d][:, :], in1=tA_[:, :], op=ALU.add), r=[wre[d], tA_], w=[wre[d]])
                S.op('dve', lambda e: e.tensor_tensor(out=wim[d][:, :], in0=tB_[:, :], in1=wim[d][:, :], op=ALU.subtract), r=[wim[d], tB_], w=[wim[d]])
                yield
                for (wb, zb, c0_) in ((wre[d], zre[d], 0), (wim[d], zim[d], 12)):
                    for sc in range(12):
                        ci = c0_ + sc
                        if d == 0:
                            S.op('dve', lambda e, wb=wb, zb=zb, ci=ci, sc=sc: e.tensor_tensor_scan(out=zb[:, sc * 128:(sc + 1) * 128], data0=rmag[d][:, sc * 128:(sc + 1) * 128], data1=wb[:, sc * 128:(sc + 1) * 128], initial=xst[d][:, ci:ci + 1], op0=ALU.mult, op1=ALU.add), r=[wb, rmag[d], xst[d]], w=[zb])
                        else:
                            S.op('dve', lambda e, wb=wb, zb=zb, ci=ci, sc=sc: e.tensor_tensor_scan(out=rev(zb, sc * 128, 128), data0=rmag[d][:, sc * 128:(sc + 1) * 128], data1=rev(wb, sc * 128, 128), initial=xst[d][:, ci:ci + 1], op0=ALU.mult, op1=ALU.add), r=[wb, rmag[d], xst[d]], w=[zb])
                    yield
                S.op('dve', lambda e: e.tensor_tensor(out=tA_[:, :], in0=zre[d][:, :], in1=Ec[:, :], op=ALU.mult), r=[zre[d], Ec], w=[tA_])
                S.op('dve', lambda e: e.tensor_tensor(out=tB_[:, :], in0=zim[d][:, :], in1=Es[:, :], op=ALU.mult), r=[zim[d], Es], w=[tB_])
                S.op('pool', lambda e: e.tensor_tensor(out=wim[d][:, :], in0=zre[d][:, :], in1=Es[:, :], op=ALU.mult), r=[zre[d], Es], w=[wim[d]])
                S.op('pool', lambda e: e.tensor_tensor(out=tP_[:, :], in0=zim[d][:, :], in1=Ec[:, :], op=ALU.mult), r=[zim[d], Ec], w=[tP_])
                yield
                S.op('dve', lambda e: e.tensor_tensor(out=wre[d][:, :], in0=tA_[:, :], in1=tB_[:, :], op=ALU.subtract), r=[tA_, tB_], w=[wre[d]])
                S.op('pool', lambda e: e.tensor_tensor(out=wim[d][:, :], in0=wim[d][:, :], in1=tP_[:, :], op=ALU.add), r=[wim[d], tP_], w=[wim[d]])
                yield
                last = 127 if d == 0 else 0
                S.op('dve', lambda e: e.tensor_copy(out=xst[d][:, 0:12].unsqueeze(2), in_=wre[d][:, :].rearrange("p (s t) -> p s t", s=12)[:, :, last:last + 1]), r=[wre[d]], w=[xst[d]])
                S.op('dve', lambda e: e.tensor_copy(out=xst[d][:, 12:24].unsqueeze(2), in_=wim[d][:, :].rearrange("p (s t) -> p s t", s=12)[:, :, last:last + 1]), r=[wim[d]], w=[xst[d]])
                b7 = bk4[3]
                for cc in range(3):
                    for k4 in range(4):
                        sc = cc * 4 + k4
                        S.op('pe', lambda e, cc=cc, sc=sc, k4=k4: e.matmul(b7[:, cc * 128:(cc + 1) * 128], lhsT=CX[0][:, sc * 128:(sc + 1) * 128], rhs=wre[d][:, sc * 128:(sc + 1) * 128], start=(k4 == 0), stop=False), r=[CX[0], wre[d]], w=[b7])
                        S.op('pe', lambda e, cc=cc, sc=sc, k4=k4: e.matmul(b7[:, cc * 128:(cc + 1) * 128], lhsT=CX[1][:, sc * 128:(sc + 1) * 128], rhs=wim[d][:, sc * 128:(sc + 1) * 128], start=False, stop=(k4 == 3)), r=[CX[1], wim[d]], w=[b7])
                yield
                if d == 0:
                    for cc in range(3):
                        S.op('dve', lambda e, cc=cc: e.scalar_tensor_tensor(out=ysb[d][:, cc * 128:(cc + 1) * 128], in0=uT[d][:, cc * 128:(cc + 1) * 128], scalar=dsk[:, cc:cc + 1], in1=b7[:, cc * 128:(cc + 1) * 128], op0=ALU.mult, op1=ALU.add), r=[uT[d], dsk, b7], w=[ysb[d]])
                else:
                    S.op('act', lambda e: e.activation(out=ysb[d][:, :], in_=b7[:, 0:384], func=AF.Copy), r=[b7], w=[ysb[d]])
                S.dma('pool', yT5[d][:, t0:t0 + 128].rearrange("(c p) t -> p c t", p=128), ysb[d][:, :].rearrange("p (c t) -> p c t", c=3), r=[ysb[d]])

            order_f = list(range(NT))
            order_b = list(range(NCT - 1, -1, -1)) + list(range(NT - 1, NCT - 1, -1))
            for i in range(NT):
                run_pair(step(order_f[i], 0), step(order_b[i], 1))
            ph.close()

        def phaseD1a():
            ph = Phase()
            cst = load_consts(ph)
            Wo = ph.tile([128, 11 * D], BF16, "Wo1")
            load_weight_bf16(ph, Wo, w_out1, 1408, D)
            hnw = bc_load(ph, hgnw[0:1, :], 128, "hnw")
            g1 = bc_load(ph, modv[1, 0:1, 2 * D:3 * D], D, "g1")
            gw = ph.tile([128, 3 * 384], F32, "gw")
            S.dma('sp', gw[:, :].rearrange("p (c n) -> p c n", c=3), gluw[:, :].rearrange("(c p) n -> p c n", p=128), w=[gw])
            gb = ph.tile([128, 3], F32, "gb")
            S.dma('sp', gb[:, :], glub[:, :], w=[gb])
            epsb = ph.tile([128, 1], F32, "epsb")
            S.op('dve', lambda e: e.memset(epsb[:, :], EPS), w=[epsb])
            idb = ph.tile([128, 128], BF16, "idb")
            S.op('dve', lambda e: e.tensor_copy(out=idb[:, :], in_=cst[:, C_I:C_I + 128]), r=[cst], w=[idb])
            a = ph.tile([128, D], F32, "a")
            b = ph.tile([128, D], F32, "b")
            g = ph.tile([128, D], F32, "g")
            sq = ph.tile([128, D], F32, "sq")
            tmp = ph.tile([128, D], F32, "tmp")
            xt = ph.tile([128, D], F32, "xt")
            mix = ph.tile([128, 1024], BF16, "mix")
            mixT = ph.tile([128, 11 * 128], BF16, "mixT")
            ss = ph.tile([128, 16], F32, "ss")
            rstd = ph.tile([128, 16], F32, "rstd")
            xo = ph.tile([128, D], F32, "xo")
            ya = ph.tile([128, 384], F32, "ya")
            yb = ph.tile([128, 384], F32, "yb")
            yc = ph.tile([128, 384], F32, "yc")
            for m in lat_tiles:
                r0 = m * 128
                S.dma('sp', a[:, :], odir[0][r0:r0 + 128, :], w=[a])
                S.dma('act', b[:, :], odir[1][r0:r0 + 128, :], w=[b])
                S.dma('sp', g[:, :], rs[r0:r0 + 128, :], w=[g])
                S.op('dve', lambda e: e.tensor_tensor(out=a[:, :], in0=a[:, :], in1=b[:, :], op=ALU.add), r=[a, b], w=[a])
                S.op('act', lambda e: e.activation(out=g[:, :], in_=g[:, :], func=AF.Silu), r=[g], w=[g])
                head_rms(ph, a, mix, mix[:, 0:1024], 8, 128, hnw, hnw[:, 0:128].unsqueeze(1).to_broadcast([128, 8, 128]), g, sq, ss, rstd, epsb, tmp)
                for q in range(2):
                    bk = banks[q]
                    bkb = bk[:, 0:256].bitcast(BF16)
                    for c in range(4):
                        cc = q * 4 + c
                        S.op('pe', lambda e, bkb=bkb, c=c, cc=cc: e.transpose(bkb[:, c * 128:(c + 1) * 128], mix[:, cc * 128:(cc + 1) * 128], idb[:, :]), r=[mix, idb], w=[bk])
                    S.op('act', lambda e, bkb=bkb, q=q: e.activation(out=mixT[:, q * 512:(q + 1) * 512], in_=bkb[:, 0:512], func=AF.Copy), r=[bk], w=[mixT])
                S.dma('sp', ya[:, :].rearrange("p (c t) -> p c t", c=3), yT5[0][:, r0:r0 + 128].rearrange("(c p) t -> p c t", p=128), w=[ya])
                S.dma('act', yb[:, :].rearrange("p (c t) -> p c t", c=3), yT5[1][:, r0:r0 + 128].rearrange("(c p) t -> p c t", p=128), w=[yb])
                S.op('dve', lambda e: e.tensor_tensor(out=ya[:, :], in0=ya[:, :], in1=yb[:, :], op=ALU.add), r=[ya, yb], w=[ya])
                S.op('dve', lambda e: e.tensor_tensor(out=yb[:, :], in0=ya[:, :], in1=ya[:, :], op=ALU.mult), r=[ya], w=[yb])
                S.op('dve', lambda e: e.tensor_scalar(out=yb[:, :], in0=yb[:, :], scalar1=0.044715, scalar2=1.0, op0=ALU.mult, op1=ALU.add), r=[yb], w=[yb])
                S.op('dve', lambda e: e.tensor_tensor(out=yb[:, :], in0=yb[:, :], in1=ya[:, :], op=ALU.mult), r=[yb, ya], w=[yb])
                S.op('act', lambda e: e.activation(out=yb[:, :], in_=yb[:, :], func=AF.Tanh, scale=0.7978845608028654), r=[yb], w=[yb])
                S.op('dve', lambda e: e.scalar_tensor_tensor(out=yb[:, :], in0=yb[:, :], scalar=1.0, in1=ya[:, :], op0=ALU.add, op1=ALU.mult), r=[yb, ya], w=[yb])
                S.op('dve', lambda e: e.tensor_scalar(out=yb[:, :], in0=yb[:, :], scalar1=0.5, scalar2=None, op0=ALU.mult), r=[yb], w=[yb])
                b2 = banks[2]
                for co in range(3):
                    for ci in range(3):
                        S.op('pe', lambda e, co=co, ci=ci: e.matmul(b2[:, co * 128:(co + 1) * 128], lhsT=gw[:, ci * 384 + co * 128:ci * 384 + (co + 1) * 128], rhs=yb[:, ci * 128:(ci + 1) * 128], start=(ci == 0), stop=(ci == 2)), r=[gw, yb], w=[b2])
                for co in range(3):
                    S.op('act', lambda e, co=co: e.activation(out=yc[:, co * 128:(co + 1) * 128], in_=b2[:, co * 128:(co + 1) * 128], func=AF.Sigmoid, bias=gb[:, co:co + 1]), r=[b2, gb], w=[yc])
                S.op('dve', lambda e: e.tensor_tensor(out=mixT[:, 1024:1408], in0=yb[:, :], in1=yc[:, :], op=ALU.mult), r=[yb, yc], w=[mixT])
                S.dma('sp', xt[:, :], xsrc(1, m), w=[xt])
                for h2 in range(2):
                    bk = banks[4 + h2]
                    for c in range(11):
                        S.op('pe', lambda e, bk=bk, c=c, h2=h2: e.matmul(bk[:, 0:512], lhsT=mixT[:, c * 128:(c + 1) * 128], rhs=Wo[:, c * D + h2 * 512:c * D + (h2 + 1) * 512], start=(c == 0), stop=(c == 10)), r=[mixT, Wo], w=[bk])
                    S.op('dve', lambda e, bk=bk, h2=h2: e.tensor_tensor(out=xo[:, h2 * 512:(h2 + 1) * 512], in0=bk[:, 0:512], in1=g1[:, h2 * 512:(h2 + 1) * 512], op=ALU.mult), r=[bk, g1], w=[xo])
                S.op('dve', lambda e: e.tensor_tensor(out=xo[:, :], in0=xo[:, :], in1=xt[:, :], op=ALU.add), r=[xo, xt], w=[xo])
                S.dma('pool', xmid[r0:r0 + 128, :], xo[:, :], r=[xo])
            ph.close()

        lat_tiles = list(range(NCT, NT))
        all_tiles = list(range(NT))
        plist = ['A0', 'B0', 'C0', 'D0a', 'MLP0', 'A1', 'B1', 'C1', 'D1a', 'MLP1']
        nph = len(plist) if upto is None else plist.index(upto) + 1 if upto in plist else 0
        if nph >= 1:
            phaseA0()
        if nph >= 2:
            phaseB0()
        if nph >= 3 and os.environ.get('KGLA', '1') == '1':
            gla_phase(8, 64, 128, qT, [kTs[0], kTs[0]], [ktoks[0], ktoks[0]], vtok, lgs, odir)
        if nph >= 4:
            phaseD0a()
        if nph < 5:
            pass
        elif depth == 1:
            phaseMLP(0, lat_tiles, True)
        else:
            phaseMLP(0, all_tiles, False)
            if nph >= 6:
                phaseA1()
            if nph >= 7:
                gla_phase(8, 128, 128, qT, kTs, ktoks, vtok, lgs, odir)
            if nph >= 8:
                phaseC1()
            if nph >= 9:
                phaseD1a()
            if nph >= 10:
                phaseMLP(1, lat_tiles, True)
        S.barrier()
    return nc


def make_consts():
    i = np.arange(128)
    t = i[:, None]
    j = i[None, :]
    eye = (t == j)
    TF = (t <= j)
    TB = (t >= j)
    SF = (t > j)
    SB = (t < j)
    MBF = np.where(j >= t, 0.0, NEG)
    MBB = np.where(j <= t, 0.0, NEG)
    ONE = np.ones((128, 128))
    w = np.arange(258)
    ML = np.broadcast_to((w % 64 != 0).astype(np.float32)[None, :], (128, 258))
    MR = np.broadcast_to((w % 64 != 1).astype(np.float32)[None, :], (128, 258))
    return np.concatenate([eye, TF, TB, SF, SB, MBF, MBB, ONE, ML, MR], axis=1).astype(np.float32)


def host_inputs(inp, b, depth=2):
    f = lambda a: np.ascontiguousarray(np.asarray(a, dtype=np.float32))
    m = {}
    m["x"] = f(inp["x"][b])
    m["ctx"] = f(inp["ctx"][b])
    cs = np.stack([np.asarray(inp["c"][b]).reshape(8, 128).T, np.asarray(inp["c_ctx"]).reshape(8, 128).T], axis=-1)
    m["cs"] = f(cs)
    m["ada_w"] = f(inp["ada_w"])
    m["ada_b"] = f(inp["ada_b"])
    m["norm1_w"] = f(inp["norm1_w"])
    m["norm2_w"] = f(inp["norm2_w"])
    m["final_norm_w"] = f(np.asarray(inp["final_norm_w"]).reshape(1, D))
    m["consts"] = make_consts()
    m["w_in0"] = f(inp["ssd_gla_w_in"][0])
    cw = np.asarray(inp["ssd_conv_w"][0]).reshape(9, 16, 128)
    m["convw"] = f(cw.transpose(2, 1, 0).reshape(128, 144))
    m["convb"] = f(np.asarray(inp["ssd_conv_b"][0]).reshape(16, 128).T)
    m["dtb"] = f(np.asarray(inp["ssd_dt_bias"][0]).reshape(1, 32))
    m["alog"] = f(np.asarray(inp["ssd_a_log"][0]).reshape(1, 32))
    m["ssdd"] = f(np.asarray(inp["ssd_d"][0]).reshape(1, 16))
    m["ssdnw"] = f(np.asarray(inp["ssd_norm_w"][0]).reshape(1, 1024))
    wgm = np.zeros((33, 1024), np.float32)
    gw = np.asarray(inp["gla_gate_w"][0])
    wgm[0:16, 0:512] = gw[0]
    wgm[16:32, 512:1024] = gw[1]
    wgm[32, :] = np.asarray(inp["gla_gate_b"][0]).reshape(1024)
    m["wg"] = wgm
    m["glanw"] = f(np.asarray(inp["gla_norm_w"][0]).reshape(1, 128))
    m["w_out0"] = f(inp["ssd_gla_w_out"][0])
    m["mlp_w1"] = f(inp["mlp_w1"])
    m["mlp_w2"] = f(inp["mlp_w2"])
    if depth > 1:
        m["w_in1"] = f(inp["hgrn_s5_w_in"][0])
        lb = np.asarray(inp["hgrn_lb_logits"], dtype=np.float32)
        m["lbl"] = f(lb.reshape(2, 2048))
        m["lblT"] = f(np.stack([lb[l].reshape(16, 128).T for l in range(2)], axis=1))
        m["hgnw"] = f(np.asarray(inp["hgrn_norm_w"][0]).reshape(1, 128))
        prm = []
        for d in range(2):
            prm.append(np.asarray(inp["s5_a_re"][0][d]).reshape(12, 128).T)
            prm.append(np.asarray(inp["s5_a_im"][0][d]).reshape(12, 128).T)
            prm.append(np.repeat(np.asarray(inp["s5_log_dt"][0][d]).reshape(12, 2, 1), 64, axis=2).reshape(12, 128).T)
        m["s5p"] = f(np.concatenate(prm, axis=1))
        sb = lambda a: np.asarray(a).reshape(12, 128, 16).transpose(1, 0, 2).reshape(128, 192)
        m["s5b"] = f(np.concatenate([sb(inp["s5_b_re"][0]), sb(inp["s5_b_im"][0])], axis=1))
        scf = lambda a: np.asarray(a).reshape(12, 2, 16, 64).transpose(1, 3, 0, 2).reshape(128, 192)
        m["s5c"] = f(np.concatenate([scf(inp["s5_c_re"][0]), scf(inp["s5_c_im"][0])], axis=1))
        m["s5d"] = f(np.asarray(inp["s5_d"][0]).reshape(3, 128).T)
        m["gluw"] = f(inp["s5_glu_w"][0])
        m["glub"] = f(np.asarray(inp["s5_glu_b"][0]).reshape(3, 128).T)
        m["w_out1"] = f(inp["hgrn_s5_w_out"][0])
    return m


_NC_CACHE = {}


def kernel(**inputs):
    B, SEQ = inputs["x"].shape[0], inputs["x"].shape[1]
    key = (SEQ, 2)
    if key not in _NC_CACHE:
        _NC_CACHE[key] = build_nc(SEQ, 2)
    nc = _NC_CACHE[key]
    in_maps = [host_inputs(inputs, b) for b in range(B)]
    res = run_bass_kernel_spmd(nc, in_maps, core_ids=list(range(B)))
    return np.stack([r["out"] for r in res.results], axis=0).astype(np.float32)
```

```python
import os
import numpy as np
from contextlib import ExitStack
KDBG = int(os.environ.get('KDBG', '9'))
import concourse.bass as bass
import concourse.mybir as mybir
from concourse.bass_utils import run_bass_kernel_spmd

F32 = mybir.dt.float32
BF16 = mybir.dt.bfloat16
AF = mybir.ActivationFunctionType
ALU = mybir.AluOpType
AX = mybir.AxisListType

D = 1024
CTX = 256
EPS = 1e-6
NEG = -30000.0


class Buf:
    def __init__(self, t, name):
        self.t = t
        self.name = name
        self.w = None
        self.r = {}

    def __getitem__(self, k):
        return self.t[k]


class Sch:
    def __init__(self, nc, es):
        self.nc = nc
        self.E = {'pe': nc.tensor, 'act': nc.scalar, 'dve': nc.vector, 'pool': nc.gpsimd, 'sp': nc.sync}
        self.semobj = {}
        self.cnt = {}
        self.seen = {}
        for e in self.E:
            self.semobj[e] = es.enter_context(nc.semaphore("s_" + e))
            self.cnt[e] = 0
            self.seen[e] = {}
        self.rings = {}
        for q in ('sp', 'pool', 'act'):
            n = 12
            keys = []
            for i in range(n):
                k = ('d', q, i)
                self.semobj[k] = es.enter_context(nc.semaphore("d_%s_%d" % (q, i)))
                keys.append(k)
            self.rings[q] = {'keys': keys, 'vals': [0] * n, 'i': 0}
        self.nins = 0
        for e in self.E:
            self.E[e].sem_clear(self.semobj[e])
        for q, ring in self.rings.items():
            for k in ring['keys']:
                self.E[q].sem_clear(self.semobj[k])
        nc.all_engine_barrier()

    def _wait(self, e, tok):
        key, val = tok
        if self.seen[e].get(key, 0) >= val:
            return
        self.E[e].wait_ge(self.semobj[key], val)
        self.seen[e][key] = val

    def _deps(self, e, r, w, is_dma):
        for b in r:
            if b.w is not None:
                self._wait(e, b.w)
        for b in w:
            if b.w is not None and (is_dma or b.w[0] != e):
                self._wait(e, b.w)
            for key, val in b.r.items():
                if is_dma or key != e:
                    self._wait(e, (key, val))

    def _upd(self, tok, r, w):
        for b in r:
            b.r[tok[0]] = tok[1]
        for b in w:
            b.w = tok
            b.r = {}

    def op(self, e, fn, r=(), w=()):
        self._deps(e, r, w, False)
        ins = fn(self.E[e])
        self.cnt[e] += 1
        ins.then_inc(self.semobj[e], 1)
        self._upd((e, self.cnt[e]), r, w)
        self.nins += 1

    def dma(self, q, out, in_, r=(), w=()):
        ring = self.rings[q]
        i = ring['i']
        ring['i'] = (i + 1) % len(ring['keys'])
        key = ring['keys'][i]
        if ring['vals'][i] > 0:
            self._wait(q, (key, ring['vals'][i]))
        self._deps(q, r, w, True)
        ins = self.E[q].dma_start(out=out, in_=in_)
        ring['vals'][i] += 16
        ins.then_inc(self.semobj[key], 16)
        self._upd((key, ring['vals'][i]), r, w)
        self.nins += 1

    def barrier(self):
        toks = [(e, self.cnt[e]) for e in self.E if self.cnt[e] > 0]
        for q, ring in self.rings.items():
            for k, v in zip(ring['keys'], ring['vals']):
                if v > 0:
                    toks.append((k, v))
        for e in self.E:
            for tok in toks:
                if tok[0] != e or True:
                    if tok[0] == e:
                        continue
                    self._wait(e, tok)
        for e in self.E:
            if self.cnt[e] > 0:
                self._wait(e, (e, self.cnt[e]))


def build_nc(SEQ, depth=2, dbg=(), upto=None):
    T = CTX + SEQ
    NT = T // 128
    NCT = CTX // 128
    nc = bass.Bass("TRN2", target_bir_lowering=False)

    def din(name, shape, dt=F32):
        return nc.dram_tensor(name, list(shape), dt, kind="ExternalInput").ap()

    def dscr(name, shape, dt=F32):
        kind = "ExternalOutput" if name in dbg else "Internal"
        return nc.dram_tensor(name, list(shape), dt, kind=kind).ap()

    x_in = din("x", [SEQ, D])
    ctx_in = din("ctx", [CTX, D])
    cs_in = din("cs", [128, 8, 2])
    ada_w = din("ada_w", [2, D, 6 * D])
    ada_b = din("ada_b", [2, 6 * D])
    norm1_w = din("norm1_w", [2, D])
    norm2_w = din("norm2_w", [2, D])
    final_w = din("final_norm_w", [1, D])
    consts = din("consts", [128, 128 * 8 + 516])
    w_in0 = din("w_in0", [D, 6208])
    convw = din("convw", [128, 16 * 9])
    convb = din("convb", [128, 16])
    dtb = din("dtb", [1, 32])
    alog = din("alog", [1, 32])
    ssdd = din("ssdd", [1, 16])
    ssdnw = din("ssdnw", [1, 1024])
    wg = din("wg", [33, 1024])
    glanw = din("glanw", [1, 128])
    w_out0 = din("w_out0", [2048, D])
    mlp_w1 = din("mlp_w1", [2, D, 4096])
    mlp_w2 = din("mlp_w2", [2, 4096, D])
    if depth > 1:
        w_in1 = din("w_in1", [D, 5504])
        lbl = din("lbl", [2, 2048])
        lblT = din("lblT", [128, 2, 16])
        hgnw = din("hgnw", [1, 128])
        s5p = din("s5p", [128, 2 * 3 * 12])
        s5b = din("s5b", [128, 2 * 12 * 16])
        s5c = din("s5c", [128, 2 * 12 * 16])
        s5d = din("s5d", [128, 3])
        gluw = din("gluw", [384, 384])
        glub = din("glub", [128, 3])
        w_out1 = din("w_out1", [1408, D])
    out = nc.dram_tensor("out", [SEQ, D], F32, kind="ExternalOutput").ap()

    modv = dscr("modv", [2, 2, 6 * D])
    xres = dscr("xres", [T, D])
    xmid = dscr("xmid", [T, D])
    zs = dscr("zs", [T, D])
    rs = dscr("rs", [T, D])
    vtok = dscr("vtok", [T, D], BF16)
    TP = T + 256
    xbcT = dscr("xbcT", [2048, TP], BF16)
    dts = dscr("dts", [T, 32])
    qT = dscr("qT", [1024, T])
    kTs = [dscr("kT0", [1024, T]), dscr("kT1", [1024, T])]
    ktoks = [dscr("ktok0", [T, 1024]), dscr("ktok1", [T, 1024])]
    lgs = [dscr("lg0", [T, 1024]), dscr("lg1", [T, 1024])]
    ydir = [dscr("ydir0", [T, D]), dscr("ydir1", [T, D])]
    odir = [dscr("odir0", [T, D]), dscr("odir1", [T, D])]
    utok = dscr("utok", [T, 384])
    yT5 = [dscr("yT5_0", [384, T]), dscr("yT5_1", [384, T])]

    es = ExitStack()
    with es:
        S = Sch(nc, es)
        banks = [Buf(es.enter_context(nc.psum_tensor("bank%d" % i, [128, 512], F32)), "bank%d" % i) for i in range(8)]

        class Phase:
            def __init__(self):
                self.es = ExitStack()
                self.n = 0

            def tile(self, shape, dt=F32, name=None):
                self.n += 1
                S.uid = getattr(S, "uid", 0) + 1
                nm = "%s_%d" % (name or "t", S.uid)
                return Buf(self.es.enter_context(nc.sbuf_tensor(nm, list(shape), dt)), nm)

            def close(self):
                S.barrier()
                self.es.close()

        def run_seq(g0, g1):
            for _ in g0:
                pass
            for _ in g1:
                pass

        def run_pair(g0, g1):
            gens = [g0, g1]
            alive = [True, True]
            while alive[0] or alive[1]:
                for i in range(2):
                    if alive[i]:
                        try:
                            next(gens[i])
                        except StopIteration:
                            alive[i] = False

        def load_consts(ph):
            cst = ph.tile([128, 128 * 8 + 516], F32, "cst")
            S.dma('sp', cst[:, :], consts[:, :], w=[cst])
            return cst

        C_I, C_TF, C_TB, C_SF, C_SB, C_MBF, C_MBB, C_ONE = [i * 128 for i in range(8)]
        C_ML = 1024
        C_MR = 1024 + 258

        def xsrc(layer, m):
            if layer == 0:
                if m < NCT:
                    return ctx_in[m * 128:(m + 1) * 128, :]
                return x_in[(m - NCT) * 128:(m - NCT + 1) * 128, :]
            return xres[m * 128:(m + 1) * 128, :]

        def bc_load(ph, src_row_ap, n, name):
            t = ph.tile([128, n], F32, name)
            S.dma('sp', t[:, :], src_row_ap.partition_broadcast(128), w=[t])
            return t

        def mod_consts(ph, layer, normw, k_shift, k_scale, src):
            A = bc_load(ph, modv[layer, src:src + 1, k_scale * D:(k_scale + 1) * D], D, "A")
            SH = bc_load(ph, modv[layer, src:src + 1, k_shift * D:(k_shift + 1) * D], D, "SH")
            NW = bc_load(ph, normw[layer:layer + 1, :], D, "NW")
            S.op('dve', lambda e: e.scalar_tensor_tensor(out=A[:, :], in0=A[:, :], scalar=1.0, in1=NW[:, :], op0=ALU.add, op1=ALU.mult), r=[A, NW], w=[A])
            return A, SH

        def load_weight_bf16(ph, Wb, wsrc, K, N, nblk=512):
            J = K // 128
            stg = [ph.tile([128, 2 * 512], F32, "wstg") for _ in range(3)]
            cnt = 0
            for n0 in range(0, N, nblk):
                n1 = min(N, n0 + nblk)
                nn = n1 - n0
                for j0 in range(0, J, 2):
                    j1 = min(J, j0 + 2)
                    st = stg[cnt % 3]
                    S.dma('sp' if cnt % 2 == 0 else 'act', st[:, 0:(j1 - j0) * nn].rearrange("p (j n) -> p j n", n=nn),
                          wsrc[j0 * 128:j1 * 128, n0:n1].rearrange("(j p) n -> p j n", p=128), w=[st])
                    eng = 'dve'
                    for j in range(j0, j1):
                        S.op(eng, lambda e, j=j, st=st: e.tensor_copy(out=Wb[:, j * N + n0:j * N + n1], in_=st[:, (j - j0) * nn:(j - j0 + 1) * nn]), r=[st], w=[Wb])
                    cnt += 1

        def rms_rstd(ph, xt, junk, ss, rstd, epsb, n, dscale):
            S.op('dve', lambda e: e.memset(ss[:, 0:1], 0.0), w=[ss])
            S.op('act', lambda e: e.activation(out=junk[:, 0:n], in_=xt[:, 0:n], func=AF.Square, accum_out=ss[:, 0:1]), r=[ss, xt], w=[junk, ss])
            S.op('act', lambda e: e.activation(out=rstd[:, 0:1], in_=ss[:, 0:1], func=AF.Sqrt, scale=dscale, bias=epsb[:, 0:1]), r=[ss, epsb], w=[rstd])
            S.op('dve', lambda e: e.reciprocal(out=rstd[:, 0:1], in_=rstd[:, 0:1]), r=[rstd], w=[rstd])

        ph = Phase()
        cs = ph.tile([128, 16], F32, "cs")
        S.dma('sp', cs[:, :], cs_in[:, :, :].rearrange("p j s -> p (j s)"), w=[cs])
        S.op('act', lambda e: e.activation(out=cs[:, :], in_=cs[:, :], func=AF.Silu), r=[cs], w=[cs])
        for layer in range(depth):
            adab = ph.tile([2, 6 * D], F32, "adab")
            S.dma('sp', adab[:, :], ada_b[layer:layer + 1, :].partition_broadcast(2), w=[adab])
            mv = ph.tile([2, 6 * D], F32, "mv")
            wst = [ph.tile([128, 8 * 512], F32, "adaw") for _ in range(2)]
            for nb in range(12):
                st = wst[nb % 2]
                S.dma('sp' if nb % 2 == 0 else 'act', st[:, :].rearrange("p (j n) -> p j n", n=512),
                      ada_w[layer, :, nb * 512:(nb + 1) * 512].rearrange("(j p) n -> p j n", p=128), w=[st])
                bk = banks[nb % 8]
                for j in range(8):
                    S.op('pe', lambda e, j=j, st=st, bk=bk: e.matmul(bk[0:2, 0:512], lhsT=cs[:, 2 * j:2 * j + 2], rhs=st[:, j * 512:(j + 1) * 512], start=(j == 0), stop=(j == 7)), r=[cs, st], w=[bk])
                S.op('dve', lambda e, bk=bk, nb=nb: e.tensor_tensor(out=mv[0:2, nb * 512:(nb + 1) * 512], in0=bk[0:2, 0:512], in1=adab[0:2, nb * 512:(nb + 1) * 512], op=ALU.add), r=[bk, adab], w=[mv])
            S.dma('pool', modv[layer, :, :], mv[0:2, :], r=[mv])
        ph.close()

        def proj_phase(layer, w_src, NIN, tok_groups, feat_groups, extra=None, tiles=None):
            ph = Phase()
            cst = load_consts(ph)
            NW = bc_load(ph, norm1_w[layer:layer + 1, :], D, "NW")
            Am = ph.tile([128, D], F32, "Am")
            SHm = ph.tile([128, D], F32, "SHm")
            cur = [None]

            def set_mod(src):
                if cur[0] == src:
                    return
                cur[0] = src
                S.dma('sp', Am[:, :], modv[layer, src:src + 1, D:2 * D].partition_broadcast(128), w=[Am])
                S.dma('sp', SHm[:, :], modv[layer, src:src + 1, 0:D].partition_broadcast(128), w=[SHm])
                S.op('dve', lambda e: e.scalar_tensor_tensor(out=Am[:, :], in0=Am[:, :], scalar=1.0, in1=NW[:, :], op0=ALU.add, op1=ALU.mult), r=[Am, NW], w=[Am])
            Wb = ph.tile([128, 8 * NIN], BF16, "Wb")
            load_weight_bf16(ph, Wb, w_src, D, NIN)
            epsb = ph.tile([128, 1], F32, "epsb")
            S.op('dve', lambda e: e.memset(epsb[:, :], EPS), w=[epsb])
            idb = ph.tile([128, 128], BF16, "idb")
            S.op('dve', lambda e: e.tensor_copy(out=idb[:, :], in_=cst[:, C_I:C_I + 128]), r=[cst], w=[idb])
            xts = [ph.tile([128, D], F32, "xt") for _ in range(2)]
            junk = ph.tile([128, D], BF16, "junk")
            tmp = ph.tile([128, D], F32, "tmp")
            hb = ph.tile([128, D], BF16, "hb")
            hTs = [ph.tile([128, D], BF16, "hT") for _ in range(2)]
            ss = ph.tile([128, 1], F32, "ss")
            rstd = ph.tile([128, 1], F32, "rstd")
            stg32 = [ph.tile([128, 512], F32, "stg32") for _ in range(4)]
            stg16 = [ph.tile([128, 512], BF16, "stg16") for _ in range(3)]
            ctxp = dict(ph=ph, cst=cst, Wb=Wb, NIN=NIN, stg32=stg32, stg16=stg16, k32=[0], k16=[0])
            if extra is not None:
                extra(ctxp)
            bi = [0]

            def nextbank():
                b = banks[bi[0] % 8]
                bi[0] += 1
                return b
            tl = list(tiles if tiles is not None else range(NT))

            def prep_load(i):
                m = tl[i]
                S.dma('sp', xts[i % 2][:, :], xsrc(layer, m), w=[xts[i % 2]])

            def prep_elem(i):
                m = tl[i]
                xt = xts[i % 2]
                rms_rstd(ph, xt, junk, ss, rstd, epsb, D, 1.0 / D)
                set_mod(1 if m < NCT else 0)
                S.op('dve', lambda e, xt=xt: e.scalar_tensor_tensor(out=tmp[:, :], in0=xt[:, :], scalar=rstd[:, 0:1], in1=Am[:, :], op0=ALU.mult, op1=ALU.mult), r=[xt, rstd, Am], w=[tmp])
                S.op('pool', lambda e: e.tensor_tensor(out=hb[:, :], in0=tmp[:, :], in1=SHm[:, :], op=ALU.add), r=[tmp, SHm], w=[hb])

            def prep_T(i):
                hT = hTs[i % 2]
                bk = nextbank()
                bkb = bk[:, :].bitcast(BF16)
                for j in range(8):
                    S.op('pe', lambda e, j=j, bkb=bkb: e.transpose(bkb[:, j * 128:(j + 1) * 128], hb[:, j * 128:(j + 1) * 128], idb[:, :]), r=[hb, idb], w=[bk])
                S.op('act', lambda e, bkb=bkb, hT=hT: e.activation(out=hT[:, :], in_=bkb[:, 0:1024], func=AF.Copy), r=[bk], w=[hT])

            def tokg(i):
                m = tl[i]
                hT = hTs[i % 2]
                for (c0, ncols, fn) in tok_groups:
                    for b0 in range(c0, c0 + ncols, 512):
                        n = min(512, c0 + ncols - b0)
                        bk = nextbank()
                        for j in range(8):
                            S.op('pe', lambda e, j=j, bk=bk, b0=b0, n=n, hT=hT: e.matmul(bk[:, 0:n], lhsT=hT[:, j * 128:(j + 1) * 128], rhs=Wb[:, j * NIN + b0:j * NIN + b0 + n], start=(j == 0), stop=(j == 7)), r=[hT, Wb], w=[bk])
                        fn(m, bk, b0 - c0, n)

            def featg(i):
                m = tl[i]
                hT = hTs[i % 2]
                for (c0, nch_total, Mw, fn) in feat_groups:
                    for q0 in range(0, nch_total, 4):
                        nch = min(4, nch_total - q0)
                        bk = nextbank()
                        for c in range(nch):
                            col = c0 + (q0 + c) * Mw
                            for j in range(8):
                                S.op('pe', lambda e, j=j, bk=bk, c=c, col=col, hT=hT: e.matmul(bk[0:Mw, c * 128:(c + 1) * 128], lhsT=Wb[:, j * NIN + col:j * NIN + col + Mw], rhs=hT[:, j * 128:(j + 1) * 128], start=(j == 0), stop=(j == 7)), r=[hT, Wb], w=[bk])
                        fn(m, bk, q0, nch)

            n_t = len(tl)
            prep_load(0)
            prep_elem(0)
            prep_T(0)
            for i in range(n_t):
                if i + 1 < n_t:
                    prep_load(i + 1)
                    prep_elem(i + 1)
                tokg(i)
                if i + 1 < n_t:
                    prep_T(i + 1)
                featg(i)
            ph.close()

        def make_tok_store(ctxp, dst, dcol0, width, func=None, scale=1.0, dt=F32):
            stg = ctxp['stg32'] if dt == F32 else ctxp['stg16']
            k = ctxp['k32'] if dt == F32 else ctxp['k16']

            def fn(m, bk, off, n):
                st = stg[k[0] % len(stg)]
                k[0] += 1
                if k[0] % 2 == 0:
                    S.op('act', lambda e: e.activation(out=st[:, 0:n], in_=bk[:, 0:n], func=AF.Copy), r=[bk], w=[st])
                else:
                    S.op('dve', lambda e: e.tensor_copy(out=st[:, 0:n], in_=bk[:, 0:n]), r=[bk], w=[st])
                S.dma('pool', dst[m * 128:(m + 1) * 128, dcol0 + off:dcol0 + off + n], st[:, 0:n], r=[st])
            return fn

        def make_feat_store(ctxp, dst, drow0, dt=F32, scale=None, func=AF.Copy, coloff=0):
            stg = ctxp['stg32'] if dt == F32 else ctxp['stg16']
            k = ctxp['k32'] if dt == F32 else ctxp['k16']

            def fn(m, bk, q0, nch):
                st = stg[k[0] % len(stg)]
                k[0] += 1
                if scale is None:
                    S.op('act', lambda e: e.activation(out=st[:, 0:nch * 128], in_=bk[:, 0:nch * 128], func=func), r=[bk], w=[st])
                else:
                    S.op('act', lambda e: e.activation(out=st[:, 0:nch * 128], in_=bk[:, 0:nch * 128], func=func, scale=scale), r=[bk], w=[st])
                r0 = drow0 + q0 * 128
                S.dma('pool', dst[r0:r0 + nch * 128, coloff + m * 128:coloff + (m + 1) * 128].rearrange("(c p) t -> p c t", p=128),
                      st[:, 0:nch * 128].rearrange("p (c t) -> p c t", c=nch), r=[st])
            return fn

        def phaseA0():
            tok_groups = []
            feat_groups = []

            def extra(c):
                ph = c['ph']
                tok_groups.append((0, 1024, make_tok_store(c, zs, 0, 1024)))
                tok_groups.append((3072, 32, make_tok_store(c, dts, 0, 32)))
                tok_groups.append((3616, 512, make_tok_store(c, ktoks[0], 0, 512)))
                tok_groups.append((4128, 1024, make_tok_store(c, vtok, 0, 1024, dt=BF16)))
                tok_groups.append((5184, 1024, make_tok_store(c, rs, 0, 1024)))
                feat_groups.append((1024, 16, 128, make_feat_store(c, xbcT, 0, dt=BF16, coloff=128)))
                feat_groups.append((3104, 4, 128, make_feat_store(c, qT, 0, scale=0.125)))
                feat_groups.append((3616, 4, 128, make_feat_store(c, kTs[0], 0)))
                lrT = ph.tile([33, 128], F32, "lrT")
                S.op('dve', lambda e: e.memset(lrT[32:33, :], 1.0), w=[lrT])
                wgt = ph.tile([33, 1024], F32, "wgt")
                S.dma('sp', wgt[:, :], wg[:, :], w=[wgt])
                gst = [ph.tile([128, 512], F32, "gst") for _ in range(2)]

                def lr_fn(m, bk, q0, nch):
                    S.op('act', lambda e: e.activation(out=lrT[0:32, :], in_=bk[0:32, 0:128], func=AF.Copy), r=[bk], w=[lrT])
                    for d in range(2):
                        bg = banks[(4 + d) % 8]
                        S.op('pe', lambda e, bg=bg, d=d: e.matmul(bg[:, 0:512], lhsT=lrT[0:33, :], rhs=wgt[0:33, d * 512:(d + 1) * 512], start=True, stop=True), r=[lrT, wgt], w=[bg])
                        g = gst[d]
                        S.op('act', lambda e, bg=bg, g=g: e.activation(out=g[:, :], in_=bg[:, 0:512], func=AF.Exp, scale=-1.0), r=[bg], w=[g])
                        S.op('act', lambda e, g=g: e.activation(out=g[:, :], in_=g[:, :], func=AF.Ln, bias=1.0), r=[g], w=[g])
                        S.op('dve', lambda e, g=g: e.tensor_scalar(out=g[:, :], in0=g[:, :], scalar1=-1.0 / 16.0, scalar2=None, op0=ALU.mult), r=[g], w=[g])
                        S.dma('pool', lgs[d][m * 128:(m + 1) * 128, 0:512], g[:, :], r=[g])
                feat_groups.append((5152, 1, 32, lr_fn))
            proj_phase(0, w_in0, 6208, tok_groups, feat_groups, extra)

        def phaseB0():
            ph = Phase()
            cst = load_consts(ph)
            cw = ph.tile([128, 144], F32, "cw")
            S.dma('sp', cw[:, :], convw[:, :], w=[cw])
            cb = ph.tile([128, 16], F32, "cb")
            S.dma('sp', cb[:, :], convb[:, :], w=[cb])
            Dg = ph.tile([128, 144 * 128], BF16, "Dg")
            for i in range(144):
                S.op('dve', lambda e, i=i: e.tensor_scalar(out=Dg[:, i * 128:(i + 1) * 128], in0=cst[:, C_I:C_I + 128], scalar1=cw[:, i:i + 1], scalar2=None, op0=ALU.mult), r=[cst, cw], w=[Dg])
            idb = ph.tile([128, 128], BF16, "idb")
            S.op('dve', lambda e: e.tensor_copy(out=idb[:, :], in_=cst[:, C_I:C_I + 128]), r=[cst], w=[idb])
            mlr = ph.tile([128, 516], BF16, "mlr")
            S.op('dve', lambda e: e.tensor_copy(out=mlr[:, :], in_=cst[:, C_ML:C_ML + 516]), r=[cst], w=[mlr])
            mb4 = []
            for d in range(2):
                t = ph.tile([128, 512], F32, "mb4")
                c0 = C_MBF if d == 0 else C_MBB
                S.op('dve', lambda e, t=t, c0=c0: e.tensor_copy(out=t[:, :].rearrange("p (a i) -> p a i", a=4), in_=cst[:, c0:c0 + 128].unsqueeze(1).to_broadcast([128, 4, 128])), r=[cst], w=[t])
                mb4.append(t)
            dtbb = bc_load(ph, dtb[0:1, :], 32, "dtbb")
            nega = bc_load(ph, alog[0:1, :], 32, "nega")
            S.op('act', lambda e: e.activation(out=nega[:, :], in_=nega[:, :], func=AF.Exp), r=[nega], w=[nega])
            S.op('dve', lambda e: e.tensor_scalar(out=nega[:, :], in0=nega[:, :], scalar1=-1.0, scalar2=None, op0=ALU.mult), r=[nega], w=[nega])
            dskb = bc_load(ph, ssdd[0:1, :], 16, "dskb")
            hT = [ph.tile([128, 1024], F32, "hT") for _ in range(2)]
            hTb = [ph.tile([128, 1024], BF16, "hTb") for _ in range(2)]
            for d in range(2):
                S.op('dve', lambda e, d=d: e.memset(hT[d][:, :], 0.0), w=[hT[d]])
                S.op('dve', lambda e, d=d: e.memset(hTb[d][:, :], 0.0), w=[hTb[d]])

            def mk(shape, dt, name):
                return [ph.tile(shape, dt, name) for _ in range(2)]
            xin = mk([128, 8 * 258], BF16, "xin")
            xl = mk([128, 8 * 258], BF16, "xl")
            xr = mk([128, 8 * 258], BF16, "xr")
            xTt = mk([128, 1024], F32, "xTt")
            BT = mk([128, 512], BF16, "BT")
            CT = mk([128, 512], BF16, "CT")
            xtok = mk([128, 1024], F32, "xtok")
            Btok = mk([128, 512], BF16, "Btok")
            dtr = mk([128, 32], F32, "dtr")
            la = mk([128, 32], F32, "la")
            sm = mk([128, 128], F32, "sm")
            Dx = mk([128, 2048], F32, "Dx")
            seg = mk([128, 2048], F32, "seg")
            scT = mk([128, 2048], BF16, "scT")
            xdt = mk([128, 1024], BF16, "xdt")
            xw = mk([128, 1024], BF16, "xw")
            ysb = mk([128, 1024], F32, "ysb")
            ytmp = mk([128, 1024], F32, "ytmp")

            def step(m, d):
                bk4 = banks[4 * d:4 * d + 4]
                t0 = m * 128
                isctx = m < NCT
                first = (m == 0) or (m == NCT)
                last = (m == NCT - 1) or (m == NT - 1)
                X = xin[d]
                X3 = X[:, :].rearrange("p (c w) -> p c w", c=8)
                w0, w1 = 0, 258
                if first:
                    w0 = 65
                if last:
                    w1 = 193
                if isctx:
                    taps = [(0, -1), (0, 0), (0, 1)]
                else:
                    taps = [(dr, dc) for dr in (-1, 0, 1) for dc in (-1, 0, 1)]
                for half in range(2):
                    if first or last:
                        S.op('dve', lambda e: e.memset(X[:, :], 0.0), w=[X])
                    S.dma('sp', X3[:, :, w0:w1],
                          xbcT[half * 1024:(half + 1) * 1024, t0 + 63 + w0:t0 + 63 + w1].rearrange("(c p) t -> p c t", p=128), w=[X])
                    if isctx:
                        srcs = {-1: X, 0: X, 1: X}
                    else:
                        S.op('dve', lambda e: e.tensor_tensor(out=xl[d][:, :].rearrange("p (c w) -> p c w", c=8), in0=X3, in1=mlr[:, 0:258].unsqueeze(1).to_broadcast([128, 8, 258]), op=ALU.mult), r=[X, mlr], w=[xl[d]])
                        S.op('pool', lambda e: e.tensor_tensor(out=xr[d][:, :].rearrange("p (c w) -> p c w", c=8), in0=X3, in1=mlr[:, 258:516].unsqueeze(1).to_broadcast([128, 8, 258]), op=ALU.mult), r=[X, mlr], w=[xr[d]])
                        srcs = {-1: xl[d], 0: X, 1: xr[d]}
                    yield
                    for c8 in range(8):
                        c = half * 8 + c8
                        bk = bk4[c // 4]
                        for ti, (dr, dc) in enumerate(taps):
                            sb = srcs[dc]
                            o0 = c8 * 258 + 65 + dr * 64 + dc
                            tap = (dr + 1) * 3 + (dc + 1)
                            S.op('pe', lambda e, bk=bk, c=c, sb=sb, o0=o0, tap=tap, ti=ti: e.matmul(bk[:, (c % 4) * 128:(c % 4 + 1) * 128], lhsT=Dg[:, (c * 9 + tap) * 128:(c * 9 + tap + 1) * 128], rhs=sb[:, o0:o0 + 128], start=(ti == 0), stop=(ti == len(taps) - 1)), r=[Dg, sb], w=[bk])
                    yield
                    for c8 in range(8):
                        c = half * 8 + c8
                        bk = bk4[c // 4]
                        if c < 8:
                            dst, oc = xTt[d], c
                        elif c < 12:
                            dst, oc = BT[d], c - 8
                        else:
                            dst, oc = CT[d], c - 12
                        S.op('act', lambda e, bk=bk, c=c, dst=dst, oc=oc: e.activation(out=dst[:, oc * 128:(oc + 1) * 128], in_=bk[:, (c % 4) * 128:(c % 4 + 1) * 128], func=AF.Silu, bias=cb[:, c:c + 1]), r=[bk, cb], w=[dst])
                    yield
                for c in range(8):
                    bk = bk4[c // 4]
                    S.op('pe', lambda e, bk=bk, c=c: e.transpose(bk[:, (c % 4) * 128:(c % 4 + 1) * 128], xTt[d][:, c * 128:(c + 1) * 128], cst[:, C_I:C_I + 128]), r=[xTt[d], cst], w=[bk])
                b2b = bk4[2][:, :].bitcast(BF16)
                for g in range(4):
                    S.op('pe', lambda e, g=g: e.transpose(b2b[:, g * 128:(g + 1) * 128], BT[d][:, g * 128:(g + 1) * 128], idb[:, :]), r=[BT[d], idb], w=[bk4[2]])
                yield
                S.op('act', lambda e: e.activation(out=xtok[d][:, 0:512], in_=bk4[0][:, :], func=AF.Copy), r=[bk4[0]], w=[xtok[d]])
                S.op('dve', lambda e: e.tensor_copy(out=xtok[d][:, 512:1024], in_=bk4[1][:, :]), r=[bk4[1]], w=[xtok[d]])
                S.op('act', lambda e: e.activation(out=Btok[d][:, :], in_=b2b[:, 0:512], func=AF.Copy), r=[bk4[2]], w=[Btok[d]])
                S.dma('sp', dtr[d][:, :], dts[t0:t0 + 128, :], w=[dtr[d]])
                S.op('dve', lambda e: e.tensor_tensor(out=dtr[d][:, :], in0=dtr[d][:, :], in1=dtbb[:, :], op=ALU.add), r=[dtr[d], dtbb], w=[dtr[d]])
                S.op('act', lambda e: e.activation(out=dtr[d][:, :], in_=dtr[d][:, :], func=AF.Exp), r=[dtr[d]], w=[dtr[d]])
                S.op('act', lambda e: e.activation(out=dtr[d][:, :], in_=dtr[d][:, :], func=AF.Ln, bias=1.0), r=[dtr[d]], w=[dtr[d]])
                S.op('dve', lambda e: e.tensor_tensor(out=la[d][:, :], in0=dtr[d][:, :], in1=nega[:, :], op=ALU.mult), r=[dtr[d], nega], w=[la[d]])
                yield
                lad = la[d][:, d * 16:(d + 1) * 16]
                dtd = dtr[d][:, d * 16:(d + 1) * 16]
                tri = C_TF if d == 0 else C_TB
                b7 = bk4[3]
                S.op('pe', lambda e: e.matmul(b7[:, 0:16], lhsT=cst[:, tri:tri + 128], rhs=lad, start=True, stop=True), r=[cst, la[d]], w=[b7])
                S.op('pe', lambda e: e.matmul(b7[:, 16:32], lhsT=cst[:, C_ONE:C_ONE + 128], rhs=lad, start=True, stop=True), r=[cst, la[d]], w=[b7])
                yield
                s = sm[d]
                S.op('dve', lambda e: e.tensor_copy(out=s[:, 0:16], in_=b7[:, 0:16]), r=[b7], w=[s])
                S.op('act', lambda e: e.activation(out=s[:, 16:32], in_=b7[:, 0:16], func=AF.Exp), r=[b7], w=[s])
                S.op('act', lambda e: e.activation(out=s[:, 32:48], in_=b7[:, 16:32], func=AF.Exp), r=[b7], w=[s])
                S.op('dve', lambda e: e.tensor_tensor(out=s[:, 48:64], in0=b7[:, 16:32], in1=s[:, 0:16], op=ALU.subtract), r=[b7, s], w=[s])
                S.op('act', lambda e: e.activation(out=s[:, 48:64], in_=s[:, 48:64], func=AF.Exp), r=[s], w=[s])
                S.op('dve', lambda e: e.tensor_tensor(out=s[:, 64:80], in0=s[:, 48:64], in1=dtd, op=ALU.mult), r=[s, dtr[d]], w=[s])
                S.op('dve', lambda e: e.tensor_tensor(out=Dx[d][:, :].rearrange("p (h i) -> p h i", h=16), in0=cst[:, C_I:C_I + 128].unsqueeze(1).to_broadcast([128, 16, 128]), in1=s[:, 0:16].unsqueeze(2).to_broadcast([128, 16, 128]), op=ALU.mult), r=[cst, s], w=[Dx[d]])
                S.op('pool', lambda e: e.tensor_tensor(out=xdt[d][:, :].rearrange("p (h q) -> p h q", h=16), in0=xtok[d][:, :].rearrange("p (h q) -> p h q", h=16), in1=dtd.unsqueeze(2).to_broadcast([128, 16, 64]), op=ALU.mult), r=[xtok[d], dtr[d]], w=[xdt[d]])
                S.op('pool', lambda e: e.tensor_tensor(out=xw[d][:, :].rearrange("p (h q) -> p h q", h=16), in0=xtok[d][:, :].rearrange("p (h q) -> p h q", h=16), in1=s[:, 64:80].unsqueeze(2).to_broadcast([128, 16, 64]), op=ALU.mult), r=[xtok[d], s], w=[xw[d]])
                yield
                for q in range(4):
                    bk = bk4[q]
                    S.op('pe', lambda e, bk=bk, q=q: e.matmul(bk[:, 0:512], lhsT=cst[:, C_ONE:C_ONE + 128], rhs=Dx[d][:, q * 512:(q + 1) * 512], start=True, stop=False), r=[cst, Dx[d]], w=[bk])
                    S.op('pe', lambda e, bk=bk, q=q: e.matmul(bk[:, 0:512], lhsT=cst[:, C_I:C_I + 128], rhs=mb4[d][:, :], start=False, stop=True), r=[cst, mb4[d]], w=[bk])
                yield
                for q in range(4):
                    bk = bk4[q]
                    S.op('dve', lambda e, bk=bk, q=q: e.tensor_tensor(out=seg[d][:, q * 512:(q + 1) * 512].rearrange("p (h i) -> p h i", h=4), in0=bk[:, 0:512].rearrange("p (h i) -> p h i", h=4), in1=s[:, 4 * q:4 * q + 4].unsqueeze(2).to_broadcast([128, 4, 128]), op=ALU.subtract), r=[bk, s], w=[seg[d]])
                yield
                S.op('act', lambda e: e.activation(out=seg[d][:, :], in_=seg[d][:, :], func=AF.Exp), r=[seg[d]], w=[seg[d]])
                for g in range(4):
                    S.op('pe', lambda e, g=g: e.matmul(bk4[0][:, g * 128:(g + 1) * 128], lhsT=BT[d][:, g * 128:(g + 1) * 128], rhs=CT[d][:, g * 128:(g + 1) * 128], start=True, stop=True), r=[BT[d], CT[d]], w=[bk4[0]])
                yield
                for g in range(4):
                    S.op('dve', lambda e, g=g: e.tensor_tensor(out=scT[d][:, g * 512:(g + 1) * 512].rearrange("p (h i) -> p h i", h=4), in0=seg[d][:, g * 512:(g + 1) * 512].rearrange("p (h i) -> p h i", h=4), in1=bk4[0][:, g * 128:(g + 1) * 128].unsqueeze(1).to_broadcast([128, 4, 128]), op=ALU.mult), r=[seg[d], bk4[0]], w=[scT[d]])
                yield
                for hd in range(16):
                    bk = bk4[1 + hd // 8]
                    S.op('pe', lambda e, bk=bk, hd=hd: e.matmul(bk[:, (hd % 8) * 64:(hd % 8 + 1) * 64], lhsT=scT[d][:, hd * 128:(hd + 1) * 128], rhs=xdt[d][:, hd * 64:(hd + 1) * 64], start=True, stop=True), r=[scT[d], xdt[d]], w=[bk])
                bint = [bk4[3], bk4[0]]
                for g in range(4):
                    bk = bint[g // 2]
                    S.op('pe', lambda e, bk=bk, g=g: e.matmul(bk[:, (g % 2) * 256:(g % 2 + 1) * 256], lhsT=CT[d][:, g * 128:(g + 1) * 128], rhs=hTb[d][:, g * 256:(g + 1) * 256], start=True, stop=True), r=[CT[d], hTb[d]], w=[bk])
                yield
                for h2 in range(2):
                    S.op('dve', lambda e, h2=h2: e.tensor_tensor(out=ytmp[d][:, h2 * 512:(h2 + 1) * 512].rearrange("p (h q) -> p h q", h=8), in0=bint[h2][:, :].rearrange("p (h q) -> p h q", h=8), in1=s[:, 16 + 8 * h2:24 + 8 * h2].unsqueeze(2).to_broadcast([128, 8, 64]), op=ALU.mult), r=[bint[h2], s], w=[ytmp[d]])
                for h2 in range(2):
                    S.op('dve', lambda e, h2=h2: e.tensor_tensor(out=ysb[d][:, h2 * 512:(h2 + 1) * 512], in0=bk4[1 + h2][:, :], in1=ytmp[d][:, h2 * 512:(h2 + 1) * 512], op=ALU.add), r=[bk4[1 + h2], ytmp[d]], w=[ysb[d]])
                if d == 0:
                    S.op('pool', lambda e: e.tensor_tensor(out=ytmp[d][:, :].rearrange("p (h q) -> p h q", h=16), in0=xtok[d][:, :].rearrange("p (h q) -> p h q", h=16), in1=dskb[:, 0:16].unsqueeze(2).to_broadcast([128, 16, 64]), op=ALU.mult), r=[xtok[d], dskb], w=[ytmp[d]])
                    S.op('pool', lambda e: e.tensor_tensor(out=ysb[d][:, :], in0=ysb[d][:, :], in1=ytmp[d][:, :], op=ALU.add), r=[ysb[d], ytmp[d]], w=[ysb[d]])
                S.dma('pool', ydir[d][t0:t0 + 128, :], ysb[d][:, :], r=[ysb[d]])
                yield
                for g in range(4):
                    bk = bk4[1 + g // 2]
                    S.op('pe', lambda e, bk=bk, g=g: e.matmul(bk[:, (g % 2) * 256:(g % 2 + 1) * 256], lhsT=Btok[d][:, g * 128:(g + 1) * 128], rhs=xw[d][:, g * 256:(g + 1) * 256], start=True, stop=True), r=[Btok[d], xw[d]], w=[bk])
                S.op('pool', lambda e: e.tensor_tensor(out=hT[d][:, :].rearrange("p (h q) -> p h q", h=16), in0=hT[d][:, :].rearrange("p (h q) -> p h q", h=16), in1=s[:, 32:48].unsqueeze(2).to_broadcast([128, 16, 64]), op=ALU.mult), r=[hT[d], s], w=[hT[d]])
                yield
                for h2 in range(2):
                    S.op('dve', lambda e, h2=h2: e.tensor_tensor(out=hT[d][:, h2 * 512:(h2 + 1) * 512], in0=hT[d][:, h2 * 512:(h2 + 1) * 512], in1=bk4[1 + h2][:, :], op=ALU.add), r=[hT[d], bk4[1 + h2]], w=[hT[d]])
                S.op('act', lambda e: e.activation(out=hTb[d][:, :], in_=hT[d][:, :], func=AF.Copy), r=[hT[d]], w=[hTb[d]])

            order_f = list(range(NT))
            order_b = list(range(NCT - 1, -1, -1)) + list(range(NT - 1, NCT - 1, -1))
            for i in range(NT):
                run_pair(step(order_f[i], 0), step(order_b[i], 1))
            ph.close()

        def gla_phase(H, dk, dv, qT_s, kT_s, ktok_s, v_s, lg_s, o_s):
            ph = Phase()
            cst = load_consts(ph)
            HK = H * dk
            HV = H * dv
            nkc = HK // 128
            hp = 128 // dk
            NCH = T // 64
            NCC = CTX // 64
            Sst = [ph.tile([128, nkc * dv], F32, "Sst") for _ in range(2)]
            Sb = [ph.tile([128, nkc * dv], BF16, "Sb") for _ in range(2)]
            for d in range(2):
                S.op('dve', lambda e, d=d: e.memset(Sst[d][:, :], 0.0), w=[Sst[d]])
                S.op('dve', lambda e, d=d: e.memset(Sb[d][:, :], 0.0), w=[Sb[d]])

            def mk(shape, dt, name):
                return [ph.tile(shape, dt, name) for _ in range(2)]
            lgt = mk([64, HK], F32, "lgt")
            kt = mk([64, HK], F32, "kt")
            vb = mk([64, HV], BF16, "vb")
            qTt = mk([128, nkc * 64], F32, "qTt")
            kTt = mk([128, nkc * 64], F32, "kTt")
            eg = mk([128, nkc * 64], F32, "eg")
            eng = mk([128, nkc * 64], F32, "eng")
            qd = [[ph.tile([128, nkc * 64], BF16, "qd") for _ in range(hp)] for _ in range(2)]
            for d in range(2):
                for sl in range(hp):
                    S.op('dve', lambda e, d=d, sl=sl: e.memset(qd[d][sl][:, :], 0.0), w=[qd[d][sl]])
            ki = mk([128, nkc * 64], BF16, "ki")
            eex = mk([64, HK], F32, "eex")
            kend = mk([64, HK], BF16, "kend")
            att = mk([64, H * 64], BF16, "att")
            osb = mk([64, HV], F32, "osb")

            def step(ci, d):
                t0 = ci * 64
                S.dma('sp', lgt[d][:, :], lg_s[d][t0:t0 + 64, 0:HK], w=[lgt[d]])
                S.dma('sp', kt[d][:, :], ktok_s[d][t0:t0 + 64, 0:HK], w=[kt[d]])
                S.dma('sp', vb[d][:, :], v_s[t0:t0 + 64, 0:HV], w=[vb[d]])
                S.dma('act', qTt[d][:, :].rearrange("p (c t) -> p c t", c=nkc), qT_s[0:HK, t0:t0 + 64].rearrange("(c p) t -> p c t", p=128), w=[qTt[d]])
                S.dma('act', kTt[d][:, :].rearrange("p (c t) -> p c t", c=nkc), kT_s[d][0:HK, t0:t0 + 64].rearrange("(c p) t -> p c t", p=128), w=[kTt[d]])
                yield
                tri = C_TF if d == 0 else C_TB
                stri = C_SF if d == 0 else C_SB
                bA = banks[0 + 4 * d]
                for kc in range(nkc):
                    S.op('pe', lambda e, kc=kc: e.matmul(bA[:, kc * 64:(kc + 1) * 64], lhsT=lgt[d][0:64, kc * 128:(kc + 1) * 128], rhs=cst[0:64, tri:tri + 64], start=True, stop=True), r=[lgt[d], cst], w=[bA])
                nb = HK // 512
                bB = [banks[1 + 4 * d], banks[2 + 4 * d]]
                for b in range(nb):
                    S.op('pe', lambda e, b=b: e.matmul(bB[b][0:64, 0:512], lhsT=cst[0:64, stri:stri + 64], rhs=lgt[d][0:64, b * 512:(b + 1) * 512], start=True, stop=True), r=[lgt[d], cst], w=[bB[b]])
                yield
                S.op('act', lambda e: e.activation(out=eg[d][:, :], in_=bA[:, 0:nkc * 64], func=AF.Exp), r=[bA], w=[eg[d]])
                S.op('act', lambda e: e.activation(out=eng[d][:, :], in_=bA[:, 0:nkc * 64], func=AF.Exp, scale=-1.0), r=[bA], w=[eng[d]])
                yield
                for sl in range(hp):
                    p0, p1 = sl * dk, (sl + 1) * dk
                    S.op('dve', lambda e, sl=sl, p0=p0, p1=p1: e.tensor_tensor(out=qd[d][sl][p0:p1, :], in0=qTt[d][p0:p1, :], in1=eg[d][p0:p1, :], op=ALU.mult), r=[qTt[d], eg[d]], w=[qd[d][sl]])
                S.op('dve', lambda e: e.tensor_tensor(out=ki[d][:, :], in0=kTt[d][:, :], in1=eng[d][:, :], op=ALU.mult), r=[kTt[d], eng[d]], w=[ki[d]])
                for b in range(nb):
                    S.op('act', lambda e, b=b: e.activation(out=eex[d][:, b * 512:(b + 1) * 512], in_=bB[b][0:64, 0:512], func=AF.Exp), r=[bB[b]], w=[eex[d]])
                S.op('dve', lambda e: e.tensor_tensor(out=kend[d][:, :], in0=kt[d][:, :], in1=eex[d][:, :], op=ALU.mult), r=[kt[d], eex[d]], w=[kend[d]])
                yield
                bC = banks[3 + 4 * d]
                for h in range(H):
                    kc, sl = divmod(h, hp)
                    S.op('pe', lambda e, h=h, kc=kc, sl=sl: e.matmul(bC[0:64, h * 64:(h + 1) * 64], lhsT=ki[d][:, kc * 64:(kc + 1) * 64], rhs=qd[d][sl][:, kc * 64:(kc + 1) * 64], start=True, stop=True), r=[ki[d], qd[d][sl]], w=[bC])
                yield
                S.op('dve', lambda e: e.tensor_tensor(out=att[d][:, :].rearrange("p (h i) -> p h i", h=H), in0=bC[0:64, 0:H * 64].rearrange("p (h i) -> p h i", h=H), in1=cst[0:64, tri:tri + 64].unsqueeze(1).to_broadcast([64, H, 64]), op=ALU.mult), r=[bC, cst], w=[att[d]])
                yield
                bO = [banks[1 + 4 * d], banks[2 + 4 * d]]
                for h in range(H):
                    bo = bO[(h * dv) // 512]
                    oc = (h * dv) % 512
                    S.op('pe', lambda e, h=h, bo=bo, oc=oc: e.matmul(bo[0:64, oc:oc + dv], lhsT=att[d][0:64, h * 64:(h + 1) * 64], rhs=vb[d][0:64, h * dv:(h + 1) * dv], start=True, stop=True), r=[att[d], vb[d]], w=[bo])
                yield
                for b in range(HV // 512):
                    S.op('act', lambda e, b=b: e.activation(out=osb[d][:, b * 512:(b + 1) * 512], in_=bO[b][0:64, 0:512], func=AF.Copy), r=[bO[b]], w=[osb[d]])
                for h in range(H):
                    kc, sl = divmod(h, hp)
                    bo = bO[(h * dv) // 512]
                    oc = (h * dv) % 512
                    S.op('pe', lambda e, h=h, bo=bo, oc=oc, kc=kc, sl=sl: e.matmul(bo[0:64, oc:oc + dv], lhsT=qd[d][sl][:, kc * 64:(kc + 1) * 64], rhs=Sb[d][:, kc * dv:(kc + 1) * dv], start=True, stop=True), r=[qd[d][sl], Sb[d]], w=[bo])
                yield
                for b in range(HV // 512):
                    S.op('dve', lambda e, b=b: e.tensor_tensor(out=osb[d][:, b * 512:(b + 1) * 512], in0=osb[d][:, b * 512:(b + 1) * 512], in1=bO[b][0:64, 0:512], op=ALU.add), r=[bO[b], osb[d]], w=[osb[d]])
                S.dma('pool', o_s[d][t0:t0 + 64, 0:HV], osb[d][:, :], r=[osb[d]])
                yield
                bS = [banks[0 + 4 * d], banks[3 + 4 * d]]
                wS = hp * dv
                for kc in range(nkc):
                    bs = bS[(kc * wS) // 512]
                    oc = (kc * wS) % 512
                    S.op('pe', lambda e, kc=kc, bs=bs, oc=oc: e.matmul(bs[:, oc:oc + wS], lhsT=kend[d][0:64, kc * 128:(kc + 1) * 128], rhs=vb[d][0:64, kc * wS:(kc + 1) * wS], start=True, stop=True), r=[kend[d], vb[d]], w=[bs])
                yield
                lastc = 63 if d == 0 else 0
                for kc in range(nkc):
                    bs = bS[(kc * wS) // 512]
                    oc = (kc * wS) % 512
                    for sl in range(hp):
                        p0, p1 = sl * dk, (sl + 1) * dk
                        S.op('dve', lambda e, kc=kc, bs=bs, oc=oc, sl=sl, p0=p0, p1=p1: e.scalar_tensor_tensor(out=Sst[d][p0:p1, kc * dv:(kc + 1) * dv], in0=Sst[d][p0:p1, kc * dv:(kc + 1) * dv], scalar=eg[d][p0:p1, kc * 64 + lastc:kc * 64 + lastc + 1], in1=bs[p0:p1, oc + sl * dv:oc + (sl + 1) * dv], op0=ALU.mult, op1=ALU.add), r=[Sst[d], eg[d], bs], w=[Sst[d]])
                S.op('act', lambda e: e.activation(out=Sb[d][:, :], in_=Sst[d][:, :], func=AF.Copy), r=[Sst[d]], w=[Sb[d]])

            order_f = list(range(NCH))
            order_b = list(range(NCC - 1, -1, -1)) + list(range(NCH - 1, NCC - 1, -1))
            for i in range(NCH):
                run_pair(step(order_f[i], 0), step(order_b[i], 1))
            ph.close()

        def head_rms(ph, src, dstbuf, dst_ap, H, hd, nwbuf, nw_bc, gate, sq, ss, rstd, epsb, tmp):
            n = H * hd
            v3 = lambda ap: ap.rearrange("p (h q) -> p h q", h=H)
            S.op('dve', lambda e: e.tensor_tensor(out=sq[:, 0:n], in0=src[:, 0:n], in1=src[:, 0:n], op=ALU.mult), r=[src], w=[sq])
            S.op('dve', lambda e: e.tensor_reduce(out=ss[:, 0:H], in_=v3(sq[:, 0:n]), axis=AX.X, op=ALU.add), r=[sq], w=[ss])
            S.op('act', lambda e: e.activation(out=rstd[:, 0:H], in_=ss[:, 0:H], func=AF.Sqrt, scale=1.0 / hd, bias=epsb[:, 0:1]), r=[ss, epsb], w=[rstd])
            S.op('dve', lambda e: e.reciprocal(out=rstd[:, 0:H], in_=rstd[:, 0:H]), r=[rstd], w=[rstd])
            S.op('dve', lambda e: e.tensor_tensor(out=v3(tmp[:, 0:n]), in0=v3(src[:, 0:n]), in1=rstd[:, 0:H].unsqueeze(2).to_broadcast([128, H, hd]), op=ALU.mult), r=[src, rstd], w=[tmp])
            S.op('dve', lambda e: e.tensor_tensor(out=v3(tmp[:, 0:n]), in0=v3(tmp[:, 0:n]), in1=nw_bc, op=ALU.mult), r=[tmp, nwbuf], w=[tmp])
            S.op('dve', lambda e: e.tensor_tensor(out=dst_ap, in0=tmp[:, 0:n], in1=gate[:, 0:n], op=ALU.mult), r=[tmp, gate], w=[dstbuf])

        def phaseD0a():
            ph = Phase()
            cst = load_consts(ph)
            Wo = ph.tile([128, 16 * D], BF16, "Wo")
            load_weight_bf16(ph, Wo, w_out0, 2048, D)
            snw = bc_load(ph, ssdnw[0:1, :], 1024, "snw")
            gnw = bc_load(ph, glanw[0:1, :], 128, "gnw")
            g1 = [bc_load(ph, modv[0, s:s + 1, 2 * D:3 * D], D, "g1") for s in range(2)]
            epsb = ph.tile([128, 1], F32, "epsb")
            S.op('dve', lambda e: e.memset(epsb[:, :], EPS), w=[epsb])
            idb = ph.tile([128, 128], BF16, "idb")
            S.op('dve', lambda e: e.tensor_copy(out=idb[:, :], in_=cst[:, C_I:C_I + 128]), r=[cst], w=[idb])
            a = ph.tile([128, D], F32, "a")
            b = ph.tile([128, D], F32, "b")
            g = ph.tile([128, D], F32, "g")
            sq = ph.tile([128, D], F32, "sq")
            tmp = ph.tile([128, D], F32, "tmp")
            xt = ph.tile([128, D], F32, "xt")
            mix = ph.tile([128, 2048], BF16, "mix")
            mixT = ph.tile([128, 2048], BF16, "mixT")
            ss = ph.tile([128, 16], F32, "ss")
            rstd = ph.tile([128, 16], F32, "rstd")
            xo = ph.tile([128, D], F32, "xo")
            for m in range(NT):
                r0 = m * 128
                for part in range(2):
                    srcs = ydir if part == 0 else odir
                    gsrc = zs if part == 0 else rs
                    S.dma('sp', a[:, :], srcs[0][r0:r0 + 128, :], w=[a])
                    S.dma('act', b[:, :], srcs[1][r0:r0 + 128, :], w=[b])
                    S.dma('sp', g[:, :], gsrc[r0:r0 + 128, :], w=[g])
                    S.op('dve', lambda e: e.tensor_tensor(out=a[:, :], in0=a[:, :], in1=b[:, :], op=ALU.add), r=[a, b], w=[a])
                    S.op('act', lambda e: e.activation(out=g[:, :], in_=g[:, :], func=AF.Silu), r=[g], w=[g])
                    if part == 0:
                        S.op('dve', lambda e: e.tensor_tensor(out=a[:, :], in0=a[:, :], in1=g[:, :], op=ALU.mult), r=[a, g], w=[a])
                        S.op('dve', lambda e: e.tensor_tensor(out=sq[:, :], in0=a[:, :], in1=a[:, :], op=ALU.mult), r=[a], w=[sq])
                        S.op('dve', lambda e: e.tensor_reduce(out=ss[:, 0:4], in_=sq[:, :].rearrange("p (h q) -> p h q", h=4), axis=AX.X, op=ALU.add), r=[sq], w=[ss])
                        S.op('act', lambda e: e.activation(out=rstd[:, 0:4], in_=ss[:, 0:4], func=AF.Sqrt, scale=1.0 / 256, bias=epsb[:, 0:1]), r=[ss, epsb], w=[rstd])
                        S.op('dve', lambda e: e.reciprocal(out=rstd[:, 0:4], in_=rstd[:, 0:4]), r=[rstd], w=[rstd])
                        S.op('dve', lambda e: e.tensor_tensor(out=tmp[:, :].rearrange("p (h q) -> p h q", h=4), in0=a[:, :].rearrange("p (h q) -> p h q", h=4), in1=rstd[:, 0:4].unsqueeze(2).to_broadcast([128, 4, 256]), op=ALU.mult), r=[a, rstd], w=[tmp])
                        S.op('dve', lambda e: e.tensor_tensor(out=mix[:, 0:1024], in0=tmp[:, :], in1=snw[:, :], op=ALU.mult), r=[tmp, snw], w=[mix])
                    else:
                        head_rms(ph, a, mix, mix[:, 1024:2048], 8, 128, gnw, gnw[:, 0:128].unsqueeze(1).to_broadcast([128, 8, 128]), g, sq, ss, rstd, epsb, tmp)
                for q in range(4):
                    bk = banks[q]
                    bkb = bk[:, 0:256].bitcast(BF16)
                    for c in range(4):
                        cc = q * 4 + c
                        S.op('pe', lambda e, bkb=bkb, c=c, cc=cc: e.transpose(bkb[:, c * 128:(c + 1) * 128], mix[:, cc * 128:(cc + 1) * 128], idb[:, :]), r=[mix, idb], w=[bk])
                    S.op('act' if q % 2 == 0 else 'dve', (lambda e, bkb=bkb, q=q: e.activation(out=mixT[:, q * 512:(q + 1) * 512], in_=bkb[:, 0:512], func=AF.Copy)) if q % 2 == 0 else (lambda e, bkb=bkb, q=q: e.tensor_copy(out=mixT[:, q * 512:(q + 1) * 512], in_=bkb[:, 0:512])), r=[bk], w=[mixT])
                S.dma('sp', xt[:, :], xsrc(0, m), w=[xt])
                for h2 in range(2):
                    bk = banks[4 + h2]
                    for c in range(16):
                        S.op('pe', lambda e, bk=bk, c=c, h2=h2: e.matmul(bk[:, 0:512], lhsT=mixT[:, c * 128:(c + 1) * 128], rhs=Wo[:, c * D + h2 * 512:c * D + (h2 + 1) * 512], start=(c == 0), stop=(c == 15)), r=[mixT, Wo], w=[bk])
                    gg = g1[1] if m < NCT else g1[0]
                    S.op('dve', lambda e, bk=bk, h2=h2, gg=gg: e.tensor_tensor(out=xo[:, h2 * 512:(h2 + 1) * 512], in0=bk[:, 0:512], in1=gg[:, h2 * 512:(h2 + 1) * 512], op=ALU.mult), r=[bk, gg], w=[xo])
                S.op('dve', lambda e: e.tensor_tensor(out=xo[:, :], in0=xo[:, :], in1=xt[:, :], op=ALU.add), r=[xo, xt], w=[xo])
                S.dma('pool', xmid[r0:r0 + 128, :], xo[:, :], r=[xo])
            ph.close()

        def phaseMLP(layer, tiles, final):
            ph = Phase()
            cst = ph.tile([128, 128], F32, "cstI")
            S.dma('sp', cst[:, :], consts[:, 0:128], w=[cst])
            W1 = ph.tile([128, 8 * 4096], BF16, "W1")
            load_weight_bf16(ph, W1, mlp_w1[layer], D, 4096)
            W2 = ph.tile([128, 32 * D], BF16, "W2")
            load_weight_bf16(ph, W2, mlp_w2[layer], 4096, D)
            NW = bc_load(ph, norm2_w[layer:layer + 1, :], D, "NW")
            Am = ph.tile([128, D], F32, "Am")
            SHm = ph.tile([128, D], F32, "SHm")
            Gs = {0: bc_load(ph, modv[layer, 0:1, 5 * D:6 * D], D, "Gl")}
            if not final:
                Gs[1] = bc_load(ph, modv[layer, 1:2, 5 * D:6 * D], D, "Gc")
            cur = [None]

            def set_mod(src):
                if cur[0] == src:
                    return
                cur[0] = src
                S.dma('sp', Am[:, :], modv[layer, src:src + 1, 4 * D:5 * D].partition_broadcast(128), w=[Am])
                S.dma('sp', SHm[:, :], modv[layer, src:src + 1, 3 * D:4 * D].partition_broadcast(128), w=[SHm])
                S.op('dve', lambda e: e.scalar_tensor_tensor(out=Am[:, :], in0=Am[:, :], scalar=1.0, in1=NW[:, :], op0=ALU.add, op1=ALU.mult), r=[Am, NW], w=[Am])
            if final:
                fnw = bc_load(ph, final_w[0:1, :], D, "fnw")
            epsb = ph.tile([128, 1], F32, "epsb")
            S.op('dve', lambda e: e.memset(epsb[:, :], EPS), w=[epsb])
            idb = ph.tile([128, 128], BF16, "idb")
            S.op('dve', lambda e: e.tensor_copy(out=idb[:, :], in_=cst[:, C_I:C_I + 128]), r=[cst], w=[idb])
            xts = [ph.tile([128, D], F32, "xt") for _ in range(2)]
            tmp = ph.tile([128, D], F32, "tmp")
            hb = ph.tile([128, D], BF16, "hb")
            junk = hb
            hT = ph.tile([128, D], BF16, "hT")
            h1s = [ph.tile([128, 512], F32, "h1") for _ in range(2)]
            h1T = ph.tile([128, 4096], BF16, "h1T")
            xo = ph.tile([128, D], F32, "xo")
            ss = ph.tile([128, 1], F32, "ss")
            rstd = ph.tile([128, 1], F32, "rstd")
            ss2 = ph.tile([128, 1], F32, "ss2")
            rstd2 = ph.tile([128, 1], F32, "rstd2")

            def prep_load(i):
                m = tiles[i]
                S.dma('sp', xts[i % 2][:, :], xmid[m * 128:(m + 1) * 128, :], w=[xts[i % 2]])

            def prep_elem(i):
                m = tiles[i]
                xt = xts[i % 2]
                rms_rstd(ph, xt, junk, ss, rstd, epsb, D, 1.0 / D)
                set_mod(1 if m < NCT else 0)
                S.op('dve', lambda e: e.scalar_tensor_tensor(out=tmp[:, :], in0=xt[:, :], scalar=rstd[:, 0:1], in1=Am[:, :], op0=ALU.mult, op1=ALU.mult), r=[xt, rstd, Am], w=[tmp])
                S.op('pool', lambda e: e.tensor_tensor(out=hb[:, :], in0=tmp[:, :], in1=SHm[:, :], op=ALU.add), r=[tmp, SHm], w=[hb])

            def prep_T(i):
                bk = banks[0]
                bkb = bk[:, :].bitcast(BF16)
                for j in range(8):
                    S.op('pe', lambda e, j=j: e.transpose(bkb[:, j * 128:(j + 1) * 128], hb[:, j * 128:(j + 1) * 128], idb[:, :]), r=[hb, idb], w=[bk])
                S.op('act', lambda e: e.activation(out=hT[:, :], in_=bkb[:, 0:1024], func=AF.Copy), r=[bk], w=[hT])

            def w1(i):
                for q in range(8):
                    bk = banks[1 + q % 5]
                    h1 = h1s[q % 2]
                    for c in range(4):
                        fc = q * 4 + c
                        for j in range(8):
                            S.op('pe', lambda e, bk=bk, c=c, fc=fc, j=j: e.matmul(bk[:, c * 128:(c + 1) * 128], lhsT=W1[:, j * 4096 + fc * 128:j * 4096 + (fc + 1) * 128], rhs=hT[:, j * 128:(j + 1) * 128], start=(j == 0), stop=(j == 7)), r=[W1, hT], w=[bk])
                    S.op('act', lambda e, bk=bk, h1=h1: e.activation(out=h1[:, :], in_=bk[:, :], func=AF.Relu), r=[bk], w=[h1])
                    S.op('dve', lambda e, q=q, h1=h1: e.tensor_tensor(out=h1T[:, q * 512:(q + 1) * 512], in0=h1[:, :], in1=h1[:, :], op=ALU.mult), r=[h1], w=[h1T])

            def w2(i):
                for h2 in range(2):
                    bk = banks[6 + h2]
                    for fc in range(32):
                        S.op('pe', lambda e, bk=bk, fc=fc, h2=h2: e.matmul(bk[:, 0:512], lhsT=h1T[:, fc * 128:(fc + 1) * 128], rhs=W2[:, fc * D + h2 * 512:fc * D + (h2 + 1) * 512], start=(fc == 0), stop=(fc == 31)), r=[h1T, W2], w=[bk])

            def epi(i):
                m = tiles[i]
                r0 = m * 128
                xt = xts[i % 2]
                G2 = Gs[1 if m < NCT else 0]
                for h2 in range(2):
                    bk = banks[6 + h2]
                    S.op('dve', lambda e, bk=bk, h2=h2: e.tensor_tensor(out=xo[:, h2 * 512:(h2 + 1) * 512], in0=bk[:, 0:512], in1=G2[:, h2 * 512:(h2 + 1) * 512], op=ALU.mult), r=[bk, G2], w=[xo])
                S.op('pool', lambda e: e.tensor_tensor(out=xo[:, :], in0=xo[:, :], in1=xt[:, :], op=ALU.add), r=[xo, xt], w=[xo])
                if not final:
                    S.dma('pool', xres[r0:r0 + 128, :], xo[:, :], r=[xo])
                else:
                    rms_rstd(ph, xo, junk, ss2, rstd2, epsb, D, 1.0 / D)
                    S.op('dve', lambda e: e.scalar_tensor_tensor(out=tmp[:, :], in0=xo[:, :], scalar=rstd2[:, 0:1], in1=fnw[:, :], op0=ALU.mult, op1=ALU.mult), r=[xo, rstd2, fnw], w=[tmp])
                    S.dma('pool', out[r0 - CTX:r0 - CTX + 128, :], tmp[:, :], r=[tmp])

            n = len(tiles)
            prep_load(0)
            prep_elem(0)
            prep_T(0)
            for i in range(n):
                if i + 1 < n:
                    prep_load(i + 1)
                w1(i)
                if i + 1 < n:
                    prep_elem(i + 1)
                w2(i)
                if i + 1 < n:
                    prep_T(i + 1)
                epi(i)
            ph.close()

        def phaseA1():
            tok_groups = []
            feat_groups = []

            def extra(c):
                ph = c['ph']
                l0 = bc_load(ph, lbl[0:1, :], 2048, "l0")
                oml = bc_load(ph, lbl[1:2, :], 2048, "oml")
                S.op('dve', lambda e: e.tensor_tensor(out=oml[:, :], in0=oml[:, :], in1=l0[:, :], op=ALU.subtract), r=[oml, l0], w=[oml])
                S.op('act', lambda e: e.activation(out=oml[:, :], in_=oml[:, :], func=AF.Sigmoid, scale=-1.0), r=[oml], w=[oml])
                omlT = ph.tile([128, 32], F32, "omlT")
                S.dma('sp', omlT[:, :], lblT[:, :, :].rearrange("p l c -> p (l c)"), w=[omlT])
                S.op('dve', lambda e: e.tensor_tensor(out=omlT[:, 16:32], in0=omlT[:, 16:32], in1=omlT[:, 0:16], op=ALU.subtract), r=[omlT], w=[omlT])
                S.op('act', lambda e: e.activation(out=omlT[:, 16:32], in_=omlT[:, 16:32], func=AF.Sigmoid, scale=-1.0), r=[omlT], w=[omlT])
                oneb = ph.tile([128, 1], F32, "oneb")
                S.op('dve', lambda e: e.memset(oneb[:, :], 1.0), w=[oneb])
                stg32, k32 = c['stg32'], c['k32']

                def nxt():
                    st = stg32[k32[0] % len(stg32)]
                    k32[0] += 1
                    return st

                def f_tok(m, bk, off, n):
                    d = off // 1024
                    col = off % 1024
                    st = nxt()
                    S.op('act', lambda e: e.activation(out=st[:, 0:n], in_=bk[:, 0:n], func=AF.Sigmoid, scale=-1.0), r=[bk], w=[st])
                    S.op('dve', lambda e: e.tensor_tensor(out=st[:, 0:n], in0=st[:, 0:n], in1=oml[:, off:off + n], op=ALU.mult), r=[st, oml], w=[st])
                    S.dma('pool', ktoks[d][m * 128:(m + 1) * 128, col:col + n], st[:, 0:n], r=[st])
                    st2 = nxt()
                    S.op('act', lambda e: e.activation(out=st2[:, 0:n], in_=st[:, 0:n], func=AF.Ln, scale=-1.0, bias=oneb[:, 0:1]), r=[st, oneb], w=[st2])
                    S.dma('pool', lgs[d][m * 128:(m + 1) * 128, col:col + n], st2[:, 0:n], r=[st2])

                def f_feat(m, bk, q0, nch):
                    st = nxt()
                    S.op('act', lambda e: e.activation(out=st[:, 0:nch * 128], in_=bk[:, 0:nch * 128], func=AF.Sigmoid, scale=-1.0), r=[bk], w=[st])
                    for cc in range(nch):
                        ch = q0 + cc
                        S.op('dve', lambda e, cc=cc, ch=ch: e.tensor_scalar(out=st[:, cc * 128:(cc + 1) * 128], in0=st[:, cc * 128:(cc + 1) * 128], scalar1=omlT[:, 16 + ch:17 + ch], scalar2=None, op0=ALU.mult), r=[st, omlT], w=[st])
                    d = q0 // 8
                    r0 = (q0 % 8) * 128
                    S.dma('pool', kTs[d][r0:r0 + nch * 128, m * 128:(m + 1) * 128].rearrange("(c p) t -> p c t", p=128),
                          st[:, 0:nch * 128].rearrange("p (c t) -> p c t", c=nch), r=[st])
                tok_groups.append((1024, 1024, make_tok_store(c, vtok, 0, 1024, dt=BF16)))
                tok_groups.append((2048, 2048, f_tok))
                tok_groups.append((4096, 1024, make_tok_store(c, rs, 0, 1024)))
                tok_groups.append((5120, 384, make_tok_store(c, utok, 0, 384)))
                feat_groups.append((0, 8, 128, make_feat_store(c, qT, 0, func=AF.Silu)))
                feat_groups.append((2048, 16, 128, f_feat))
            proj_phase(1, w_in1, 5504, tok_groups, feat_groups, extra)

        def phaseC1():
            ph = Phase()
            cst = load_consts(ph)
            PI = float(np.pi)
            prm = ph.tile([128, 72], F32, "prm")
            S.dma('sp', prm[:, :], s5p[:, :], w=[prm])
            bri = ph.tile([128, 384], F32, "bri")
            S.dma('sp', bri[:, :], s5b[:, :], w=[bri])
            cri = ph.tile([128, 384], F32, "cri")
            S.dma('sp', cri[:, :], s5c[:, :], w=[cri])
            dsk = ph.tile([128, 3], F32, "dsk")
            S.dma('sp', dsk[:, :], s5d[:, :], w=[dsk])
            negpi = ph.tile([128, 1], F32, "negpi")
            S.op('dve', lambda e: e.memset(negpi[:, :], -PI), w=[negpi])
            w12 = ph.tile([128, 12 * 12], F32, "w12")

            def V(i):
                return w12[:, i * 12:(i + 1) * 12]
            CX = [ph.tile([128, 12 * 128], F32, "CX") for _ in range(2)]
            bbX = ph.tile([128, 12 * 128], F32, "bbX")
            BbT = [[ph.tile([128, 12 * 128], F32, "BbT") for _ in range(2)] for _ in range(2)]
            Ecs = [[ph.tile([128, 12 * 128], F32, "E") for _ in range(2)] for _ in range(2)]
            rmag = [ph.tile([128, 12 * 128], F32, "rmag") for _ in range(2)]
            bb = ph.tile([128, 2 * 192], F32, "bb")
            tA = ph.tile([128, 12 * 128], F32, "tA")
            tB = ph.tile([128, 12 * 128], F32, "tB")

            def place(dst, src_ap3, negate=False):
                S.op('dve', lambda e: e.memset(dst[:, :], 0.0), w=[dst])
                for sc in range(12):
                    for g2 in range(2):
                        gl = (2 * sc + g2) % 8
                        p0, p1 = g2 * 64, (g2 + 1) * 64
                        if negate:
                            S.op('dve', lambda e, sc=sc, gl=gl, p0=p0, p1=p1: e.tensor_scalar(out=dst[p0:p1, sc * 128 + gl * 16:sc * 128 + gl * 16 + 16], in0=src_ap3(sc, p0, p1), scalar1=-1.0, scalar2=None, op0=ALU.mult), r=[bb, cri], w=[dst])
                        else:
                            S.op('dve', lambda e, sc=sc, gl=gl, p0=p0, p1=p1: e.tensor_copy(out=dst[p0:p1, sc * 128 + gl * 16:sc * 128 + gl * 16 + 16], in_=src_ap3(sc, p0, p1)), r=[bb, cri], w=[dst])
            place(CX[0], lambda sc, p0, p1: cri[p0:p1, sc * 16:(sc + 1) * 16])
            place(CX[1], lambda sc, p0, p1: cri[p0:p1, 192 + sc * 16:192 + (sc + 1) * 16], negate=True)

            def tt(out, a, b, op, r, w):
                S.op('dve', lambda e: e.tensor_tensor(out=out, in0=a, in1=b, op=op), r=r, w=w)

            def sin_of(out, theta, shift):
                S.op('dve', lambda e: e.tensor_scalar(out=V(10), in0=theta, scalar1=shift + PI, scalar2=None, op0=ALU.add), r=[w12], w=[w12])
                for kk in range(1, 5):
                    S.op('dve', lambda e, kk=kk: e.tensor_scalar(out=V(11), in0=V(10), scalar1=2.0 * PI * kk, scalar2=-2.0 * PI, op0=ALU.is_ge, op1=ALU.mult), r=[w12], w=[w12])
                    if kk == 1:
                        S.op('dve', lambda e: e.tensor_tensor(out=V(9), in0=V(10), in1=V(11), op=ALU.add), r=[w12], w=[w12])
                    else:
                        S.op('dve', lambda e: e.tensor_tensor(out=V(9), in0=V(9), in1=V(11), op=ALU.add), r=[w12], w=[w12])
                S.op('act', lambda e: e.activation(out=out, in_=V(9), func=AF.Sin, bias=negpi[:, 0:1]), r=[w12, negpi], w=[w12])

            for d in range(2):
                are = prm[:, d * 36:d * 36 + 12]
                aim = prm[:, d * 36 + 12:d * 36 + 24]
                ldt = prm[:, d * 36 + 24:d * 36 + 36]
                S.op('act', lambda e, ldt=ldt: e.activation(out=V(0), in_=ldt, func=AF.Exp), r=[prm], w=[w12])
                tt(V(1), are, V(0), ALU.mult, [prm, w12], [w12])
                S.op('act', lambda e: e.activation(out=V(1), in_=V(1), func=AF.Exp), r=[w12], w=[w12])
                tt(V(2), aim, V(0), ALU.mult, [prm, w12], [w12])
                sin_of(V(3), V(2), 0.0)
                sin_of(V(4), V(2), PI / 2)
                tt(V(5), V(1), V(4), ALU.mult, [w12], [w12])
                tt(V(6), V(1), V(3), ALU.mult, [w12], [w12])
                tt(V(7), are, are, ALU.mult, [prm], [w12])
                tt(V(8), aim, aim, ALU.mult, [prm], [w12])
                tt(V(7), V(7), V(8), ALU.add, [w12], [w12])
                S.op('dve', lambda e: e.reciprocal(out=V(7), in_=V(7)), r=[w12], w=[w12])
                S.op('dve', lambda e: e.tensor_scalar(out=V(5), in0=V(5), scalar1=-1.0, scalar2=None, op0=ALU.add), r=[w12], w=[w12])
                tt(V(8), V(5), are, ALU.mult, [w12, prm], [w12])
                tt(V(9), V(6), aim, ALU.mult, [w12, prm], [w12])
                tt(V(8), V(8), V(9), ALU.add, [w12], [w12])
                tt(V(8), V(8), V(7), ALU.mult, [w12], [w12])
                tt(V(9), V(6), are, ALU.mult, [w12, prm], [w12])
                tt(V(10), V(5), aim, ALU.mult, [w12, prm], [w12])
                tt(V(9), V(9), V(10), ALU.subtract, [w12], [w12])
                tt(V(9), V(9), V(7), ALU.mult, [w12], [w12])
                b3 = lambda c0: bri[:, c0:c0 + 192].rearrange("p (s c) -> p s c", s=12)
                o3 = lambda c0: bb[:, c0:c0 + 192].rearrange("p (s c) -> p s c", s=12)
                t3 = tA[:, 0:192].rearrange("p (s c) -> p s c", s=12)
                zr3 = V(8).unsqueeze(2).to_broadcast([128, 12, 16])
                zi3 = V(9).unsqueeze(2).to_broadcast([128, 12, 16])
                tt(o3(0), b3(0), zr3, ALU.mult, [bri, w12], [bb])
                tt(t3, b3(192), zi3, ALU.mult, [bri, w12], [tA])
                tt(o3(0), o3(0), t3, ALU.subtract, [bb, tA], [bb])
                tt(o3(192), b3(192), zr3, ALU.mult, [bri, w12], [bb])
                tt(t3, b3(0), zi3, ALU.mult, [bri, w12], [tA])
                tt(o3(192), o3(192), t3, ALU.add, [bb, tA], [bb])
                for ri in range(2):
                    place(bbX, lambda sc, p0, p1, ri=ri: bb[p0:p1, ri * 192 + sc * 16:ri * 192 + (sc + 1) * 16])
                    for q in range(3):
                        bk = banks[q]
                        for c4 in range(4):
                            sc = q * 4 + c4
                            S.op('pe', lambda e, bk=bk, c4=c4, sc=sc: e.transpose(bk[:, c4 * 128:(c4 + 1) * 128], bbX[:, sc * 128:(sc + 1) * 128], cst[:, C_I:C_I + 128]), r=[bbX, cst], w=[bk])
                        S.op('act', lambda e, bk=bk, q=q, ri=ri, d=d: e.activation(out=BbT[d][ri][:, q * 512:(q + 1) * 512], in_=bk[:, :], func=AF.Copy), r=[bk], w=[BbT[d][ri]])
                Ec, Es = Ecs[d]
                Ec3 = Ec[:, :].rearrange("p (s t) -> p s t", s=12)
                Es3 = Es[:, :].rearrange("p (s t) -> p s t", s=12)
                i0 = 0 if d == 0 else 127
                S.op('dve', lambda e: e.tensor_copy(out=Ec3[:, :, i0:i0 + 1], in_=V(4).unsqueeze(2)), r=[w12], w=[Ec])
                S.op('dve', lambda e: e.tensor_copy(out=Es3[:, :, i0:i0 + 1], in_=V(3).unsqueeze(2)), r=[w12], w=[Es])
                tA3 = tA[:, :].rearrange("p (s t) -> p s t", s=12)
                tB3 = tB[:, :].rearrange("p (s t) -> p s t", s=12)
                for k in range(7):
                    n = 1 << k
                    if d == 0:
                        src = slice(0, n); dst = slice(n, 2 * n); piv = n - 1
                    else:
                        src = slice(128 - n, 128); dst = slice(128 - 2 * n, 128 - n); piv = 128 - n
                    pc = Ec3[:, :, piv:piv + 1].to_broadcast([128, 12, n])
                    ps_ = Es3[:, :, piv:piv + 1].to_broadcast([128, 12, n])
                    tt(tA3[:, :, 0:n], Ec3[:, :, src], pc, ALU.mult, [Ec], [tA])
                    tt(tB3[:, :, 0:n], Es3[:, :, src], ps_, ALU.mult, [Es], [tB])
                    tt(tA3[:, :, 0:n], tA3[:, :, 0:n], tB3[:, :, 0:n], ALU.subtract, [tA, tB], [tA])
                    tt(tB3[:, :, 0:n], Ec3[:, :, src], ps_, ALU.mult, [Ec, Es], [tB])
                    tt(tA3[:, :, 64:64 + n], Es3[:, :, src], pc, ALU.mult, [Ec, Es], [tA])
                    tt(Es3[:, :, dst], tB3[:, :, 0:n], tA3[:, :, 64:64 + n], ALU.add, [tA, tB], [Es])
                    S.op('dve', lambda e, dst=dst, n=n: e.tensor_copy(out=Ec3[:, :, dst], in_=tA3[:, :, 0:n]), r=[tA], w=[Ec])
                S.op('dve', lambda e, d=d: e.tensor_copy(out=rmag[d][:, :].rearrange("p (s t) -> p s t", s=12), in_=V(1).unsqueeze(2).to_broadcast([128, 12, 128])), r=[w12], w=[rmag[d]])

            def mk(shape, dt, name):
                return [ph.tile(shape, dt, name) for _ in range(2)]
            ut = mk([128, 384], F32, "ut")
            uT = mk([128, 384], F32, "uT")
            wre = mk([128, 1536], F32, "wre")
            wim = mk([128, 1536], F32, "wim")
            zre = mk([128, 1536], F32, "zre")
            zim = mk([128, 1536], F32, "zim")
            tmpA = [bbX, ph.tile([128, 1536], F32, "tmpA")]
            tmpB = [tA, ph.tile([128, 1536], F32, "tmpB")]
            tmpP = [tB, ph.tile([128, 1536], F32, "tmpP")]
            xst = mk([128, 24], F32, "xst")
            ysb = mk([128, 384], F32, "ysb")
            for d in range(2):
                S.op('dve', lambda e, d=d: e.memset(xst[d][:, :], 0.0), w=[xst[d]])

            def rev(buf, c0, n):
                a = buf[:, c0:c0 + n]
                return bass.AP(a.tensor, a.offset + (n - 1), [[a.ap[0][0], 128], [-1, n]])

            def step(m, d):
                bk4 = banks[4 * d:4 * d + 4]
                t0 = m * 128
                Ec, Es = Ecs[d]
                tA_, tB_, tP_ = tmpA[d], tmpB[d], tmpP[d]
                S.dma('sp', ut[d][:, :], utok[t0:t0 + 128, :], w=[ut[d]])
                b6 = bk4[3]
                for cc in range(3):
                    S.op('pe', lambda e, cc=cc: e.transpose(b6[:, cc * 128:(cc + 1) * 128], ut[d][:, cc * 128:(cc + 1) * 128], cst[:, C_I:C_I + 128]), r=[ut[d], cst], w=[b6])
                yield
                S.op('act', lambda e: e.activation(out=uT[d][:, :], in_=b6[:, 0:384], func=AF.Copy), r=[b6], w=[uT[d]])
                for ri in range(2):
                    for sc in range(12):
                        bk = bk4[sc // 4]
                        cc = sc // 4
                        S.op('pe', lambda e, bk=bk, sc=sc, cc=cc, ri=ri: e.matmul(bk[:, (sc % 4) * 128:(sc % 4 + 1) * 128], lhsT=BbT[d][ri][:, sc * 128:(sc + 1) * 128], rhs=uT[d][:, cc * 128:(cc + 1) * 128], start=True, stop=True), r=[BbT[d][ri], uT[d]], w=[bk])
                    yield
                    for q in range(3):
                        sl = slice(q * 512, (q + 1) * 512)
                        bq = bk4[q]
                        if ri == 0:
                            S.op('dve', lambda e, sl=sl, bq=bq: e.tensor_tensor(out=wre[d][:, sl], in0=bq[:, :], in1=Ec[:, sl], op=ALU.mult), r=[bq, Ec], w=[wre[d]])
                            S.op('dve', lambda e, sl=sl, bq=bq: e.tensor_tensor(out=wim[d][:, sl], in0=bq[:, :], in1=Es[:, sl], op=ALU.mult), r=[bq, Es], w=[wim[d]])
                        else:
                            S.op('dve', lambda e, sl=sl, bq=bq: e.tensor_tensor(out=tA_[:, sl], in0=bq[:, :], in1=Es[:, sl], op=ALU.mult), r=[bq, Es], w=[tA_])
                            S.op('dve', lambda e, sl=sl, bq=bq: e.tensor_tensor(out=tB_[:, sl], in0=bq[:, :], in1=Ec[:, sl], op=ALU.mult), r=[bq, Ec], w=[tB_])
                    yield
                S.op('pool', lambda e: e.tensor_tensor(out=wre[d][:, :], in0=wre[d][:, :], in1=tA_[:, :], op=ALU.add), r=[wre[d], tA_], w=[wre[d]])
                S.op('dve', lambda e: e.tensor_tensor(out=wim[d][:, :], in0=tB_[:, :], in1=wim[d][:, :], op=ALU.subtract), r=[wim[d], tB_], w=[wim[d]])
                yield
                for (wb, zb, c0_) in ((wre[d], zre[d], 0), (wim[d], zim[d], 12)):
                    for sc in range(12):
                        ci = c0_ + sc
                        if d == 0:
                            S.op('dve', lambda e, wb=wb, zb=zb, ci=ci, sc=sc: e.tensor_tensor_scan(out=zb[:, sc * 128:(sc + 1) * 128], data0=rmag[d][:, sc * 128:(sc + 1) * 128], data1=wb[:, sc * 128:(sc + 1) * 128], initial=xst[d][:, ci:ci + 1], op0=ALU.mult, op1=ALU.add), r=[wb, rmag[d], xst[d]], w=[zb])
                        else:
                            S.op('dve', lambda e, wb=wb, zb=zb, ci=ci, sc=sc: e.tensor_tensor_scan(out=rev(zb, sc * 128, 128), data0=rmag[d][:, sc * 128:(sc + 1) * 128], data1=rev(wb, sc * 128, 128), initial=xst[d][:, ci:ci + 1], op0=ALU.mult, op1=ALU.add), r=[wb, rmag[d], xst[d]], w=[zb])
                    yield
                S.op('dve', lambda e: e.tensor_tensor(out=tA_[:, :], in0=zre[d][:, :], in1=Ec[:, :], op=ALU.mult), r=[zre[d], Ec], w=[tA_])
                S.op('dve', lambda e: e.tensor_tensor(out=tB_[:, :], in0=zim[d][:, :], in1=Es[:, :], op=ALU.mult), r=[zim[d], Es], w=[tB_])
                S.op('pool', lambda e: e.tensor_tensor(out=wim[d][:, :], in0=zre[d][:, :], in1=Es[:, :], op=ALU.mult), r=[zre[d], Es], w=[wim[d]])
                S.op('pool', lambda e: e.tensor_tensor(out=tP_[:, :], in0=zim[d][:, :], in1=Ec[:, :], op=ALU.mult), r=[zim[d], Ec], w=[tP_])
                yield
                S.op('dve', lambda e: e.tensor_tensor(out=wre[d][:, :], in0=tA_[:, :], in1=tB_[:, :], op=ALU.subtract), r=[tA_, tB_], w=[wre[d]])
                S.op('pool', lambda e: e.tensor_tensor(out=wim[d][:, :], in0=wim[d][:, :], in1=tP_[:, :], op=ALU.add), r=[wim[d], tP_], w=[wim[d]])
                yield
                last = 127 if d == 0 else 0
                S.op('dve', lambda e: e.tensor_copy(out=xst[d][:, 0:12].unsqueeze(2), in_=wre[d][:, :].rearrange("p (s t) -> p s t", s=12)[:, :, last:last + 1]), r=[wre[d]], w=[xst[d]])
                S.op('dve', lambda e: e.tensor_copy(out=xst[d][:, 12:24].unsqueeze(2), in_=wim[d][:, :].rearrange("p (s t) -> p s t", s=12)[:, :, last:last + 1]), r=[wim[d]], w=[xst[d]])
                b7 = bk4[3]
                for cc in range(3):
                    for k4 in range(4):
                        sc = cc * 4 + k4
                        S.op('pe', lambda e, cc=cc, sc=sc, k4=k4: e.matmul(b7[:, cc * 128:(cc + 1) * 128], lhsT=CX[0][:, sc * 128:(sc + 1) * 128], rhs=wre[d][:, sc * 128:(sc + 1) * 128], start=(k4 == 0), stop=False), r=[CX[0], wre[d]], w=[b7])
                        S.op('pe', lambda e, cc=cc, sc=sc, k4=k4: e.matmul(b7[:, cc * 128:(cc + 1) * 128], lhsT=CX[1][:, sc * 128:(sc + 1) * 128], rhs=wim[d][:, sc * 128:(sc + 1) * 128], start=False, stop=(k4 == 3)), r=[CX[1], wim[d]], w=[b7])
                yield
                if d == 0:
                    for cc in range(3):
                        S.op('dve', lambda e, cc=cc: e.scalar_tensor_tensor(out=ysb[d][:, cc * 128:(cc + 1) * 128], in0=uT[d][:, cc * 128:(cc + 1) * 128], scalar=dsk[:, cc:cc + 1], in1=b7[:, cc * 128:(cc + 1) * 128], op0=ALU.mult, op1=ALU.add), r=[uT[d], dsk, b7], w=[ysb[d]])
                else:
                    S.op('act', lambda e: e.activation(out=ysb[d][:, :], in_=b7[:, 0:384], func=AF.Copy), r=[b7], w=[ysb[d]])
                S.dma('pool', yT5[d][:, t0:t0 + 128].rearrange("(c p) t -> p c t", p=128), ysb[d][:, :].rearrange("p (c t) -> p c t", c=3), r=[ysb[d]])

            order_f = list(range(NT))
            order_b = list(range(NCT - 1, -1, -1)) + list(range(NT - 1, NCT - 1, -1))
            for i in range(NT):
                run_pair(step(order_f[i], 0), step(order_b[i], 1))
            ph.close()

        def phaseD1a():
            ph = Phase()
            cst = load_consts(ph)
            Wo = ph.tile([128, 11 * D], BF16, "Wo1")
            load_weight_bf16(ph, Wo, w_out1, 1408, D)
            hnw = bc_load(ph, hgnw[0:1, :], 128, "hnw")
            g1 = bc_load(ph, modv[1, 0:1, 2 * D:3 * D], D, "g1")
            gw = ph.tile([128, 3 * 384], F32, "gw")
            S.dma('sp', gw[:, :].rearrange("p (c n) -> p c n", c=3), gluw[:, :].rearrange("(c p) n -> p c n", p=128), w=[gw])
            gb = ph.tile([128, 3], F32, "gb")
            S.dma('sp', gb[:, :], glub[:, :], w=[gb])
            epsb = ph.tile([128, 1], F32, "epsb")
            S.op('dve', lambda e: e.memset(epsb[:, :], EPS), w=[epsb])
            idb = ph.tile([128, 128], BF16, "idb")
            S.op('dve', lambda e: e.tensor_copy(out=idb[:, :], in_=cst[:, C_I:C_I + 128]), r=[cst], w=[idb])
            a = ph.tile([128, D], F32, "a")
            b = ph.tile([128, D], F32, "b")
            g = ph.tile([128, D], F32, "g")
            sq = ph.tile([128, D], F32, "sq")
            tmp = ph.tile([128, D], F32, "tmp")
            xt = ph.tile([128, D], F32, "xt")
            mix = ph.tile([128, 1024], BF16, "mix")
            mixT = ph.tile([128, 11 * 128], BF16, "mixT")
            ss = ph.tile([128, 16], F32, "ss")
            rstd = ph.tile([128, 16], F32, "rstd")
            xo = ph.tile([128, D], F32, "xo")
            ya = ph.tile([128, 384], F32, "ya")
            yb = ph.tile([128, 384], F32, "yb")
            yc = ph.tile([128, 384], F32, "yc")
            for m in lat_tiles:
                r0 = m * 128
                S.dma('sp', a[:, :], odir[0][r0:r0 + 128, :], w=[a])
                S.dma('act', b[:, :], odir[1][r0:r0 + 128, :], w=[b])
                S.dma('sp', g[:, :], rs[r0:r0 + 128, :], w=[g])
                S.op('dve', lambda e: e.tensor_tensor(out=a[:, :], in0=a[:, :], in1=b[:, :], op=ALU.add), r=[a, b], w=[a])
                S.op('act', lambda e: e.activation(out=g[:, :], in_=g[:, :], func=AF.Silu), r=[g], w=[g])
                head_rms(ph, a, mix, mix[:, 0:1024], 8, 128, hnw, hnw[:, 0:128].unsqueeze(1).to_broadcast([128, 8, 128]), g, sq, ss, rstd, epsb, tmp)
                for q in range(2):
                    bk = banks[q]
                    bkb = bk[:, 0:256].bitcast(BF16)
                    for c in range(4):
                        cc = q * 4 + c
                        S.op('pe', lambda e, bkb=bkb, c=c, cc=cc: e.transpose(bkb[:, c * 128:(c + 1) * 128], mix[:, cc * 128:(cc + 1) * 128], idb[:, :]), r=[mix, idb], w=[bk])
                    S.op('act', lambda e, bkb=bkb, q=q: e.activation(out=mixT[:, q * 512:(q + 1) * 512], in_=bkb[:, 0:512], func=AF.Copy), r=[bk], w=[mixT])
                S.dma('sp', ya[:, :].rearrange("p (c t) -> p c t", c=3), yT5[0][:, r0:r0 + 128].rearrange("(c p) t -> p c t", p=128), w=[ya])
                S.dma('act', yb[:, :].rearrange("p (c t) -> p c t", c=3), yT5[1][:, r0:r0 + 128].rearrange("(c p) t -> p c t", p=128), w=[yb])
                S.op('dve', lambda e: e.tensor_tensor(out=ya[:, :], in0=ya[:, :], in1=yb[:, :], op=ALU.add), r=[ya, yb], w=[ya])
                S.op('dve', lambda e: e.tensor_tensor(out=yb[:, :], in0=ya[:, :], in1=ya[:, :], op=ALU.mult), r=[ya], w=[yb])
                S.op('dve', lambda e: e.tensor_scalar(out=yb[:, :], in0=yb[:, :], scalar1=0.044715, scalar2=1.0, op0=ALU.mult, op1=ALU.add), r=[yb], w=[yb])
                S.op('dve', lambda e: e.tensor_tensor(out=yb[:, :], in0=yb[:, :], in1=ya[:, :], op=ALU.mult), r=[yb, ya], w=[yb])
                S.op('act', lambda e: e.activation(out=yb[:, :], in_=yb[:, :], func=AF.Tanh, scale=0.7978845608028654), r=[yb], w=[yb])
                S.op('dve', lambda e: e.scalar_tensor_tensor(out=yb[:, :], in0=yb[:, :], scalar=1.0, in1=ya[:, :], op0=ALU.add, op1=ALU.mult), r=[yb, ya], w=[yb])
                S.op('dve', lambda e: e.tensor_scalar(out=yb[:, :], in0=yb[:, :], scalar1=0.5, scalar2=None, op0=ALU.mult), r=[yb], w=[yb])
                b2 = banks[2]
                for co in range(3):
                    for ci in range(3):
                        S.op('pe', lambda e, co=co, ci=ci: e.matmul(b2[:, co * 128:(co + 1) * 128], lhsT=gw[:, ci * 384 + co * 128:ci * 384 + (co + 1) * 128], rhs=yb[:, ci * 128:(ci + 1) * 128], start=(ci == 0), stop=(ci == 2)), r=[gw, yb], w=[b2])
                for co in range(3):
                    S.op('act', lambda e, co=co: e.activation(out=yc[:, co * 128:(co + 1) * 128], in_=b2[:, co * 128:(co + 1) * 128], func=AF.Sigmoid, bias=gb[:, co:co + 1]), r=[b2, gb], w=[yc])
                S.op('dve', lambda e: e.tensor_tensor(out=mixT[:, 1024:1408], in0=yb[:, :], in1=yc[:, :], op=ALU.mult), r=[yb, yc], w=[mixT])
                S.dma('sp', xt[:, :], xsrc(1, m), w=[xt])
                for h2 in range(2):
                    bk = banks[4 + h2]
                    for c in range(11):
                        S.op('pe', lambda e, bk=bk, c=c, h2=h2: e.matmul(bk[:, 0:512], lhsT=mixT[:, c * 128:(c + 1) * 128], rhs=Wo[:, c * D + h2 * 512:c * D + (h2 + 1) * 512], start=(c == 0), stop=(c == 10)), r=[mixT, Wo], w=[bk])
                    S.op('dve', lambda e, bk=bk, h2=h2: e.tensor_tensor(out=xo[:, h2 * 512:(h2 + 1) * 512], in0=bk[:, 0:512], in1=g1[:, h2 * 512:(h2 + 1) * 512], op=ALU.mult), r=[bk, g1], w=[xo])
                S.op('dve', lambda e: e.tensor_tensor(out=xo[:, :], in0=xo[:, :], in1=xt[:, :], op=ALU.add), r=[xo, xt], w=[xo])
                S.dma('pool', xmid[r0:r0 + 128, :], xo[:, :], r=[xo])
            ph.close()

        lat_tiles = list(range(NCT, NT))
        all_tiles = list(range(NT))
        plist = ['A0', 'B0', 'C0', 'D0a', 'MLP0', 'A1', 'B1', 'C1', 'D1a', 'MLP1']
        nph = len(plist) if upto is None else plist.index(upto) + 1 if upto in plist else 0
        if nph >= 1:
            phaseA0()
        if nph >= 2:
            phaseB0()
        if nph >= 3 and os.environ.get('KGLA', '1') == '1':
            gla_phase(8, 64, 128, qT, [kTs[0], kTs[0]], [ktoks[0], ktoks[0]], vtok, lgs, odir)
        if nph >= 4:
            phaseD0a()
        if nph < 5:
            pass
        elif depth == 1:
            phaseMLP(0, lat_tiles, True)
        else:
            phaseMLP(0, all_tiles, False)
            if nph >= 6:
                phaseA1()
            if nph >= 7:
                gla_phase(8, 128, 128, qT, kTs, ktoks, vtok, lgs, odir)
            if nph >= 8:
                phaseC1()
            if nph >= 9:
                phaseD1a()
            if nph >= 10:
                phaseMLP(1, lat_tiles, True)
        S.barrier()
    return nc


def make_consts():
    i = np.arange(128)
    t = i[:, None]
    j = i[None, :]
    eye = (t == j)
    TF = (t <= j)
    TB = (t >= j)
    SF = (t > j)
    SB = (t < j)
    MBF = np.where(j >= t, 0.0, NEG)
    MBB = np.where(j <= t, 0.0, NEG)
    ONE = np.ones((128, 128))
    w = np.arange(258)
    ML = np.broadcast_to((w % 64 != 0).astype(np.float32)[None, :], (128, 258))
    MR = np.broadcast_to((w % 64 != 1).astype(np.float32)[None, :], (128, 258))
    return np.concatenate([eye, TF, TB, SF, SB, MBF, MBB, ONE, ML, MR], axis=1).astype(np.float32)


def host_inputs(inp, b, depth=2):
    f = lambda a: np.ascontiguousarray(np.asarray(a, dtype=np.float32))
    m = {}
    m["x"] = f(inp["x"][b])
    m["ctx"] = f(inp["ctx"][b])
    cs = np.stack([np.asarray(inp["c"][b]).reshape(8, 128).T, np.asarray(inp["c_ctx"]).reshape(8, 128).T], axis=-1)
    m["cs"] = f(cs)
    m["ada_w"] = f(inp["ada_w"])
    m["ada_b"] = f(inp["ada_b"])
    m["norm1_w"] = f(inp["norm1_w"])
    m["norm2_w"] = f(inp["norm2_w"])
    m["final_norm_w"] = f(np.asarray(inp["final_norm_w"]).reshape(1, D))
    m["consts"] = make_consts()
    m["w_in0"] = f(inp["ssd_gla_w_in"][0])
    cw = np.asarray(inp["ssd_conv_w"][0]).reshape(9, 16, 128)
    m["convw"] = f(cw.transpose(2, 1, 0).reshape(128, 144))
    m["convb"] = f(np.asarray(inp["ssd_conv_b"][0]).reshape(16, 128).T)
    m["dtb"] = f(np.asarray(inp["ssd_dt_bias"][0]).reshape(1, 32))
    m["alog"] = f(np.asarray(inp["ssd_a_log"][0]).reshape(1, 32))
    m["ssdd"] = f(np.asarray(inp["ssd_d"][0]).reshape(1, 16))
    m["ssdnw"] = f(np.asarray(inp["ssd_norm_w"][0]).reshape(1, 1024))
    wgm = np.zeros((33, 1024), np.float32)
    gw = np.asarray(inp["gla_gate_w"][0])
    wgm[0:16, 0:512] = gw[0]
    wgm[16:32, 512:1024] = gw[1]
    wgm[32, :] = np.asarray(inp["gla_gate_b"][0]).reshape(1024)
    m["wg"] = wgm
    m["glanw"] = f(np.asarray(inp["gla_norm_w"][0]).reshape(1, 128))
    m["w_out0"] = f(inp["ssd_gla_w_out"][0])
    m["mlp_w1"] = f(inp["mlp_w1"])
    m["mlp_w2"] = f(inp["mlp_w2"])
    if depth > 1:
        m["w_in1"] = f(inp["hgrn_s5_w_in"][0])
        lb = np.asarray(inp["hgrn_lb_logits"], dtype=np.float32)
        m["lbl"] = f(lb.reshape(2, 2048))
        m["lblT"] = f(np.stack([lb[l].reshape(16, 128).T for l in range(2)], axis=1))
        m["hgnw"] = f(np.asarray(inp["hgrn_norm_w"][0]).reshape(1, 128))
        prm = []
        for d in range(2):
            prm.append(np.asarray(inp["s5_a_re"][0][d]).reshape(12, 128).T)
            prm.append(np.asarray(inp["s5_a_im"][0][d]).reshape(12, 128).T)
            prm.append(np.repeat(np.asarray(inp["s5_log_dt"][0][d]).reshape(12, 2, 1), 64, axis=2).reshape(12, 128).T)
        m["s5p"] = f(np.concatenate(prm, axis=1))
        sb = lambda a: np.asarray(a).reshape(12, 128, 16).transpose(1, 0, 2).reshape(128, 192)
        m["s5b"] = f(np.concatenate([sb(inp["s5_b_re"][0]), sb(inp["s5_b_im"][0])], axis=1))
        scf = lambda a: np.asarray(a).reshape(12, 2, 16, 64).transpose(1, 3, 0, 2).reshape(128, 192)
        m["s5c"] = f(np.concatenate([scf(inp["s5_c_re"][0]), scf(inp["s5_c_im"][0])], axis=1))
        m["s5d"] = f(np.asarray(inp["s5_d"][0]).reshape(3, 128).T)
        m["gluw"] = f(inp["s5_glu_w"][0])
        m["glub"] = f(np.asarray(inp["s5_glu_b"][0]).reshape(3, 128).T)
        m["w_out1"] = f(inp["hgrn_s5_w_out"][0])
    return m


_NC_CACHE = {}


def kernel(**inputs):
    B, SEQ = inputs["x"].shape[0], inputs["x"].shape[1]
    key = (SEQ, 2)
    if key not in _NC_CACHE:
        _NC_CACHE[key] = build_nc(SEQ, 2)
    nc = _NC_CACHE[key]
    in_maps = [host_inputs(inputs, b) for b in range(B)]
    res = run_bass_kernel_spmd(nc, in_maps, core_ids=list(range(B)))
    return np.stack([r["out"] for r in res.results], axis=0).astype(np.float32)
```

```python
import os
import numpy as np
from contextlib import ExitStack
KDBG = int(os.environ.get('KDBG', '9'))
import concourse.bass as bass
import concourse.mybir as mybir
from concourse.bass_utils import run_bass_kernel_spmd

F32 = mybir.dt.float32
BF16 = mybir.dt.bfloat16
AF = mybir.ActivationFunctionType
ALU = mybir.AluOpType
AX = mybir.AxisListType

D = 1024
CTX = 256
EPS = 1e-6
NEG = -30000.0


class Buf:
    def __init__(self, t, name):
        self.t = t
        self.name = name
        self.w = None
        self.r = {}

    def __getitem__(self, k):
        return self.t[k]


class Sch:
    def __init__(self, nc, es):
        self.nc = nc
        self.E = {'pe': nc.tensor, 'act': nc.scalar, 'dve': nc.vector, 'pool': nc.gpsimd, 'sp': nc.sync}
        self.semobj = {}
        self.cnt = {}
        self.seen = {}
        for e in self.E:
            self.semobj[e] = es.enter_context(nc.semaphore("s_" + e))
            self.cnt[e] = 0
            self.seen[e] = {}
        self.rings = {}
        for q in ('sp', 'pool', 'act'):
            n = 12
            keys = []
            for i in range(n):
                k = ('d', q, i)
                self.semobj[k] = es.enter_context(nc.semaphore("d_%s_%d" % (q, i)))
                keys.append(k)
            self.rings[q] = {'keys': keys, 'vals': [0] * n, 'i': 0}
        self.nins = 0
        for e in self.E:
            self.E[e].sem_clear(self.semobj[e])
        for q, ring in self.rings.items():
            for k in ring['keys']:
                self.E[q].sem_clear(self.semobj[k])
        nc.all_engine_barrier()

    def _wait(self, e, tok):
        key, val = tok
        if self.seen[e].get(key, 0) >= val:
            return
        self.E[e].wait_ge(self.semobj[key], val)
        self.seen[e][key] = val

    def _deps(self, e, r, w, is_dma):
        for b in r:
            if b.w is not None:
                self._wait(e, b.w)
        for b in w:
            if b.w is not None and (is_dma or b.w[0] != e):
                self._wait(e, b.w)
            for key, val in b.r.items():
                if is_dma or key != e:
                    self._wait(e, (key, val))

    def _upd(self, tok, r, w):
        for b in r:
            b.r[tok[0]] = tok[1]
        for b in w:
            b.w = tok
            b.r = {}

    def op(self, e, fn, r=(), w=()):
        self._deps(e, r, w, False)
        ins = fn(self.E[e])
        self.cnt[e] += 1
        ins.then_inc(self.semobj[e], 1)
        self._upd((e, self.cnt[e]), r, w)
        self.nins += 1

    def dma(self, q, out, in_, r=(), w=()):
        ring = self.rings[q]
        i = ring['i']
        ring['i'] = (i + 1) % len(ring['keys'])
        key = ring['keys'][i]
        if ring['vals'][i] > 0:
            self._wait(q, (key, ring['vals'][i]))
        self._deps(q, r, w, True)
        ins = self.E[q].dma_start(out=out, in_=in_)
        ring['vals'][i] += 16
        ins.then_inc(self.semobj[key], 16)
        self._upd((key, ring['vals'][i]), r, w)
        self.nins += 1

    def barrier(self):
        toks = [(e, self.cnt[e]) for e in self.E if self.cnt[e] > 0]
        for q, ring in self.rings.items():
            for k, v in zip(ring['keys'], ring['vals']):
                if v > 0:
                    toks.append((k, v))
        for e in self.E:
            for tok in toks:
                if tok[0] != e or True:
                    if tok[0] == e:
                        continue
                    self._wait(e, tok)
        for e in self.E:
            if self.cnt[e] > 0:
                self._wait(e, (e, self.cnt[e]))


def build_nc(SEQ, depth=2, dbg=(), upto=None):
    T = CTX + SEQ
    NT = T // 128
    NCT = CTX // 128
    nc = bass.Bass("TRN2", target_bir_lowering=False)

    def din(name, shape, dt=F32):
        return nc.dram_tensor(name, list(shape), dt, kind="ExternalInput").ap()

    def dscr(name, shape, dt=F32):
        kind = "ExternalOutput" if name in dbg else "Internal"
        return nc.dram_tensor(name, list(shape), dt, kind=kind).ap()

    x_in = din("x", [SEQ, D])
    ctx_in = din("ctx", [CTX, D])
    cs_in = din("cs", [128, 8, 2])
    ada_w = din("ada_w", [2, D, 6 * D])
    ada_b = din("ada_b", [2, 6 * D])
    norm1_w = din("norm1_w", [2, D])
    norm2_w = din("norm2_w", [2, D])
    final_w = din("final_norm_w", [1, D])
    consts = din("consts", [128, 128 * 8 + 516])
    w_in0 = din("w_in0", [D, 6208])
    convw = din("convw", [128, 16 * 9])
    convb = din("convb", [128, 16])
    dtb = din("dtb", [1, 32])
    alog = din("alog", [1, 32])
    ssdd = din("ssdd", [1, 16])
    ssdnw = din("ssdnw", [1, 1024])
    wg = din("wg", [33, 1024])
    glanw = din("glanw", [1, 128])
    w_out0 = din("w_out0", [2048, D])
    mlp_w1 = din("mlp_w1", [2, D, 4096])
    mlp_w2 = din("mlp_w2", [2, 4096, D])
    if depth > 1:
        w_in1 = din("w_in1", [D, 5504])
        lbl = din("lbl", [2, 2048])
        lblT = din("lblT", [128, 2, 16])
        hgnw = din("hgnw", [1, 128])
        s5p = din("s5p", [128, 2 * 3 * 12])
        s5b = din("s5b", [128, 2 * 12 * 16])
        s5c = din("s5c", [128, 2 * 12 * 16])
        s5d = din("s5d", [128, 3])
        gluw = din("gluw", [384, 384])
        glub = din("glub", [128, 3])
        w_out1 = din("w_out1", [1408, D])
    out = nc.dram_tensor("out", [SEQ, D], F32, kind="ExternalOutput").ap()

    modv = dscr("modv", [2, 2, 6 * D])
    xres = dscr("xres", [T, D])
    xmid = dscr("xmid", [T, D])
    zs = dscr("zs", [T, D])
    rs = dscr("rs", [T, D])
    vtok = dscr("vtok", [T, D], BF16)
    TP = T + 256
    xbcT = dscr("xbcT", [2048, TP], BF16)
    dts = dscr("dts", [T, 32])
    qT = dscr("qT", [1024, T])
    kTs = [dscr("kT0", [1024, T]), dscr("kT1", [1024, T])]
    ktoks = [dscr("ktok0", [T, 1024]), dscr("ktok1", [T, 1024])]
    lgs = [dscr("lg0", [T, 1024]), dscr("lg1", [T, 1024])]
    ydir = [dscr("ydir0", [T, D]), dscr("ydir1", [T, D])]
    odir = [dscr("odir0", [T, D]), dscr("odir1", [T, D])]
    utok = dscr("utok", [T, 384])
    yT5 = [dscr("yT5_0", [384, T]), dscr("yT5_1", [384, T])]

    es = ExitStack()
    with es:
        S = Sch(nc, es)
        banks = [Buf(es.enter_context(nc.psum_tensor("bank%d" % i, [128, 512], F32)), "bank%d" % i) for i in range(8)]

        class Phase:
            def __init__(self):
                self.es = ExitStack()
                self.n = 0

            def tile(self, shape, dt=F32, name=None):
                self.n += 1
                S.uid = getattr(S, "uid", 0) + 1
                nm = "%s_%d" % (name or "t", S.uid)
                return Buf(self.es.enter_context(nc.sbuf_tensor(nm, list(shape), dt)), nm)

            def close(self):
                S.barrier()
                self.es.close()

        def run_seq(g0, g1):
            for _ in g0:
                pass
            for _ in g1:
                pass

        def run_pair(g0, g1):
            gens = [g0, g1]
            alive = [True, True]
            while alive[0] or alive[1]:
                for i in range(2):
                    if alive[i]:
                        try:
                            next(gens[i])
                        except StopIteration:
                            alive[i] = False

        def load_consts(ph):
            cst = ph.tile([128, 128 * 8 + 516], F32, "cst")
            S.dma('sp', cst[:, :], consts[:, :], w=[cst])
            return cst

        C_I, C_TF, C_TB, C_SF, C_SB, C_MBF, C_MBB, C_ONE = [i * 128 for i in range(8)]
        C_ML = 1024
        C_MR = 1024 + 258

        def xsrc(layer, m):
            if layer == 0:
                if m < NCT:
                    return ctx_in[m * 128:(m + 1) * 128, :]
                return x_in[(m - NCT) * 128:(m - NCT + 1) * 128, :]
            return xres[m * 128:(m + 1) * 128, :]

        def bc_load(ph, src_row_ap, n, name):
            t = ph.tile([128, n], F32, name)
            S.dma('sp', t[:, :], src_row_ap.partition_broadcast(128), w=[t])
            return t

        def mod_consts(ph, layer, normw, k_shift, k_scale, src):
            A = bc_load(ph, modv[layer, src:src + 1, k_scale * D:(k_scale + 1) * D], D, "A")
            SH = bc_load(ph, modv[layer, src:src + 1, k_shift * D:(k_shift + 1) * D], D, "SH")
            NW = bc_load(ph, normw[layer:layer + 1, :], D, "NW")
            S.op('dve', lambda e: e.scalar_tensor_tensor(out=A[:, :], in0=A[:, :], scalar=1.0, in1=NW[:, :], op0=ALU.add, op1=ALU.mult), r=[A, NW], w=[A])
            return A, SH

        def load_weight_bf16(ph, Wb, wsrc, K, N, nblk=512):
            J = K // 128
            stg = [ph.tile([128, 2 * 512], F32, "wstg") for _ in range(3)]
            cnt = 0
            for n0 in range(0, N, nblk):
                n1 = min(N, n0 + nblk)
                nn = n1 - n0
                for j0 in range(0, J, 2):
                    j1 = min(J, j0 + 2)
                    st = stg[cnt % 3]
                    S.dma('sp' if cnt % 2 == 0 else 'act', st[:, 0:(j1 - j0) * nn].rearrange("p (j n) -> p j n", n=nn),
                          wsrc[j0 * 128:j1 * 128, n0:n1].rearrange("(j p) n -> p j n", p=128), w=[st])
                    eng = 'dve'
                    for j in range(j0, j1):
                        S.op(eng, lambda e, j=j, st=st: e.tensor_copy(out=Wb[:, j * N + n0:j * N + n1], in_=st[:, (j - j0) * nn:(j - j0 + 1) * nn]), r=[st], w=[Wb])
                    cnt += 1

        def rms_rstd(ph, xt, junk, ss, rstd, epsb, n, dscale):
            S.op('dve', lambda e: e.memset(ss[:, 0:1], 0.0), w=[ss])
            S.op('act', lambda e: e.activation(out=junk[:, 0:n], in_=xt[:, 0:n], func=AF.Square, accum_out=ss[:, 0:1]), r=[ss, xt], w=[junk, ss])
            S.op('act', lambda e: e.activation(out=rstd[:, 0:1], in_=ss[:, 0:1], func=AF.Sqrt, scale=dscale, bias=epsb[:, 0:1]), r=[ss, epsb], w=[rstd])
            S.op('dve', lambda e: e.reciprocal(out=rstd[:, 0:1], in_=rstd[:, 0:1]), r=[rstd], w=[rstd])

        ph = Phase()
        cs = ph.tile([128, 16], F32, "cs")
        S.dma('sp', cs[:, :], cs_in[:, :, :].rearrange("p j s -> p (j s)"), w=[cs])
        S.op('act', lambda e: e.activation(out=cs[:, :], in_=cs[:, :], func=AF.Silu), r=[cs], w=[cs])
        for layer in range(depth):
            adab = ph.tile([2, 6 * D], F32, "adab")
            S.dma('sp', adab[:, :], ada_b[layer:layer + 1, :].partition_broadcast(2), w=[adab])
            mv = ph.tile([2, 6 * D], F32, "mv")
            wst = [ph.tile([128, 8 * 512], F32, "adaw") for _ in range(2)]
            for nb in range(12):
                st = wst[nb % 2]
                S.dma('sp' if nb % 2 == 0 else 'act', st[:, :].rearrange("p (j n) -> p j n", n=512),
                      ada_w[layer, :, nb * 512:(nb + 1) * 512].rearrange("(j p) n -> p j n", p=128), w=[st])
                bk = banks[nb % 8]
                for j in range(8):
                    S.op('pe', lambda e, j=j, st=st, bk=bk: e.matmul(bk[0:2, 0:512], lhsT=cs[:, 2 * j:2 * j + 2], rhs=st[:, j * 512:(j + 1) * 512], start=(j == 0), stop=(j == 7)), r=[cs, st], w=[bk])
                S.op('dve', lambda e, bk=bk, nb=nb: e.tensor_tensor(out=mv[0:2, nb * 512:(nb + 1) * 512], in0=bk[0:2, 0:512], in1=adab[0:2, nb * 512:(nb + 1) * 512], op=ALU.add), r=[bk, adab], w=[mv])
            S.dma('pool', modv[layer, :, :], mv[0:2, :], r=[mv])
        ph.close()

        def proj_phase(layer, w_src, NIN, tok_groups, feat_groups, extra=None, tiles=None):
            ph = Phase()
            cst = load_consts(ph)
            NW = bc_load(ph, norm1_w[layer:layer + 1, :], D, "NW")
            Am = ph.tile([128, D], F32, "Am")
            SHm = ph.tile([128, D], F32, "SHm")
            cur = [None]

            def set_mod(src):
                if cur[0] == src:
                    return
                cur[0] = src
                S.dma('sp', Am[:, :], modv[layer, src:src + 1, D:2 * D].partition_broadcast(128), w=[Am])
                S.dma('sp', SHm[:, :], modv[layer, src:src + 1, 0:D].partition_broadcast(128), w=[SHm])
                S.op('dve', lambda e: e.scalar_tensor_tensor(out=Am[:, :], in0=Am[:, :], scalar=1.0, in1=NW[:, :], op0=ALU.add, op1=ALU.mult), r=[Am, NW], w=[Am])
            Wb = ph.tile([128, 8 * NIN], BF16, "Wb")
            load_weight_bf16(ph, Wb, w_src, D, NIN)
            epsb = ph.tile([128, 1], F32, "epsb")
            S.op('dve', lambda e: e.memset(epsb[:, :], EPS), w=[epsb])
            idb = ph.tile([128, 128], BF16, "idb")
            S.op('dve', lambda e: e.tensor_copy(out=idb[:, :], in_=cst[:, C_I:C_I + 128]), r=[cst], w=[idb])
            xts = [ph.tile([128, D], F32, "xt") for _ in range(2)]
            junk = ph.tile([128, D], BF16, "junk")
            tmp = ph.tile([128, D], F32, "tmp")
            hb = ph.tile([128, D], BF16, "hb")
            hTs = [ph.tile([128, D], BF16, "hT") for _ in range(2)]
            ss = ph.tile([128, 1], F32, "ss")
            rstd = ph.tile([128, 1], F32, "rstd")
            stg32 = [ph.tile([128, 512], F32, "stg32") for _ in range(4)]
            stg16 = [ph.tile([128, 512], BF16, "stg16") for _ in range(3)]
            ctxp = dict(ph=ph, cst=cst, Wb=Wb, NIN=NIN, stg32=stg32, stg16=stg16, k32=[0], k16=[0])
            if extra is not None:
                extra(ctxp)
            bi = [0]

            def nextbank():
                b = banks[bi[0] % 8]
                bi[0] += 1
                return b
            tl = list(tiles if tiles is not None else range(NT))

            def prep_load(i):
                m = tl[i]
                S.dma('sp', xts[i % 2][:, :], xsrc(layer, m), w=[xts[i % 2]])

            def prep_elem(i):
                m = tl[i]
                xt = xts[i % 2]
                rms_rstd(ph, xt, junk, ss, rstd, epsb, D, 1.0 / D)
                set_mod(1 if m < NCT else 0)
                S.op('dve', lambda e, xt=xt: e.scalar_tensor_tensor(out=tmp[:, :], in0=xt[:, :], scalar=rstd[:, 0:1], in1=Am[:, :], op0=ALU.mult, op1=ALU.mult), r=[xt, rstd, Am], w=[tmp])
                S.op('pool', lambda e: e.tensor_tensor(out=hb[:, :], in0=tmp[:, :], in1=SHm[:, :], op=ALU.add), r=[tmp, SHm], w=[hb])

            def prep_T(i):
                hT = hTs[i % 2]
                bk = nextbank()
                bkb = bk[:, :].bitcast(BF16)
                for j in range(8):
                    S.op('pe', lambda e, j=j, bkb=bkb: e.transpose(bkb[:, j * 128:(j + 1) * 128], hb[:, j * 128:(j + 1) * 128], idb[:, :]), r=[hb, idb], w=[bk])
                S.op('act', lambda e, bkb=bkb, hT=hT: e.activation(out=hT[:, :], in_=bkb[:, 0:1024], func=AF.Copy), r=[bk], w=[hT])

            def tokg(i):
                m = tl[i]
                hT = hTs[i % 2]
                for (c0, ncols, fn) in tok_groups:
                    for b0 in range(c0, c0 + ncols, 512):
                        n = min(512, c0 + ncols - b0)
                        bk = nextbank()
                        for j in range(8):
                            S.op('pe', lambda e, j=j, bk=bk, b0=b0, n=n, hT=hT: e.matmul(bk[:, 0:n], lhsT=hT[:, j * 128:(j + 1) * 128], rhs=Wb[:, j * NIN + b0:j * NIN + b0 + n], start=(j == 0), stop=(j == 7)), r=[hT, Wb], w=[bk])
                        fn(m, bk, b0 - c0, n)

            def featg(i):
                m = tl[i]
                hT = hTs[i % 2]
                for (c0, nch_total, Mw, fn) in feat_groups:
                    for q0 in range(0, nch_total, 4):
                        nch = min(4, nch_total - q0)
                        bk = nextbank()
                        for c in range(nch):
                            col = c0 + (q0 + c) * Mw
                            for j in range(8):
                                S.op('pe', lambda e, j=j, bk=bk, c=c, col=col, hT=hT: e.matmul(bk[0:Mw, c * 128:(c + 1) * 128], lhsT=Wb[:, j * NIN + col:j * NIN + col + Mw], rhs=hT[:, j * 128:(j + 1) * 128], start=(j == 0), stop=(j == 7)), r=[hT, Wb], w=[bk])
                        fn(m, bk, q0, nch)

            n_t = len(tl)
            prep_load(0)
            prep_elem(0)
            prep_T(0)
            for i in range(n_t):
                if i + 1 < n_t:
                    prep_load(i + 1)
                    prep_elem(i + 1)
                tokg(i)
                if i + 1 < n_t:
                    prep_T(i + 1)
                featg(i)
            ph.close()

        def make_tok_store(ctxp, dst, dcol0, width, func=None, scale=1.0, dt=F32):
            stg = ctxp['stg32'] if dt == F32 else ctxp['stg16']
            k = ctxp['k32'] if dt == F32 else ctxp['k16']

            def fn(m, bk, off, n):
                st = stg[k[0] % len(stg)]
                k[0] += 1
                if k[0] % 2 == 0:
                    S.op('act', lambda e: e.activation(out=st[:, 0:n], in_=bk[:, 0:n], func=AF.Copy), r=[bk], w=[st])
                else:
                    S.op('dve', lambda e: e.tensor_copy(out=st[:, 0:n], in_=bk[:, 0:n]), r=[bk], w=[st])
                S.dma('pool', dst[m * 128:(m + 1) * 128, dcol0 + off:dcol0 + off + n], st[:, 0:n], r=[st])
            return fn

        def make_feat_store(ctxp, dst, drow0, dt=F32, scale=None, func=AF.Copy, coloff=0):
            stg = ctxp['stg32'] if dt == F32 else ctxp['stg16']
            k = ctxp['k32'] if dt == F32 else ctxp['k16']

            def fn(m, bk, q0, nch):
                st = stg[k[0] % len(stg)]
                k[0] += 1
                if scale is None:
                    S.op('act', lambda e: e.activation(out=st[:, 0:nch * 128], in_=bk[:, 0:nch * 128], func=func), r=[bk], w=[st])
                else:
                    S.op('act', lambda e: e.activation(out=st[:, 0:nch * 128], in_=bk[:, 0:nch * 128], func=func, scale=scale), r=[bk], w=[st])
                r0 = drow0 + q0 * 128
                S.dma('pool', dst[r0:r0 + nch * 128, coloff + m * 128:coloff + (m + 1) * 128].rearrange("(c p) t -> p c t", p=128),
                      st[:, 0:nch * 128].rearrange("p (c t) -> p c t", c=nch), r=[st])
            return fn

        def phaseA0():
            tok_groups = []
            feat_groups = []

            def extra(c):
                ph = c['ph']
                tok_groups.append((0, 1024, make_tok_store(c, zs, 0, 1024)))
                tok_groups.append((3072, 32, make_tok_store(c, dts, 0, 32)))
                tok_groups.append((3616, 512, make_tok_store(c, ktoks[0], 0, 512)))
                tok_groups.append((4128, 1024, make_tok_store(c, vtok, 0, 1024, dt=BF16)))
                tok_groups.append((5184, 1024, make_tok_store(c, rs, 0, 1024)))
                feat_groups.append((1024, 16, 128, make_feat_store(c, xbcT, 0, dt=BF16, coloff=128)))
                feat_groups.append((3104, 4, 128, make_feat_store(c, qT, 0, scale=0.125)))
                feat_groups.append((3616, 4, 128, make_feat_store(c, kTs[0], 0)))
                lrT = ph.tile([33, 128], F32, "lrT")
                S.op('dve', lambda e: e.memset(lrT[32:33, :], 1.0), w=[lrT])
                wgt = ph.tile([33, 1024], F32, "wgt")
                S.dma('sp', wgt[:, :], wg[:, :], w=[wgt])
                gst = [ph.tile([128, 512], F32, "gst") for _ in range(2)]

                def lr_fn(m, bk, q0, nch):
                    S.op('act', lambda e: e.activation(out=lrT[0:32, :], in_=bk[0:32, 0:128], func=AF.Copy), r=[bk], w=[lrT])
                    for d in range(2):
                        bg = banks[(4 + d) % 8]
                        S.op('pe', lambda e, bg=bg, d=d: e.matmul(bg[:, 0:512], lhsT=lrT[0:33, :], rhs=wgt[0:33, d * 512:(d + 1) * 512], start=True, stop=True), r=[lrT, wgt], w=[bg])
                        g = gst[d]
                        S.op('act', lambda e, bg=bg, g=g: e.activation(out=g[:, :], in_=bg[:, 0:512], func=AF.Exp, scale=-1.0), r=[bg], w=[g])
                        S.op('act', lambda e, g=g: e.activation(out=g[:, :], in_=g[:, :], func=AF.Ln, bias=1.0), r=[g], w=[g])
                        S.op('dve', lambda e, g=g: e.tensor_scalar(out=g[:, :], in0=g[:, :], scalar1=-1.0 / 16.0, scalar2=None, op0=ALU.mult), r=[g], w=[g])
                        S.dma('pool', lgs[d][m * 128:(m + 1) * 128, 0:512], g[:, :], r=[g])
                feat_groups.append((5152, 1, 32, lr_fn))
            proj_phase(0, w_in0, 6208, tok_groups, feat_groups, extra)

        def phaseB0():
            ph = Phase()
            cst = load_consts(ph)
            cw = ph.tile([128, 144], F32, "cw")
            S.dma('sp', cw[:, :], convw[:, :], w=[cw])
            cb = ph.tile([128, 16], F32, "cb")
            S.dma('sp', cb[:, :], convb[:, :], w=[cb])
            Dg = ph.tile([128, 144 * 128], BF16, "Dg")
            for i in range(144):
                S.op('dve', lambda e, i=i: e.tensor_scalar(out=Dg[:, i * 128:(i + 1) * 128], in0=cst[:, C_I:C_I + 128], scalar1=cw[:, i:i + 1], scalar2=None, op0=ALU.mult), r=[cst, cw], w=[Dg])
            idb = ph.tile([128, 128], BF16, "idb")
            S.op('dve', lambda e: e.tensor_copy(out=idb[:, :], in_=cst[:, C_I:C_I + 128]), r=[cst], w=[idb])
            mlr = ph.tile([128, 516], BF16, "mlr")
            S.op('dve', lambda e: e.tensor_copy(out=mlr[:, :], in_=cst[:, C_ML:C_ML + 516]), r=[cst], w=[mlr])
            mb4 = []
            for d in range(2):
                t = ph.tile([128, 512], F32, "mb4")
                c0 = C_MBF if d == 0 else C_MBB
                S.op('dve', lambda e, t=t, c0=c0: e.tensor_copy(out=t[:, :].rearrange("p (a i) -> p a i", a=4), in_=cst[:, c0:c0 + 128].unsqueeze(1).to_broadcast([128, 4, 128])), r=[cst], w=[t])
                mb4.append(t)
            dtbb = bc_load(ph, dtb[0:1, :], 32, "dtbb")
            nega = bc_load(ph, alog[0:1, :], 32, "nega")
            S.op('act', lambda e: e.activation(out=nega[:, :], in_=nega[:, :], func=AF.Exp), r=[nega], w=[nega])
            S.op('dve', lambda e: e.tensor_scalar(out=nega[:, :], in0=nega[:, :], scalar1=-1.0, scalar2=None, op0=ALU.mult), r=[nega], w=[nega])
            dskb = bc_load(ph, ssdd[0:1, :], 16, "dskb")
            hT = [ph.tile([128, 1024], F32, "hT") for _ in range(2)]
            hTb = [ph.tile([128, 1024], BF16, "hTb") for _ in range(2)]
            for d in range(2):
                S.op('dve', lambda e, d=d: e.memset(hT[d][:, :], 0.0), w=[hT[d]])
                S.op('dve', lambda e, d=d: e.memset(hTb[d][:, :], 0.0), w=[hTb[d]])

            def mk(shape, dt, name):
                return [ph.tile(shape, dt, name) for _ in range(2)]
            xin = mk([128, 8 * 258], BF16, "xin")
            xl = mk([128, 8 * 258], BF16, "xl")
            xr = mk([128, 8 * 258], BF16, "xr")
            xTt = mk([128, 1024], F32, "xTt")
            BT = mk([128, 512], BF16, "BT")
            CT = mk([128, 512], BF16, "CT")
            xtok = mk([128, 1024], F32, "xtok")
            Btok = mk([128, 512], BF16, "Btok")
            dtr = mk([128, 32], F32, "dtr")
            la = mk([128, 32], F32, "la")
            sm = mk([128, 128], F32, "sm")
            Dx = mk([128, 2048], F32, "Dx")
            seg = mk([128, 2048], F32, "seg")
            scT = mk([128, 2048], BF16, "scT")
            xdt = mk([128, 1024], BF16, "xdt")
            xw = mk([128, 1024], BF16, "xw")
            ysb = mk([128, 1024], F32, "ysb")
            ytmp = mk([128, 1024], F32, "ytmp")

            def step(m, d):
                bk4 = banks[4 * d:4 * d + 4]
                t0 = m * 128
                isctx = m < NCT
                first = (m == 0) or (m == NCT)
                last = (m == NCT - 1) or (m == NT - 1)
                X = xin[d]
                X3 = X[:, :].rearrange("p (c w) -> p c w", c=8)
                w0, w1 = 0, 258
                if first:
                    w0 = 65
                if last:
                    w1 = 193
                if isctx:
                    taps = [(0, -1), (0, 0), (0, 1)]
                else:
                    taps = [(dr, dc) for dr in (-1, 0, 1) for dc in (-1, 0, 1)]
                for half in range(2):
                    if first or last:
                        S.op('dve', lambda e: e.memset(X[:, :], 0.0), w=[X])
                    S.dma('sp', X3[:, :, w0:w1],
                          xbcT[half * 1024:(half + 1) * 1024, t0 + 63 + w0:t0 + 63 + w1].rearrange("(c p) t -> p c t", p=128), w=[X])
                    if isctx:
                        srcs = {-1: X, 0: X, 1: X}
                    else:
                        S.op('dve', lambda e: e.tensor_tensor(out=xl[d][:, :].rearrange("p (c w) -> p c w", c=8), in0=X3, in1=mlr[:, 0:258].unsqueeze(1).to_broadcast([128, 8, 258]), op=ALU.mult), r=[X, mlr], w=[xl[d]])
                        S.op('pool', lambda e: e.tensor_tensor(out=xr[d][:, :].rearrange("p (c w) -> p c w", c=8), in0=X3, in1=mlr[:, 258:516].unsqueeze(1).to_broadcast([128, 8, 258]), op=ALU.mult), r=[X, mlr], w=[xr[d]])
                        srcs = {-1: xl[d], 0: X, 1: xr[d]}
                    yield
                    for c8 in range(8):
                        c = half * 8 + c8
                        bk = bk4[c // 4]
                        for ti, (dr, dc) in enumerate(taps):
                            sb = srcs[dc]
                            o0 = c8 * 258 + 65 + dr * 64 + dc
                            tap = (dr + 1) * 3 + (dc + 1)
                            S.op('pe', lambda e, bk=bk, c=c, sb=sb, o0=o0, tap=tap, ti=ti: e.matmul(bk[:, (c % 4) * 128:(c % 4 + 1) * 128], lhsT=Dg[:, (c * 9 + tap) * 128:(c * 9 + tap + 1) * 128], rhs=sb[:, o0:o0 + 128], start=(ti == 0), stop=(ti == len(taps) - 1)), r=[Dg, sb], w=[bk])
                    yield
                    for c8 in range(8):
                        c = half * 8 + c8
                        bk = bk4[c // 4]
                        if c < 8:
                            dst, oc = xTt[d], c
                        elif c < 12:
                            dst, oc = BT[d], c - 8
                        else:
                            dst, oc = CT[d], c - 12
                        S.op('act', lambda e, bk=bk, c=c, dst=dst, oc=oc: e.activation(out=dst[:, oc * 128:(oc + 1) * 128], in_=bk[:, (c % 4) * 128:(c % 4 + 1) * 128], func=AF.Silu, bias=cb[:, c:c + 1]), r=[bk, cb], w=[dst])
                    yield
                for c in range(8):
                    bk = bk4[c // 4]
                    S.op('pe', lambda e, bk=bk, c=c: e.transpose(bk[:, (c % 4) * 128:(c % 4 + 1) * 128], xTt[d][:, c * 128:(c + 1) * 128], cst[:, C_I:C_I + 128]), r=[xTt[d], cst], w=[bk])
                b2b = bk4[2][:, :].bitcast(BF16)
                for g in range(4):
                    S.op('pe', lambda e, g=g: e.transpose(b2b[:, g * 128:(g + 1) * 128], BT[d][:, g * 128:(g + 1) * 128], idb[:, :]), r=[BT[d], idb], w=[bk4[2]])
                yield
                S.op('act', lambda e: e.activation(out=xtok[d][:, 0:512], in_=bk4[0][:, :], func=AF.Copy), r=[bk4[0]], w=[xtok[d]])
                S.op('dve', lambda e: e.tensor_copy(out=xtok[d][:, 512:1024], in_=bk4[1][:, :]), r=[bk4[1]], w=[xtok[d]])
                S.op('act', lambda e: e.activation(out=Btok[d][:, :], in_=b2b[:, 0:512], func=AF.Copy), r=[bk4[2]], w=[Btok[d]])
                S.dma('sp', dtr[d][:, :], dts[t0:t0 + 128, :], w=[dtr[d]])
                S.op('dve', lambda e: e.tensor_tensor(out=dtr[d][:, :], in0=dtr[d][:, :], in1=dtbb[:, :], op=ALU.add), r=[dtr[d], dtbb], w=[dtr[d]])
                S.op('act', lambda e: e.activation(out=dtr[d][:, :], in_=dtr[d][:, :], func=AF.Exp), r=[dtr[d]], w=[dtr[d]])
                S.op('act', lambda e: e.activation(out=dtr[d][:, :], in_=dtr[d][:, :], func=AF.Ln, bias=1.0), r=[dtr[d]], w=[dtr[d]])
                S.op('dve', lambda e: e.tensor_tensor(out=la[d][:, :], in0=dtr[d][:, :], in1=nega[:, :], op=ALU.mult), r=[dtr[d], nega], w=[la[d]])
                yield
                lad = la[d][:, d * 16:(d + 1) * 16]
                dtd = dtr[d][:, d * 16:(d + 1) * 16]
                tri = C_TF if d == 0 else C_TB
                b7 = bk4[3]
                S.op('pe', lambda e: e.matmul(b7[:, 0:16], lhsT=cst[:, tri:tri + 128], rhs=lad, start=True, stop=True), r=[cst, la[d]], w=[b7])
                S.op('pe', lambda e: e.matmul(b7[:, 16:32], lhsT=cst[:, C_ONE:C_ONE + 128], rhs=lad, start=True, stop=True), r=[cst, la[d]], w=[b7])
                yield
                s = sm[d]
                S.op('dve', lambda e: e.tensor_copy(out=s[:, 0:16], in_=b7[:, 0:16]), r=[b7], w=[s])
                S.op('act', lambda e: e.activation(out=s[:, 16:32], in_=b7[:, 0:16], func=AF.Exp), r=[b7], w=[s])
                S.op('act', lambda e: e.activation(out=s[:, 32:48], in_=b7[:, 16:32], func=AF.Exp), r=[b7], w=[s])
                S.op('dve', lambda e: e.tensor_tensor(out=s[:, 48:64], in0=b7[:, 16:32], in1=s[:, 0:16], op=ALU.subtract), r=[b7, s], w=[s])
                S.op('act', lambda e: e.activation(out=s[:, 48:64], in_=s[:, 48:64], func=AF.Exp), r=[s], w=[s])
                S.op('dve', lambda e: e.tensor_tensor(out=s[:, 64:80], in0=s[:, 48:64], in1=dtd, op=ALU.mult), r=[s, dtr[d]], w=[s])
                S.op('dve', lambda e: e.tensor_tensor(out=Dx[d][:, :].rearrange("p (h i) -> p h i", h=16), in0=cst[:, C_I:C_I + 128].unsqueeze(1).to_broadcast([128, 16, 128]), in1=s[:, 0:16].unsqueeze(2).to_broadcast([128, 16, 128]), op=ALU.mult), r=[cst, s], w=[Dx[d]])
                S.op('pool', lambda e: e.tensor_tensor(out=xdt[d][:, :].rearrange("p (h q) -> p h q", h=16), in0=xtok[d][:, :].rearrange("p (h q) -> p h q", h=16), in1=dtd.unsqueeze(2).to_broadcast([128, 16, 64]), op=ALU.mult), r=[xtok[d], dtr[d]], w=[xdt[d]])
                S.op('pool', lambda e: e.tensor_tensor(out=xw[d][:, :].rearrange("p (h q) -> p h q", h=16), in0=xtok[d][:, :].rearrange("p (h q) -> p h q", h=16), in1=s[:, 64:80].unsqueeze(2).to_broadcast([128, 16, 64]), op=ALU.mult), r=[xtok[d], s], w=[xw[d]])
                yield
                for q in range(4):
                    bk = bk4[q]
                    S.op('pe', lambda e, bk=bk, q=q: e.matmul(bk[:, 0:512], lhsT=cst[:, C_ONE:C_ONE + 128], rhs=Dx[d][:, q * 512:(q + 1) * 512], start=True, stop=False), r=[cst, Dx[d]], w=[bk])
                    S.op('pe', lambda e, bk=bk, q=q: e.matmul(bk[:, 0:512], lhsT=cst[:, C_I:C_I + 128], rhs=mb4[d][:, :], start=False, stop=True), r=[cst, mb4[d]], w=[bk])
                yield
                for q in range(4):
                    bk = bk4[q]
                    S.op('dve', lambda e, bk=bk, q=q: e.tensor_tensor(out=seg[d][:, q * 512:(q + 1) * 512].rearrange("p (h i) -> p h i", h=4), in0=bk[:, 0:512].rearrange("p (h i) -> p h i", h=4), in1=s[:, 4 * q:4 * q + 4].unsqueeze(2).to_broadcast([128, 4, 128]), op=ALU.subtract), r=[bk, s], w=[seg[d]])
                yield
                S.op('act', lambda e: e.activation(out=seg[d][:, :], in_=seg[d][:, :], func=AF.Exp), r=[seg[d]], w=[seg[d]])
                for g in range(4):
                    S.op('pe', lambda e, g=g: e.matmul(bk4[0][:, g * 128:(g + 1) * 128], lhsT=BT[d][:, g * 128:(g + 1) * 128], rhs=CT[d][:, g * 128:(g + 1) * 128], start=True, stop=True), r=[BT[d], CT[d]], w=[bk4[0]])
                yield
                for g in range(4):
                    S.op('dve', lambda e, g=g: e.tensor_tensor(out=scT[d][:, g * 512:(g + 1) * 512].rearrange("p (h i) -> p h i", h=4), in0=seg[d][:, g * 512:(g + 1) * 512].rearrange("p (h i) -> p h i", h=4), in1=bk4[0][:, g * 128:(g + 1) * 128].unsqueeze(1).to_broadcast([128, 4, 128]), op=ALU.mult), r=[seg[d], bk4[0]], w=[scT[d]])
                yield
                for hd in range(16):
                    bk = bk4[1 + hd // 8]
                    S.op('pe', lambda e, bk=bk, hd=hd: e.matmul(bk[:, (hd % 8) * 64:(hd % 8 + 1) * 64], lhsT=scT[d][:, hd * 128:(hd + 1) * 128], rhs=xdt[d][:, hd * 64:(hd + 1) * 64], start=True, stop=True), r=[scT[d], xdt[d]], w=[bk])
                bint = [bk4[3], bk4[0]]
                for g in range(4):
                    bk = bint[g // 2]
                    S.op('pe', lambda e, bk=bk, g=g: e.matmul(bk[:, (g % 2) * 256:(g % 2 + 1) * 256], lhsT=CT[d][:, g * 128:(g + 1) * 128], rhs=hTb[d][:, g * 256:(g + 1) * 256], start=True, stop=True), r=[CT[d], hTb[d]], w=[bk])
                yield
                for h2 in range(2):
                    S.op('dve', lambda e, h2=h2: e.tensor_tensor(out=ytmp[d][:, h2 * 512:(h2 + 1) * 512].rearrange("p (h q) -> p h q", h=8), in0=bint[h2][:, :].rearrange("p (h q) -> p h q", h=8), in1=s[:, 16 + 8 * h2:24 + 8 * h2].unsqueeze(2).to_broadcast([128, 8, 64]), op=ALU.mult), r=[bint[h2], s], w=[ytmp[d]])
                for h2 in range(2):
                    S.op('dve', lambda e, h2=h2: e.tensor_tensor(out=ysb[d][:, h2 * 512:(h2 + 1) * 512], in0=bk4[1 + h2][:, :], in1=ytmp[d][:, h2 * 512:(h2 + 1) * 512], op=ALU.add), r=[bk4[1 + h2], ytmp[d]], w=[ysb[d]])
                if d == 0:
                    S.op('pool', lambda e: e.tensor_tensor(out=ytmp[d][:, :].rearrange("p (h q) -> p h q", h=16), in0=xtok[d][:, :].rearrange("p (h q) -> p h q", h=16), in1=dskb[:, 0:16].unsqueeze(2).to_broadcast([128, 16, 64]), op=ALU.mult), r=[xtok[d], dskb], w=[ytmp[d]])
                    S.op('pool', lambda e: e.tensor_tensor(out=ysb[d][:, :], in0=ysb[d][:, :], in1=ytmp[d][:, :], op=ALU.add), r=[ysb[d], ytmp[d]], w=[ysb[d]])
                S.dma('pool', ydir[d][t0:t0 + 128, :], ysb[d][:, :], r=[ysb[d]])
                yield
                for g in range(4):
                    bk = bk4[1 + g // 2]
                    S.op('pe', lambda e, bk=bk, g=g: e.matmul(bk[:, (g % 2) * 256:(g % 2 + 1) * 256], lhsT=Btok[d][:, g * 128:(g + 1) * 128], rhs=xw[d][:, g * 256:(g + 1) * 256], start=True, stop=True), r=[Btok[d], xw[d]], w=[bk])
                S.op('pool', lambda e: e.tensor_tensor(out=hT[d][:, :].rearrange("p (h q) -> p h q", h=16), in0=hT[d][:, :].rearrange("p (h q) -> p h q", h=16), in1=s[:, 32:48].unsqueeze(2).to_broadcast([128, 16, 64]), op=ALU.mult), r=[hT[d], s], w=[hT[d]])
                yield
                for h2 in range(2):
                    S.op('dve', lambda e, h2=h2: e.tensor_tensor(out=hT[d][:, h2 * 512:(h2 + 1) * 512], in0=hT[d][:, h2 * 512:(h2 + 1) * 512], in1=bk4[1 + h2][:, :], op=ALU.add), r=[hT[d], bk4[1 + h2]], w=[hT[d]])
                S.op('act', lambda e: e.activation(out=hTb[d][:, :], in_=hT[d][:, :], func=AF.Copy), r=[hT[d]], w=[hTb[d]])

            order_f = list(range(NT))
            order_b = list(range(NCT - 1, -1, -1)) + list(range(NT - 1, NCT - 1, -1))
            for i in range(NT):
                run_pair(step(order_f[i], 0), step(order_b[i], 1))
            ph.close()

        def gla_phase(H, dk, dv, qT_s, kT_s, ktok_s, v_s, lg_s, o_s):
            ph = Phase()
            cst = load_consts(ph)
            HK = H * dk
            HV = H * dv
            nkc = HK // 128
            hp = 128 // dk
            NCH = T // 64
            NCC = CTX // 64
            Sst = [ph.tile([128, nkc * dv], F32, "Sst") for _ in range(2)]
            Sb = [ph.tile([128, nkc * dv], BF16, "Sb") for _ in range(2)]
            for d in range(2):
                S.op('dve', lambda e, d=d: e.memset(Sst[d][:, :], 0.0), w=[Sst[d]])
                S.op('dve', lambda e, d=d: e.memset(Sb[d][:, :], 0.0), w=[Sb[d]])

            def mk(shape, dt, name):
                return [ph.tile(shape, dt, name) for _ in range(2)]
            lgt = mk([64, HK], F32, "lgt")
            kt = mk([64, HK], F32, "kt")
            vb = mk([64, HV], BF16, "vb")
            qTt = mk([128, nkc * 64], F32, "qTt")
            kTt = mk([128, nkc * 64], F32, "kTt")
            eg = mk([128, nkc * 64], F32, "eg")
            eng = mk([128, nkc * 64], F32, "eng")
            qd = [[ph.tile([128, nkc * 64], BF16, "qd") for _ in range(hp)] for _ in range(2)]
            for d in range(2):
                for sl in range(hp):
                    S.op('dve', lambda e, d=d, sl=sl: e.memset(qd[d][sl][:, :], 0.0), w=[qd[d][sl]])
            ki = mk([128, nkc * 64], BF16, "ki")
            eex = mk([64, HK], F32, "eex")
            kend = mk([64, HK], BF16, "kend")
            att = mk([64, H * 64], BF16, "att")
            osb = mk([64, HV], F32, "osb")

            def step(ci, d):
                t0 = ci * 64
                S.dma('sp', lgt[d][:, :], lg_s[d][t0:t0 + 64, 0:HK], w=[lgt[d]])
                S.dma('sp', kt[d][:, :], ktok_s[d][t0:t0 + 64, 0:HK], w=[kt[d]])
                S.dma('sp', vb[d][:, :], v_s[t0:t0 + 64, 0:HV], w=[vb[d]])
                S.dma('act', qTt[d][:, :].rearrange("p (c t) -> p c t", c=nkc), qT_s[0:HK, t0:t0 + 64].rearrange("(c p) t -> p c t", p=128), w=[qTt[d]])
                S.dma('act', kTt[d][:, :].rearrange("p (c t) -> p c t", c=nkc), kT_s[d][0:HK, t0:t0 + 64].rearrange("(c p) t -> p c t", p=128), w=[kTt[d]])
                yield
                tri = C_TF if d == 0 else C_TB
                stri = C_SF if d == 0 else C_SB
                bA = banks[0 + 4 * d]
                for kc in range(nkc):
                    S.op('pe', lambda e, kc=kc: e.matmul(bA[:, kc * 64:(kc + 1) * 64], lhsT=lgt[d][0:64, kc * 128:(kc + 1) * 128], rhs=cst[0:64, tri:tri + 64], start=True, stop=True), r=[lgt[d], cst], w=[bA])
                nb = HK // 512
                bB = [banks[1 + 4 * d], banks[2 + 4 * d]]
                for b in range(nb):
                    S.op('pe', lambda e, b=b: e.matmul(bB[b][0:64, 0:512], lhsT=cst[0:64, stri:stri + 64], rhs=lgt[d][0:64, b * 512:(b + 1) * 512], start=True, stop=True), r=[lgt[d], cst], w=[bB[b]])
                yield
                S.op('act', lambda e: e.activation(out=eg[d][:, :], in_=bA[:, 0:nkc * 64], func=AF.Exp), r=[bA], w=[eg[d]])
                S.op('act', lambda e: e.activation(out=eng[d][:, :], in_=bA[:, 0:nkc * 64], func=AF.Exp, scale=-1.0), r=[bA], w=[eng[d]])
                yield
                for sl in range(hp):
                    p0, p1 = sl * dk, (sl + 1) * dk
                    S.op('dve', lambda e, sl=sl, p0=p0, p1=p1: e.tensor_tensor(out=qd[d][sl][p0:p1, :], in0=qTt[d][p0:p1, :], in1=eg[d][p0:p1, :], op=ALU.mult), r=[qTt[d], eg[d]], w=[qd[d][sl]])
                S.op('dve', lambda e: e.tensor_tensor(out=ki[d][:, :], in0=kTt[d][:, :], in1=eng[d][:, :], op=ALU.mult), r=[kTt[d], eng[d]], w=[ki[d]])
                for b in range(nb):
                    S.op('act', lambda e, b=b: e.activation(out=eex[d][:, b * 512:(b + 1) * 512], in_=bB[b][0:64, 0:512], func=AF.Exp), r=[bB[b]], w=[eex[d]])
                S.op('dve', lambda e: e.tensor_tensor(out=kend[d][:, :], in0=kt[d][:, :], in1=eex[d][:, :], op=ALU.mult), r=[kt[d], eex[d]], w=[kend[d]])
                yield
                bC = banks[3 + 4 * d]
                for h in range(H):
                    kc, sl = divmod(h, hp)
                    S.op('pe', lambda e, h=h, kc=kc, sl=sl: e.matmul(bC[0:64, h * 64:(h + 1) * 64], lhsT=ki[d][:, kc * 64:(kc + 1) * 64], rhs=qd[d][sl][:, kc * 64:(kc + 1) * 64], start=True, stop=True), r=[ki[d], qd[d][sl]], w=[bC])
                yield
                S.op('dve', lambda e: e.tensor_tensor(out=att[d][:, :].rearrange("p (h i) -> p h i", h=H), in0=bC[0:64, 0:H * 64].rearrange("p (h i) -> p h i", h=H), in1=cst[0:64, tri:tri + 64].unsqueeze(1).to_broadcast([64, H, 64]), op=ALU.mult), r=[bC, cst], w=[att[d]])
                yield
                bO = [banks[1 + 4 * d], banks[2 + 4 * d]]
                for h in range(H):
                    bo = bO[(h * dv) // 512]
                    oc = (h * dv) % 512
                    S.op('pe', lambda e, h=h, bo=bo, oc=oc: e.matmul(bo[0:64, oc:oc + dv], lhsT=att[d][0:64, h * 64:(h + 1) * 64], rhs=vb[d][0:64, h * dv:(h + 1) * dv], start=True, stop=True), r=[att[d], vb[d]], w=[bo])
                yield
                for b in range(HV // 512):
                    S.op('act', lambda e, b=b: e.activation(out=osb[d][:, b * 512:(b + 1) * 512], in_=bO[b][0:64, 0:512], func=AF.Copy), r=[bO[b]], w=[osb[d]])
                for h in range(H):
                    kc, sl = divmod(h, hp)
                    bo = bO[(h * dv) // 512]
                    oc = (h * dv) % 512
                    S.op('pe', lambda e, h=h, bo=bo, oc=oc, kc=kc, sl=sl: e.matmul(bo[0:64, oc:oc + dv], lhsT=qd[d][sl][:, kc * 64:(kc + 1) * 64], rhs=Sb[d][:, kc * dv:(kc + 1) * dv], start=True, stop=True), r=[qd[d][sl], Sb[d]], w=[bo])
                yield
                for b in range(HV // 512):
                    S.op('dve', lambda e, b=b: e.tensor_tensor(out=osb[d][:, b * 512:(b + 1) * 512], in0=osb[d][:, b * 512:(b + 1) * 512], in1=bO[b][0:64, 0:512], op=ALU.add), r=[bO[b], osb[d]], w=[osb[d]])
                S.dma('pool', o_s[d][t0:t0 + 64, 0:HV], osb[d][:, :], r=[osb[d]])
                yield
                bS = [banks[0 + 4 * d], banks[3 + 4 * d]]
                wS = hp * dv
                for kc in range(nkc):
                    bs = bS[(kc * wS) // 512]
                    oc = (kc * wS) % 512
                    S.op('pe', lambda e, kc=kc, bs=bs, oc=oc: e.matmul(bs[:, oc:oc + wS], lhsT=kend[d][0:64, kc * 128:(kc + 1) * 128], rhs=vb[d][0:64, kc * wS:(kc + 1) * wS], start=True, stop=True), r=[kend[d], vb[d]], w=[bs])
                yield
                lastc = 63 if d == 0 else 0
                for kc in range(nkc):
                    bs = bS[(kc * wS) // 512]
                    oc = (kc * wS) % 512
                    for sl in range(hp):
                        p0, p1 = sl * dk, (sl + 1) * dk
                        S.op('dve', lambda e, kc=kc, bs=bs, oc=oc, sl=sl, p0=p0, p1=p1: e.scalar_tensor_tensor(out=Sst[d][p0:p1, kc * dv:(kc + 1) * dv], in0=Sst[d][p0:p1, kc * dv:(kc + 1) * dv], scalar=eg[d][p0:p1, kc * 64 + lastc:kc * 64 + lastc + 1], in1=bs[p0:p1, oc + sl * dv:oc + (sl + 1) * dv], op0=ALU.mult, op1=ALU.add), r=[Sst[d], eg[d], bs], w=[Sst[d]])
                S.op('act', lambda e: e.activation(out=Sb[d][:, :], in_=Sst[d][:, :], func=AF.Copy), r=[Sst[d]], w=[Sb[d]])

            order_f = list(range(NCH))
            order_b = list(range(NCC - 1, -1, -1)) + list(range(NCH - 1, NCC - 1, -1))
            for i in range(NCH):
                run_pair(step(order_f[i], 0), step(order_b[i], 1))
            ph.close()

        def head_rms(ph, src, dstbuf, dst_ap, H, hd, nwbuf, nw_bc, gate, sq, ss, rstd, epsb, tmp):
            n = H * hd
            v3 = lambda ap: ap.rearrange("p (h q) -> p h q", h=H)
            S.op('dve', lambda e: e.tensor_tensor(out=sq[:, 0:n], in0=src[:, 0:n], in1=src[:, 0:n], op=ALU.mult), r=[src], w=[sq])
            S.op('dve', lambda e: e.tensor_reduce(out=ss[:, 0:H], in_=v3(sq[:, 0:n]), axis=AX.X, op=ALU.add), r=[sq], w=[ss])
            S.op('act', lambda e: e.activation(out=rstd[:, 0:H], in_=ss[:, 0:H], func=AF.Sqrt, scale=1.0 / hd, bias=epsb[:, 0:1]), r=[ss, epsb], w=[rstd])
            S.op('dve', lambda e: e.reciprocal(out=rstd[:, 0:H], in_=rstd[:, 0:H]), r=[rstd], w=[rstd])
            S.op('dve', lambda e: e.tensor_tensor(out=v3(tmp[:, 0:n]), in0=v3(src[:, 0:n]), in1=rstd[:, 0:H].unsqueeze(2).to_broadcast([128, H, hd]), op=ALU.mult), r=[src, rstd], w=[tmp])
            S.op('dve', lambda e: e.tensor_tensor(out=v3(tmp[:, 0:n]), in0=v3(tmp[:, 0:n]), in1=nw_bc, op=ALU.mult), r=[tmp, nwbuf], w=[tmp])
            S.op('dve', lambda e: e.tensor_tensor(out=dst_ap, in0=tmp[:, 0:n], in1=gate[:, 0:n], op=ALU.mult), r=[tmp, gate], w=[dstbuf])

        def phaseD0a():
            ph = Phase()
            cst = load_consts(ph)
            Wo = ph.tile([128, 16 * D], BF16, "Wo")
            load_weight_bf16(ph, Wo, w_out0, 2048, D)
            snw = bc_load(ph, ssdnw[0:1, :], 1024, "snw")
            gnw = bc_load(ph, glanw[0:1, :], 128, "gnw")
            g1 = [bc_load(ph, modv[0, s:s + 1, 2 * D:3 * D], D, "g1") for s in range(2)]
            epsb = ph.tile([128, 1], F32, "epsb")
            S.op('dve', lambda e: e.memset(epsb[:, :], EPS), w=[epsb])
            idb = ph.tile([128, 128], BF16, "idb")
            S.op('dve', lambda e: e.tensor_copy(out=idb[:, :], in_=cst[:, C_I:C_I + 128]), r=[cst], w=[idb])
            A4 = [[ph.tile([128, D], F32, "a") for _ in range(2)] for _ in range(2)]
            B4 = [[ph.tile([128, D], F32, "b") for _ in range(2)] for _ in range(2)]
            G4 = [[ph.tile([128, D], F32, "g") for _ in range(2)] for _ in range(2)]
            XT2 = [ph.tile([128, D], F32, "xt2") for _ in range(2)]
            sq = ph.tile([128, D], F32, "sq")
            tmp = ph.tile([128, D], F32, "tmp")
            xt = ph.tile([128, D], F32, "xt")
            mix = ph.tile([128, 2048], BF16, "mix")
            mixT = ph.tile([128, 2048], BF16, "mixT")
            ss = ph.tile([128, 16], F32, "ss")
            rstd = ph.tile([128, 16], F32, "rstd")
            xo = ph.tile([128, D], F32, "xo")
            for m in range(NT):
                r0 = m * 128
                xt = XT2[m % 2]
                S.dma('sp', xt[:, :], xsrc(0, m), w=[xt])
                for part in range(2):
                    a, b, g = A4[part][m % 2], B4[part][m % 2], G4[part][m % 2]
                    srcs = ydir if part == 0 else odir
                    gsrc = zs if part == 0 else rs
                    S.dma('sp', a[:, :], srcs[0][r0:r0 + 128, :], w=[a])
                    S.dma('act', b[:, :], srcs[1][r0:r0 + 128, :], w=[b])
                    S.dma('sp', g[:, :], gsrc[r0:r0 + 128, :], w=[g])
                    S.op('dve', lambda e: e.tensor_tensor(out=a[:, :], in0=a[:, :], in1=b[:, :], op=ALU.add), r=[a, b], w=[a])
                    S.op('act', lambda e: e.activation(out=g[:, :], in_=g[:, :], func=AF.Silu), r=[g], w=[g])
                    if part == 0:
                        S.op('dve', lambda e: e.tensor_tensor(out=a[:, :], in0=a[:, :], in1=g[:, :], op=ALU.mult), r=[a, g], w=[a])
                        S.op('dve', lambda e: e.tensor_tensor(out=sq[:, :], in0=a[:, :], in1=a[:, :], op=ALU.mult), r=[a], w=[sq])
                        S.op('dve', lambda e: e.tensor_reduce(out=ss[:, 0:4], in_=sq[:, :].rearrange("p (h q) -> p h q", h=4), axis=AX.X, op=ALU.add), r=[sq], w=[ss])
                        S.op('act', lambda e: e.activation(out=rstd[:, 0:4], in_=ss[:, 0:4], func=AF.Sqrt, scale=1.0 / 256, bias=epsb[:, 0:1]), r=[ss, epsb], w=[rstd])
                        S.op('dve', lambda e: e.reciprocal(out=rstd[:, 0:4], in_=rstd[:, 0:4]), r=[rstd], w=[rstd])
                        S.op('dve', lambda e: e.tensor_tensor(out=tmp[:, :].rearrange("p (h q) -> p h q", h=4), in0=a[:, :].rearrange("p (h q) -> p h q", h=4), in1=rstd[:, 0:4].unsqueeze(2).to_broadcast([128, 4, 256]), op=ALU.mult), r=[a, rstd], w=[tmp])
                        S.op('dve', lambda e: e.tensor_tensor(out=mix[:, 0:1024], in0=tmp[:, :], in1=snw[:, :], op=ALU.mult), r=[tmp, snw], w=[mix])
                    else:
                        head_rms(ph, a, mix, mix[:, 1024:2048], 8, 128, gnw, gnw[:, 0:128].unsqueeze(1).to_broadcast([128, 8, 128]), g, sq, ss, rstd, epsb, tmp)
                for q in range(4):
                    bk = banks[q]
                    bkb = bk[:, 0:256].bitcast(BF16)
                    for c in range(4):
                        cc = q * 4 + c
                        S.op('pe', lambda e, bkb=bkb, c=c, cc=cc: e.transpose(bkb[:, c * 128:(c + 1) * 128], mix[:, cc * 128:(cc + 1) * 128], idb[:, :]), r=[mix, idb], w=[bk])
                    S.op('act' if q % 2 == 0 else 'dve', (lambda e, bkb=bkb, q=q: e.activation(out=mixT[:, q * 512:(q + 1) * 512], in_=bkb[:, 0:512], func=AF.Copy)) if q % 2 == 0 else (lambda e, bkb=bkb, q=q: e.tensor_copy(out=mixT[:, q * 512:(q + 1) * 512], in_=bkb[:, 0:512])), r=[bk], w=[mixT])
                for h2 in range(2):
                    bk = banks[4 + h2]
                    for c in range(16):
                        S.op('pe', lambda e, bk=bk, c=c, h2=h2: e.matmul(bk[:, 0:512], lhsT=mixT[:, c * 128:(c + 1) * 128], rhs=Wo[:, c * D + h2 * 512:c * D + (h2 + 1) * 512], start=(c == 0), stop=(c == 15)), r=[mixT, Wo], w=[bk])
                    gg = g1[1] if m < NCT else g1[0]
                    S.op('dve', lambda e, bk=bk, h2=h2, gg=gg: e.tensor_tensor(out=xo[:, h2 * 512:(h2 + 1) * 512], in0=bk[:, 0:512], in1=gg[:, h2 * 512:(h2 + 1) * 512], op=ALU.mult), r=[bk, gg], w=[xo])
                S.op('dve', lambda e: e.tensor_tensor(out=xo[:, :], in0=xo[:, :], in1=xt[:, :], op=ALU.add), r=[xo, xt], w=[xo])
                S.dma('pool', xmid[r0:r0 + 128, :], xo[:, :], r=[xo])
            ph.close()

        def phaseMLP(layer, tiles, final):
            ph = Phase()
            cst = ph.tile([128, 128], F32, "cstI")
            S.dma('sp', cst[:, :], consts[:, 0:128], w=[cst])
            W1 = ph.tile([128, 8 * 4096], BF16, "W1")
            load_weight_bf16(ph, W1, mlp_w1[layer], D, 4096)
            W2 = ph.tile([128, 32 * D], BF16, "W2")
            load_weight_bf16(ph, W2, mlp_w2[layer], 4096, D)
            NW = bc_load(ph, norm2_w[layer:layer + 1, :], D, "NW")
            Am = ph.tile([128, D], F32, "Am")
            SHm = ph.tile([128, D], F32, "SHm")
            Gs = {0: bc_load(ph, modv[layer, 0:1, 5 * D:6 * D], D, "Gl")}
            if not final:
                Gs[1] = bc_load(ph, modv[layer, 1:2, 5 * D:6 * D], D, "Gc")
            cur = [None]

            def set_mod(src):
                if cur[0] == src:
                    return
                cur[0] = src
                S.dma('sp', Am[:, :], modv[layer, src:src + 1, 4 * D:5 * D].partition_broadcast(128), w=[Am])
                S.dma('sp', SHm[:, :], modv[layer, src:src + 1, 3 * D:4 * D].partition_broadcast(128), w=[SHm])
                S.op('dve', lambda e: e.scalar_tensor_tensor(out=Am[:, :], in0=Am[:, :], scalar=1.0, in1=NW[:, :], op0=ALU.add, op1=ALU.mult), r=[Am, NW], w=[Am])
            if final:
                fnw = bc_load(ph, final_w[0:1, :], D, "fnw")
            epsb = ph.tile([128, 1], F32, "epsb")
            S.op('dve', lambda e: e.memset(epsb[:, :], EPS), w=[epsb])
            idb = ph.tile([128, 128], BF16, "idb")
            S.op('dve', lambda e: e.tensor_copy(out=idb[:, :], in_=cst[:, C_I:C_I + 128]), r=[cst], w=[idb])
            xts = [ph.tile([128, D], F32, "xt") for _ in range(2)]
            tmp = ph.tile([128, D], F32, "tmp")
            hb = ph.tile([128, D], BF16, "hb")
            junk = hb
            hT = ph.tile([128, D], BF16, "hT")
            h1s = [ph.tile([128, 512], F32, "h1") for _ in range(2)]
            h1T = ph.tile([128, 4096], BF16, "h1T")
            xo = ph.tile([128, D], F32, "xo")
            ss = ph.tile([128, 1], F32, "ss")
            rstd = ph.tile([128, 1], F32, "rstd")
            ss2 = ph.tile([128, 1], F32, "ss2")
            rstd2 = ph.tile([128, 1], F32, "rstd2")

            def prep_load(i):
                m = tiles[i]
                S.dma('sp', xts[i % 2][:, :], xmid[m * 128:(m + 1) * 128, :], w=[xts[i % 2]])

            def prep_elem(i):
                m = tiles[i]
                xt = xts[i % 2]
                rms_rstd(ph, xt, junk, ss, rstd, epsb, D, 1.0 / D)
                set_mod(1 if m < NCT else 0)
                S.op('dve', lambda e: e.scalar_tensor_tensor(out=tmp[:, :], in0=xt[:, :], scalar=rstd[:, 0:1], in1=Am[:, :], op0=ALU.mult, op1=ALU.mult), r=[xt, rstd, Am], w=[tmp])
                S.op('pool', lambda e: e.tensor_tensor(out=hb[:, :], in0=tmp[:, :], in1=SHm[:, :], op=ALU.add), r=[tmp, SHm], w=[hb])

            def prep_T(i):
                bk = banks[0]
                bkb = bk[:, :].bitcast(BF16)
                for j in range(8):
                    S.op('pe', lambda e, j=j: e.transpose(bkb[:, j * 128:(j + 1) * 128], hb[:, j * 128:(j + 1) * 128], idb[:, :]), r=[hb, idb], w=[bk])
                S.op('act', lambda e: e.activation(out=hT[:, :], in_=bkb[:, 0:1024], func=AF.Copy), r=[bk], w=[hT])

            def w1(i):
                for q in range(8):
                    bk = banks[1 + q % 5]
                    h1 = h1s[q % 2]
                    for c in range(4):
                        fc = q * 4 + c
                        for j in range(8):
                            S.op('pe', lambda e, bk=bk, c=c, fc=fc, j=j: e.matmul(bk[:, c * 128:(c + 1) * 128], lhsT=W1[:, j * 4096 + fc * 128:j * 4096 + (fc + 1) * 128], rhs=hT[:, j * 128:(j + 1) * 128], start=(j == 0), stop=(j == 7)), r=[W1, hT], w=[bk])
                    S.op('act', lambda e, bk=bk, h1=h1: e.activation(out=h1[:, :], in_=bk[:, :], func=AF.Relu), r=[bk], w=[h1])
                    S.op('dve', lambda e, q=q, h1=h1: e.tensor_tensor(out=h1T[:, q * 512:(q + 1) * 512], in0=h1[:, :], in1=h1[:, :], op=ALU.mult), r=[h1], w=[h1T])

            def w2(i):
                for h2 in range(2):
                    bk = banks[6 + h2]
                    for fc in range(32):
                        S.op('pe', lambda e, bk=bk, fc=fc, h2=h2: e.matmul(bk[:, 0:512], lhsT=h1T[:, fc * 128:(fc + 1) * 128], rhs=W2[:, fc * D + h2 * 512:fc * D + (h2 + 1) * 512], start=(fc == 0), stop=(fc == 31)), r=[h1T, W2], w=[bk])

            def epi(i):
                m = tiles[i]
                r0 = m * 128
                xt = xts[i % 2]
                G2 = Gs[1 if m < NCT else 0]
                for h2 in range(2):
                    bk = banks[6 + h2]
                    S.op('dve', lambda e, bk=bk, h2=h2: e.tensor_tensor(out=xo[:, h2 * 512:(h2 + 1) * 512], in0=bk[:, 0:512], in1=G2[:, h2 * 512:(h2 + 1) * 512], op=ALU.mult), r=[bk, G2], w=[xo])
                S.op('pool', lambda e: e.tensor_tensor(out=xo[:, :], in0=xo[:, :], in1=xt[:, :], op=ALU.add), r=[xo, xt], w=[xo])
                if not final:
                    S.dma('pool', xres[r0:r0 + 128, :], xo[:, :], r=[xo])
                else:
                    rms_rstd(ph, xo, junk, ss2, rstd2, epsb, D, 1.0 / D)
                    S.op('dve', lambda e: e.scalar_tensor_tensor(out=tmp[:, :], in0=xo[:, :], scalar=rstd2[:, 0:1], in1=fnw[:, :], op0=ALU.mult, op1=ALU.mult), r=[xo, rstd2, fnw], w=[tmp])
                    S.dma('pool', out[r0 - CTX:r0 - CTX + 128, :], tmp[:, :], r=[tmp])

            n = len(tiles)
            prep_load(0)
            prep_elem(0)
            prep_T(0)
            for i in range(n):
                if i + 1 < n:
                    prep_load(i + 1)
                w1(i)
                if i + 1 < n:
                    prep_elem(i + 1)
                w2(i)
                if i + 1 < n:
                    prep_T(i + 1)
                epi(i)
            ph.close()

        def phaseA1():
            tok_groups = []
            feat_groups = []

            def extra(c):
                ph = c['ph']
                l0 = bc_load(ph, lbl[0:1, :], 2048, "l0")
                oml = bc_load(ph, lbl[1:2, :], 2048, "oml")
                S.op('dve', lambda e: e.tensor_tensor(out=oml[:, :], in0=oml[:, :], in1=l0[:, :], op=ALU.subtract), r=[oml, l0], w=[oml])
                S.op('act', lambda e: e.activation(out=oml[:, :], in_=oml[:, :], func=AF.Sigmoid, scale=-1.0), r=[oml], w=[oml])
                omlT = ph.tile([128, 32], F32, "omlT")
                S.dma('sp', omlT[:, :], lblT[:, :, :].rearrange("p l c -> p (l c)"), w=[omlT])
                S.op('dve', lambda e: e.tensor_tensor(out=omlT[:, 16:32], in0=omlT[:, 16:32], in1=omlT[:, 0:16], op=ALU.subtract), r=[omlT], w=[omlT])
                S.op('act', lambda e: e.activation(out=omlT[:, 16:32], in_=omlT[:, 16:32], func=AF.Sigmoid, scale=-1.0), r=[omlT], w=[omlT])
                oneb = ph.tile([128, 1], F32, "oneb")
                S.op('dve', lambda e: e.memset(oneb[:, :], 1.0), w=[oneb])
                stg32, k32 = c['stg32'], c['k32']

                def nxt():
                    st = stg32[k32[0] % len(stg32)]
                    k32[0] += 1
                    return st

                def f_tok(m, bk, off, n):
                    d = off // 1024
                    col = off % 1024
                    st = nxt()
                    S.op('act', lambda e: e.activation(out=st[:, 0:n], in_=bk[:, 0:n], func=AF.Sigmoid, scale=-1.0), r=[bk], w=[st])
                    S.op('dve', lambda e: e.tensor_tensor(out=st[:, 0:n], in0=st[:, 0:n], in1=oml[:, off:off + n], op=ALU.mult), r=[st, oml], w=[st])
                    S.dma('pool', ktoks[d][m * 128:(m + 1) * 128, col:col + n], st[:, 0:n], r=[st])
                    st2 = nxt()
                    S.op('act', lambda e: e.activation(out=st2[:, 0:n], in_=st[:, 0:n], func=AF.Ln, scale=-1.0, bias=oneb[:, 0:1]), r=[st, oneb], w=[st2])
                    S.dma('pool', lgs[d][m * 128:(m + 1) * 128, col:col + n], st2[:, 0:n], r=[st2])

                def f_feat(m, bk, q0, nch):
                    st = nxt()
                    S.op('act', lambda e: e.activation(out=st[:, 0:nch * 128], in_=bk[:, 0:nch * 128], func=AF.Sigmoid, scale=-1.0), r=[bk], w=[st])
                    for cc in range(nch):
                        ch = q0 + cc
                        S.op('dve', lambda e, cc=cc, ch=ch: e.tensor_scalar(out=st[:, cc * 128:(cc + 1) * 128], in0=st[:, cc * 128:(cc + 1) * 128], scalar1=omlT[:, 16 + ch:17 + ch], scalar2=None, op0=ALU.mult), r=[st, omlT], w=[st])
                    d = q0 // 8
                    r0 = (q0 % 8) * 128
                    S.dma('pool', kTs[d][r0:r0 + nch * 128, m * 128:(m + 1) * 128].rearrange("(c p) t -> p c t", p=128),
                          st[:, 0:nch * 128].rearrange("p (c t) -> p c t", c=nch), r=[st])
                tok_groups.append((1024, 1024, make_tok_store(c, vtok, 0, 1024, dt=BF16)))
                tok_groups.append((2048, 2048, f_tok))
                tok_groups.append((4096, 1024, make_tok_store(c, rs, 0, 1024)))
                tok_groups.append((5120, 384, make_tok_store(c, utok, 0, 384)))
                feat_groups.append((0, 8, 128, make_feat_store(c, qT, 0, func=AF.Silu)))
                feat_groups.append((2048, 16, 128, f_feat))
            proj_phase(1, w_in1, 5504, tok_groups, feat_groups, extra)

        def phaseC1():
            ph = Phase()
            cst = load_consts(ph)
            PI = float(np.pi)
            prm = ph.tile([128, 72], F32, "prm")
            S.dma('sp', prm[:, :], s5p[:, :], w=[prm])
            bri = ph.tile([128, 384], F32, "bri")
            S.dma('sp', bri[:, :], s5b[:, :], w=[bri])
            cri = ph.tile([128, 384], F32, "cri")
            S.dma('sp', cri[:, :], s5c[:, :], w=[cri])
            dsk = ph.tile([128, 3], F32, "dsk")
            S.dma('sp', dsk[:, :], s5d[:, :], w=[dsk])
            negpi = ph.tile([128, 1], F32, "negpi")
            S.op('dve', lambda e: e.memset(negpi[:, :], -PI), w=[negpi])
            w12 = ph.tile([128, 12 * 12], F32, "w12")

            def V(i):
                return w12[:, i * 12:(i + 1) * 12]
            CX = [ph.tile([128, 12 * 128], F32, "CX") for _ in range(2)]
            bbX = ph.tile([128, 12 * 128], F32, "bbX")
            BbT = [[ph.tile([128, 12 * 128], F32, "BbT") for _ in range(2)] for _ in range(2)]
            Ecs = [[ph.tile([128, 12 * 128], F32, "E") for _ in range(2)] for _ in range(2)]
            rmag = [ph.tile([128, 12 * 128], F32, "rmag") for _ in range(2)]
            bb = ph.tile([128, 2 * 192], F32, "bb")
            tA = ph.tile([128, 12 * 128], F32, "tA")
            tB = ph.tile([128, 12 * 128], F32, "tB")

            def place(dst, src_ap3, negate=False):
                S.op('dve', lambda e: e.memset(dst[:, :], 0.0), w=[dst])
                for sc in range(12):
                    for g2 in range(2):
                        gl = (2 * sc + g2) % 8
                        p0, p1 = g2 * 64, (g2 + 1) * 64
                        if negate:
                            S.op('dve', lambda e, sc=sc, gl=gl, p0=p0, p1=p1: e.tensor_scalar(out=dst[p0:p1, sc * 128 + gl * 16:sc * 128 + gl * 16 + 16], in0=src_ap3(sc, p0, p1), scalar1=-1.0, scalar2=None, op0=ALU.mult), r=[bb, cri], w=[dst])
                        else:
                            S.op('dve', lambda e, sc=sc, gl=gl, p0=p0, p1=p1: e.tensor_copy(out=dst[p0:p1, sc * 128 + gl * 16:sc * 128 + gl * 16 + 16], in_=src_ap3(sc, p0, p1)), r=[bb, cri], w=[dst])
            place(CX[0], lambda sc, p0, p1: cri[p0:p1, sc * 16:(sc + 1) * 16])
            place(CX[1], lambda sc, p0, p1: cri[p0:p1, 192 + sc * 16:192 + (sc + 1) * 16], negate=True)

            def tt(out, a, b, op, r, w):
                S.op('dve', lambda e: e.tensor_tensor(out=out, in0=a, in1=b, op=op), r=r, w=w)

            def sin_of(out, theta, shift):
                S.op('dve', lambda e: e.tensor_scalar(out=V(10), in0=theta, scalar1=shift + PI, scalar2=None, op0=ALU.add), r=[w12], w=[w12])
                for kk in range(1, 5):
                    S.op('dve', lambda e, kk=kk: e.tensor_scalar(out=V(11), in0=V(10), scalar1=2.0 * PI * kk, scalar2=-2.0 * PI, op0=ALU.is_ge, op1=ALU.mult), r=[w12], w=[w12])
                    if kk == 1:
                        S.op('dve', lambda e: e.tensor_tensor(out=V(9), in0=V(10), in1=V(11), op=ALU.add), r=[w12], w=[w12])
                    else:
                        S.op('dve', lambda e: e.tensor_tensor(out=V(9), in0=V(9), in1=V(11), op=ALU.add), r=[w12], w=[w12])
                S.op('act', lambda e: e.activation(out=out, in_=V(9), func=AF.Sin, bias=negpi[:, 0:1]), r=[w12, negpi], w=[w12])

            for d in range(2):
                are = prm[:, d * 36:d * 36 + 12]
                aim = prm[:, d * 36 + 12:d * 36 + 24]
                ldt = prm[:, d * 36 + 24:d * 36 + 36]
                S.op('act', lambda e, ldt=ldt: e.activation(out=V(0), in_=ldt, func=AF.Exp), r=[prm], w=[w12])
                tt(V(1), are, V(0), ALU.mult, [prm, w12], [w12])
                S.op('act', lambda e: e.activation(out=V(1), in_=V(1), func=AF.Exp), r=[w12], w=[w12])
                tt(V(2), aim, V(0), ALU.mult, [prm, w12], [w12])
                sin_of(V(3), V(2), 0.0)
                sin_of(V(4), V(2), PI / 2)
                tt(V(5), V(1), V(4), ALU.mult, [w12], [w12])
                tt(V(6), V(1), V(3), ALU.mult, [w12], [w12])
                tt(V(7), are, are, ALU.mult, [prm], [w12])
                tt(V(8), aim, aim, ALU.mult, [prm], [w12])
                tt(V(7), V(7), V(8), ALU.add, [w12], [w12])
                S.op('dve', lambda e: e.reciprocal(out=V(7), in_=V(7)), r=[w12], w=[w12])
                S.op('dve', lambda e: e.tensor_scalar(out=V(5), in0=V(5), scalar1=-1.0, scalar2=None, op0=ALU.add), r=[w12], w=[w12])
                tt(V(8), V(5), are, ALU.mult, [w12, prm], [w12])
                tt(V(9), V(6), aim, ALU.mult, [w12, prm], [w12])
                tt(V(8), V(8), V(9), ALU.add, [w12], [w12])
                tt(V(8), V(8), V(7), ALU.mult, [w12], [w12])
                tt(V(9), V(6), are, ALU.mult, [w12, prm], [w12])
                tt(V(10), V(5), aim, ALU.mult, [w12, prm], [w12])
                tt(V(9), V(9), V(10), ALU.subtract, [w12], [w12])
                tt(V(9), V(9), V(7), ALU.mult, [w12], [w12])
                b3 = lambda c0: bri[:, c0:c0 + 192].rearrange("p (s c) -> p s c", s=12)
                o3 = lambda c0: bb[:, c0:c0 + 192].rearrange("p (s c) -> p s c", s=12)
                t3 = tA[:, 0:192].rearrange("p (s c) -> p s c", s=12)
                zr3 = V(8).unsqueeze(2).to_broadcast([128, 12, 16])
                zi3 = V(9).unsqueeze(2).to_broadcast([128, 12, 16])
                tt(o3(0), b3(0), zr3, ALU.mult, [bri, w12], [bb])
                tt(t3, b3(192), zi3, ALU.mult, [bri, w12], [tA])
                tt(o3(0), o3(0), t3, ALU.subtract, [bb, tA], [bb])
                tt(o3(192), b3(192), zr3, ALU.mult, [bri, w12], [bb])
                tt(t3, b3(0), zi3, ALU.mult, [bri, w12], [tA])
                tt(o3(192), o3(192), t3, ALU.add, [bb, tA], [bb])
                for ri in range(2):
                    place(bbX, lambda sc, p0, p1, ri=ri: bb[p0:p1, ri * 192 + sc * 16:ri * 192 + (sc + 1) * 16])
                    for q in range(3):
                        bk = banks[q]
                        for c4 in range(4):
                            sc = q * 4 + c4
                            S.op('pe', lambda e, bk=bk, c4=c4, sc=sc: e.transpose(bk[:, c4 * 128:(c4 + 1) * 128], bbX[:, sc * 128:(sc + 1) * 128], cst[:, C_I:C_I + 128]), r=[bbX, cst], w=[bk])
                        S.op('act', lambda e, bk=bk, q=q, ri=ri, d=d: e.activation(out=BbT[d][ri][:, q * 512:(q + 1) * 512], in_=bk[:, :], func=AF.Copy), r=[bk], w=[BbT[d][ri]])
                Ec, Es = Ecs[d]
                Ec3 = Ec[:, :].rearrange("p (s t) -> p s t", s=12)
                Es3 = Es[:, :].rearrange("p (s t) -> p s t", s=12)
                i0 = 0 if d == 0 else 127
                S.op('dve', lambda e: e.tensor_copy(out=Ec3[:, :, i0:i0 + 1], in_=V(4).unsqueeze(2)), r=[w12], w=[Ec])
                S.op('dve', lambda e: e.tensor_copy(out=Es3[:, :, i0:i0 + 1], in_=V(3).unsqueeze(2)), r=[w12], w=[Es])
                tA3 = tA[:, :].rearrange("p (s t) -> p s t", s=12)
                tB3 = tB[:, :].rearrange("p (s t) -> p s t", s=12)
                for k in range(7):
                    n = 1 << k
                    if d == 0:
                        src = slice(0, n); dst = slice(n, 2 * n); piv = n - 1
                    else:
                        src = slice(128 - n, 128); dst = slice(128 - 2 * n, 128 - n); piv = 128 - n
                    pc = Ec3[:, :, piv:piv + 1].to_broadcast([128, 12, n])
                    ps_ = Es3[:, :, piv:piv + 1].to_broadcast([128, 12, n])
                    tt(tA3[:, :, 0:n], Ec3[:, :, src], pc, ALU.mult, [Ec], [tA])
                    tt(tB3[:, :, 0:n], Es3[:, :, src], ps_, ALU.mult, [Es], [tB])
                    tt(tA3[:, :, 0:n], tA3[:, :, 0:n], tB3[:, :, 0:n], ALU.subtract, [tA, tB], [tA])
                    tt(tB3[:, :, 0:n], Ec3[:, :, src], ps_, ALU.mult, [Ec, Es], [tB])
                    tt(tA3[:, :, 64:64 + n], Es3[:, :, src], pc, ALU.mult, [Ec, Es], [tA])
                    tt(Es3[:, :, dst], tB3[:, :, 0:n], tA3[:, :, 64:64 + n], ALU.add, [tA, tB], [Es])
                    S.op('dve', lambda e, dst=dst, n=n: e.tensor_copy(out=Ec3[:, :, dst], in_=tA3[:, :, 0:n]), r=[tA], w=[Ec])
                S.op('dve', lambda e, d=d: e.tensor_copy(out=rmag[d][:, :].rearrange("p (s t) -> p s t", s=12), in_=V(1).unsqueeze(2).to_broadcast([128, 12, 128])), r=[w12], w=[rmag[d]])

            def mk(shape, dt, name):
                return [ph.tile(shape, dt, name) for _ in range(2)]
            ut = mk([128, 384], F32, "ut")
            uT = mk([128, 384], F32, "uT")
            wre = mk([128, 1536], F32, "wre")
            wim = mk([128, 1536], F32, "wim")
            zre = mk([128, 1536], F32, "zre")
            zim = mk([128, 1536], F32, "zim")
            tmpA = [bbX, ph.tile([128, 1536], F32, "tmpA")]
            tmpB = [tA, ph.tile([128, 1536], F32, "tmpB")]
            tmpP = [tB, ph.tile([128, 1536], F32, "tmpP")]
            xst = mk([128, 24], F32, "xst")
            ysb = mk([128, 384], F32, "ysb")
            for d in range(2):
                S.op('dve', lambda e, d=d: e.memset(xst[d][:, :], 0.0), w=[xst[d]])

            def rev(buf, c0, n):
                a = buf[:, c0:c0 + n]
                return bass.AP(a.tensor, a.offset + (n - 1), [[a.ap[0][0], 128], [-1, n]])

            def step(m, d):
                bk4 = banks[4 * d:4 * d + 4]
                t0 = m * 128
                Ec, Es = Ecs[d]
                tA_, tB_, tP_ = tmpA[d], tmpB[d], tmpP[d]
                S.dma('sp', ut[d][:, :], utok[t0:t0 + 128, :], w=[ut[d]])
                b6 = bk4[3]
                for cc in range(3):
                    S.op('pe', lambda e, cc=cc: e.transpose(b6[:, cc * 128:(cc + 1) * 128], ut[d][:, cc * 128:(cc + 1) * 128], cst[:, C_I:C_I + 128]), r=[ut[d], cst], w=[b6])
                yield
                S.op('act', lambda e: e.activation(out=uT[d][:, :], in_=b6[:, 0:384], func=AF.Copy), r=[b6], w=[uT[d]])
                for ri in range(2):
                    for sc in range(12):
                        bk = bk4[sc // 4]
                        cc = sc // 4
                        S.op('pe', lambda e, bk=bk, sc=sc, cc=cc, ri=ri: e.matmul(bk[:, (sc % 4) * 128:(sc % 4 + 1) * 128], lhsT=BbT[d][ri][:, sc * 128:(sc + 1) * 128], rhs=uT[d][:, cc * 128:(cc + 1) * 128], start=True, stop=True), r=[BbT[d][ri], uT[d]], w=[bk])
                    yield
                    for q in range(3):
                        sl = slice(q * 512, (q + 1) * 512)
                        bq = bk4[q]
                        if ri == 0:
                            S.op('dve', lambda e, sl=sl, bq=bq: e.tensor_tensor(out=wre[d][:, sl], in0=bq[:, :], in1=Ec[:, sl], op=ALU.mult), r=[bq, Ec], w=[wre[d]])
                            S.op('dve', lambda e, sl=sl, bq=bq: e.tensor_tensor(out=wim[d][:, sl], in0=bq[:, :], in1=Es[:, sl], op=ALU.mult), r=[bq, Es], w=[wim[d]])
                        else:
                            S.op('dve', lambda e, sl=sl, bq=bq: e.tensor_tensor(out=tA_[:, sl], in0=bq[:, :], in1=Es[:, sl], op=ALU.mult), r=[bq, Es], w=[tA_])
                            S.op('dve', lambda e, sl=sl, bq=bq: e.tensor_tensor(out=tB_[:, sl], in0=bq[:, :], in1=Ec[:, sl], op=ALU.mult), r=[bq, Ec], w=[tB_])
                    yield
                S.op('pool', lambda e: e.tensor_tensor(out=wre[d][:, :], in0=wre[d][:, :], in1=tA_[:, :], op=ALU.add), r=[wre[d], tA_], w=[wre[d]])
                S.op('dve', lambda e: e.tensor_tensor(out=wim[d][:, :], in0=tB_[:, :], in1=wim[d][:, :], op=ALU.subtract), r=[wim[d], tB_], w=[wim[d]])
                yield
                for (wb, zb, c0_) in ((wre[d], zre[d], 0), (wim[d], zim[d], 12)):
                    for sc in range(12):
                        ci = c0_ + sc
                        if d == 0:
                            S.op('dve', lambda e, wb=wb, zb=zb, ci=ci, sc=sc: e.tensor_tensor_scan(out=zb[:, sc * 128:(sc + 1) * 128], data0=rmag[d][:, sc * 128:(sc + 1) * 128], data1=wb[:, sc * 128:(sc + 1) * 128], initial=xst[d][:, ci:ci + 1], op0=ALU.mult, op1=ALU.add), r=[wb, rmag[d], xst[d]], w=[zb])
                        else:
                            S.op('dve', lambda e, wb=wb, zb=zb, ci=ci, sc=sc: e.tensor_tensor_scan(out=rev(zb, sc * 128, 128), data0=rmag[d][:, sc * 128:(sc + 1) * 128], data1=rev(wb, sc * 128, 128), initial=xst[d][:, ci:ci + 1], op0=ALU.mult, op1=ALU.add), r=[wb, rmag[d], xst[d]], w=[zb])
                    yield
                S.op('dve', lambda e: e.tensor_tensor(out=tA_[:, :], in0=zre[d][:, :], in1=Ec[:, :], op=ALU.mult), r=[zre[d], Ec], w=[tA_])
                S.op('dve', lambda e: e.tensor_tensor(out=tB_[:, :], in0=zim[d][:, :], in1=Es[:, :], op=ALU.mult), r=[zim[d], Es], w=[tB_])
                S.op('pool', lambda e: e.tensor_tensor(out=wim[d][:, :], in0=zre[d][:, :], in1=Es[:, :], op=ALU.mult), r=[zre[d], Es], w=[wim[d]])
                S.op('pool', lambda e: e.tensor_tensor(out=tP_[:, :], in0=zim[d][:, :], in1=Ec[:, :], op=ALU.mult), r=[zim[d], Ec], w=[tP_])
                yield
                S.op('dve', lambda e: e.tensor_tensor(out=wre[d][:, :], in0=tA_[:, :], in1=tB_[:, :], op=ALU.subtract), r=[tA_, tB_], w=[wre[d]])
                S.op('pool', lambda e: e.tensor_tensor(out=wim[d][:, :], in0=wim[d][:, :], in1=tP_[:, :], op=ALU.add), r=[wim[d], tP_], w=[wim[d]])
                yield
                last = 127 if d == 0 else 0
                S.op('dve', lambda e: e.tensor_copy(out=xst[d][:, 0:12].unsqueeze(2), in_=wre[d][:, :].rearrange("p (s t) -> p s t", s=12)[:, :, last:last + 1]), r=[wre[d]], w=[xst[d]])
                S.op('dve', lambda e: e.tensor_copy(out=xst[d][:, 12:24].unsqueeze(2), in_=wim[d][:, :].rearrange("p (s t) -> p s t", s=12)[:, :, last:last + 1]), r=[wim[d]], w=[xst[d]])
                b7 = bk4[3]
                for cc in range(3):
                    for k4 in range(4):
                        sc = cc * 4 + k4
                        S.op('pe', lambda e, cc=cc, sc=sc, k4=k4: e.matmul(b7[:, cc * 128:(cc + 1) * 128], lhsT=CX[0][:, sc * 128:(sc + 1) * 128], rhs=wre[d][:, sc * 128:(sc + 1) * 128], start=(k4 == 0), stop=False), r=[CX[0], wre[d]], w=[b7])
                        S.op('pe', lambda e, cc=cc, sc=sc, k4=k4: e.matmul(b7[:, cc * 128:(cc + 1) * 128], lhsT=CX[1][:, sc * 128:(sc + 1) * 128], rhs=wim[d][:, sc * 128:(sc + 1) * 128], start=False, stop=(k4 == 3)), r=[CX[1], wim[d]], w=[b7])
                yield
                if d == 0:
                    for cc in range(3):
                        S.op('dve', lambda e, cc=cc: e.scalar_tensor_tensor(out=ysb[d][:, cc * 128:(cc + 1) * 128], in0=uT[d][:, cc * 128:(cc + 1) * 128], scalar=dsk[:, cc:cc + 1], in1=b7[:, cc * 128:(cc + 1) * 128], op0=ALU.mult, op1=ALU.add), r=[uT[d], dsk, b7], w=[ysb[d]])
                else:
                    S.op('act', lambda e: e.activation(out=ysb[d][:, :], in_=b7[:, 0:384], func=AF.Copy), r=[b7], w=[ysb[d]])
                S.dma('pool', yT5[d][:, t0:t0 + 128].rearrange("(c p) t -> p c t", p=128), ysb[d][:, :].rearrange("p (c t) -> p c t", c=3), r=[ysb[d]])

            order_f = list(range(NT))
            order_b = list(range(NCT - 1, -1, -1)) + list(range(NT - 1, NCT - 1, -1))
            for i in range(NT):
                run_pair(step(order_f[i], 0), step(order_b[i], 1))
            ph.close()

        def phaseD1a():
            ph = Phase()
            cst = load_consts(ph)
            Wo = ph.tile([128, 11 * D], BF16, "Wo1")
            load_weight_bf16(ph, Wo, w_out1, 1408, D)
            hnw = bc_load(ph, hgnw[0:1, :], 128, "hnw")
            g1 = bc_load(ph, modv[1, 0:1, 2 * D:3 * D], D, "g1")
            gw = ph.tile([128, 3 * 384], F32, "gw")
            S.dma('sp', gw[:, :].rearrange("p (c n) -> p c n", c=3), gluw[:, :].rearrange("(c p) n -> p c n", p=128), w=[gw])
            gb = ph.tile([128, 3], F32, "gb")
            S.dma('sp', gb[:, :], glub[:, :], w=[gb])
            epsb = ph.tile([128, 1], F32, "epsb")
            S.op('dve', lambda e: e.memset(epsb[:, :], EPS), w=[epsb])
            idb = ph.tile([128, 128], BF16, "idb")
            S.op('dve', lambda e: e.tensor_copy(out=idb[:, :], in_=cst[:, C_I:C_I + 128]), r=[cst], w=[idb])
            a = ph.tile([128, D], F32, "a")
            b = ph.tile([128, D], F32, "b")
            g = ph.tile([128, D], F32, "g")
            sq = ph.tile([128, D], F32, "sq")
            tmp = ph.tile([128, D], F32, "tmp")
            xt = ph.tile([128, D], F32, "xt")
            mix = ph.tile([128, 1024], BF16, "mix")
            mixT = ph.tile([128, 11 * 128], BF16, "mixT")
            ss = ph.tile([128, 16], F32, "ss")
            rstd = ph.tile([128, 16], F32, "rstd")
            xo = ph.tile([128, D], F32, "xo")
            ya = ph.tile([128, 384], F32, "ya")
            yb = ph.tile([128, 384], F32, "yb")
            yc = ph.tile([128, 384], F32, "yc")
            for m in lat_tiles:
                r0 = m * 128
                S.dma('sp', a[:, :], odir[0][r0:r0 + 128, :], w=[a])
                S.dma('act', b[:, :], odir[1][r0:r0 + 128, :], w=[b])
                S.dma('sp', g[:, :], rs[r0:r0 + 128, :], w=[g])
                S.op('dve', lambda e: e.tensor_tensor(out=a[:, :], in0=a[:, :], in1=b[:, :], op=ALU.add), r=[a, b], w=[a])
                S.op('act', lambda e: e.activation(out=g[:, :], in_=g[:, :], func=AF.Silu), r=[g], w=[g])
                head_rms(ph, a, mix, mix[:, 0:1024], 8, 128, hnw, hnw[:, 0:128].unsqueeze(1).to_broadcast([128, 8, 128]), g, sq, ss, rstd, epsb, tmp)
                for q in range(2):
                    bk = banks[q]
                    bkb = bk[:, 0:256].bitcast(BF16)
                    for c in range(4):
                        cc = q * 4 + c
                        S.op('pe', lambda e, bkb=bkb, c=c, cc=cc: e.transpose(bkb[:, c * 128:(c + 1) * 128], mix[:, cc * 128:(cc + 1) * 128], idb[:, :]), r=[mix, idb], w=[bk])
                    S.op('act', lambda e, bkb=bkb, q=q: e.activation(out=mixT[:, q * 512:(q + 1) * 512], in_=bkb[:, 0:512], func=AF.Copy), r=[bk], w=[mixT])
                S.dma('sp', ya[:, :].rearrange("p (c t) -> p c t", c=3), yT5[0][:, r0:r0 + 128].rearrange("(c p) t -> p c t", p=128), w=[ya])
                S.dma('act', yb[:, :].rearrange("p (c t) -> p c t", c=3), yT5[1][:, r0:r0 + 128].rearrange("(c p) t -> p c t", p=128), w=[yb])
                S.op('dve', lambda e: e.tensor_tensor(out=ya[:, :], in0=ya[:, :], in1=yb[:, :], op=ALU.add), r=[ya, yb], w=[ya])
                S.op('dve', lambda e: e.tensor_tensor(out=yb[:, :], in0=ya[:, :], in1=ya[:, :], op=ALU.mult), r=[ya], w=[yb])
                S.op('dve', lambda e: e.tensor_scalar(out=yb[:, :], in0=yb[:, :], scalar1=0.044715, scalar2=1.0, op0=ALU.mult, op1=ALU.add), r=[yb], w=[yb])
                S.op('dve', lambda e: e.tensor_tensor(out=yb[:, :], in0=yb[:, :], in1=ya[:, :], op=ALU.mult), r=[yb, ya], w=[yb])
                S.op('act', lambda e: e.activation(out=yb[:, :], in_=yb[:, :], func=AF.Tanh, scale=0.7978845608028654), r=[yb], w=[yb])
                S.op('dve', lambda e: e.scalar_tensor_tensor(out=yb[:, :], in0=yb[:, :], scalar=1.0, in1=ya[:, :], op0=ALU.add, op1=ALU.mult), r=[yb, ya], w=[yb])
                S.op('dve', lambda e: e.tensor_scalar(out=yb[:, :], in0=yb[:, :], scalar1=0.5, scalar2=None, op0=ALU.mult), r=[yb], w=[yb])
                b2 = banks[2]
                for co in range(3):
                    for ci in range(3):
                        S.op('pe', lambda e, co=co, ci=ci: e.matmul(b2[:, co * 128:(co + 1) * 128], lhsT=gw[:, ci * 384 + co * 128:ci * 384 + (co + 1) * 128], rhs=yb[:, ci * 128:(ci + 1) * 128], start=(ci == 0), stop=(ci == 2)), r=[gw, yb], w=[b2])
                for co in range(3):
                    S.op('act', lambda e, co=co: e.activation(out=yc[:, co * 128:(co + 1) * 128], in_=b2[:, co * 128:(co + 1) * 128], func=AF.Sigmoid, bias=gb[:, co:co + 1]), r=[b2, gb], w=[yc])
                S.op('dve', lambda e: e.tensor_tensor(out=mixT[:, 1024:1408], in0=yb[:, :], in1=yc[:, :], op=ALU.mult), r=[yb, yc], w=[mixT])
                S.dma('sp', xt[:, :], xsrc(1, m), w=[xt])
                for h2 in range(2):
                    bk = banks[4 + h2]
                    for c in range(11):
                        S.op('pe', lambda e, bk=bk, c=c, h2=h2: e.matmul(bk[:, 0:512], lhsT=mixT[:, c * 128:(c + 1) * 128], rhs=Wo[:, c * D + h2 * 512:c * D + (h2 + 1) * 512], start=(c == 0), stop=(c == 10)), r=[mixT, Wo], w=[bk])
                    S.op('dve', lambda e, bk=bk, h2=h2: e.tensor_tensor(out=xo[:, h2 * 512:(h2 + 1) * 512], in0=bk[:, 0:512], in1=g1[:, h2 * 512:(h2 + 1) * 512], op=ALU.mult), r=[bk, g1], w=[xo])
                S.op('dve', lambda e: e.tensor_tensor(out=xo[:, :], in0=xo[:, :], in1=xt[:, :], op=ALU.add), r=[xo, xt], w=[xo])
                S.dma('pool', xmid[r0:r0 + 128, :], xo[:, :], r=[xo])
            ph.close()

        lat_tiles = list(range(NCT, NT))
        all_tiles = list(range(NT))
        plist = ['A0', 'B0', 'C0', 'D0a', 'MLP0', 'A1', 'B1', 'C1', 'D1a', 'MLP1']
        nph = len(plist) if upto is None else plist.index(upto) + 1 if upto in plist else 0
        if nph >= 1:
            phaseA0()
        if nph >= 2:
            phaseB0()
        if nph >= 3 and os.environ.get('KGLA', '1') == '1':
            gla_phase(8, 64, 128, qT, [kTs[0], kTs[0]], [ktoks[0], ktoks[0]], vtok, lgs, odir)
        if nph >= 4:
            phaseD0a()
        if nph < 5:
            pass
        elif depth == 1:
            phaseMLP(0, lat_tiles, True)
        else:
            phaseMLP(0, all_tiles, False)
            if nph >= 6:
                phaseA1()
            if nph >= 7:
                gla_phase(8, 128, 128, qT, kTs, ktoks, vtok, lgs, odir)
            if nph >= 8:
                phaseC1()
            if nph >= 9:
                phaseD1a()
            if nph >= 10:
                phaseMLP(1, lat_tiles, True)
        S.barrier()
    return nc


def make_consts():
    i = np.arange(128)
    t = i[:, None]
    j = i[None, :]
    eye = (t == j)
    TF = (t <= j)
    TB = (t >= j)
    SF = (t > j)
    SB = (t < j)
    MBF = np.where(j >= t, 0.0, NEG)
    MBB = np.where(j <= t, 0.0, NEG)
    ONE = np.ones((128, 128))
    w = np.arange(258)
    ML = np.broadcast_to((w % 64 != 0).astype(np.float32)[None, :], (128, 258))
    MR = np.broadcast_to((w % 64 != 1).astype(np.float32)[None, :], (128, 258))
    return np.concatenate([eye, TF, TB, SF, SB, MBF, MBB, ONE, ML, MR], axis=1).astype(np.float32)


def host_inputs(inp, b, depth=2):
    f = lambda a: np.ascontiguousarray(np.asarray(a, dtype=np.float32))
    m = {}
    m["x"] = f(inp["x"][b])
    m["ctx"] = f(inp["ctx"][b])
    cs = np.stack([np.asarray(inp["c"][b]).reshape(8, 128).T, np.asarray(inp["c_ctx"]).reshape(8, 128).T], axis=-1)
    m["cs"] = f(cs)
    m["ada_w"] = f(inp["ada_w"])
    m["ada_b"] = f(inp["ada_b"])
    m["norm1_w"] = f(inp["norm1_w"])
    m["norm2_w"] = f(inp["norm2_w"])
    m["final_norm_w"] = f(np.asarray(inp["final_norm_w"]).reshape(1, D))
    m["consts"] = make_consts()
    m["w_in0"] = f(inp["ssd_gla_w_in"][0])
    cw = np.asarray(inp["ssd_conv_w"][0]).reshape(9, 16, 128)
    m["convw"] = f(cw.transpose(2, 1, 0).reshape(128, 144))
    m["convb"] = f(np.asarray(inp["ssd_conv_b"][0]).reshape(16, 128).T)
    m["dtb"] = f(np.asarray(inp["ssd_dt_bias"][0]).reshape(1, 32))
    m["alog"] = f(np.asarray(inp["ssd_a_log"][0]).reshape(1, 32))
    m["ssdd"] = f(np.asarray(inp["ssd_d"][0]).reshape(1, 16))
    m["ssdnw"] = f(np.asarray(inp["ssd_norm_w"][0]).reshape(1, 1024))
    wgm = np.zeros((33, 1024), np.float32)
    gw = np.asarray(inp["gla_gate_w"][0])
    wgm[0:16, 0:512] = gw[0]
    wgm[16:32, 512:1024] = gw[1]
    wgm[32, :] = np.asarray(inp["gla_gate_b"][0]).reshape(1024)
    m["wg"] = wgm
    m["glanw"] = f(np.asarray(inp["gla_norm_w"][0]).reshape(1, 128))
    m["w_out0"] = f(inp["ssd_gla_w_out"][0])
    m["mlp_w1"] = f(inp["mlp_w1"])
    m["mlp_w2"] = f(inp["mlp_w2"])
    if depth > 1:
        m["w_in1"] = f(inp["hgrn_s5_w_in"][0])
        lb = np.asarray(inp["hgrn_lb_logits"], dtype=np.float32)
        m["lbl"] = f(lb.reshape(2, 2048))
        m["lblT"] = f(np.stack([lb[l].reshape(16, 128).T for l in range(2)], axis=1))
        m["hgnw"] = f(np.asarray(inp["hgrn_norm_w"][0]).reshape(1, 128))
        prm = []
        for d in range(2):
            prm.append(np.asarray(inp["s5_a_re"][0][d]).reshape(12, 128).T)
            prm.append(np.asarray(inp["s5_a_im"][0][d]).reshape(12, 128).T)
            prm.append(np.repeat(np.asarray(inp["s5_log_dt"][0][d]).reshape(12, 2, 1), 64, axis=2).reshape(12, 128).T)
        m["s5p"] = f(np.concatenate(prm, axis=1))
        sb = lambda a: np.asarray(a).reshape(12, 128, 16).transpose(1, 0, 2).reshape(128, 192)
        m["s5b"] = f(np.concatenate([sb(inp["s5_b_re"][0]), sb(inp["s5_b_im"][0])], axis=1))
        scf = lambda a: np.asarray(a).reshape(12, 2, 16, 64).transpose(1, 3, 0, 2).reshape(128, 192)
        m["s5c"] = f(np.concatenate([scf(inp["s5_c_re"][0]), scf(inp["s5_c_im"][0])], axis=1))
        m["s5d"] = f(np.asarray(inp["s5_d"][0]).reshape(3, 128).T)
        m["gluw"] = f(inp["s5_glu_w"][0])
        m["glub"] = f(np.asarray(inp["s5_glu_b"][0]).reshape(3, 128).T)
        m["w_out1"] = f(inp["hgrn_s5_w_out"][0])
    return m


_NC_CACHE = {}


def kernel(**inputs):
    B, SEQ = inputs["x"].shape[0], inputs["x"].shape[1]
    key = (SEQ, 2)
    if key not in _NC_CACHE:
        _NC_CACHE[key] = build_nc(SEQ, 2)
    nc = _NC_CACHE[key]
    in_maps = [host_inputs(inputs, b) for b in range(B)]
    res = run_bass_kernel_spmd(nc, in_maps, core_ids=list(range(B)))
    return np.stack([r["out"] for r in res.results], axis=0).astype(np.float32)
```
